# Optimizing a Trainium2 kernel written in Bass

```python
import jax, jax.numpy as jnp
from jax import lax
import numpy as np

D_MODEL = 1024
BATCH = 1
SEQ = 16384
DEPTH = 4

RET_HEADS = 4
RET_DK = 64
RET_DV = 128
RET_CHUNK = 128
RET_THETA = 10000.0
DSA_HEADS = 8
DSA_KV_HEADS = 2
DSA_DH = 64
DSA_ROT = DSA_DH // 4
ROPE_THETA = 500000.0
IDX_HEADS = 4
IDX_DH = 64
IDX_ROT = IDX_DH // 4
TOPK_MAX = 256
Q_BLOCK = 128
GLA_HEADS = 4
GLA_DK = 64
GLA_DV = 128
GLA_RANK = 16
GLA_TAU = 16.0
GLA_CHUNK = 64

N_BRANCH = 3
RMS_EPS = 1e-6
RET_W = RET_HEADS * RET_DV
DSA_W = DSA_HEADS * DSA_DH
GLA_W = GLA_HEADS * GLA_DV

IN_SPLITS = (
    ("ret_q", RET_HEADS * RET_DK), ("ret_k", RET_HEADS * RET_DK),
    ("ret_v", RET_W), ("ret_g", RET_W),
    ("dsa_q", DSA_W), ("dsa_k", DSA_KV_HEADS * DSA_DH), ("dsa_v", DSA_KV_HEADS * DSA_DH),
    ("dsa_g", DSA_W),
    ("idx_q", IDX_HEADS * IDX_DH), ("idx_k", IDX_DH), ("idx_w", IDX_HEADS),
    ("gla_q", GLA_HEADS * GLA_DK), ("gla_k", GLA_HEADS * GLA_DK),
    ("gla_v", GLA_W), ("gla_g", GLA_W), ("gla_a", GLA_RANK),
    ("merge", N_BRANCH * D_MODEL),
)
IN_WIDTH = sum(w for _, w in IN_SPLITS)

kernel_name = "hybrid_retention_dsa_gla_gated_merge"


def rms_norm(x, gain=None):
    xf = x.astype(jnp.float32)
    y = xf * lax.rsqrt(jnp.mean(xf * xf, axis=-1, keepdims=True) + RMS_EPS)
    if gain is not None:
        y = y * gain.astype(jnp.float32)
    return y.astype(x.dtype)


def split_proj(p):
    bounds = np.cumsum([w for _, w in IN_SPLITS])[:-1].tolist()
    parts = jnp.split(p, bounds, axis=-1)
    return {name: t for (name, _), t in zip(IN_SPLITS, parts)}


def rope(x, pos, rot_dim, theta):
    half = rot_dim // 2
    freqs = theta ** (-jnp.arange(half, dtype=jnp.float32) * 2.0 / rot_dim)
    ang = pos.astype(jnp.float32)[..., None] * freqs
    cos = jnp.cos(ang)[:, :, None, :]
    sin = jnp.sin(ang)[:, :, None, :]
    xf = x.astype(jnp.float32)
    x1 = xf[..., :half]
    x2 = xf[..., half:rot_dim]
    out = jnp.concatenate([x1 * cos - x2 * sin, x2 * cos + x1 * sin, xf[..., rot_dim:]], axis=-1)
    return out.astype(x.dtype)


def retention(q, k, v, pos):
    dt = v.dtype
    B, S, H, dk = q.shape
    dv = v.shape[-1]
    C = RET_CHUNK
    n = S // C
    q = rope(q, pos, dk, RET_THETA).astype(jnp.float32) * (dk ** -0.5)
    k = rope(k, pos, dk, RET_THETA).astype(jnp.float32)
    v = v.astype(jnp.float32)
    log_g = jnp.log1p(-jnp.exp2(-5.0 - jnp.arange(H, dtype=jnp.float32)))
    idx = jnp.arange(C, dtype=jnp.float32)
    rel = idx[:, None] - idx[None, :]
    decay = jnp.where(rel >= 0, jnp.exp(log_g[:, None, None] * jnp.maximum(rel, 0.0)), 0.0)
    qc = q.reshape(B, n, C, H, dk)
    kc = k.reshape(B, n, C, H, dk)
    vc = v.reshape(B, n, C, H, dv)
    scores = jnp.einsum('bnihd,bnjhd->bnhij', qc, kc) * decay
    o_intra = jnp.einsum('bnhij,bnjhe->bnihe', scores, vc)
    to_end = jnp.exp((C - 1 - idx)[:, None] * log_g[None, :])
    kv = jnp.einsum('bnjhd,bnjhe->bnhde', kc * to_end[None, None, :, :, None], vc)
    chunk_decay = jnp.exp(C * log_g)[None, :, None, None]

    def step(state, kv_n):
        return chunk_decay * state + kv_n, state

    _, prev = lax.scan(step, jnp.zeros((B, H, dk, dv), jnp.float32), jnp.moveaxis(kv, 1, 0))
    prev = jnp.moveaxis(prev, 0, 1)
    q_dec = qc * jnp.exp((idx + 1.0)[:, None] * log_g[None, :])[None, None, :, :, None]
    o_cross = jnp.einsum('bnihd,bnhde->bnihe', q_dec, prev)
    return (o_intra + o_cross).reshape(B, S, H, dv).astype(dt)


def gla(q, k, v, log_a):
    dt = v.dtype
    B, S, H, dk = q.shape
    dv = v.shape[-1]
    C = GLA_CHUNK
    n = S // C
    q = q.astype(jnp.float32) * (dk ** -0.5)
    k = k.astype(jnp.float32)
    v = v.astype(jnp.float32)
    log_a = log_a.astype(jnp.float32)
    causal = jnp.tril(jnp.ones((C, C), bool))

    def to_chunks(t):
        return jnp.moveaxis(t.reshape(B, n, C, H, t.shape[-1]), 1, 0)

    def step(state, inp):
        qn, kn, vn, an = inp
        bcum = jnp.cumsum(an, axis=1)
        diff = bcum[:, :, None] - bcum[:, None, :]
        w = jnp.exp(jnp.where(causal[None, :, :, None, None], diff, -jnp.inf))
        attn = jnp.einsum('bihd,bjhd,bijhd->bhij', qn, kn, w)
        o = (jnp.einsum('bhij,bjhe->bihe', attn, vn)
             + jnp.einsum('bihd,bhde->bihe', qn * jnp.exp(bcum), state))
        b_last = bcum[:, -1]
        new = (jnp.exp(b_last)[..., None] * state
               + jnp.einsum('bjhd,bjhe->bhde', kn * jnp.exp(b_last[:, None] - bcum), vn))
        return new, o

    _, o = lax.scan(step, jnp.zeros((B, H, dk, dv), jnp.float32),
                    (to_chunks(q), to_chunks(k), to_chunks(v), to_chunks(log_a)))
    return jnp.moveaxis(o, 0, 1).reshape(B, S, H, dv).astype(dt)


def dsa_attention(q, k, v, q_idx, k_idx, w_idx, pos):
    B, S, H, dh = q.shape
    KV = k.shape[2]
    G = H // KV
    topk = min(TOPK_MAX, S // 4)
    QB = Q_BLOCK
    nb = S // QB
    q = rope(q, pos, DSA_ROT, ROPE_THETA)
    k = rope(k, pos, DSA_ROT, ROPE_THETA)
    q_idx = rope(q_idx, pos, IDX_ROT, ROPE_THETA)
    k_idx_f = rope(k_idx[:, :, None], pos, IDX_ROT, ROPE_THETA)[:, :, 0].astype(jnp.float32)
    w_f = w_idx.astype(jnp.float32) * (IDX_HEADS ** -0.5)
    key_pos = jnp.arange(S)

    def blockify(t):
        return jnp.moveaxis(t.reshape(B, nb, QB, *t.shape[2:]), 1, 0)

    def one_block(inp):
        qb, qib, wb, tb = inp
        s = jnp.einsum('bqhd,bsd->bqhs', qib.astype(jnp.float32), k_idx_f) * (IDX_DH ** -0.5)
        score = jnp.einsum('bqh,bqhs->bqs', wb, jax.nn.relu(s))
        visible = key_pos[None, :] <= tb[:, None]
        score = jnp.where(visible[None], score, -jnp.inf)
        _, sel = lax.top_k(score, topk)
        valid = sel <= tb[None, :, None]
        ks = jax.vmap(lambda kk, ii: kk[ii])(k, sel)
        vs = jax.vmap(lambda vv, ii: vv[ii])(v, sel)
        qg = qb.reshape(B, QB, KV, G, dh)
        logits = jnp.einsum('bqngd,bqknd->bqngk', qg, ks).astype(jnp.float32) * (dh ** -0.5)
        logits = jnp.where(valid[:, :, None, None, :], logits, -jnp.inf)
        p = jax.nn.softmax(logits, axis=-1).astype(vs.dtype)
        o = jnp.einsum('bqngk,bqknd->bqngd', p, vs)
        return o.reshape(B, QB, H * dh)

    t_blocks = jnp.arange(S).reshape(nb, QB)
    out = lax.map(one_block, (blockify(q), blockify(q_idx), blockify(w_f), t_blocks))
    return jnp.moveaxis(out, 0, 1).reshape(B, S, H * dh)


def setup_inputs(seed: int = 0) -> dict:
    key = jax.random.key(seed)
    ks = jax.random.split(key, 16)
    D = D_MODEL
    nrm = jax.random.normal
    x = nrm(ks[0], (BATCH, SEQ, D), jnp.float32)
    c = nrm(ks[1], (BATCH, D), jnp.float32)
    positions = (jnp.arange(SEQ, dtype=jnp.int32)[None, :]
                 + jax.random.randint(ks[2], (BATCH, 1), 0, 4096, dtype=jnp.int32))
    ada_w = nrm(ks[3], (DEPTH, D, 3 * D), jnp.float32) * (0.1 * D ** -0.5)
    ada_b = nrm(ks[4], (DEPTH, 3 * D), jnp.float32) * 0.02
    pre_norm = 1.0 + 0.05 * nrm(ks[5], (DEPTH, D), jnp.float32)
    post_norm = 1.0 + 0.05 * nrm(ks[6], (DEPTH, D), jnp.float32)
    w_in = nrm(ks[7], (DEPTH, D, IN_WIDTH), jnp.float32) * (D ** -0.5)
    gla_w_lr = nrm(ks[8], (DEPTH, GLA_RANK, GLA_HEADS * GLA_DK), jnp.float32) * (GLA_RANK ** -0.5)
    gla_b_lr = 0.1 * nrm(ks[9], (DEPTH, GLA_HEADS * GLA_DK), jnp.float32)
    w_br_ret = nrm(ks[10], (DEPTH, RET_W, D), jnp.float32) * (RET_W ** -0.5)
    w_br_dsa = nrm(ks[11], (DEPTH, DSA_W, D), jnp.float32) * (DSA_W ** -0.5)
    w_br_gla = nrm(ks[12], (DEPTH, GLA_W, D), jnp.float32) * (GLA_W ** -0.5)
    w_out = nrm(ks[13], (DEPTH, D, D), jnp.float32) * (D ** -0.5)
    return {"x": x, "c": c, "positions": positions, "ada_w": ada_w, "ada_b": ada_b,
            "pre_norm": pre_norm, "post_norm": post_norm, "w_in": w_in,
            "gla_w_lr": gla_w_lr, "gla_b_lr": gla_b_lr, "w_br_ret": w_br_ret,
            "w_br_dsa": w_br_dsa, "w_br_gla": w_br_gla, "w_out": w_out}


def reference(x, c, positions, ada_w, ada_b, pre_norm, post_norm, w_in, gla_w_lr, gla_b_lr,
              w_br_ret, w_br_dsa, w_br_gla, w_out):
    B, S, D = x.shape
    c_act = jax.nn.silu(c)
    for l in range(DEPTH):
        mod = c_act @ ada_w[l] + ada_b[l]
        shift, scale, gate = jnp.split(mod, 3, axis=-1)
        h = rms_norm(x, pre_norm[l]) * (1.0 + scale[:, None]) + shift[:, None]
        p = split_proj(h @ w_in[l])
        ret = retention(p["ret_q"].reshape(B, S, RET_HEADS, RET_DK),
                        p["ret_k"].reshape(B, S, RET_HEADS, RET_DK),
                        p["ret_v"].reshape(B, S, RET_HEADS, RET_DV), positions)
        ret = rms_norm(ret).reshape(B, S, RET_W) * jax.nn.silu(p["ret_g"])
        dsa = dsa_attention(p["dsa_q"].reshape(B, S, DSA_HEADS, DSA_DH),
                            p["dsa_k"].reshape(B, S, DSA_KV_HEADS, DSA_DH),
                            p["dsa_v"].reshape(B, S, DSA_KV_HEADS, DSA_DH),
                            p["idx_q"].reshape(B, S, IDX_HEADS, IDX_DH),
                            p["idx_k"], p["idx_w"], positions)
        dsa = dsa * jax.nn.silu(p["dsa_g"])
        log_a = jax.nn.log_sigmoid((p["gla_a"] @ gla_w_lr[l] + gla_b_lr[l]).astype(jnp.float32)) / GLA_TAU
        gl = gla(p["gla_q"].reshape(B, S, GLA_HEADS, GLA_DK),
                 p["gla_k"].reshape(B, S, GLA_HEADS, GLA_DK),
                 p["gla_v"].reshape(B, S, GLA_HEADS, GLA_DV),
                 log_a.reshape(B, S, GLA_HEADS, GLA_DK))
        gl = rms_norm(gl).reshape(B, S, GLA_W) * jax.nn.silu(p["gla_g"])
        g = jax.nn.sigmoid(p["merge"]).reshape(B, S, N_BRANCH, D)
        y = (g[:, :, 0] * (ret @ w_br_ret[l])
             + g[:, :, 1] * (dsa @ w_br_dsa[l])
             + g[:, :, 2] * (gl @ w_br_gla[l]))
        y = y @ w_out[l]
        x = x + gate[:, None] * rms_norm(y, post_norm[l])
    return x
```

```python
import numpy as np
from contextlib import ExitStack
import concourse.bass as bass
import concourse.mybir as mybir
from concourse.bass_utils import run_bass_kernel_spmd

F32 = mybir.dt.float32
BF16 = mybir.dt.bfloat16
I32 = mybir.dt.int32
ALU = mybir.AluOpType
AF = mybir.ActivationFunctionType
AX = mybir.AxisListType

EPOCH = 20000
NDMA_SEM = 12


class Buf:
    def __init__(self, name, handle=None):
        self.name = name
        self.h = handle
        self.last_w = None
        self.readers = []

    def ap(self):
        return self.h[:]


class _Recorder:
    def __init__(self):
        self.call = None

    def __getattr__(self, name):
        def f(*a, **k):
            assert self.call is None
            self.call = (name, a, k)
            return None
        return f


class Prog:
    ENGS = ("tensor", "vector", "scalar", "gpsimd", "sync")

    def __init__(self, nc):
        self.nc = nc
        self.stack = ExitStack()
        self.stream = {e: [] for e in self.ENGS}
        self.count = {e: 0 for e in self.ENGS}
        self.known = {e: {} for e in self.ENGS}
        self.dma_n = {}
        self.dma_rr = {e: 0 for e in self.ENGS}
        self.nbuf = 0
        self.pending = {e: [] for e in self.ENGS}
        self.stacks = [self.stack]

    def sb(self, name, shape, dtype):
        self.nbuf += 1
        name = f"{name}_u{self.nbuf}"
        h = self.stacks[-1].enter_context(self.nc.sbuf_tensor(name, list(shape), dtype))
        return Buf(name, h)

    def push_scope(self):
        st = ExitStack()
        self.stacks.append(st)

    def pop_scope(self):
        self.barrier()
        self.stacks.pop().close()

    def barrier(self):
        toks = []
        for e in self.ENGS:
            c = self.count[e]
            if c > 0 and e != "sync":
                toks.append((("E", e, (c - 1) // EPOCH), (c - 1) % EPOCH + 1))
        for key, n in self.dma_n.items():
            toks.append((key, 16 * n))
        for e in self.ENGS:
            self.pending[e] = list(toks)

    def _take_pending(self, eng, waits):
        kn = self.known[eng]
        for key, val in self.pending[eng]:
            if key[0] == "E" and key[1] == eng:
                continue
            if kn.get(key, -1) >= val:
                continue
            if any(k == key and v >= val for k, v in waits):
                continue
            kn[key] = val
            waits.append((key, val))
        self.pending[eng] = []

    def ps(self, name, shape, dtype=F32):
        h = self.stack.enter_context(self.nc.psum_tensor(name, list(shape), dtype))
        return Buf(name, h)

    def alias(self, name, buf):
        raise NotImplementedError

    def _deps(self, eng, reads, writes, is_dma):
        deps = {}

        def add(tok, same_ok):
            if tok is None:
                return
            key, val = tok
            if (not is_dma) and key[0] == "E" and key[1] == eng and not same_ok:
                return
            if deps.get(key, -1) < val:
                deps[key] = val

        for b in reads:
            add(b.last_w, True)
        for b in writes:
            add(b.last_w, False)
            for t in b.readers:
                add(t, False)
        kn = self.known[eng]
        out = []
        for key, val in deps.items():
            if key[0] == "E":
                later = [k for k in kn if k[0] == "E" and k[1] == key[1] and k[2] > key[2]]
                if later:
                    continue
            if kn.get(key, -1) >= val:
                continue
            kn[key] = val
            out.append((key, val))
        return out

    def _commit(self, tok, reads, writes):
        for b in writes:
            b.last_w = tok
            b.readers = []
        for b in reads:
            if b in writes:
                continue
            rs = [t for t in b.readers if t[0] != tok[0]]
            rs.append(tok)
            b.readers = rs

    def op(self, eng, fn, reads=(), writes=()):
        reads = list(reads)
        writes = list(writes)
        waits = self._deps(eng, reads, writes, False)
        self._take_pending(eng, waits)
        self.count[eng] += 1
        c = self.count[eng]
        key = ("E", eng, (c - 1) // EPOCH)
        tok = (key, (c - 1) % EPOCH + 1)
        rec = _Recorder()
        fn(rec)
        self.stream[eng].append((waits, rec.call, key))
        self._commit(tok, reads, writes)
        return tok

    def dma(self, out_ap, in_ap, reads=(), writes=(), queue="sync", **kw):
        reads = list(reads)
        writes = list(writes)
        i = self.dma_rr[queue]
        self.dma_rr[queue] = (i + 1) % NDMA_SEM
        key = ("D", queue, i)
        n = self.dma_n.get(key, 0)
        waits = self._deps(queue, reads, writes, True)
        self._take_pending(queue, waits)
        if n > 0 and self.known[queue].get(key, -1) < 16 * n:
            self.known[queue][key] = 16 * n
            waits.append((key, 16 * n))
        self.dma_n[key] = n + 1
        tok = (key, 16 * (n + 1))
        kk = dict(kw)
        kk["out"] = out_ap
        kk["in_"] = in_ap
        self.stream[queue].append((waits, ("dma_start", (), kk), key))
        self._commit(tok, reads, writes)
        return tok

    def finish(self):
        nc = self.nc
        sems = {}

        def sem(key):
            if key not in sems:
                nm = "s_" + "_".join(str(k) for k in key)
                sems[key] = self.stack.enter_context(nc.semaphore(nm))
            return sems[key]

        final_waits = []
        for key, n in self.dma_n.items():
            final_waits.append((key, 16 * n))
        for eng in self.ENGS:
            for waits, fn, key in self.stream[eng]:
                sem(key)
                for k, v in waits:
                    sem(k)

        streams = self.stream

        def emit(eng_name):
            def body(e):
                for waits, fn, key in streams[eng_name]:
                    for k, v in waits:
                        e.wait_ge(sems[k], v)
                    ins = getattr(e, fn[0])(*fn[1], **fn[2])
                    ins.then_inc(sems[key], 16 if key[0] == "D" else 1)
                if eng_name == "sync":
                    for k, v in final_waits:
                        e.wait_ge(sems[k], v)
            return body

        with nc.Block() as block:
            block.sync(emit("sync"))
            block.tensor(emit("tensor"))
            block.vector(emit("vector"))
            block.scalar(emit("scalar"))
            block.gpsimd(emit("gpsimd"))
        while self.stacks:
            self.stacks.pop().close()


D = 1024
IN_SPLITS = (("ret_q", 256), ("ret_k", 256), ("ret_v", 512), ("ret_g", 512),
             ("dsa_q", 512), ("dsa_k", 128), ("dsa_v", 128), ("dsa_g", 512),
             ("idx_q", 256), ("idx_k", 64), ("idx_w", 4),
             ("gla_q", 256), ("gla_k", 256), ("gla_v", 512), ("gla_g", 512), ("gla_a", 16),
             ("merge", 3072))
OFF = {}
_o = 0
for _n, _w in IN_SPLITS:
    OFF[_n] = _o
    _o += _w
IN_WIDTH = _o
RMS_EPS = 1e-6
TOPK = 256
BIGM = 32768.0
LOG_G = [float(np.log1p(-2.0 ** (-5.0 - h))) for h in range(4)]


class Cfg:
    def __init__(self, S=16384, NCORE=8, DEPTH=4, SG=2, NIT=22):
        self.S, self.NCORE, self.DEPTH = S, NCORE, DEPTH
        self.TPC = S // 128 // NCORE
        self.T = self.TPC * 128
        self.GK = NCORE * 128
        self.BLK = min(512, self.GK)
        self.NB = self.BLK // 128
        self.BPG = self.GK // self.BLK
        self.SG = min(SG, self.TPC)
        self.NIT = NIT
        self.TOPK = min(256, S // 4)


def host_consts():
    f = np.float32
    j = np.arange(128)[:, None]
    i = np.arange(128)[None, :]
    c = {}
    c["ident"] = np.eye(128, dtype=f)
    same = (j // 64) == (i // 64)
    c["triL128"] = (j <= i).astype(f)
    c["triL64"] = ((j <= i) & same).astype(f)
    c["triU64"] = ((j > i) & same).astype(f)
    ci = np.zeros((128, 4), f)
    ci[:64, 0] = 1
    ci[64:, 1] = 1
    ci[:, 2] = 1
    c["chunkind"] = ci
    lg = np.array(LOG_G, np.float64)
    dt = np.zeros((128, 4, 128), np.float64)
    for h in range(4):
        dt[:, h, :] = np.where(i >= j, np.exp(lg[h] * np.maximum(i - j, 0)), 0.0)
    c["DTret"] = (dt * 0.125).astype(f)
    c["DTgla"] = np.repeat(((j <= i) & same).astype(f)[:, None, :], 4, axis=1)
    qd = np.zeros((128, 4, 128), np.float64)
    for h in range(4):
        qd[:, h, :] = np.exp(lg[h] * (i + 1.0))
    c["QDret"] = (qd * 0.125).astype(f)
    kf = np.zeros((128, 4), np.float64)
    for h in range(4):
        kf[:, h] = np.exp(lg[h] * (127.0 - np.arange(128)))
    c["kfacR"] = kf.astype(f)
    dr = np.zeros((128, 4), np.float64)
    for h in range(4):
        dr[:, h] = np.exp(lg[h] * 128.0)
    c["decR"] = dr.astype(f)
    fr = np.zeros((128, 4), f)
    p = np.arange(128) % 64
    half = 32
    fr_ret = (np.float32(10000.0) ** (-(np.arange(half, dtype=f)) * f(2.0) / f(64))).astype(f)
    fr[:, 0] = fr_ret[p % 32]
    fr[:, 1] = np.where(p < 32, -1.0, 1.0)
    fr_d = (np.float32(500000.0) ** (-(np.arange(8, dtype=f)) * f(2.0) / f(16))).astype(f)
    fr[:, 2] = np.where(p < 16, fr_d[p % 8], 0.0)
    fr[:, 3] = np.where(p < 8, -1.0, np.where(p < 16, 1.0, 0.0))
    c["ropefs"] = fr
    return c


def swap_cols(lo, rot, width, nheads):
    idx = []
    for h in range(nheads):
        base = lo + h * width
        half = rot // 2
        for d in range(width):
            if d < half:
                idx.append(base + d + half)
            elif d < rot:
                idx.append(base + d - half)
            else:
                idx.append(base + d)
    return idx


A_COLS = [("retk", 256), ("retk_sw", 256), ("retv", 512), ("dsak", 128), ("dsak_sw", 128),
          ("dsav", 128), ("idxk", 64), ("idxk_sw", 64), ("glak", 256), ("glav", 512), ("glaa", 16)]


def col_layout(cols):
    off, o = {}, 0
    for n, w in cols:
        off[n] = (o, w)
        o += w
    return off, o


class Builder:
    def __init__(self, cfg, name):
        self.cfg = cfg
        self.nc = bass.Bass("TRN2", target_bir_lowering=False, name=name)
        self.P = Prog(self.nc)
        self.din = {}
        self.dout = {}
        self.bank_i = 0

    def inp(self, name, shape, dtype=F32):
        t = self.nc.dram_tensor(name, list(shape), dtype, kind="ExternalInput")
        self.din[name] = t
        return t.ap()

    def outp(self, name, shape, dtype=F32):
        t = self.nc.dram_tensor(name, list(shape), dtype, kind="ExternalOutput")
        self.dout[name] = t
        return t.ap()

    def make_banks(self):
        self.banks = [self.P.ps(f"bank{i}", [128, 512], F32) for i in range(8)]

    def bank(self):
        b = self.banks[self.bank_i % 8]
        self.bank_i += 1
        return b

    def load_consts(self, names):
        P = self.P
        hc = host_consts()
        self.c = {}
        self.cb = {}
        for n in names:
            shp = list(hc[n].shape)
            ap = self.inp("c_" + n, shp)
            t = P.sb("sc_" + n, shp, F32)
            P.dma(t.ap(), ap, writes=[t])
            self.c[n] = t
        self.identb = P.sb("identb", [128, 128], BF16)
        P.op("vector", lambda e: e.tensor_copy(self.identb.ap(), self.c["ident"].ap()),
             reads=[self.c["ident"]], writes=[self.identb])
        self.onesb = P.sb("onesb", [128, 128], BF16)
        P.op("vector", lambda e: e.memset(self.onesb.ap(), 1.0), writes=[self.onesb])

    def load_weight(self, dram_ap, c0, ncols, name, kc=8, rows=128):
        P = self.P
        wt = P.sb(name, [rows, kc, ncols], BF16)
        CH = 64 if hasattr(self, "wstage") else 128
        if not hasattr(self, "wstage"):
            self.wstage = [P.sb(f"wstage{i}", [128, 8, CH], F32) for i in range(2)]
            self.wstage_i = 0
        for s in range(0, ncols, CH):
            n = min(CH, ncols - s)
            st = self.wstage[self.wstage_i % 2]
            self.wstage_i += 1
            P.dma(st.ap()[0:rows, 0:kc, 0:n], dram_ap[:, :, c0 + s:c0 + s + n], writes=[st])
            P.op("gpsimd", lambda e, st=st, s=s, n=n: e.tensor_copy(
                wt.ap()[:, :, s:s + n], st.ap()[0:rows, 0:kc, 0:n]), reads=[st], writes=[wt])
        return wt

    def prealloc(self, ncore):
        P = self.P
        self.wstage = [P.sb(f"wstage{i}", [128, 8, 64], F32) for i in range(2)]
        self.wstage_i = 0
        self.rp = [P.sb(f"rp{i}", [128, 512], F32) for i in range(2)]
        self.a17 = P.sb("a17", [17, 128], BF16)
        P.op("vector", lambda e: e.memset(self.a17.ap(), 1.0), writes=[self.a17])
        self.e1 = P.sb("gl_e1", [128, 256], F32)
        self.sp = P.sb("gl_sp", [128, 256], F32)
        self.kvt = [P.sb(f"kvt{i}", [64, 512], F32) for i in range(2)]
        self.dcg_g = P.sb("dcg_g", [64, ncore, 4], F32)

    def emit_mod(self, cvec_ap, adaw_ap, adab_ap, pre_ap, post_ap, want_gate):
        P = self.P
        self.G = P.sb("G", [128, 1024], F32)
        self.Sh = P.sb("Sh", [128, 1024], F32)
        if want_gate:
            self.GP = P.sb("GP", [128, 1024], F32)
        P.push_scope()
        cv = P.sb("cv", [128, 8], F32)
        P.dma(cv.ap(), cvec_ap, writes=[cv])
        ca = P.sb("ca", [128, 8], F32)
        P.op("scalar", lambda e: e.activation(ca.ap(), cv.ap(), AF.Silu), reads=[cv], writes=[ca])
        cbc = P.sb("cbc", [128, 8, 128], F32)
        P.op("vector", lambda e: e.tensor_copy(cbc.ap(), ca.ap().unsqueeze(2).to_broadcast([128, 8, 128])),
             reads=[ca], writes=[cbc])
        mod = P.sb("mod", [128, 3072], F32)
        bias = P.sb("modb", [128, 3072], F32)
        P.dma(bias.ap(), adab_ap.to_broadcast([128, 3072]), writes=[bias])
        wst = [P.sb(f"adaw{i}", [128, 8, 512], F32) for i in range(2)]
        for ci in range(6):
            w = wst[ci % 2]
            P.dma(w.ap(), adaw_ap[:, :, ci * 512:(ci + 1) * 512], writes=[w])
            bk = self.bank()
            for kc in range(8):
                P.op("tensor", lambda e, bk=bk, w=w, kc=kc: e.matmul(
                    bk.ap(), cbc.ap()[:, kc, :], w.ap()[:, kc, :], start=(kc == 0), stop=(kc == 7)),
                    reads=[cbc, w], writes=[bk])
            P.op("vector", lambda e, bk=bk, ci=ci: e.tensor_tensor(
                mod.ap()[:, ci * 512:(ci + 1) * 512], bk.ap(), bias.ap()[:, ci * 512:(ci + 1) * 512], ALU.add),
                reads=[bk, bias], writes=[mod])
        pre = P.sb("preb", [128, 1024], F32)
        P.dma(pre.ap(), pre_ap.to_broadcast([128, 1024]), writes=[pre])
        P.op("vector", lambda e: e.scalar_tensor_tensor(
            self.G.ap(), mod.ap()[:, 1024:2048], 1.0, pre.ap(), ALU.add, ALU.mult),
            reads=[mod, pre], writes=[self.G])
        P.op("vector", lambda e: e.tensor_copy(self.Sh.ap(), mod.ap()[:, 0:1024]), reads=[mod], writes=[self.Sh])
        if want_gate:
            post = P.sb("postb", [128, 1024], F32)
            P.dma(post.ap(), post_ap.to_broadcast([128, 1024]), writes=[post])
            P.op("vector", lambda e: e.tensor_tensor(self.GP.ap(), mod.ap()[:, 2048:3072], post.ap(), ALU.mult),
                 reads=[mod, post], writes=[self.GP])
        P.pop_scope()

    def emit_ropes(self, pos_ap, specs):
        P = self.P
        outs = []
        for fcol, scol, name in specs:
            outs.append((P.sb(name + "_cos", [128, self.cfg.T], BF16), P.sb(name + "_sin", [128, self.cfg.T], BF16)))
        P.push_scope()
        for (fcol, scol, name), (cb_, sb_) in zip(specs, outs):
            self.emit_rope(pos_ap, fcol, scol, name, cb_, sb_)
        P.pop_scope()
        del self.posf
        return outs

    def emit_rope(self, pos_ap, fcol, scol, name, cosb, sinb):
        P = self.P
        T = self.cfg.T
        fs = self.c["ropefs"]
        if not hasattr(self, "posf"):
            posi = P.sb("posi", [128, T], I32)
            P.dma(posi.ap(), pos_ap.to_broadcast([128, T]), writes=[posi])
            self.posf = P.sb("posf", [128, T], F32)
            P.op("vector", lambda e: e.tensor_copy(self.posf.ap(), posi.ap()), reads=[posi], writes=[self.posf])
            self.rtmp = [P.sb(f"rtmp{i}", [128, T], F32) for i in range(4)]
        ang, kf, r, rd = self.rtmp
        posf = self.posf
        PI = float(np.pi)
        C1 = 6.28125
        C2 = float(2.0 * np.pi - 6.28125)
        MAG = 12582912.0
        P.op("vector", lambda e: e.tensor_scalar(ang.ap(), posf.ap(), fs.ap()[:, fcol:fcol + 1], None, ALU.mult),
             reads=[posf, fs], writes=[ang])

        def reduce_and_sin(src, dst_final, shift):
            dst = rd
            if shift != 0.0:
                P.op("vector", lambda e: e.tensor_scalar(r.ap(), src.ap(), shift, None, ALU.add),
                     reads=[src], writes=[r])
                s2 = r
            else:
                s2 = src
            P.op("vector", lambda e: e.tensor_scalar(kf.ap(), s2.ap(), float(1.0 / (2 * np.pi)), MAG, ALU.mult, ALU.add),
                 reads=[s2], writes=[kf])
            P.op("vector", lambda e: e.tensor_scalar(kf.ap(), kf.ap(), MAG, None, ALU.subtract),
                 reads=[kf], writes=[kf])
            P.op("vector", lambda e: e.scalar_tensor_tensor(dst.ap(), kf.ap(), -C1, s2.ap(), ALU.mult, ALU.add),
                 reads=[kf, s2], writes=[dst])
            P.op("vector", lambda e: e.scalar_tensor_tensor(dst.ap(), kf.ap(), -C2, dst.ap(), ALU.mult, ALU.add),
                 reads=[kf, dst], writes=[dst])
            P.op("vector", lambda e: e.tensor_scalar(dst.ap(), dst.ap(), PI, -PI, ALU.min, ALU.max),
                 reads=[dst], writes=[dst])
            P.op("scalar", lambda e: e.activation(dst_final.ap(), dst.ap(), AF.Sin), reads=[dst], writes=[dst_final])

        reduce_and_sin(ang, sinb, 0.0)
        reduce_and_sin(ang, cosb, float(np.pi / 2))
        P.op("vector", lambda e: e.tensor_scalar(sinb.ap(), sinb.ap(), fs.ap()[:, scol:scol + 1], None, ALU.mult),
             reads=[sinb, fs], writes=[sinb])
        return cosb, sinb

    def emit_norm_T(self, x_tile_ap_dram, hT, slot):
        P = self.P
        if not hasattr(self, "xin"):
            self.xin = [P.sb(f"xin{i}", [128, 1024], F32) for i in range(1)]
            self.xin_i = 0
            self.hjunk = P.sb("hjunk", [128, 1024], BF16)
            self.h1 = P.sb("h1", [128, 1024], F32)
            self.hb = P.sb("hb", [128, 1024], BF16)
            self.nst = P.sb("nst", [128, 4], F32)
        xt = self.xin[0]
        self.xin_i += 1
        nst = self.nst
        P.dma(xt.ap(), x_tile_ap_dram, writes=[xt])
        P.op("scalar", lambda e: e.activation(self.hjunk.ap(), xt.ap(), AF.Square, accum_out=nst.ap()[:, 0:1]),
             reads=[xt], writes=[self.hjunk, nst])
        P.op("vector", lambda e: e.tensor_scalar(nst.ap()[:, 1:2], nst.ap()[:, 0:1], 1.0 / D, RMS_EPS, ALU.mult, ALU.add),
             reads=[nst], writes=[nst])
        P.op("scalar", lambda e: e.activation(nst.ap()[:, 2:3], nst.ap()[:, 1:2], AF.Sqrt), reads=[nst], writes=[nst])
        P.op("vector", lambda e: e.reciprocal(nst.ap()[:, 3:4], nst.ap()[:, 2:3]), reads=[nst], writes=[nst])
        P.op("vector", lambda e: e.scalar_tensor_tensor(self.h1.ap(), xt.ap(), nst.ap()[:, 3:4], self.G.ap(), ALU.mult, ALU.mult),
             reads=[xt, nst, self.G], writes=[self.h1])
        P.op("gpsimd", lambda e: e.tensor_tensor(self.hb.ap(), self.h1.ap(), self.Sh.ap(), ALU.add),
             reads=[self.h1, self.Sh], writes=[self.hb])
        for half in range(2):
            bk = self.bank()
            for q in range(4):
                kc = half * 4 + q
                P.op("tensor", lambda e, bk=bk, q=q, kc=kc: e.matmul(
                    bk.ap()[:, q * 128:(q + 1) * 128], self.hb.ap()[:, kc * 128:(kc + 1) * 128], self.identb.ap(),
                    start=True, stop=True), reads=[self.hb, self.identb], writes=[bk])
            P.op("scalar", lambda e, bk=bk, half=half: e.activation(
                hT.ap()[:, slot, half * 4:half * 4 + 4, :], bk.ap().rearrange("p (a b) -> p a b", a=4), AF.Copy),
                reads=[bk], writes=[hT])

    def proj_fm(self, out_ap, bk, w, c0, m, hT, slot, nslots=1):
        P = self.P
        for kc in range(8):
            if nslots == 1:
                rhs = hT.ap()[:, slot, kc, :]
            else:
                rhs = hT.ap()[:, slot:slot + nslots, kc, :]
            P.op("tensor", lambda e, kc=kc, rhs=rhs: e.matmul(
                out_ap, w.ap()[:, kc, c0:c0 + m], rhs, start=(kc == 0), stop=(kc == 7)),
                reads=[w, hT], writes=[bk])

    def proj_tm(self, out_ap, bk, w, c0, n, hT, slot):
        P = self.P
        for kc in range(8):
            P.op("tensor", lambda e, kc=kc: e.matmul(
                out_ap, hT.ap()[:, slot, kc, :], w.ap()[:, kc, c0:c0 + n], start=(kc == 0), stop=(kc == 7)),
                reads=[w, hT], writes=[bk])

    def rope_fm(self, dst_ap, dst_buf, bx, bs, npart, ncols_ap, cosb, sinb, tok0, scale=None):
        P = self.P
        if not hasattr(self, "rp"):
            self.rp = [P.sb(f"rp{i}", [128, 512], F32) for i in range(2)]
        t0, t1 = self.rp
        nh = ncols_ap // 128
        cosv = cosb.ap()[0:npart, tok0:tok0 + 128].unsqueeze(1).to_broadcast([npart, nh, 128])
        sinv = sinb.ap()[0:npart, tok0:tok0 + 128].unsqueeze(1).to_broadcast([npart, nh, 128])
        v = lambda b: b.ap()[0:npart, 0:ncols_ap].rearrange("p (h t) -> p h t", h=nh)
        P.op("vector", lambda e: e.tensor_tensor(v(t0), v(bx), cosv, ALU.mult), reads=[bx, cosb], writes=[t0])
        P.op("vector", lambda e: e.tensor_tensor(v(t1), v(bs), sinv, ALU.mult), reads=[bs, sinb], writes=[t1])
        if scale is None:
            P.op("vector", lambda e: e.tensor_tensor(dst_ap, v(t0), v(t1), ALU.add), reads=[t0, t1], writes=[dst_buf])
        else:
            P.op("vector", lambda e: e.scalar_tensor_tensor(dst_ap, v(t0), 1.0, v(t1), ALU.mult, ALU.add),
                 reads=[t0, t1], writes=[dst_buf])


def gla_decay_common(B, hT, slot, wA, aoff, wlr17):
    P = B.P
    if not hasattr(B, "a17"):
        B.a17 = P.sb("a17", [17, 128], BF16)
        P.op("vector", lambda e: e.memset(B.a17.ap(), 1.0), writes=[B.a17])
        B.e1 = P.sb("gl_e1", [128, 256], F32)
        B.sp = P.sb("gl_sp", [128, 256], F32)
    bk = B.bank()
    B.proj_fm(bk.ap()[0:16, 0:128], bk, wA, aoff, 16, hT, slot)
    P.op("vector", lambda e: e.tensor_copy(B.a17.ap()[0:16, :], bk.ap()[0:16, 0:128]), reads=[bk], writes=[B.a17])
    bz = B.bank()
    P.op("tensor", lambda e: e.matmul(bz.ap()[:, 0:256], B.a17.ap(), wlr17.ap(), start=True, stop=True),
         reads=[B.a17, wlr17], writes=[bz])
    P.op("scalar", lambda e: e.activation(B.e1.ap(), bz.ap()[:, 0:256], AF.Exp, scale=-1.0), reads=[bz], writes=[B.e1])
    P.op("scalar", lambda e: e.activation(B.sp.ap(), B.e1.ap(), AF.Ln, bias=1.0), reads=[B.e1], writes=[B.sp])
    return B.sp


def load_wlr17(B, wlr_ap, blr_ap):
    P = B.P
    st = P.sb("wlr_st", [17, 256], F32)
    P.dma(st.ap()[0:16, :], wlr_ap, writes=[st])
    P.dma(st.ap()[16:17, :], blr_ap, writes=[st])
    w = P.sb("wlr17", [17, 256], BF16)
    P.op("vector", lambda e: e.tensor_copy(w.ap(), st.ap()), reads=[st], writes=[w])
    return w


def build_A(cfg):
    B = Builder(cfg, "phaseA")
    P = B.P
    TPC, T = cfg.TPC, cfg.T
    aoff, ncolA = col_layout(A_COLS)
    x = B.inp("x", [TPC, 128, 1024])
    pos = B.inp("pos", [1, T], I32)
    cvec = B.inp("cvec", [128, 8])
    adaw = B.inp("adaw", [128, 8, 3072])
    adab = B.inp("adab", [1, 3072])
    pre = B.inp("pre", [1, 1024])
    WA = B.inp("WA", [128, 8, ncolA])
    wlr = B.inp("wlr", [16, 256])
    blr = B.inp("blr", [1, 256])
    oKT = B.outp("KT", [128, T], BF16)
    oV = B.outp("V", [TPC, 128, 128], BF16)
    oIK = B.outp("IK", [64, T], BF16)
    okvR = B.outp("kvR", [TPC, 64, 512])
    okvG = B.outp("kvG", [TPC, 64, 512])
    odecG = B.outp("decG", [TPC, 64, 4])

    B.make_banks()
    B.load_consts(["ident", "triU64", "chunkind", "kfacR", "ropefs"])
    B.emit_mod(cvec, adaw, adab, pre, None, False)
    (cosR, sinR), (cosD, sinD) = B.emit_ropes(pos, [(0, 1, "rr"), (2, 3, "rd")])
    wA = B.load_weight(WA, 0, ncolA, "wA")
    wlr17 = load_wlr17(B, wlr, blr)
    hT = P.sb("hT", [128, 1, 8, 128], BF16)

    kTb = P.sb("kTb", [64, 4, 128], BF16)
    khat = P.sb("khat", [128, 4, 64], BF16)
    vtok = P.sb("vtok", [128, 512], BF16)
    kvs = [P.sb(f"kvs{i}", [64, 512], F32) for i in range(2)]
    ktb = P.sb("ktb", [128, 128], BF16)
    vb = P.sb("vb", [128, 128], BF16)
    ikb = P.sb("ikb", [64, 128], BF16)
    kfac = P.sb("kfac", [128, 256], F32)
    gkhat = P.sb("gkhat", [128, 256], BF16)
    gvtok = P.sb("gvtok", [128, 512], BF16)
    dec = P.sb("dec", [64, 4, 4], F32)
    kv1s = P.sb("kv1s", [64, 512], F32)
    decT = P.sb("decT", [64, 4], F32)
    ident = B.c["ident"]

    for s in range(TPC):
        t0 = s * 128
        B.emit_norm_T(x[s], hT, 0)
        bx, bs = B.bank(), B.bank()
        for h in range(4):
            B.proj_fm(bx.ap()[0:64, h * 128:(h + 1) * 128], bx, wA, aoff["retk"][0] + h * 64, 64, hT, 0)
            B.proj_fm(bs.ap()[0:64, h * 128:(h + 1) * 128], bs, wA, aoff["retk_sw"][0] + h * 64, 64, hT, 0)
        B.rope_fm(kTb.ap(), kTb, bx, bs, 64, 512, cosR, sinR, t0)
        bt = B.bank()
        for h in range(4):
            P.op("tensor", lambda e, h=h: e.matmul(bt.ap()[:, h * 64:(h + 1) * 64], kTb.ap()[:, h, :],
                                                   B.identb.ap()[0:64, 0:64], start=True, stop=True),
                 reads=[kTb, B.identb], writes=[bt])
        P.op("vector", lambda e: e.tensor_tensor(
            khat.ap(), bt.ap()[:, 0:256].rearrange("p (h d) -> p h d", h=4),
            B.c["kfacR"].ap().unsqueeze(2).to_broadcast([128, 4, 64]), ALU.mult),
            reads=[bt, B.c["kfacR"]], writes=[khat])
        bv = B.bank()
        B.proj_tm(bv.ap(), bv, wA, aoff["retv"][0], 512, hT, 0)
        P.op("scalar", lambda e: e.activation(vtok.ap(), bv.ap(), AF.Copy), reads=[bv], writes=[vtok])
        bkv = B.bank()
        for h in range(4):
            P.op("tensor", lambda e, h=h: e.matmul(bkv.ap()[0:64, h * 128:(h + 1) * 128], khat.ap()[:, h, :],
                                                   vtok.ap()[:, h * 128:(h + 1) * 128], start=True, stop=True),
                 reads=[khat, vtok], writes=[bkv])
        kv = kvs[0]
        P.op("scalar", lambda e: e.activation(kv.ap(), bkv.ap()[0:64, :], AF.Copy), reads=[bkv], writes=[kv])
        P.dma(okvR[s], kv.ap(), reads=[kv])
        bx, bs = B.bank(), B.bank()
        B.proj_fm(bx.ap()[:, 0:128], bx, wA, aoff["dsak"][0], 128, hT, 0)
        B.proj_fm(bs.ap()[:, 0:128], bs, wA, aoff["dsak_sw"][0], 128, hT, 0)
        B.rope_fm(ktb.ap().unsqueeze(1), ktb, bx, bs, 128, 128, cosD, sinD, t0)
        P.dma(oKT[:, t0:t0 + 128], ktb.ap(), reads=[ktb])
        bv = B.bank()
        B.proj_tm(bv.ap()[:, 0:128], bv, wA, aoff["dsav"][0], 128, hT, 0)
        P.op("scalar", lambda e: e.activation(vb.ap(), bv.ap()[:, 0:128], AF.Copy), reads=[bv], writes=[vb])
        P.dma(oV[s], vb.ap(), reads=[vb])
        bx, bs = B.bank(), B.bank()
        B.proj_fm(bx.ap()[0:64, 0:128], bx, wA, aoff["idxk"][0], 64, hT, 0)
        B.proj_fm(bs.ap()[0:64, 0:128], bs, wA, aoff["idxk_sw"][0], 64, hT, 0)
        B.rope_fm(ikb.ap().unsqueeze(1), ikb, bx, bs, 64, 128, cosD, sinD, t0)
        P.dma(oIK[:, t0:t0 + 128], ikb.ap(), reads=[ikb])
        sp = gla_decay_common(B, hT, 0, wA, aoff["glaa"][0], wlr17)
        bd = B.bank()
        P.op("tensor", lambda e: e.matmul(bd.ap()[:, 0:256], B.c["triU64"].ap(), sp.ap(), start=True, stop=True),
             reads=[B.c["triU64"], sp], writes=[bd])
        P.op("scalar", lambda e: e.activation(kfac.ap(), bd.ap()[:, 0:256], AF.Exp, scale=-1.0 / 16.0),
             reads=[bd], writes=[kfac])
        bk = B.bank()
        B.proj_tm(bk.ap()[:, 0:256], bk, wA, aoff["glak"][0], 256, hT, 0)
        P.op("vector", lambda e: e.tensor_tensor(gkhat.ap(), bk.ap()[:, 0:256], kfac.ap(), ALU.mult),
             reads=[bk, kfac], writes=[gkhat])
        bv = B.bank()
        B.proj_tm(bv.ap(), bv, wA, aoff["glav"][0], 512, hT, 0)
        P.op("scalar", lambda e: e.activation(gvtok.ap(), bv.ap(), AF.Copy), reads=[bv], writes=[gvtok])
        b0, b1 = B.bank(), B.bank()
        for ch, bb in ((0, b0), (1, b1)):
            for h in range(4):
                P.op("tensor", lambda e, h=h, ch=ch, bb=bb: e.matmul(
                    bb.ap()[0:64, h * 128:(h + 1) * 128], gkhat.ap()[ch * 64:(ch + 1) * 64, h * 64:(h + 1) * 64],
                    gvtok.ap()[ch * 64:(ch + 1) * 64, h * 128:(h + 1) * 128], start=True, stop=True),
                    reads=[gkhat, gvtok], writes=[bb])
        bs_ = B.bank()
        for h in range(4):
            P.op("tensor", lambda e, h=h: e.matmul(bs_.ap()[0:64, h * 4:(h + 1) * 4], sp.ap()[:, h * 64:(h + 1) * 64],
                                                   B.c["chunkind"].ap(), start=True, stop=True),
                 reads=[sp, B.c["chunkind"]], writes=[bs_])
        P.op("scalar", lambda e: e.activation(dec.ap(), bs_.ap()[0:64, 0:16].rearrange("p (h c) -> p h c", h=4),
                                              AF.Exp, scale=-1.0 / 16.0), reads=[bs_], writes=[dec])
        P.op("scalar", lambda e: e.activation(kv1s.ap(), b1.ap()[0:64, :], AF.Copy), reads=[b1], writes=[kv1s])
        kv = kvs[1]
        for h in range(4):
            P.op("vector", lambda e, h=h: e.scalar_tensor_tensor(
                kv.ap()[:, h * 128:(h + 1) * 128], b0.ap()[0:64, h * 128:(h + 1) * 128], dec.ap()[:, h, 1:2],
                kv1s.ap()[:, h * 128:(h + 1) * 128], ALU.mult, ALU.add),
                reads=[b0, dec, kv1s], writes=[kv])
        P.dma(okvG[s], kv.ap(), reads=[kv])
        P.op("vector", lambda e: e.tensor_copy(decT.ap(), dec.ap()[:, :, 2]), reads=[dec], writes=[decT])
        P.dma(odecG[s], decT.ap(), reads=[decT])
    P.finish()
    return B


def prep_common(inputs, cfg, l, core):
    raise NotImplementedError


def a_col_index():
    ar = np.arange
    idx = {
        "retk": OFF["ret_k"] + ar(256), "retk_sw": np.array(swap_cols(OFF["ret_k"], 64, 64, 4)),
        "retv": OFF["ret_v"] + ar(512),
        "dsak": OFF["dsa_k"] + ar(128), "dsak_sw": np.array(swap_cols(OFF["dsa_k"], 16, 64, 2)),
        "dsav": OFF["dsa_v"] + ar(128),
        "idxk": OFF["idx_k"] + ar(64), "idxk_sw": np.array(swap_cols(OFF["idx_k"], 16, 64, 1)),
        "glak": OFF["gla_k"] + ar(256), "glav": OFF["gla_v"] + ar(512), "glaa": OFF["gla_a"] + ar(16),
    }
    return np.concatenate([idx[n] for n, _ in A_COLS])


def kc_layout(w):
    n = w.shape[1]
    return np.ascontiguousarray(w.reshape(8, 128, n).transpose(1, 0, 2))


def core_tiles(cfg, c):
    return [k * cfg.NCORE + c for k in range(cfg.TPC)]


def const_inputs(names):
    hc = host_consts()
    return {"c_" + n: hc[n] for n in names}


def host_inputs_A(inp, cfg, l):
    x = np.asarray(inp["x"])[0].reshape(cfg.S // 128, 128, D)
    pos = np.asarray(inp["positions"])[0].reshape(cfg.S // 128, 128)
    shared = {
        "cvec": np.ascontiguousarray(np.asarray(inp["c"])[0].reshape(8, 128).T),
        "adaw": kc_layout(np.asarray(inp["ada_w"])[l]),
        "adab": np.asarray(inp["ada_b"])[l][None, :],
        "pre": np.asarray(inp["pre_norm"])[l][None, :],
        "WA": kc_layout(np.asarray(inp["w_in"])[l][:, a_col_index()]),
        "wlr": np.asarray(inp["gla_w_lr"])[l],
        "blr": np.asarray(inp["gla_b_lr"])[l][None, :],
    }
    shared.update(const_inputs(["ident", "triU64", "chunkind", "kfacR", "ropefs"]))
    maps = []
    for c in range(cfg.NCORE):
        tl = core_tiles(cfg, c)
        m = dict(shared)
        m["x"] = np.ascontiguousarray(x[tl])
        m["pos"] = np.ascontiguousarray(pos[tl].reshape(1, cfg.T)).astype(np.int32)
        maps.append(m)
    return maps


def np_inputs(S, seed=0, depth=4):
    r = np.random.RandomState(seed)
    f = np.float32
    n = lambda *s: r.randn(*s).astype(f)
    Dm = D
    return {
        "x": n(1, S, Dm), "c": n(1, Dm),
        "positions": (np.arange(S, dtype=np.int32)[None, :] + np.int32(r.randint(0, 4096))),
        "ada_w": n(depth, Dm, 3 * Dm) * f(0.1 * Dm ** -0.5), "ada_b": n(depth, 3 * Dm) * f(0.02),
        "pre_norm": 1 + f(0.05) * n(depth, Dm), "post_norm": 1 + f(0.05) * n(depth, Dm),
        "w_in": n(depth, Dm, IN_WIDTH) * f(Dm ** -0.5),
        "gla_w_lr": n(depth, 16, 256) * f(0.25), "gla_b_lr": f(0.1) * n(depth, 256),
        "w_br_ret": n(depth, 512, Dm) * f(512 ** -0.5), "w_br_dsa": n(depth, 512, Dm) * f(512 ** -0.5),
        "w_br_gla": n(depth, 512, Dm) * f(512 ** -0.5), "w_out": n(depth, Dm, Dm) * f(Dm ** -0.5),
    }


B_COLS = [("retq", 256), ("retq_sw", 256), ("retk", 256), ("retk_sw", 256), ("retv", 512), ("retg", 512),
          ("mg0", 1024),
          ("glaq", 256), ("glak", 256), ("glav", 512), ("glag", 512), ("glaa", 16), ("mg2", 1024),
          ("dsaq", 512), ("dsaq_sw", 512), ("idxq", 256), ("idxq_sw", 256), ("idxw", 4), ("dsag", 512),
          ("mg1", 1024)]


def b_col_index():
    ar = np.arange
    pair = np.concatenate([np.concatenate([ar(64) + j * 64, ar(64) + (j + 4) * 64]) for j in range(4)])
    dq = OFF["dsa_q"] + ar(512)
    dq_sw = np.array(swap_cols(OFF["dsa_q"], 16, 64, 8))
    idx = {
        "retq": OFF["ret_q"] + ar(256), "retq_sw": np.array(swap_cols(OFF["ret_q"], 64, 64, 4)),
        "retk": OFF["ret_k"] + ar(256), "retk_sw": np.array(swap_cols(OFF["ret_k"], 64, 64, 4)),
        "retv": OFF["ret_v"] + ar(512), "retg": OFF["ret_g"] + ar(512),
        "mg0": OFF["merge"] + ar(1024), "mg1": OFF["merge"] + 1024 + ar(1024), "mg2": OFF["merge"] + 2048 + ar(1024),
        "glaq": OFF["gla_q"] + ar(256), "glak": OFF["gla_k"] + ar(256), "glav": OFF["gla_v"] + ar(512),
        "glag": OFF["gla_g"] + ar(512), "glaa": OFF["gla_a"] + ar(16),
        "dsaq": dq[pair], "dsaq_sw": dq_sw[pair],
        "idxq": OFF["idx_q"] + ar(256), "idxq_sw": np.array(swap_cols(OFF["idx_q"], 16, 64, 4)),
        "idxw": OFF["idx_w"] + ar(4), "dsag": OFF["dsa_g"] + ar(512),
    }
    return np.concatenate([idx[n] for n, _ in B_COLS])


def build_B(cfg):
    B = Builder(cfg, "phaseB")
    P = B.P
    TPC, T, NCORE, SG = cfg.TPC, cfg.T, cfg.NCORE, cfg.SG
    GK, BLK, NB, BPG = cfg.GK, cfg.BLK, cfg.NB, cfg.BPG
    boff, ncolB = col_layout(B_COLS)
    x = B.inp("x", [TPC, 128, 1024])
    pos = B.inp("pos", [1, T], I32)
    cvec = B.inp("cvec", [128, 8])
    adaw = B.inp("adaw", [128, 8, 3072])
    adab = B.inp("adab", [1, 3072])
    pre = B.inp("pre", [1, 1024])
    post = B.inp("post", [1, 1024])
    WB = B.inp("WB", [128, 8, ncolB])
    wbr_r = B.inp("wbr_r", [128, 4, 1024])
    wbr_g = B.inp("wbr_g", [128, 4, 1024])
    wbr_d = B.inp("wbr_d", [64, 8, 1024])
    wout_d = B.inp("wout", [128, 8, 1024])
    wlr = B.inp("wlr", [16, 256])
    blr = B.inp("blr", [1, 256])
    KTa = B.inp("KTa", [NCORE, 128, T], BF16)
    Va = B.inp("Va", [NCORE, TPC, 128, 128], BF16)
    IKa = B.inp("IKa", [NCORE, 64, T], BF16)
    kvRa = B.inp("kvRa", [NCORE, TPC, 64, 512])
    kvGa = B.inp("kvGa", [NCORE, TPC, 64, 512])
    decGa = B.inp("decGa", [NCORE, TPC, 64, 4])
    sel_d = B.inp("sel", [128, NCORE])
    pen_d = B.inp("pen", [128, GK], BF16)
    xo = B.outp("xo", [TPC, 128, 1024])

    B.make_banks()
    B.load_consts(["ident", "triL128", "triL64", "triU64", "DTret", "DTgla", "QDret", "decR", "ropefs"])
    sel = P.sb("sel", [128, NCORE], F32)
    P.dma(sel.ap(), sel_d, writes=[sel])
    pen = P.sb("pen", [128, GK], BF16)
    P.dma(pen.ap(), pen_d, writes=[pen])
    ident4 = P.sb("ident4", [128, 4, 128], BF16)
    P.op("vector", lambda e: e.tensor_copy(ident4.ap(), B.c["ident"].ap().unsqueeze(1).to_broadcast([128, 4, 128])),
         reads=[B.c["ident"]], writes=[ident4])
    B.emit_mod(cvec, adaw, adab, pre, post, True)
    (cosR, sinR), (cosD, sinD) = B.emit_ropes(pos, [(0, 1, "rr"), (2, 3, "rd")])
    wlr17 = load_wlr17(B, wlr, blr)
    B.prealloc(NCORE)
    SR = P.sb("SR", [64, 4, 128], F32)
    SGs = P.sb("SGs", [64, 4, 128], F32)
    P.op("vector", lambda e: e.memset(SR.ap(), 0.0), writes=[SR])
    P.op("vector", lambda e: e.memset(SGs.ap(), 0.0), writes=[SGs])
    hT = P.sb("hT", [128, SG, 8, 128], BF16)
    ysum = P.sb("ysum", [128, SG, 8, 128], BF16)
    dsaout = P.sb("dsaout", [64, SG, 8, 128], BF16)
    cap = P.sb("cap", [64, 4, 128], F32)
    capb = P.sb("capb", [64, 4, 128], BF16)

    def load_w(names, tag):
        c0 = boff[names[0]][0]
        n = sum(boff[k][1] for k in names)
        assert boff[names[-1]][0] + boff[names[-1]][1] == c0 + n
        return B.load_weight(WB, c0, n, tag), c0

    def load_w2(dram_ap, rows, kc, name):
        return B.load_weight(dram_ap, 0, 1024, name, kc=kc, rows=rows)

    def scan(S, kv_all, dec_src, s, tag):
        if dec_src is not None:
            dcg = B.dcg_g
            P.dma(dcg.ap(), dec_src[:, s].rearrange("j d n -> d j n"), writes=[dcg])
        for j in range(NCORE):
            kvt = B.kvt[j % 2]
            P.dma(kvt.ap(), kv_all[j, s], writes=[kvt])
            if j == 0:
                P.op("vector", lambda e: e.tensor_scalar(cap.ap(), S.ap(), sel.ap()[0:64, 0:1], None, ALU.mult),
                     reads=[S, sel], writes=[cap])
            else:
                P.op("vector", lambda e: e.scalar_tensor_tensor(cap.ap(), S.ap(), sel.ap()[0:64, j:j + 1], cap.ap(),
                                                                ALU.mult, ALU.add), reads=[S, sel, cap], writes=[cap])
            if dec_src is None:
                dv = B.c["decR"].ap()[0:64, :].unsqueeze(2).to_broadcast([64, 4, 128])
                rd = [B.c["decR"]]
            else:
                dv = dcg.ap()[:, j, :].unsqueeze(2).to_broadcast([64, 4, 128])
                rd = [dcg]
            P.op("vector", lambda e: e.tensor_tensor(S.ap(), S.ap(), dv, ALU.mult), reads=[S] + rd, writes=[S])
            P.op("vector", lambda e: e.tensor_tensor(S.ap(), S.ap(), kvt.ap().rearrange("p (h e) -> p h e", h=4),
                                                     ALU.add), reads=[S, kvt], writes=[S])
        P.op("scalar", lambda e: e.activation(capb.ap(), cap.ap(), AF.Copy), reads=[cap], writes=[capb])

    def tail(bO, w, c0, gname, mgname, wbr, ls, first, tl):
        sq, r1, sg, tt, bro, mg, t2 = tl
        P.op("scalar", lambda e: e.activation(sq.ap(), bO.ap(), AF.Square), reads=[bO], writes=[sq])
        bQ = B.bank()
        P.op("tensor", lambda e: e.matmul(bQ.ap(), B.onesb.ap(), sq.ap(), start=True, stop=True),
             reads=[B.onesb, sq], writes=[bQ])
        P.op("vector", lambda e: e.tensor_scalar(r1.ap(), bQ.ap(), 1.0 / 128.0, RMS_EPS, ALU.mult, ALU.add),
             reads=[bQ], writes=[r1])
        P.op("scalar", lambda e: e.activation(r1.ap(), r1.ap(), AF.Sqrt), reads=[r1], writes=[r1])
        P.op("vector", lambda e: e.reciprocal(r1.ap(), r1.ap()), reads=[r1], writes=[r1])
        bG = B.bank()
        g0 = boff[gname][0] - c0
        for h in range(4):
            B.proj_fm(bG.ap()[:, h * 128:(h + 1) * 128], bG, w, g0 + h * 128, 128, hT, ls)
        P.op("scalar", lambda e: e.activation(sg.ap(), bG.ap(), AF.Silu), reads=[bG], writes=[sg])
        P.op("vector", lambda e: e.tensor_tensor(tt.ap(), bO.ap(), r1.ap(), ALU.mult), reads=[bO, r1], writes=[tt])
        P.op("gpsimd", lambda e: e.tensor_tensor(bro.ap(), tt.ap(), sg.ap(), ALU.mult), reads=[tt, sg], writes=[bro])
        merge(lambda nc, out_ap, bk: [P.op("tensor", lambda e, h=h: e.matmul(
            out_ap, wbr.ap()[:, h, nc * 128:(nc + 1) * 128], bro.ap()[:, h * 128:(h + 1) * 128],
            start=(h == 0), stop=(h == 3)), reads=[wbr, bro], writes=[bk]) for h in range(4)],
            w, boff[mgname][0] - c0, ls, first, mg, t2)

    def merge(emit_branch, w, m0, ls, first, mg, t2):
        bY = [B.bank(), B.bank()]
        for nc in range(8):
            bk = bY[nc // 4]
            emit_branch(nc, bk.ap()[:, (nc % 4) * 128:(nc % 4 + 1) * 128], bk)
        bM = [B.bank(), B.bank()]
        for nc in range(8):
            bk = bM[nc // 4]
            B.proj_fm(bk.ap()[:, (nc % 4) * 128:(nc % 4 + 1) * 128], bk, w, m0 + nc * 128, 128, hT, ls)
        for hf in range(2):
            P.op("scalar", lambda e: e.activation(mg.ap(), bM[hf].ap(), AF.Sigmoid), reads=[bM[hf]], writes=[mg])
            yv = ysum.ap()[:, ls, hf * 4:(hf + 1) * 4, :]
            m3 = mg.ap().rearrange("p (a b) -> p a b", a=4)
            b3 = bY[hf].ap().rearrange("p (a b) -> p a b", a=4)
            if first:
                P.op("vector", lambda e: e.tensor_tensor(yv, m3, b3, ALU.mult), reads=[mg, bY[hf]], writes=[ysum])
            else:
                P.op("vector", lambda e: e.tensor_tensor(t2.ap(), mg.ap(), bY[hf].ap(), ALU.mult),
                     reads=[mg, bY[hf]], writes=[t2])
                P.op("gpsimd", lambda e: e.tensor_tensor(yv, yv, t2.ap().rearrange("p (a b) -> p a b", a=4), ALU.add),
                     reads=[ysum, t2], writes=[ysum])

    def tail_bufs():
        return (P.sb("t_sq", [128, 512], BF16), P.sb("t_r1", [128, 512], F32), P.sb("t_sg", [128, 512], F32),
                P.sb("t_tt", [128, 512], F32), P.sb("t_bro", [128, 512], BF16), P.sb("t_mg", [128, 512], F32),
                P.sb("t_t2", [128, 512], F32))

    for g0 in range(0, TPC, SG):
        P.push_scope()
        for s in range(g0, g0 + SG):
            B.emit_norm_T(x[s], hT, s - g0)
        P.pop_scope()
        del B.xin
        P.push_scope()
        w, c0 = load_w(["retq", "retq_sw", "retk", "retk_sw", "retv", "retg", "mg0"], "w_ret")
        wbr = load_w2(wbr_r, 128, 4, "wbr_ret")
        tl = tail_bufs()
        qTb = P.sb("qTb", [64, 4, 128], BF16)
        kTb = P.sb("kTb", [64, 4, 128], BF16)
        qhat = P.sb("qhat", [64, 4, 128], BF16)
        vtok = P.sb("vtok", [128, 512], BF16)
        Sm = P.sb("Sm", [128, 512], BF16)
        for s in range(g0, g0 + SG):
            ls = s - g0
            t0 = s * 128
            scan(SR, kvRa, None, s, "r")
            for nm, dst in (("retq", qTb), ("retk", kTb)):
                bx, bs = B.bank(), B.bank()
                for h in range(4):
                    B.proj_fm(bx.ap()[0:64, h * 128:(h + 1) * 128], bx, w, boff[nm][0] - c0 + h * 64, 64, hT, ls)
                    B.proj_fm(bs.ap()[0:64, h * 128:(h + 1) * 128], bs, w, boff[nm + "_sw"][0] - c0 + h * 64, 64, hT, ls)
                B.rope_fm(dst.ap(), dst, bx, bs, 64, 512, cosR, sinR, t0)
            bv = B.bank()
            B.proj_tm(bv.ap(), bv, w, boff["retv"][0] - c0, 512, hT, ls)
            P.op("scalar", lambda e: e.activation(vtok.ap(), bv.ap(), AF.Copy), reads=[bv], writes=[vtok])
            bS = B.bank()
            for h in range(4):
                P.op("tensor", lambda e: e.matmul(bS.ap()[:, h * 128:(h + 1) * 128], kTb.ap()[:, h, :], qTb.ap()[:, h, :],
                                                  start=True, stop=True), reads=[kTb, qTb], writes=[bS])
            P.op("vector", lambda e: e.tensor_tensor(Sm.ap(), bS.ap(), B.c["DTret"].ap().rearrange("p h i -> p (h i)"),
                                                     ALU.mult), reads=[bS, B.c["DTret"]], writes=[Sm])
            P.op("gpsimd", lambda e: e.tensor_tensor(qhat.ap(), qTb.ap(), B.c["QDret"].ap()[0:64], ALU.mult),
                 reads=[qTb, B.c["QDret"]], writes=[qhat])
            bO = B.bank()
            for h in range(4):
                o_ap = bO.ap()[:, h * 128:(h + 1) * 128]
                P.op("tensor", lambda e: e.matmul(o_ap, vtok.ap()[:, h * 128:(h + 1) * 128], Sm.ap()[:, h * 128:(h + 1) * 128],
                                                  start=True, stop=False), reads=[vtok, Sm], writes=[bO])
                P.op("tensor", lambda e: e.matmul(o_ap, capb.ap()[:, h, :], qhat.ap()[:, h, :], start=False, stop=True),
                     reads=[capb, qhat], writes=[bO])
            tail(bO, w, c0, "retg", "mg0", wbr, ls, True, tl)
        P.pop_scope()
        P.push_scope()
        w, c0 = load_w(["glaq", "glak", "glav", "glag", "glaa", "mg2"], "w_gla")
        wbr = load_w2(wbr_g, 128, 4, "wbr_gla")
        tl = tail_bufs()
        eq = P.sb("eq", [64, 512], F32)
        ek = P.sb("ek", [64, 512], F32)
        e128 = P.sb("e128", [64, 512], F32)
        qt = P.sb("qt", [64, 4, 128], BF16)
        qh = P.sb("qh", [64, 4, 128], BF16)
        kt = P.sb("kt", [64, 4, 128], BF16)
        kfac = P.sb("kfac", [128, 256], F32)
        gkhat = P.sb("gkhat", [128, 256], BF16)
        vtok = P.sb("gvtok", [128, 512], BF16)
        kv0b = P.sb("kv0b", [64, 512], BF16)
        Sm = P.sb("gSm", [128, 512], BF16)
        for s in range(g0, g0 + SG):
            ls = s - g0
            scan(SGs, kvGa, decGa, s, "g")
            sp = gla_decay_common(B, hT, ls, w, boff["glaa"][0] - c0, wlr17)
            bC64, bC128 = B.bank(), B.bank()
            for h in range(4):
                for bb, tri in ((bC64, "triL64"), (bC128, "triL128")):
                    P.op("tensor", lambda e: e.matmul(bb.ap()[0:64, h * 128:(h + 1) * 128], sp.ap()[:, h * 64:(h + 1) * 64],
                                                      B.c[tri].ap(), start=True, stop=True), reads=[sp, B.c[tri]], writes=[bb])
            P.op("scalar", lambda e: e.activation(eq.ap(), bC64.ap()[0:64, :], AF.Exp, scale=-1.0 / 16), reads=[bC64], writes=[eq])
            P.op("scalar", lambda e: e.activation(ek.ap(), bC64.ap()[0:64, :], AF.Exp, scale=1.0 / 16), reads=[bC64], writes=[ek])
            P.op("scalar", lambda e: e.activation(e128.ap(), bC128.ap()[0:64, :], AF.Exp, scale=-1.0 / 16), reads=[bC128], writes=[e128])
            bq = B.bank()
            for h in range(4):
                B.proj_fm(bq.ap()[0:64, h * 128:(h + 1) * 128], bq, w, boff["glaq"][0] - c0 + h * 64, 64, hT, ls)
            f2 = lambda b: b.ap().rearrange("p h t -> p (h t)")
            P.op("vector", lambda e: e.scalar_tensor_tensor(f2(qt), bq.ap()[0:64, :], 0.125, eq.ap(), ALU.mult, ALU.mult),
                 reads=[bq, eq], writes=[qt])
            P.op("vector", lambda e: e.scalar_tensor_tensor(f2(qh), bq.ap()[0:64, :], 0.125, e128.ap(), ALU.mult, ALU.mult),
                 reads=[bq, e128], writes=[qh])
            bk = B.bank()
            for h in range(4):
                B.proj_fm(bk.ap()[0:64, h * 128:(h + 1) * 128], bk, w, boff["glak"][0] - c0 + h * 64, 64, hT, ls)
            P.op("vector", lambda e: e.tensor_tensor(f2(kt), bk.ap()[0:64, :], ek.ap(), ALU.mult), reads=[bk, ek], writes=[kt])
            bd = B.bank()
            P.op("tensor", lambda e: e.matmul(bd.ap()[:, 0:256], B.c["triU64"].ap(), sp.ap(), start=True, stop=True),
                 reads=[B.c["triU64"], sp], writes=[bd])
            P.op("scalar", lambda e: e.activation(kfac.ap(), bd.ap()[:, 0:256], AF.Exp, scale=-1.0 / 16.0), reads=[bd], writes=[kfac])
            bkt = B.bank()
            B.proj_tm(bkt.ap()[:, 0:256], bkt, w, boff["glak"][0] - c0, 256, hT, ls)
            P.op("vector", lambda e: e.tensor_tensor(gkhat.ap(), bkt.ap()[:, 0:256], kfac.ap(), ALU.mult),
                 reads=[bkt, kfac], writes=[gkhat])
            bv = B.bank()
            B.proj_tm(bv.ap(), bv, w, boff["glav"][0] - c0, 512, hT, ls)
            P.op("scalar", lambda e: e.activation(vtok.ap(), bv.ap(), AF.Copy), reads=[bv], writes=[vtok])
            b0 = B.bank()
            for h in range(4):
                P.op("tensor", lambda e: e.matmul(b0.ap()[0:64, h * 128:(h + 1) * 128], gkhat.ap()[0:64, h * 64:(h + 1) * 64],
                                                  vtok.ap()[0:64, h * 128:(h + 1) * 128], start=True, stop=True),
                     reads=[gkhat, vtok], writes=[b0])
            P.op("scalar", lambda e: e.activation(kv0b.ap(), b0.ap()[0:64, :], AF.Copy), reads=[b0], writes=[kv0b])
            bS = B.bank()
            for h in range(4):
                P.op("tensor", lambda e: e.matmul(bS.ap()[:, h * 128:(h + 1) * 128], kt.ap()[:, h, :], qt.ap()[:, h, :],
                                                  start=True, stop=True), reads=[kt, qt], writes=[bS])
            P.op("vector", lambda e: e.tensor_tensor(Sm.ap(), bS.ap(), B.c["DTgla"].ap().rearrange("p h i -> p (h i)"),
                                                     ALU.mult), reads=[bS, B.c["DTgla"]], writes=[Sm])
            bO = B.bank()
            for h in range(4):
                o_ap = bO.ap()[:, h * 128:(h + 1) * 128]
                P.op("tensor", lambda e: e.matmul(o_ap, vtok.ap()[:, h * 128:(h + 1) * 128], Sm.ap()[:, h * 128:(h + 1) * 128],
                                                  start=True, stop=False), reads=[vtok, Sm], writes=[bO])
                P.op("tensor", lambda e: e.matmul(o_ap, capb.ap()[:, h, :], qh.ap()[:, h, :], start=False, stop=False),
                     reads=[capb, qh], writes=[bO])
                P.op("tensor", lambda e: e.matmul(bO.ap()[:, h * 128 + 64:(h + 1) * 128], kv0b.ap()[:, h * 128:(h + 1) * 128],
                                                  qt.ap()[:, h, 64:128], start=False, stop=True), reads=[kv0b, qt], writes=[bO])
            tail(bO, w, c0, "glag", "mg2", wbr, ls, False, tl)
        P.pop_scope()
        dsa_stage(B, cfg, g0, locals())
        P.push_scope()
        wo = load_w2(wout_d, 128, 8, "w_out")
        xt = P.sb("f_xt", [128, 1024], F32)
        fj = P.sb("f_junk", [128, 512], BF16)
        fst = P.sb("f_st", [128, 8], F32)
        ft = P.sb("f_t", [128, 1024], F32)
        fo = P.sb("f_o", [128, 1024], F32)
        for s in range(g0, g0 + SG):
            ls = s - g0
            P.dma(xt.ap(), x[s], writes=[xt])
            bh = [B.bank(), B.bank()]
            for hf in range(2):
                for nc in range(8):
                    P.op("tensor", lambda e: e.matmul(bh[hf].ap(), ysum.ap()[:, ls, nc, :], wo.ap()[:, nc, hf * 512:(hf + 1) * 512],
                                                      start=(nc == 0), stop=(nc == 7)), reads=[ysum, wo], writes=[bh[hf]])
                P.op("scalar", lambda e: e.activation(fj.ap(), bh[hf].ap(), AF.Square, accum_out=fst.ap()[:, hf:hf + 1]),
                     reads=[bh[hf]], writes=[fj, fst])
            P.op("vector", lambda e: e.tensor_tensor(fst.ap()[:, 2:3], fst.ap()[:, 0:1], fst.ap()[:, 1:2], ALU.add), reads=[fst], writes=[fst])
            P.op("vector", lambda e: e.tensor_scalar(fst.ap()[:, 3:4], fst.ap()[:, 2:3], 1.0 / D, RMS_EPS, ALU.mult, ALU.add),
                 reads=[fst], writes=[fst])
            P.op("scalar", lambda e: e.activation(fst.ap()[:, 4:5], fst.ap()[:, 3:4], AF.Sqrt), reads=[fst], writes=[fst])
            P.op("vector", lambda e: e.reciprocal(fst.ap()[:, 5:6], fst.ap()[:, 4:5]), reads=[fst], writes=[fst])
            for hf in range(2):
                sl = slice(hf * 512, (hf + 1) * 512)
                P.op("vector", lambda e: e.scalar_tensor_tensor(ft.ap()[:, sl], bh[hf].ap(), fst.ap()[:, 5:6], B.GP.ap()[:, sl],
                                                                ALU.mult, ALU.mult), reads=[bh[hf], fst, B.GP], writes=[ft])
            P.op("gpsimd", lambda e: e.tensor_tensor(fo.ap(), ft.ap(), xt.ap(), ALU.add), reads=[ft, xt], writes=[fo])
            P.dma(xo[s], fo.ap(), reads=[fo])
        P.pop_scope()
    P.finish()
    return B


def dsa_stage(B, cfg, g0, env):
    P = B.P
    TPC, T, NCORE, SG = cfg.TPC, cfg.T, cfg.NCORE, cfg.SG
    GK, BLK, NB, BPG, NIT = cfg.GK, cfg.BLK, cfg.NB, cfg.BPG, cfg.NIT
    hT, ysum, dsaout, boff = env["hT"], env["ysum"], env["dsaout"], env["boff"]
    KTa, Va, IKa, pen, ident4 = env["KTa"], env["Va"], env["IKa"], env["pen"], env["ident4"]
    cosD, sinD = env["cosD"], env["sinD"]
    P.push_scope()
    w, c0 = env["load_w"](["dsaq", "dsaq_sw", "idxq", "idxq_sw", "idxw", "dsag"], "w_dsa")
    NMAX = TPC * GK
    scores = P.sb("scores", [128, NMAX], F32)
    CH = min(512, NMAX)
    junk = P.sb("cjunk", [128, CH], BF16)
    QT = P.sb("QT", [128, 4, 128], BF16)
    IQ = P.sb("IQ", [64, 4, 128], BF16)
    wq = P.sb("wq", [128, 4], F32)
    rl = [P.sb(f"rl{i}", [128, BLK], F32) for i in range(2)]
    ikb = [P.sb(f"ikb{i}", [64, NB, 128], BF16) for i in range(2)]
    ktb = [P.sb(f"ktb{i}", [128, NB, 128], BF16) for i in range(2)]
    vbk = [P.sb(f"vbk{i}", [128, NB, 2, 128], BF16) for i in range(2)]
    for v in vbk:
        P.op("vector", lambda e: e.memset(v.ap(), 1.0), writes=[v])
    nbmax = TPC * BPG
    mx = P.sb("mx", [128, nbmax], F32)
    mn = P.sb("mn", [128, nbmax], F32)
    bst = P.sb("bst", [128, 16], F32)
    mlo = P.sb("mlo", [128, BLK], BF16)
    mhi = P.sb("mhi", [128, BLK], BF16)
    band = P.sb("band", [128, BLK], BF16)
    cum = [P.sb(f"cum{i}", [128, BLK], F32) for i in range(2)]
    tsel = P.sb("tsel", [128, BLK], BF16)
    mb = [P.sb(f"mb{i}", [128, BLK], BF16) for i in range(2)]
    PT = [P.sb(f"PT{i}", [128, 512], BF16) for i in range(2)]
    rc = P.sb("rc", [64, 512], F32)
    on = P.sb("on", [64, 512], BF16)
    sgd = P.sb("sgd", [64, 8, 128], BF16)
    acc = [B.banks[6], B.banks[7]]
    saved_i = B.bank_i
    rot = {"i": 0}

    def bank6():
        b = B.banks[rot["i"] % 6]
        rot["i"] += 1
        return b
    B_bank = B.bank
    B.bank = bank6
    col = lambda i: bst.ap()[:, i:i + 1]

    for s in range(g0, g0 + SG):
        ls = s - g0
        t0 = s * 128
        bx, bs = bank6(), bank6()
        for j in range(4):
            B.proj_fm(bx.ap()[:, j * 128:(j + 1) * 128], bx, w, boff["dsaq"][0] - c0 + j * 128, 128, hT, ls)
            B.proj_fm(bs.ap()[:, j * 128:(j + 1) * 128], bs, w, boff["dsaq_sw"][0] - c0 + j * 128, 128, hT, ls)
        B.rope_fm(QT.ap(), QT, bx, bs, 128, 512, cosD, sinD, t0)
        bx, bs = bank6(), bank6()
        for h in range(4):
            B.proj_fm(bx.ap()[0:64, h * 128:(h + 1) * 128], bx, w, boff["idxq"][0] - c0 + h * 64, 64, hT, ls)
            B.proj_fm(bs.ap()[0:64, h * 128:(h + 1) * 128], bs, w, boff["idxq_sw"][0] - c0 + h * 64, 64, hT, ls)
        B.rope_fm(IQ.ap(), IQ, bx, bs, 64, 512, cosD, sinD, t0)
        bw = bank6()
        B.proj_tm(bw.ap()[:, 0:4], bw, w, boff["idxw"][0] - c0, 4, hT, ls)
        P.op("vector", lambda e: e.tensor_scalar(wq.ap(), bw.ap()[:, 0:4], 0.0625, None, ALU.mult), reads=[bw], writes=[wq])
        nb = (s + 1) * BPG
        n = (s + 1) * GK
        for bi in range(nb):
            gq, blk = bi // BPG, bi % BPG
            ik = ikb[bi % 2]
            P.dma(ik.ap(), IKa[blk * NB:(blk + 1) * NB, :, gq * 128:(gq + 1) * 128].rearrange("j d t -> d j t"), writes=[ik])
            scb = scores.ap()[:, bi * BLK:(bi + 1) * BLK]
            for h in range(4):
                bI = bank6()
                P.op("tensor", lambda e: e.matmul(bI.ap()[:, 0:BLK], IQ.ap()[:, h, :], ik.ap().rearrange("d j t -> d (j t)"),
                                                  start=True, stop=True), reads=[IQ, ik], writes=[bI])
                r = rl[h % 2]
                P.op("scalar", lambda e: e.activation(r.ap(), bI.ap()[:, 0:BLK], AF.Relu), reads=[bI], writes=[r])
                if h == 0:
                    P.op("vector", lambda e: e.tensor_scalar(scb, r.ap(), wq.ap()[:, 0:1], None, ALU.mult),
                         reads=[r, wq], writes=[scores])
                else:
                    P.op("vector", lambda e: e.scalar_tensor_tensor(scb, r.ap(), wq.ap()[:, h:h + 1], scb, ALU.mult, ALU.add),
                         reads=[r, wq, scores], writes=[scores])
            P.op("vector", lambda e: e.tensor_reduce(mx.ap()[:, bi:bi + 1], scb, AX.X, ALU.max), reads=[scores], writes=[mx])
            P.op("vector", lambda e: e.tensor_reduce(mn.ap()[:, bi:bi + 1], scb, AX.X, ALU.min), reads=[scores], writes=[mn])
            if gq == s:
                P.op("vector", lambda e: e.tensor_tensor(scb, scb, pen.ap()[:, blk * BLK:(blk + 1) * BLK], ALU.add),
                     reads=[scores, pen], writes=[scores])
        P.op("vector", lambda e: e.tensor_reduce(col(8), mx.ap()[:, 0:nb], AX.X, ALU.max), reads=[mx], writes=[bst])
        P.op("vector", lambda e: e.tensor_scalar(col(1), col(8), 1.0, None, ALU.add), reads=[bst], writes=[bst])
        P.op("vector", lambda e: e.tensor_reduce(col(8), mn.ap()[:, 0:nb], AX.X, ALU.min), reads=[mn, bst], writes=[bst])
        P.op("vector", lambda e: e.tensor_scalar(col(0), col(8), -1.0, None, ALU.add), reads=[bst], writes=[bst])
        P.op("vector", lambda e: e.memset(col(6), 0.0), reads=[bst], writes=[bst])
        for it in range(NIT):
            P.op("vector", lambda e: e.tensor_tensor(col(8), col(0), col(1), ALU.add), reads=[bst], writes=[bst])
            P.op("vector", lambda e: e.tensor_scalar(col(2), col(8), 0.5, None, ALU.mult), reads=[bst], writes=[bst])
            for ci, cs in enumerate(range(0, n, CH)):
                ce = min(n, cs + CH)
                P.op("vector", lambda e: e.tensor_scalar(junk.ap()[:, 0:ce - cs], scores.ap()[:, cs:ce], col(2),
                                                         (col(3) if ci > 0 else None), ALU.is_ge, ALU.add, accum_out=col(3)),
                     reads=[scores, bst], writes=[bst, junk])
            P.op("vector", lambda e: e.tensor_scalar(col(4), col(3), cfg.TOPK - 0.5, None, ALU.is_ge), reads=[bst], writes=[bst])
            P.op("vector", lambda e: e.tensor_tensor(col(5), col(2), col(0), ALU.subtract), reads=[bst], writes=[bst])
            P.op("vector", lambda e: e.scalar_tensor_tensor(col(0), col(5), col(4), col(0), ALU.mult, ALU.add), reads=[bst], writes=[bst])
            P.op("vector", lambda e: e.tensor_tensor(col(5), col(1), col(2), ALU.subtract), reads=[bst], writes=[bst])
            P.op("vector", lambda e: e.scalar_tensor_tensor(col(1), col(5), col(4), col(2), ALU.mult, ALU.add), reads=[bst], writes=[bst])
            P.op("vector", lambda e: e.tensor_tensor(col(5), col(6), col(3), ALU.subtract), reads=[bst], writes=[bst])
            P.op("vector", lambda e: e.scalar_tensor_tensor(col(6), col(5), col(4), col(3), ALU.mult, ALU.add), reads=[bst], writes=[bst])
        P.op("vector", lambda e: e.tensor_scalar(col(7), col(6), -BIGM, cfg.TOPK * BIGM, ALU.mult, ALU.add), reads=[bst], writes=[bst])
        for bi in range(nb):
            gq, blk = bi // BPG, bi % BPG
            kt_, vb_ = ktb[bi % 2], vbk[bi % 2]
            P.dma(kt_.ap(), KTa[blk * NB:(blk + 1) * NB, :, gq * 128:(gq + 1) * 128].rearrange("j d t -> d j t"), writes=[kt_])
            for jj in range(NB):
                P.dma(vb_.ap()[:, jj, :, 0:64], Va[blk * NB + jj, gq].rearrange("s (k d) -> s k d", k=2), writes=[vb_])
            scb = scores.ap()[:, bi * BLK:(bi + 1) * BLK]
            m = mb[bi % 2]
            cm, cprev = cum[bi % 2], cum[(bi + 1) % 2]
            P.op("vector", lambda e: e.tensor_scalar(mlo.ap(), scb, col(0), BIGM, ALU.is_ge, ALU.mult), reads=[scores, bst], writes=[mlo])
            P.op("vector", lambda e: e.tensor_scalar(mhi.ap(), scb, col(1), BIGM, ALU.is_ge, ALU.mult), reads=[scores, bst], writes=[mhi])
            P.op("vector", lambda e: e.tensor_tensor(band.ap(), mlo.ap(), mhi.ap(), ALU.subtract), reads=[mlo, mhi], writes=[band])
            P.op("vector", lambda e: e.tensor_tensor_scan(cm.ap(), band.ap(), band.ap(),
                                                          (0.0 if bi == 0 else cprev.ap()[:, BLK - 1:BLK]), ALU.add, ALU.max),
                 reads=[band] + ([] if bi == 0 else [cprev]), writes=[cm])
            P.op("vector", lambda e: e.scalar_tensor_tensor(tsel.ap(), cm.ap(), col(7), band.ap(), ALU.is_le, ALU.mult),
                 reads=[cm, bst, band], writes=[tsel])
            P.op("vector", lambda e: e.scalar_tensor_tensor(m.ap(), tsel.ap(), -BIGM, mhi.ap(), ALU.add, ALU.add),
                 reads=[tsel, mhi], writes=[m])
            for jj in range(NB):
                for kvn in range(2):
                    bL = bank6()
                    P.op("tensor", lambda e: e.matmul(bL.ap(), kt_.ap()[kvn * 64:(kvn + 1) * 64, jj, :],
                                                      QT.ap()[kvn * 64:(kvn + 1) * 64].rearrange("p j t -> p (j t)"),
                                                      start=True, stop=False), reads=[kt_, QT], writes=[bL])
                    P.op("tensor", lambda e: e.matmul(bL.ap(), m.ap()[:, jj * 128:(jj + 1) * 128],
                                                      ident4.ap().rearrange("p j t -> p (j t)"), start=False, stop=True),
                         reads=[m, ident4], writes=[bL])
                    pt = PT[rot["i"] % 2]
                    P.op("scalar", lambda e: e.activation(pt.ap(), bL.ap(), AF.Exp, scale=0.125), reads=[bL], writes=[pt])
                    first = (bi == 0 and jj == 0)
                    last = (bi == nb - 1 and jj == NB - 1)
                    P.op("tensor", lambda e: e.matmul(acc[kvn].ap(), vb_.ap()[:, jj, kvn, :], pt.ap(), start=first, stop=last),
                         reads=[vb_, pt], writes=[acc[kvn]])
        bg = [bank6(), bank6()]
        for hd in range(8):
            B.proj_fm(bg[hd // 4].ap()[0:64, (hd % 4) * 128:(hd % 4 + 1) * 128], bg[hd // 4], w,
                      boff["dsag"][0] - c0 + hd * 64, 64, hT, ls)
        for hf in range(2):
            P.op("scalar", lambda e: e.activation(sgd.ap()[:, hf * 4:(hf + 1) * 4, :],
                                                  bg[hf].ap()[0:64, :].rearrange("p (a b) -> p a b", a=4), AF.Silu),
                 reads=[bg[hf]], writes=[sgd])
        for kvn in range(2):
            P.op("vector", lambda e: e.reciprocal(rc.ap(), acc[kvn].ap()[64:128, :]), reads=[acc[kvn]], writes=[rc])
            P.op("vector", lambda e: e.tensor_tensor(on.ap(), acc[kvn].ap()[0:64, :], rc.ap(), ALU.mult),
                 reads=[acc[kvn], rc], writes=[on])
            P.op("gpsimd", lambda e: e.tensor_tensor(dsaout.ap()[:, ls, kvn * 4:(kvn + 1) * 4, :],
                                                     on.ap().rearrange("p (a b) -> p a b", a=4),
                                                     sgd.ap()[:, kvn * 4:(kvn + 1) * 4, :], ALU.mult),
                 reads=[on, sgd], writes=[dsaout])
    B.bank = B_bank
    P.pop_scope()
    P.push_scope()
    w, c0 = env["load_w"](["mg1"], "w_mg1")
    wbr = B.load_weight(env["wbr_d"], 0, 1024, "wbr_dsa", kc=8, rows=64)
    mg = P.sb("d_mg", [128, 512], F32)
    t2 = P.sb("d_t2", [128, 512], F32)
    for s in range(g0, g0 + SG):
        ls = s - g0
        env["merge"](lambda nc, out_ap, bk: [P.op("tensor", lambda e, hd=hd: e.matmul(
            out_ap, wbr.ap()[:, hd, nc * 128:(nc + 1) * 128], dsaout.ap()[:, ls, hd, :],
            start=(hd == 0), stop=(hd == 7)), reads=[wbr, dsaout], writes=[bk]) for hd in range(8)],
            w, 0, ls, False, mg, t2)
    P.pop_scope()


B_CONSTS = ["ident", "triL128", "triL64", "triU64", "DTret", "DTgla", "QDret", "decR", "ropefs"]


def host_inputs_B(inp, cfg, l, xcur, resA):
    pos = np.asarray(inp["positions"])[0].reshape(cfg.S // 128, 128)
    st = lambda k: np.stack([np.asarray(r[k]) for r in resA])
    er = lambda w, h: np.ascontiguousarray(np.asarray(w)[l].reshape(h, 512 // h, D).transpose(1, 0, 2))
    shared = {
        "cvec": np.ascontiguousarray(np.asarray(inp["c"])[0].reshape(8, 128).T),
        "adaw": kc_layout(np.asarray(inp["ada_w"])[l]),
        "adab": np.asarray(inp["ada_b"])[l][None, :],
        "pre": np.asarray(inp["pre_norm"])[l][None, :],
        "post": np.asarray(inp["post_norm"])[l][None, :],
        "WB": kc_layout(np.asarray(inp["w_in"])[l][:, b_col_index()]),
        "wbr_r": er(inp["w_br_ret"], 4), "wbr_g": er(inp["w_br_gla"], 4), "wbr_d": er(inp["w_br_dsa"], 8),
        "wout": kc_layout(np.asarray(inp["w_out"])[l]),
        "wlr": np.asarray(inp["gla_w_lr"])[l],
        "blr": np.asarray(inp["gla_b_lr"])[l][None, :],
        "KTa": st("KT"), "Va": st("V"), "IKa": st("IK"),
        "kvRa": st("kvR"), "kvGa": st("kvG"), "decGa": st("decG"),
    }
    shared.update(const_inputs(B_CONSTS))
    maps = []
    for c in range(cfg.NCORE):
        tl = core_tiles(cfg, c)
        m = dict(shared)
        m["x"] = np.ascontiguousarray(xcur[tl])
        m["pos"] = np.ascontiguousarray(pos[tl].reshape(1, cfg.T)).astype(np.int32)
        sel = np.zeros((128, cfg.NCORE), np.float32)
        sel[:, c] = 1.0
        m["sel"] = sel
        kidx = np.arange(cfg.GK)[None, :]
        qidx = (c * 128 + np.arange(128))[:, None]
        import ml_dtypes
        m["pen"] = np.where(kidx > qidx, np.float32(-1e30), np.float32(0.0)).astype(ml_dtypes.bfloat16)
        maps.append(m)
    return maps


_CACHE = {}


def run_layers(inp, cfg):
    x = np.asarray(inp["x"])[0].reshape(cfg.S // 128, 128, D).astype(np.float32)
    cores = list(range(cfg.NCORE))
    for l in range(cfg.DEPTH):
        if "A" not in _CACHE:
            _CACHE["A"] = build_A(cfg)
        inp_l = dict(inp)
        inp_l["x"] = x.reshape(1, cfg.S, D)
        resA = run_bass_kernel_spmd(_CACHE["A"].nc, host_inputs_A(inp_l, cfg, l), core_ids=cores).results
        if "B" not in _CACHE:
            _CACHE["B"] = build_B(cfg)
        resB = run_bass_kernel_spmd(_CACHE["B"].nc, host_inputs_B(inp, cfg, l, x, resA), core_ids=cores).results
        xn = np.empty_like(x)
        for c in cores:
            xn[core_tiles(cfg, c)] = np.asarray(resB[c]["xo"])
        x = xn
    return x.reshape(1, cfg.S, D)


def kernel(**inputs):
    cfg = Cfg()
    return run_layers(inputs, cfg).astype(np.float32)
```

```python
import numpy as np
from contextlib import ExitStack
import concourse.bass as bass
import concourse.mybir as mybir
from concourse.bass_utils import run_bass_kernel_spmd

F32 = mybir.dt.float32
BF16 = mybir.dt.bfloat16
I32 = mybir.dt.int32
ALU = mybir.AluOpType
AF = mybir.ActivationFunctionType
AX = mybir.AxisListType

EPOCH = 20000
NDMA_SEM = 12


class Buf:
    def __init__(self, name, handle=None):
        self.name = name
        self.h = handle
        self.last_w = None
        self.readers = []

    def ap(self):
        return self.h[:]


class _Recorder:
    def __init__(self):
        self.call = None

    def __getattr__(self, name):
        def f(*a, **k):
            assert self.call is None
            self.call = (name, a, k)
            return None
        return f


class Prog:
    ENGS = ("tensor", "vector", "scalar", "gpsimd", "sync")

    def __init__(self, nc):
        self.nc = nc
        self.stack = ExitStack()
        self.stream = {e: [] for e in self.ENGS}
        self.count = {e: 0 for e in self.ENGS}
        self.known = {e: {} for e in self.ENGS}
        self.dma_n = {}
        self.dma_rr = {e: 0 for e in self.ENGS}
        self.nbuf = 0
        self.pending = {e: [] for e in self.ENGS}
        self.stacks = [self.stack]

    def sb(self, name, shape, dtype):
        self.nbuf += 1
        name = f"{name}_u{self.nbuf}"
        h = self.stacks[-1].enter_context(self.nc.sbuf_tensor(name, list(shape), dtype))
        return Buf(name, h)

    def push_scope(self):
        st = ExitStack()
        self.stacks.append(st)

    def pop_scope(self):
        self.barrier()
        self.stacks.pop().close()

    def barrier(self):
        toks = []
        for e in self.ENGS:
            c = self.count[e]
            if c > 0 and e != "sync":
                toks.append((("E", e, (c - 1) // EPOCH), (c - 1) % EPOCH + 1))
        for key, n in self.dma_n.items():
            toks.append((key, 16 * n))
        for e in self.ENGS:
            self.pending[e] = list(toks)

    def _take_pending(self, eng, waits):
        kn = self.known[eng]
        for key, val in self.pending[eng]:
            if key[0] == "E" and key[1] == eng:
                continue
            if kn.get(key, -1) >= val:
                continue
            if any(k == key and v >= val for k, v in waits):
                continue
            kn[key] = val
            waits.append((key, val))
        self.pending[eng] = []

    def ps(self, name, shape, dtype=F32):
        h = self.stack.enter_context(self.nc.psum_tensor(name, list(shape), dtype))
        return Buf(name, h)

    def alias(self, name, buf):
        raise NotImplementedError

    def _deps(self, eng, reads, writes, is_dma):
        deps = {}

        def add(tok, same_ok):
            if tok is None:
                return
            key, val = tok
            if (not is_dma) and key[0] == "E" and key[1] == eng and not same_ok:
                return
            if deps.get(key, -1) < val:
                deps[key] = val

        for b in reads:
            add(b.last_w, True)
        for b in writes:
            add(b.last_w, False)
            for t in b.readers:
                add(t, False)
        kn = self.known[eng]
        out = []
        for key, val in deps.items():
            if key[0] == "E":
                later = [k for k in kn if k[0] == "E" and k[1] == key[1] and k[2] > key[2]]
                if later:
                    continue
            if kn.get(key, -1) >= val:
                continue
            kn[key] = val
            out.append((key, val))
        return out

    def _commit(self, tok, reads, writes):
        for b in writes:
            b.last_w = tok
            b.readers = []
        for b in reads:
            if b in writes:
                continue
            rs = [t for t in b.readers if t[0] != tok[0]]
            rs.append(tok)
            b.readers = rs

    def op(self, eng, fn, reads=(), writes=()):
        reads = list(reads)
        writes = list(writes)
        waits = self._deps(eng, reads, writes, False)
        self._take_pending(eng, waits)
        self.count[eng] += 1
        c = self.count[eng]
        key = ("E", eng, (c - 1) // EPOCH)
        tok = (key, (c - 1) % EPOCH + 1)
        rec = _Recorder()
        fn(rec)
        self.stream[eng].append((waits, rec.call, key))
        self._commit(tok, reads, writes)
        return tok

    def dma(self, out_ap, in_ap, reads=(), writes=(), queue="sync", **kw):
        reads = list(reads)
        writes = list(writes)
        i = self.dma_rr[queue]
        self.dma_rr[queue] = (i + 1) % NDMA_SEM
        key = ("D", queue, i)
        n = self.dma_n.get(key, 0)
        waits = self._deps(queue, reads, writes, True)
        self._take_pending(queue, waits)
        if n > 0 and self.known[queue].get(key, -1) < 16 * n:
            self.known[queue][key] = 16 * n
            waits.append((key, 16 * n))
        self.dma_n[key] = n + 1
        tok = (key, 16 * (n + 1))
        kk = dict(kw)
        kk["out"] = out_ap
        kk["in_"] = in_ap
        self.stream[queue].append((waits, ("dma_start", (), kk), key))
        self._commit(tok, reads, writes)
        return tok

    def coll(self, kind, ins, outs, ranks, reads=(), writes=()):
        reads, writes = list(reads), list(writes)
        queue = "gpsimd"
        key = ("D", "coll", 0)
        n = self.dma_n.get(key, 0)
        waits = self._deps(queue, reads, writes, True)
        self._take_pending(queue, waits)
        if n > 0 and self.known[queue].get(key, -1) < 16 * n:
            self.known[queue][key] = 16 * n
            waits.append((key, 16 * n))
        self.dma_n[key] = n + 1
        tok = (key, 16 * (n + 1))
        call = ("collective_compute", (kind, ALU.bypass), dict(replica_groups=[list(ranks)], ins=list(ins), outs=list(outs)))
        self.stream[queue].append((waits, call, key))
        self._commit(tok, reads, writes)
        return tok

    def finish(self):
        nc = self.nc
        sems = {}

        def sem(key):
            if key not in sems:
                nm = "s_" + "_".join(str(k) for k in key)
                sems[key] = self.stack.enter_context(nc.semaphore(nm))
            return sems[key]

        final_waits = []
        for key, n in self.dma_n.items():
            final_waits.append((key, 16 * n))
        for eng in self.ENGS:
            for waits, fn, key in self.stream[eng]:
                sem(key)
                for k, v in waits:
                    sem(k)

        streams = self.stream

        def emit(eng_name):
            def body(e):
                for waits, fn, key in streams[eng_name]:
                    for k, v in waits:
                        e.wait_ge(sems[k], v)
                    ins = getattr(e, fn[0])(*fn[1], **fn[2])
                    ins.then_inc(sems[key], 16 if key[0] == "D" else 1)
                if eng_name == "sync":
                    for k, v in final_waits:
                        e.wait_ge(sems[k], v)
            return body

        with nc.Block() as block:
            block.sync(emit("sync"))
            block.tensor(emit("tensor"))
            block.vector(emit("vector"))
            block.scalar(emit("scalar"))
            block.gpsimd(emit("gpsimd"))
        while self.stacks:
            self.stacks.pop().close()


D = 1024
IN_SPLITS = (("ret_q", 256), ("ret_k", 256), ("ret_v", 512), ("ret_g", 512),
             ("dsa_q", 512), ("dsa_k", 128), ("dsa_v", 128), ("dsa_g", 512),
             ("idx_q", 256), ("idx_k", 64), ("idx_w", 4),
             ("gla_q", 256), ("gla_k", 256), ("gla_v", 512), ("gla_g", 512), ("gla_a", 16),
             ("merge", 3072))
OFF = {}
_o = 0
for _n, _w in IN_SPLITS:
    OFF[_n] = _o
    _o += _w
IN_WIDTH = _o
RMS_EPS = 1e-6
TOPK = 256
BIGM = 32768.0
LOG_G = [float(np.log1p(-2.0 ** (-5.0 - h))) for h in range(4)]


class Cfg:
    def __init__(self, S=16384, NCORE=8, DEPTH=4, SG=2, NIT=22):
        self.S, self.NCORE, self.DEPTH = S, NCORE, DEPTH
        self.TPC = S // 128 // NCORE
        self.T = self.TPC * 128
        self.GK = NCORE * 128
        self.BLK = min(512, self.GK)
        self.NB = self.BLK // 128
        self.BPG = self.GK // self.BLK
        self.SG = min(SG, self.TPC)
        self.NIT = NIT
        self.TOPK = min(256, S // 4)


def host_consts():
    f = np.float32
    j = np.arange(128)[:, None]
    i = np.arange(128)[None, :]
    c = {}
    c["ident"] = np.eye(128, dtype=f)
    same = (j // 64) == (i // 64)
    c["triL128"] = (j <= i).astype(f)
    c["triL64"] = ((j <= i) & same).astype(f)
    c["triU64"] = ((j > i) & same).astype(f)
    ci = np.zeros((128, 4), f)
    ci[:64, 0] = 1
    ci[64:, 1] = 1
    ci[:, 2] = 1
    c["chunkind"] = ci
    lg = np.array(LOG_G, np.float64)
    dt = np.zeros((128, 4, 128), np.float64)
    for h in range(4):
        dt[:, h, :] = np.where(i >= j, np.exp(lg[h] * np.maximum(i - j, 0)), 0.0)
    c["DTret"] = (dt * 0.125).astype(f)
    c["DTgla"] = np.repeat(((j <= i) & same).astype(f)[:, None, :], 4, axis=1)
    qd = np.zeros((128, 4, 128), np.float64)
    for h in range(4):
        qd[:, h, :] = np.exp(lg[h] * (i + 1.0))
    c["QDret"] = (qd * 0.125).astype(f)
    kf = np.zeros((128, 4), np.float64)
    for h in range(4):
        kf[:, h] = np.exp(lg[h] * (127.0 - np.arange(128)))
    c["kfacR"] = kf.astype(f)
    dr = np.zeros((128, 4), np.float64)
    for h in range(4):
        dr[:, h] = np.exp(lg[h] * 128.0)
    c["decR"] = dr.astype(f)
    fr = np.zeros((128, 4), f)
    p = np.arange(128) % 64
    half = 32
    fr_ret = (np.float32(10000.0) ** (-(np.arange(half, dtype=f)) * f(2.0) / f(64))).astype(f)
    fr[:, 0] = fr_ret[p % 32]
    fr[:, 1] = np.where(p < 32, -1.0, 1.0)
    fr_d = (np.float32(500000.0) ** (-(np.arange(8, dtype=f)) * f(2.0) / f(16))).astype(f)
    fr[:, 2] = np.where(p < 16, fr_d[p % 8], 0.0)
    fr[:, 3] = np.where(p < 8, -1.0, np.where(p < 16, 1.0, 0.0))
    c["ropefs"] = fr
    c["pw2"] = np.repeat((2.0 ** -(np.arange(32, dtype=np.float64) + 1.0))[None, :], 128, axis=0).astype(f)
    return c


def swap_cols(lo, rot, width, nheads):
    idx = []
    for h in range(nheads):
        base = lo + h * width
        half = rot // 2
        for d in range(width):
            if d < half:
                idx.append(base + d + half)
            elif d < rot:
                idx.append(base + d - half)
            else:
                idx.append(base + d)
    return idx


A_COLS = [("retk", 256), ("retk_sw", 256), ("retv", 512), ("dsak", 128), ("dsak_sw", 128),
          ("dsav", 128), ("idxk", 64), ("idxk_sw", 64), ("glak", 256), ("glav", 512), ("glaa", 16)]


def col_layout(cols):
    off, o = {}, 0
    for n, w in cols:
        off[n] = (o, w)
        o += w
    return off, o


class Builder:
    def __init__(self, cfg, name):
        self.cfg = cfg
        self.nc = bass.Bass("TRN2", target_bir_lowering=False, name=name)
        self.P = Prog(self.nc)
        self.din = {}
        self.dout = {}
        self.bank_i = 0

    def inp(self, name, shape, dtype=F32):
        t = self.nc.dram_tensor(name, list(shape), dtype, kind="ExternalInput")
        self.din[name] = t
        return t.ap()

    def outp(self, name, shape, dtype=F32):
        t = self.nc.dram_tensor(name, list(shape), dtype, kind="ExternalOutput")
        self.dout[name] = t
        return t.ap()

    def make_banks(self):
        self.banks = [self.P.ps(f"bank{i}", [128, 512], F32) for i in range(8)]

    def bank(self):
        b = self.banks[self.bank_i % 8]
        self.bank_i += 1
        return b

    def load_consts(self, names):
        P = self.P
        hc = host_consts()
        self.c = {}
        self.cb = {}
        for n in names:
            shp = list(hc[n].shape)
            ap = self.inp("c_" + n, shp)
            t = P.sb("sc_" + n, shp, F32)
            P.dma(t.ap(), ap, writes=[t])
            self.c[n] = t
        self.identb = P.sb("identb", [128, 128], BF16)
        P.op("vector", lambda e: e.tensor_copy(self.identb.ap(), self.c["ident"].ap()),
             reads=[self.c["ident"]], writes=[self.identb])
        self.onesb = P.sb("onesb", [128, 128], BF16)
        P.op("vector", lambda e: e.memset(self.onesb.ap(), 1.0), writes=[self.onesb])

    def load_weight(self, dram_ap, c0, ncols, name, kc=8, rows=128):
        P = self.P
        wt = P.sb(name, [rows, kc, ncols], BF16)
        CH = 64 if hasattr(self, "wstage") else 128
        if not hasattr(self, "wstage"):
            self.wstage = [P.sb(f"wstage{i}", [128, 8, CH], F32) for i in range(2)]
            self.wstage_i = 0
        for s in range(0, ncols, CH):
            n = min(CH, ncols - s)
            st = self.wstage[self.wstage_i % 2]
            self.wstage_i += 1
            P.dma(st.ap()[0:rows, 0:kc, 0:n], dram_ap[:, :, c0 + s:c0 + s + n], writes=[st])
            P.op("gpsimd", lambda e, st=st, s=s, n=n: e.tensor_copy(
                wt.ap()[:, :, s:s + n], st.ap()[0:rows, 0:kc, 0:n]), reads=[st], writes=[wt])
        return wt

    def prealloc(self, ncore):
        P = self.P
        self.wstage = [P.sb(f"wstage{i}", [128, 8, 64], F32) for i in range(2)]
        self.wstage_i = 0
        self.rp = [P.sb(f"rp{i}", [128, 512], F32) for i in range(2)]
        self.a17 = P.sb("a17", [17, 128], BF16)
        P.op("vector", lambda e: e.memset(self.a17.ap(), 1.0), writes=[self.a17])
        self.e1 = P.sb("gl_e1", [128, 256], F32)
        self.sp = P.sb("gl_sp", [128, 256], F32)
        self.kvt = [P.sb(f"kvt{i}", [64, 512], F32) for i in range(2)]
        self.dcg_g = P.sb("dcg_g", [64, ncore, 4], F32)

    def emit_mod(self, cvec_ap, adaw_ap, adab_ap, pre_ap, post_ap, want_gate):
        P = self.P
        self.G = P.sb("G", [128, 1024], F32)
        self.Sh = P.sb("Sh", [128, 1024], F32)
        if want_gate:
            self.GP = P.sb("GP", [128, 1024], F32)
        P.push_scope()
        cv = P.sb("cv", [128, 8], F32)
        P.dma(cv.ap(), cvec_ap, writes=[cv])
        ca = P.sb("ca", [128, 8], F32)
        P.op("scalar", lambda e: e.activation(ca.ap(), cv.ap(), AF.Silu), reads=[cv], writes=[ca])
        cbc = P.sb("cbc", [128, 8, 128], F32)
        P.op("vector", lambda e: e.tensor_copy(cbc.ap(), ca.ap().unsqueeze(2).to_broadcast([128, 8, 128])),
             reads=[ca], writes=[cbc])
        mod = P.sb("mod", [128, 3072], F32)
        bias = P.sb("modb", [128, 3072], F32)
        P.dma(bias.ap(), adab_ap.to_broadcast([128, 3072]), writes=[bias])
        wst = [P.sb(f"adaw{i}", [128, 8, 512], F32) for i in range(2)]
        for ci in range(6):
            w = wst[ci % 2]
            P.dma(w.ap(), adaw_ap[:, :, ci * 512:(ci + 1) * 512], writes=[w])
            bk = self.bank()
            for kc in range(8):
                P.op("tensor", lambda e, bk=bk, w=w, kc=kc: e.matmul(
                    bk.ap(), cbc.ap()[:, kc, :], w.ap()[:, kc, :], start=(kc == 0), stop=(kc == 7)),
                    reads=[cbc, w], writes=[bk])
            P.op("vector", lambda e, bk=bk, ci=ci: e.tensor_tensor(
                mod.ap()[:, ci * 512:(ci + 1) * 512], bk.ap(), bias.ap()[:, ci * 512:(ci + 1) * 512], ALU.add),
                reads=[bk, bias], writes=[mod])
        pre = P.sb("preb", [128, 1024], F32)
        P.dma(pre.ap(), pre_ap.to_broadcast([128, 1024]), writes=[pre])
        P.op("vector", lambda e: e.scalar_tensor_tensor(
            self.G.ap(), mod.ap()[:, 1024:2048], 1.0, pre.ap(), ALU.add, ALU.mult),
            reads=[mod, pre], writes=[self.G])
        P.op("vector", lambda e: e.tensor_copy(self.Sh.ap(), mod.ap()[:, 0:1024]), reads=[mod], writes=[self.Sh])
        if want_gate:
            post = P.sb("postb", [128, 1024], F32)
            P.dma(post.ap(), post_ap.to_broadcast([128, 1024]), writes=[post])
            P.op("vector", lambda e: e.tensor_tensor(self.GP.ap(), mod.ap()[:, 2048:3072], post.ap(), ALU.mult),
                 reads=[mod, post], writes=[self.GP])
        P.pop_scope()

    def emit_ropes(self, pos_ap, specs):
        P = self.P
        outs = []
        for fcol, scol, name in specs:
            outs.append((P.sb(name + "_cos", [128, self.cfg.T], BF16), P.sb(name + "_sin", [128, self.cfg.T], BF16)))
        P.push_scope()
        for (fcol, scol, name), (cb_, sb_) in zip(specs, outs):
            self.emit_rope(pos_ap, fcol, scol, name, cb_, sb_)
        P.pop_scope()
        del self.posf
        return outs

    def emit_rope(self, pos_ap, fcol, scol, name, cosb, sinb):
        P = self.P
        T = self.cfg.T
        fs = self.c["ropefs"]
        if not hasattr(self, "posf"):
            posi = P.sb("posi", [128, T], I32)
            P.dma(posi.ap(), pos_ap.to_broadcast([128, T]), writes=[posi])
            self.posf = P.sb("posf", [128, T], F32)
            P.op("vector", lambda e: e.tensor_copy(self.posf.ap(), posi.ap()), reads=[posi], writes=[self.posf])
            self.rtmp = [P.sb(f"rtmp{i}", [128, T], F32) for i in range(4)]
        ang, kf, r, rd = self.rtmp
        posf = self.posf
        PI = float(np.pi)
        C1 = 6.28125
        C2 = float(2.0 * np.pi - 6.28125)
        MAG = 12582912.0
        P.op("vector", lambda e: e.tensor_scalar(ang.ap(), posf.ap(), fs.ap()[:, fcol:fcol + 1], None, ALU.mult),
             reads=[posf, fs], writes=[ang])

        def reduce_and_sin(src, dst_final, shift):
            dst = rd
            if shift != 0.0:
                P.op("vector", lambda e: e.tensor_scalar(r.ap(), src.ap(), shift, None, ALU.add),
                     reads=[src], writes=[r])
                s2 = r
            else:
                s2 = src
            P.op("vector", lambda e: e.tensor_scalar(kf.ap(), s2.ap(), float(1.0 / (2 * np.pi)), MAG, ALU.mult, ALU.add),
                 reads=[s2], writes=[kf])
            P.op("vector", lambda e: e.tensor_scalar(kf.ap(), kf.ap(), MAG, None, ALU.subtract),
                 reads=[kf], writes=[kf])
            P.op("vector", lambda e: e.scalar_tensor_tensor(dst.ap(), kf.ap(), -C1, s2.ap(), ALU.mult, ALU.add),
                 reads=[kf, s2], writes=[dst])
            P.op("vector", lambda e: e.scalar_tensor_tensor(dst.ap(), kf.ap(), -C2, dst.ap(), ALU.mult, ALU.add),
                 reads=[kf, dst], writes=[dst])
            P.op("vector", lambda e: e.tensor_scalar(dst.ap(), dst.ap(), PI, -PI, ALU.min, ALU.max),
                 reads=[dst], writes=[dst])
            P.op("scalar", lambda e: e.activation(dst_final.ap(), dst.ap(), AF.Sin), reads=[dst], writes=[dst_final])

        reduce_and_sin(ang, sinb, 0.0)
        reduce_and_sin(ang, cosb, float(np.pi / 2))
        P.op("vector", lambda e: e.tensor_scalar(sinb.ap(), sinb.ap(), fs.ap()[:, scol:scol + 1], None, ALU.mult),
             reads=[sinb, fs], writes=[sinb])
        return cosb, sinb

    def emit_norm_T(self, x_tile_ap_dram, hT, slot):
        P = self.P
        if not hasattr(self, "xin"):
            self.xin = [P.sb(f"xin{i}", [128, 1024], F32) for i in range(1)]
            self.xin_i = 0
            self.hjunk = P.sb("hjunk", [128, 1024], BF16)
            self.h1 = P.sb("h1", [128, 1024], F32)
            self.hb = P.sb("hb", [128, 1024], BF16)
            self.nst = P.sb("nst", [128, 4], F32)
        xt = self.xin[0]
        self.xin_i += 1
        nst = self.nst
        P.dma(xt.ap(), x_tile_ap_dram, writes=[xt])
        P.op("scalar", lambda e: e.activation(self.hjunk.ap(), xt.ap(), AF.Square, accum_out=nst.ap()[:, 0:1]),
             reads=[xt], writes=[self.hjunk, nst])
        P.op("vector", lambda e: e.tensor_scalar(nst.ap()[:, 1:2], nst.ap()[:, 0:1], 1.0 / D, RMS_EPS, ALU.mult, ALU.add),
             reads=[nst], writes=[nst])
        P.op("scalar", lambda e: e.activation(nst.ap()[:, 2:3], nst.ap()[:, 1:2], AF.Sqrt), reads=[nst], writes=[nst])
        P.op("vector", lambda e: e.reciprocal(nst.ap()[:, 3:4], nst.ap()[:, 2:3]), reads=[nst], writes=[nst])
        P.op("vector", lambda e: e.scalar_tensor_tensor(self.h1.ap(), xt.ap(), nst.ap()[:, 3:4], self.G.ap(), ALU.mult, ALU.mult),
             reads=[xt, nst, self.G], writes=[self.h1])
        P.op("gpsimd", lambda e: e.tensor_tensor(self.hb.ap(), self.h1.ap(), self.Sh.ap(), ALU.add),
             reads=[self.h1, self.Sh], writes=[self.hb])
        for half in range(2):
            bk = self.bank()
            for q in range(4):
                kc = half * 4 + q
                P.op("tensor", lambda e, bk=bk, q=q, kc=kc: e.matmul(
                    bk.ap()[:, q * 128:(q + 1) * 128], self.hb.ap()[:, kc * 128:(kc + 1) * 128], self.identb.ap(),
                    start=True, stop=True), reads=[self.hb, self.identb], writes=[bk])
            P.op("scalar", lambda e, bk=bk, half=half: e.activation(
                hT.ap()[:, slot, half * 4:half * 4 + 4, :], bk.ap().rearrange("p (a b) -> p a b", a=4), AF.Copy),
                reads=[bk], writes=[hT])

    def proj_fm(self, out_ap, bk, w, c0, m, hT, slot, nslots=1):
        P = self.P
        for kc in range(8):
            if nslots == 1:
                rhs = hT.ap()[:, slot, kc, :]
            else:
                rhs = hT.ap()[:, slot:slot + nslots, kc, :]
            P.op("tensor", lambda e, kc=kc, rhs=rhs: e.matmul(
                out_ap, w.ap()[:, kc, c0:c0 + m], rhs, start=(kc == 0), stop=(kc == 7)),
                reads=[w, hT], writes=[bk])

    def proj_tm(self, out_ap, bk, w, c0, n, hT, slot):
        P = self.P
        for kc in range(8):
            P.op("tensor", lambda e, kc=kc: e.matmul(
                out_ap, hT.ap()[:, slot, kc, :], w.ap()[:, kc, c0:c0 + n], start=(kc == 0), stop=(kc == 7)),
                reads=[w, hT], writes=[bk])

    def rope_fm(self, dst_ap, dst_buf, bx, bs, npart, ncols_ap, cosb, sinb, tok0, scale=None):
        P = self.P
        if not hasattr(self, "rp"):
            self.rp = [P.sb(f"rp{i}", [128, 512], F32) for i in range(2)]
        t0, t1 = self.rp
        nh = ncols_ap // 128
        cosv = cosb.ap()[0:npart, tok0:tok0 + 128].unsqueeze(1).to_broadcast([npart, nh, 128])
        sinv = sinb.ap()[0:npart, tok0:tok0 + 128].unsqueeze(1).to_broadcast([npart, nh, 128])
        v = lambda b: b.ap()[0:npart, 0:ncols_ap].rearrange("p (h t) -> p h t", h=nh)
        P.op("vector", lambda e: e.tensor_tensor(v(t0), v(bx), cosv, ALU.mult), reads=[bx, cosb], writes=[t0])
        P.op("vector", lambda e: e.tensor_tensor(v(t1), v(bs), sinv, ALU.mult), reads=[bs, sinb], writes=[t1])
        if scale is None:
            P.op("vector", lambda e: e.tensor_tensor(dst_ap, v(t0), v(t1), ALU.add), reads=[t0, t1], writes=[dst_buf])
        else:
            P.op("vector", lambda e: e.scalar_tensor_tensor(dst_ap, v(t0), 1.0, v(t1), ALU.mult, ALU.add),
                 reads=[t0, t1], writes=[dst_buf])


def gla_decay_common(B, hT, slot, wA, aoff, wlr17):
    P = B.P
    if not hasattr(B, "a17"):
        B.a17 = P.sb("a17", [17, 128], BF16)
        P.op("vector", lambda e: e.memset(B.a17.ap(), 1.0), writes=[B.a17])
        B.e1 = P.sb("gl_e1", [128, 256], F32)
        B.sp = P.sb("gl_sp", [128, 256], F32)
    bk = B.bank()
    B.proj_fm(bk.ap()[0:16, 0:128], bk, wA, aoff, 16, hT, slot)
    P.op("vector", lambda e: e.tensor_copy(B.a17.ap()[0:16, :], bk.ap()[0:16, 0:128]), reads=[bk], writes=[B.a17])
    bz = B.bank()
    P.op("tensor", lambda e: e.matmul(bz.ap()[:, 0:256], B.a17.ap(), wlr17.ap(), start=True, stop=True),
         reads=[B.a17, wlr17], writes=[bz])
    P.op("scalar", lambda e: e.activation(B.e1.ap(), bz.ap()[:, 0:256], AF.Exp, scale=-1.0), reads=[bz], writes=[B.e1])
    P.op("scalar", lambda e: e.activation(B.sp.ap(), B.e1.ap(), AF.Ln, bias=1.0), reads=[B.e1], writes=[B.sp])
    return B.sp


def load_wlr17(B, wlr_ap, blr_ap):
    P = B.P
    st = P.sb("wlr_st", [17, 256], F32)
    P.dma(st.ap()[0:16, :], wlr_ap, writes=[st])
    P.dma(st.ap()[16:17, :], blr_ap, writes=[st])
    w = P.sb("wlr17", [17, 256], BF16)
    P.op("vector", lambda e: e.tensor_copy(w.ap(), st.ap()), reads=[st], writes=[w])
    return w


def build_A(cfg):
    B = Builder(cfg, "phaseA")
    P = B.P
    TPC, T = cfg.TPC, cfg.T
    aoff, ncolA = col_layout(A_COLS)
    x = B.inp("x", [TPC, 128, 1024])
    pos = B.inp("pos", [1, T], I32)
    cvec = B.inp("cvec", [128, 8])
    adaw = B.inp("adaw", [128, 8, 3072])
    adab = B.inp("adab", [1, 3072])
    pre = B.inp("pre", [1, 1024])
    WA = B.inp("WA", [128, 8, ncolA])
    wlr = B.inp("wlr", [16, 256])
    blr = B.inp("blr", [1, 256])
    oKT = B.outp("KT", [128, T], BF16)
    oV = B.outp("V", [TPC, 128, 128], BF16)
    oIK = B.outp("IK", [64, T], BF16)
    okvR = B.outp("kvR", [TPC, 64, 512])
    okvG = B.outp("kvG", [TPC, 64, 512])
    odecG = B.outp("decG", [TPC, 64, 4])

    B.make_banks()
    B.load_consts(["ident", "triU64", "chunkind", "kfacR", "ropefs"])
    B.emit_mod(cvec, adaw, adab, pre, None, False)
    (cosR, sinR), (cosD, sinD) = B.emit_ropes(pos, [(0, 1, "rr"), (2, 3, "rd")])
    wA = B.load_weight(WA, 0, ncolA, "wA")
    wlr17 = load_wlr17(B, wlr, blr)
    hT = P.sb("hT", [128, 1, 8, 128], BF16)

    kTb = P.sb("kTb", [64, 4, 128], BF16)
    khat = P.sb("khat", [128, 4, 64], BF16)
    vtok = P.sb("vtok", [128, 512], BF16)
    kvs = [P.sb(f"kvs{i}", [64, 512], F32) for i in range(2)]
    ktb = P.sb("ktb", [128, 128], BF16)
    vb = P.sb("vb", [128, 128], BF16)
    ikb = P.sb("ikb", [64, 128], BF16)
    kfac = P.sb("kfac", [128, 256], F32)
    gkhat = P.sb("gkhat", [128, 256], BF16)
    gvtok = P.sb("gvtok", [128, 512], BF16)
    dec = P.sb("dec", [64, 4, 4], F32)
    kv1s = P.sb("kv1s", [64, 512], F32)
    decT = P.sb("decT", [64, 4], F32)
    ident = B.c["ident"]

    for s in range(TPC):
        t0 = s * 128
        B.emit_norm_T(x[s], hT, 0)
        bx, bs = B.bank(), B.bank()
        for h in range(4):
            B.proj_fm(bx.ap()[0:64, h * 128:(h + 1) * 128], bx, wA, aoff["retk"][0] + h * 64, 64, hT, 0)
            B.proj_fm(bs.ap()[0:64, h * 128:(h + 1) * 128], bs, wA, aoff["retk_sw"][0] + h * 64, 64, hT, 0)
        B.rope_fm(kTb.ap(), kTb, bx, bs, 64, 512, cosR, sinR, t0)
        bt = B.bank()
        for h in range(4):
            P.op("tensor", lambda e, h=h: e.matmul(bt.ap()[:, h * 64:(h + 1) * 64], kTb.ap()[:, h, :],
                                                   B.identb.ap()[0:64, 0:64], start=True, stop=True),
                 reads=[kTb, B.identb], writes=[bt])
        P.op("vector", lambda e: e.tensor_tensor(
            khat.ap(), bt.ap()[:, 0:256].rearrange("p (h d) -> p h d", h=4),
            B.c["kfacR"].ap().unsqueeze(2).to_broadcast([128, 4, 64]), ALU.mult),
            reads=[bt, B.c["kfacR"]], writes=[khat])
        bv = B.bank()
        B.proj_tm(bv.ap(), bv, wA, aoff["retv"][0], 512, hT, 0)
        P.op("scalar", lambda e: e.activation(vtok.ap(), bv.ap(), AF.Copy), reads=[bv], writes=[vtok])
        bkv = B.bank()
        for h in range(4):
            P.op("tensor", lambda e, h=h: e.matmul(bkv.ap()[0:64, h * 128:(h + 1) * 128], khat.ap()[:, h, :],
                                                   vtok.ap()[:, h * 128:(h + 1) * 128], start=True, stop=True),
                 reads=[khat, vtok], writes=[bkv])
        kv = kvs[0]
        P.op("scalar", lambda e: e.activation(kv.ap(), bkv.ap()[0:64, :], AF.Copy), reads=[bkv], writes=[kv])
        P.dma(okvR[s], kv.ap(), reads=[kv])
        bx, bs = B.bank(), B.bank()
        B.proj_fm(bx.ap()[:, 0:128], bx, wA, aoff["dsak"][0], 128, hT, 0)
        B.proj_fm(bs.ap()[:, 0:128], bs, wA, aoff["dsak_sw"][0], 128, hT, 0)
        B.rope_fm(ktb.ap().unsqueeze(1), ktb, bx, bs, 128, 128, cosD, sinD, t0)
        P.dma(oKT[:, t0:t0 + 128], ktb.ap(), reads=[ktb])
        bv = B.bank()
        B.proj_tm(bv.ap()[:, 0:128], bv, wA, aoff["dsav"][0], 128, hT, 0)
        P.op("scalar", lambda e: e.activation(vb.ap(), bv.ap()[:, 0:128], AF.Copy), reads=[bv], writes=[vb])
        P.dma(oV[s], vb.ap(), reads=[vb])
        bx, bs = B.bank(), B.bank()
        B.proj_fm(bx.ap()[0:64, 0:128], bx, wA, aoff["idxk"][0], 64, hT, 0)
        B.proj_fm(bs.ap()[0:64, 0:128], bs, wA, aoff["idxk_sw"][0], 64, hT, 0)
        B.rope_fm(ikb.ap().unsqueeze(1), ikb, bx, bs, 64, 128, cosD, sinD, t0)
        P.dma(oIK[:, t0:t0 + 128], ikb.ap(), reads=[ikb])
        sp = gla_decay_common(B, hT, 0, wA, aoff["glaa"][0], wlr17)
        bd = B.bank()
        P.op("tensor", lambda e: e.matmul(bd.ap()[:, 0:256], B.c["triU64"].ap(), sp.ap(), start=True, stop=True),
             reads=[B.c["triU64"], sp], writes=[bd])
        P.op("scalar", lambda e: e.activation(kfac.ap(), bd.ap()[:, 0:256], AF.Exp, scale=-1.0 / 16.0),
             reads=[bd], writes=[kfac])
        bk = B.bank()
        B.proj_tm(bk.ap()[:, 0:256], bk, wA, aoff["glak"][0], 256, hT, 0)
        P.op("vector", lambda e: e.tensor_tensor(gkhat.ap(), bk.ap()[:, 0:256], kfac.ap(), ALU.mult),
             reads=[bk, kfac], writes=[gkhat])
        bv = B.bank()
        B.proj_tm(bv.ap(), bv, wA, aoff["glav"][0], 512, hT, 0)
        P.op("scalar", lambda e: e.activation(gvtok.ap(), bv.ap(), AF.Copy), reads=[bv], writes=[gvtok])
        b0, b1 = B.bank(), B.bank()
        for ch, bb in ((0, b0), (1, b1)):
            for h in range(4):
                P.op("tensor", lambda e, h=h, ch=ch, bb=bb: e.matmul(
                    bb.ap()[0:64, h * 128:(h + 1) * 128], gkhat.ap()[ch * 64:(ch + 1) * 64, h * 64:(h + 1) * 64],
                    gvtok.ap()[ch * 64:(ch + 1) * 64, h * 128:(h + 1) * 128], start=True, stop=True),
                    reads=[gkhat, gvtok], writes=[bb])
        bs_ = B.bank()
        for h in range(4):
            P.op("tensor", lambda e, h=h: e.matmul(bs_.ap()[0:64, h * 4:(h + 1) * 4], sp.ap()[:, h * 64:(h + 1) * 64],
                                                   B.c["chunkind"].ap(), start=True, stop=True),
                 reads=[sp, B.c["chunkind"]], writes=[bs_])
        P.op("scalar", lambda e: e.activation(dec.ap(), bs_.ap()[0:64, 0:16].rearrange("p (h c) -> p h c", h=4),
                                              AF.Exp, scale=-1.0 / 16.0), reads=[bs_], writes=[dec])
        P.op("scalar", lambda e: e.activation(kv1s.ap(), b1.ap()[0:64, :], AF.Copy), reads=[b1], writes=[kv1s])
        kv = kvs[1]
        for h in range(4):
            P.op("vector", lambda e, h=h: e.scalar_tensor_tensor(
                kv.ap()[:, h * 128:(h + 1) * 128], b0.ap()[0:64, h * 128:(h + 1) * 128], dec.ap()[:, h, 1:2],
                kv1s.ap()[:, h * 128:(h + 1) * 128], ALU.mult, ALU.add),
                reads=[b0, dec, kv1s], writes=[kv])
        P.dma(okvG[s], kv.ap(), reads=[kv])
        P.op("vector", lambda e: e.tensor_copy(decT.ap(), dec.ap()[:, :, 2]), reads=[dec], writes=[decT])
        P.dma(odecG[s], decT.ap(), reads=[decT])
    P.finish()
    return B


def prep_common(inputs, cfg, l, core):
    raise NotImplementedError


def a_col_index():
    ar = np.arange
    idx = {
        "retk": OFF["ret_k"] + ar(256), "retk_sw": np.array(swap_cols(OFF["ret_k"], 64, 64, 4)),
        "retv": OFF["ret_v"] + ar(512),
        "dsak": OFF["dsa_k"] + ar(128), "dsak_sw": np.array(swap_cols(OFF["dsa_k"], 16, 64, 2)),
        "dsav": OFF["dsa_v"] + ar(128),
        "idxk": OFF["idx_k"] + ar(64), "idxk_sw": np.array(swap_cols(OFF["idx_k"], 16, 64, 1)),
        "glak": OFF["gla_k"] + ar(256), "glav": OFF["gla_v"] + ar(512), "glaa": OFF["gla_a"] + ar(16),
    }
    return np.concatenate([idx[n] for n, _ in A_COLS])


def kc_layout(w):
    n = w.shape[1]
    return np.ascontiguousarray(w.reshape(8, 128, n).transpose(1, 0, 2))


def core_tiles(cfg, c):
    return [k * cfg.NCORE + c for k in range(cfg.TPC)]


def const_inputs(names):
    hc = host_consts()
    return {"c_" + n: hc[n] for n in names}


def host_inputs_A(inp, cfg, l):
    x = np.asarray(inp["x"])[0].reshape(cfg.S // 128, 128, D)
    pos = np.asarray(inp["positions"])[0].reshape(cfg.S // 128, 128)
    shared = {
        "cvec": np.ascontiguousarray(np.asarray(inp["c"])[0].reshape(8, 128).T),
        "adaw": kc_layout(np.asarray(inp["ada_w"])[l]),
        "adab": np.asarray(inp["ada_b"])[l][None, :],
        "pre": np.asarray(inp["pre_norm"])[l][None, :],
        "WA": kc_layout(np.asarray(inp["w_in"])[l][:, a_col_index()]),
        "wlr": np.asarray(inp["gla_w_lr"])[l],
        "blr": np.asarray(inp["gla_b_lr"])[l][None, :],
    }
    shared.update(const_inputs(["ident", "triU64", "chunkind", "kfacR", "ropefs"]))
    maps = []
    for c in range(cfg.NCORE):
        tl = core_tiles(cfg, c)
        m = dict(shared)
        m["x"] = np.ascontiguousarray(x[tl])
        m["pos"] = np.ascontiguousarray(pos[tl].reshape(1, cfg.T)).astype(np.int32)
        maps.append(m)
    return maps


def np_inputs(S, seed=0, depth=4):
    r = np.random.RandomState(seed)
    f = np.float32
    n = lambda *s: r.randn(*s).astype(f)
    Dm = D
    return {
        "x": n(1, S, Dm), "c": n(1, Dm),
        "positions": (np.arange(S, dtype=np.int32)[None, :] + np.int32(r.randint(0, 4096))),
        "ada_w": n(depth, Dm, 3 * Dm) * f(0.1 * Dm ** -0.5), "ada_b": n(depth, 3 * Dm) * f(0.02),
        "pre_norm": 1 + f(0.05) * n(depth, Dm), "post_norm": 1 + f(0.05) * n(depth, Dm),
        "w_in": n(depth, Dm, IN_WIDTH) * f(Dm ** -0.5),
        "gla_w_lr": n(depth, 16, 256) * f(0.25), "gla_b_lr": f(0.1) * n(depth, 256),
        "w_br_ret": n(depth, 512, Dm) * f(512 ** -0.5), "w_br_dsa": n(depth, 512, Dm) * f(512 ** -0.5),
        "w_br_gla": n(depth, 512, Dm) * f(512 ** -0.5), "w_out": n(depth, Dm, Dm) * f(Dm ** -0.5),
    }


B_COLS = [("retq", 256), ("retq_sw", 256), ("retk", 256), ("retk_sw", 256), ("retv", 512), ("retg", 512),
          ("mg0", 1024),
          ("glaq", 256), ("glak", 256), ("glav", 512), ("glag", 512), ("glaa", 16), ("mg2", 1024),
          ("dsaq", 512), ("dsaq_sw", 512), ("idxq", 256), ("idxq_sw", 256), ("idxw", 4), ("dsag", 512),
          ("mg1", 1024)]


def b_col_index():
    ar = np.arange
    pair = np.concatenate([np.concatenate([ar(64) + j * 64, ar(64) + (j + 4) * 64]) for j in range(4)])
    dq = OFF["dsa_q"] + ar(512)
    dq_sw = np.array(swap_cols(OFF["dsa_q"], 16, 64, 8))
    idx = {
        "retq": OFF["ret_q"] + ar(256), "retq_sw": np.array(swap_cols(OFF["ret_q"], 64, 64, 4)),
        "retk": OFF["ret_k"] + ar(256), "retk_sw": np.array(swap_cols(OFF["ret_k"], 64, 64, 4)),
        "retv": OFF["ret_v"] + ar(512), "retg": OFF["ret_g"] + ar(512),
        "mg0": OFF["merge"] + ar(1024), "mg1": OFF["merge"] + 1024 + ar(1024), "mg2": OFF["merge"] + 2048 + ar(1024),
        "glaq": OFF["gla_q"] + ar(256), "glak": OFF["gla_k"] + ar(256), "glav": OFF["gla_v"] + ar(512),
        "glag": OFF["gla_g"] + ar(512), "glaa": OFF["gla_a"] + ar(16),
        "dsaq": dq[pair], "dsaq_sw": dq_sw[pair],
        "idxq": OFF["idx_q"] + ar(256), "idxq_sw": np.array(swap_cols(OFF["idx_q"], 16, 64, 4)),
        "idxw": OFF["idx_w"] + ar(4), "dsag": OFF["dsa_g"] + ar(512),
    }
    return np.concatenate([idx[n] for n, _ in B_COLS])


def build_B(cfg):
    B = Builder(cfg, "phaseB")
    P = B.P
    TPC, T, NCORE, SG = cfg.TPC, cfg.T, cfg.NCORE, cfg.SG
    GK, BLK, NB, BPG = cfg.GK, cfg.BLK, cfg.NB, cfg.BPG
    boff, ncolB = col_layout(B_COLS)
    x = B.inp("x", [TPC, 128, 1024])
    pos = B.inp("pos", [1, T], I32)
    cvec = B.inp("cvec", [128, 8])
    adaw = B.inp("adaw", [128, 8, 3072])
    adab = B.inp("adab", [1, 3072])
    pre = B.inp("pre", [1, 1024])
    post = B.inp("post", [1, 1024])
    WB = B.inp("WB", [128, 8, ncolB])
    wbr_r = B.inp("wbr_r", [128, 4, 1024])
    wbr_g = B.inp("wbr_g", [128, 4, 1024])
    wbr_d = B.inp("wbr_d", [64, 8, 1024])
    wout_d = B.inp("wout", [128, 8, 1024])
    wlr = B.inp("wlr", [16, 256])
    blr = B.inp("blr", [1, 256])
    KTa = B.inp("KTa", [NCORE, 128, T], BF16)
    Va = B.inp("Va", [NCORE, TPC, 128, 128], BF16)
    IKa = B.inp("IKa", [NCORE, 64, T], BF16)
    kvRa = B.inp("kvRa", [NCORE, TPC, 64, 512])
    kvGa = B.inp("kvGa", [NCORE, TPC, 64, 512])
    decGa = B.inp("decGa", [NCORE, TPC, 64, 4])
    sel_d = B.inp("sel", [128, NCORE])
    pen_d = B.inp("pen", [128, GK], BF16)
    xo = B.outp("xo", [TPC, 128, 1024])

    B.make_banks()
    B.load_consts(B_CONSTS)
    sel = P.sb("sel", [128, NCORE], F32)
    P.dma(sel.ap(), sel_d, writes=[sel])
    pen = P.sb("pen", [128, GK], BF16)
    P.dma(pen.ap(), pen_d, writes=[pen])
    ident4 = P.sb("ident4", [128, 4, 128], BF16)
    P.op("vector", lambda e: e.tensor_copy(ident4.ap(), B.c["ident"].ap().unsqueeze(1).to_broadcast([128, 4, 128])),
         reads=[B.c["ident"]], writes=[ident4])
    B.emit_mod(cvec, adaw, adab, pre, post, True)
    (cosR, sinR), (cosD, sinD) = B.emit_ropes(pos, [(0, 1, "rr"), (2, 3, "rd")])
    wlr17 = load_wlr17(B, wlr, blr)
    B.prealloc(NCORE)
    SR = P.sb("SR", [64, 4, 128], F32)
    SGs = P.sb("SGs", [64, 4, 128], F32)
    P.op("vector", lambda e: e.memset(SR.ap(), 0.0), writes=[SR])
    P.op("vector", lambda e: e.memset(SGs.ap(), 0.0), writes=[SGs])
    hT = P.sb("hT", [128, SG, 8, 128], BF16)
    ysum = P.sb("ysum", [128, SG, 8, 128], BF16)
    dsaout = P.sb("dsaout", [64, SG, 8, 128], BF16)
    cap = P.sb("cap", [64, 4, 128], F32)
    capb = P.sb("capb", [64, 4, 128], BF16)

    def load_w(names, tag):
        c0 = boff[names[0]][0]
        n = sum(boff[k][1] for k in names)
        assert boff[names[-1]][0] + boff[names[-1]][1] == c0 + n
        return B.load_weight(WB, c0, n, tag), c0

    def load_w2(dram_ap, rows, kc, name):
        return B.load_weight(dram_ap, 0, 1024, name, kc=kc, rows=rows)

    def scan(S, kv_all, dec_src, s, tag):
        if dec_src is not None:
            dcg = B.dcg_g
            P.dma(dcg.ap(), dec_src[:, s].rearrange("j d n -> d j n"), writes=[dcg])
        for j in range(NCORE):
            kvt = B.kvt[j % 2]
            P.dma(kvt.ap(), kv_all[j, s], writes=[kvt])
            if j == 0:
                P.op("vector", lambda e: e.tensor_scalar(cap.ap(), S.ap(), sel.ap()[0:64, 0:1], None, ALU.mult),
                     reads=[S, sel], writes=[cap])
            else:
                P.op("vector", lambda e: e.scalar_tensor_tensor(cap.ap(), S.ap(), sel.ap()[0:64, j:j + 1], cap.ap(),
                                                                ALU.mult, ALU.add), reads=[S, sel, cap], writes=[cap])
            if dec_src is None:
                dv = B.c["decR"].ap()[0:64, :].unsqueeze(2).to_broadcast([64, 4, 128])
                rd = [B.c["decR"]]
            else:
                dv = dcg.ap()[:, j, :].unsqueeze(2).to_broadcast([64, 4, 128])
                rd = [dcg]
            P.op("vector", lambda e: e.tensor_tensor(S.ap(), S.ap(), dv, ALU.mult), reads=[S] + rd, writes=[S])
            P.op("vector", lambda e: e.tensor_tensor(S.ap(), S.ap(), kvt.ap().rearrange("p (h e) -> p h e", h=4),
                                                     ALU.add), reads=[S, kvt], writes=[S])
        P.op("scalar", lambda e: e.activation(capb.ap(), cap.ap(), AF.Copy), reads=[cap], writes=[capb])

    def tail(bO, w, c0, gname, mgname, wbr, ls, first, tl):
        sq, r1, sg, tt, bro, mg, t2 = tl
        P.op("scalar", lambda e: e.activation(sq.ap(), bO.ap(), AF.Square), reads=[bO], writes=[sq])
        bQ = B.bank()
        P.op("tensor", lambda e: e.matmul(bQ.ap(), B.onesb.ap(), sq.ap(), start=True, stop=True),
             reads=[B.onesb, sq], writes=[bQ])
        P.op("vector", lambda e: e.tensor_scalar(r1.ap(), bQ.ap(), 1.0 / 128.0, RMS_EPS, ALU.mult, ALU.add),
             reads=[bQ], writes=[r1])
        P.op("scalar", lambda e: e.activation(r1.ap(), r1.ap(), AF.Sqrt), reads=[r1], writes=[r1])
        P.op("vector", lambda e: e.reciprocal(r1.ap(), r1.ap()), reads=[r1], writes=[r1])
        bG = B.bank()
        g0 = boff[gname][0] - c0
        for h in range(4):
            B.proj_fm(bG.ap()[:, h * 128:(h + 1) * 128], bG, w, g0 + h * 128, 128, hT, ls)
        P.op("scalar", lambda e: e.activation(sg.ap(), bG.ap(), AF.Silu), reads=[bG], writes=[sg])
        P.op("vector", lambda e: e.tensor_tensor(tt.ap(), bO.ap(), r1.ap(), ALU.mult), reads=[bO, r1], writes=[tt])
        P.op("gpsimd", lambda e: e.tensor_tensor(bro.ap(), tt.ap(), sg.ap(), ALU.mult), reads=[tt, sg], writes=[bro])
        merge(lambda nc, out_ap, bk: [P.op("tensor", lambda e, h=h: e.matmul(
            out_ap, wbr.ap()[:, h, nc * 128:(nc + 1) * 128], bro.ap()[:, h * 128:(h + 1) * 128],
            start=(h == 0), stop=(h == 3)), reads=[wbr, bro], writes=[bk]) for h in range(4)],
            w, boff[mgname][0] - c0, ls, first, mg, t2)

    def merge(emit_branch, w, m0, ls, first, mg, t2):
        bY = [B.bank(), B.bank()]
        for nc in range(8):
            bk = bY[nc // 4]
            emit_branch(nc, bk.ap()[:, (nc % 4) * 128:(nc % 4 + 1) * 128], bk)
        bM = [B.bank(), B.bank()]
        for nc in range(8):
            bk = bM[nc // 4]
            B.proj_fm(bk.ap()[:, (nc % 4) * 128:(nc % 4 + 1) * 128], bk, w, m0 + nc * 128, 128, hT, ls)
        for hf in range(2):
            P.op("scalar", lambda e: e.activation(mg.ap(), bM[hf].ap(), AF.Sigmoid), reads=[bM[hf]], writes=[mg])
            yv = ysum.ap()[:, ls, hf * 4:(hf + 1) * 4, :]
            m3 = mg.ap().rearrange("p (a b) -> p a b", a=4)
            b3 = bY[hf].ap().rearrange("p (a b) -> p a b", a=4)
            if first:
                P.op("vector", lambda e: e.tensor_tensor(yv, m3, b3, ALU.mult), reads=[mg, bY[hf]], writes=[ysum])
            else:
                P.op("vector", lambda e: e.tensor_tensor(t2.ap(), mg.ap(), bY[hf].ap(), ALU.mult),
                     reads=[mg, bY[hf]], writes=[t2])
                P.op("gpsimd", lambda e: e.tensor_tensor(yv, yv, t2.ap().rearrange("p (a b) -> p a b", a=4), ALU.add),
                     reads=[ysum, t2], writes=[ysum])

    def tail_bufs():
        return (P.sb("t_sq", [128, 512], BF16), P.sb("t_r1", [128, 512], F32), P.sb("t_sg", [128, 512], F32),
                P.sb("t_tt", [128, 512], F32), P.sb("t_bro", [128, 512], BF16), P.sb("t_mg", [128, 512], F32),
                P.sb("t_t2", [128, 512], F32))

    for g0 in range(0, TPC, SG):
        P.push_scope()
        for s in range(g0, g0 + SG):
            B.emit_norm_T(x[s], hT, s - g0)
        P.pop_scope()
        del B.xin
        P.push_scope()
        w, c0 = load_w(["retq", "retq_sw", "retk", "retk_sw", "retv", "retg", "mg0"], "w_ret")
        wbr = load_w2(wbr_r, 128, 4, "wbr_ret")
        tl = tail_bufs()
        qTb = P.sb("qTb", [64, 4, 128], BF16)
        kTb = P.sb("kTb", [64, 4, 128], BF16)
        qhat = P.sb("qhat", [64, 4, 128], BF16)
        vtok = P.sb("vtok", [128, 512], BF16)
        Sm = P.sb("Sm", [128, 512], BF16)
        for s in range(g0, g0 + SG):
            ls = s - g0
            t0 = s * 128
            scan(SR, kvRa, None, s, "r")
            for nm, dst in (("retq", qTb), ("retk", kTb)):
                bx, bs = B.bank(), B.bank()
                for h in range(4):
                    B.proj_fm(bx.ap()[0:64, h * 128:(h + 1) * 128], bx, w, boff[nm][0] - c0 + h * 64, 64, hT, ls)
                    B.proj_fm(bs.ap()[0:64, h * 128:(h + 1) * 128], bs, w, boff[nm + "_sw"][0] - c0 + h * 64, 64, hT, ls)
                B.rope_fm(dst.ap(), dst, bx, bs, 64, 512, cosR, sinR, t0)
            bv = B.bank()
            B.proj_tm(bv.ap(), bv, w, boff["retv"][0] - c0, 512, hT, ls)
            P.op("scalar", lambda e: e.activation(vtok.ap(), bv.ap(), AF.Copy), reads=[bv], writes=[vtok])
            bS = B.bank()
            for h in range(4):
                P.op("tensor", lambda e: e.matmul(bS.ap()[:, h * 128:(h + 1) * 128], kTb.ap()[:, h, :], qTb.ap()[:, h, :],
                                                  start=True, stop=True), reads=[kTb, qTb], writes=[bS])
            P.op("vector", lambda e: e.tensor_tensor(Sm.ap(), bS.ap(), B.c["DTret"].ap().rearrange("p h i -> p (h i)"),
                                                     ALU.mult), reads=[bS, B.c["DTret"]], writes=[Sm])
            P.op("gpsimd", lambda e: e.tensor_tensor(qhat.ap(), qTb.ap(), B.c["QDret"].ap()[0:64], ALU.mult),
                 reads=[qTb, B.c["QDret"]], writes=[qhat])
            bO = B.bank()
            for h in range(4):
                o_ap = bO.ap()[:, h * 128:(h + 1) * 128]
                P.op("tensor", lambda e: e.matmul(o_ap, vtok.ap()[:, h * 128:(h + 1) * 128], Sm.ap()[:, h * 128:(h + 1) * 128],
                                                  start=True, stop=False), reads=[vtok, Sm], writes=[bO])
                P.op("tensor", lambda e: e.matmul(o_ap, capb.ap()[:, h, :], qhat.ap()[:, h, :], start=False, stop=True),
                     reads=[capb, qhat], writes=[bO])
            tail(bO, w, c0, "retg", "mg0", wbr, ls, True, tl)
        P.pop_scope()
        P.push_scope()
        w, c0 = load_w(["glaq", "glak", "glav", "glag", "glaa", "mg2"], "w_gla")
        wbr = load_w2(wbr_g, 128, 4, "wbr_gla")
        tl = tail_bufs()
        eq = P.sb("eq", [64, 512], F32)
        ek = P.sb("ek", [64, 512], F32)
        e128 = P.sb("e128", [64, 512], F32)
        qt = P.sb("qt", [64, 4, 128], BF16)
        qh = P.sb("qh", [64, 4, 128], BF16)
        kt = P.sb("kt", [64, 4, 128], BF16)
        kfac = P.sb("kfac", [128, 256], F32)
        gkhat = P.sb("gkhat", [128, 256], BF16)
        vtok = P.sb("gvtok", [128, 512], BF16)
        kv0b = P.sb("kv0b", [64, 512], BF16)
        Sm = P.sb("gSm", [128, 512], BF16)
        for s in range(g0, g0 + SG):
            ls = s - g0
            scan(SGs, kvGa, decGa, s, "g")
            sp = gla_decay_common(B, hT, ls, w, boff["glaa"][0] - c0, wlr17)
            bC64, bC128 = B.bank(), B.bank()
            for h in range(4):
                for bb, tri in ((bC64, "triL64"), (bC128, "triL128")):
                    P.op("tensor", lambda e: e.matmul(bb.ap()[0:64, h * 128:(h + 1) * 128], sp.ap()[:, h * 64:(h + 1) * 64],
                                                      B.c[tri].ap(), start=True, stop=True), reads=[sp, B.c[tri]], writes=[bb])
            P.op("scalar", lambda e: e.activation(eq.ap(), bC64.ap()[0:64, :], AF.Exp, scale=-1.0 / 16), reads=[bC64], writes=[eq])
            P.op("scalar", lambda e: e.activation(ek.ap(), bC64.ap()[0:64, :], AF.Exp, scale=1.0 / 16), reads=[bC64], writes=[ek])
            P.op("scalar", lambda e: e.activation(e128.ap(), bC128.ap()[0:64, :], AF.Exp, scale=-1.0 / 16), reads=[bC128], writes=[e128])
            bq = B.bank()
            for h in range(4):
                B.proj_fm(bq.ap()[0:64, h * 128:(h + 1) * 128], bq, w, boff["glaq"][0] - c0 + h * 64, 64, hT, ls)
            f2 = lambda b: b.ap().rearrange("p h t -> p (h t)")
            P.op("vector", lambda e: e.scalar_tensor_tensor(f2(qt), bq.ap()[0:64, :], 0.125, eq.ap(), ALU.mult, ALU.mult),
                 reads=[bq, eq], writes=[qt])
            P.op("vector", lambda e: e.scalar_tensor_tensor(f2(qh), bq.ap()[0:64, :], 0.125, e128.ap(), ALU.mult, ALU.mult),
                 reads=[bq, e128], writes=[qh])
            bk = B.bank()
            for h in range(4):
                B.proj_fm(bk.ap()[0:64, h * 128:(h + 1) * 128], bk, w, boff["glak"][0] - c0 + h * 64, 64, hT, ls)
            P.op("vector", lambda e: e.tensor_tensor(f2(kt), bk.ap()[0:64, :], ek.ap(), ALU.mult), reads=[bk, ek], writes=[kt])
            bd = B.bank()
            P.op("tensor", lambda e: e.matmul(bd.ap()[:, 0:256], B.c["triU64"].ap(), sp.ap(), start=True, stop=True),
                 reads=[B.c["triU64"], sp], writes=[bd])
            P.op("scalar", lambda e: e.activation(kfac.ap(), bd.ap()[:, 0:256], AF.Exp, scale=-1.0 / 16.0), reads=[bd], writes=[kfac])
            bkt = B.bank()
            B.proj_tm(bkt.ap()[:, 0:256], bkt, w, boff["glak"][0] - c0, 256, hT, ls)
            P.op("vector", lambda e: e.tensor_tensor(gkhat.ap(), bkt.ap()[:, 0:256], kfac.ap(), ALU.mult),
                 reads=[bkt, kfac], writes=[gkhat])
            bv = B.bank()
            B.proj_tm(bv.ap(), bv, w, boff["glav"][0] - c0, 512, hT, ls)
            P.op("scalar", lambda e: e.activation(vtok.ap(), bv.ap(), AF.Copy), reads=[bv], writes=[vtok])
            b0 = B.bank()
            for h in range(4):
                P.op("tensor", lambda e: e.matmul(b0.ap()[0:64, h * 128:(h + 1) * 128], gkhat.ap()[0:64, h * 64:(h + 1) * 64],
                                                  vtok.ap()[0:64, h * 128:(h + 1) * 128], start=True, stop=True),
                     reads=[gkhat, vtok], writes=[b0])
            P.op("scalar", lambda e: e.activation(kv0b.ap(), b0.ap()[0:64, :], AF.Copy), reads=[b0], writes=[kv0b])
            bS = B.bank()
            for h in range(4):
                P.op("tensor", lambda e: e.matmul(bS.ap()[:, h * 128:(h + 1) * 128], kt.ap()[:, h, :], qt.ap()[:, h, :],
                                                  start=True, stop=True), reads=[kt, qt], writes=[bS])
            P.op("vector", lambda e: e.tensor_tensor(Sm.ap(), bS.ap(), B.c["DTgla"].ap().rearrange("p h i -> p (h i)"),
                                                     ALU.mult), reads=[bS, B.c["DTgla"]], writes=[Sm])
            bO = B.bank()
            for h in range(4):
                o_ap = bO.ap()[:, h * 128:(h + 1) * 128]
                P.op("tensor", lambda e: e.matmul(o_ap, vtok.ap()[:, h * 128:(h + 1) * 128], Sm.ap()[:, h * 128:(h + 1) * 128],
                                                  start=True, stop=False), reads=[vtok, Sm], writes=[bO])
                P.op("tensor", lambda e: e.matmul(o_ap, capb.ap()[:, h, :], qh.ap()[:, h, :], start=False, stop=False),
                     reads=[capb, qh], writes=[bO])
                P.op("tensor", lambda e: e.matmul(bO.ap()[:, h * 128 + 64:(h + 1) * 128], kv0b.ap()[:, h * 128:(h + 1) * 128],
                                                  qt.ap()[:, h, 64:128], start=False, stop=True), reads=[kv0b, qt], writes=[bO])
            tail(bO, w, c0, "glag", "mg2", wbr, ls, False, tl)
        P.pop_scope()
        dsa_stage(B, cfg, g0, locals())
        P.push_scope()
        wo = load_w2(wout_d, 128, 8, "w_out")
        xt = P.sb("f_xt", [128, 1024], F32)
        fj = P.sb("f_junk", [128, 512], BF16)
        fst = P.sb("f_st", [128, 8], F32)
        ft = P.sb("f_t", [128, 1024], F32)
        fo = P.sb("f_o", [128, 1024], F32)
        for s in range(g0, g0 + SG):
            ls = s - g0
            P.dma(xt.ap(), x[s], writes=[xt])
            bh = [B.bank(), B.bank()]
            for hf in range(2):
                for nc in range(8):
                    P.op("tensor", lambda e: e.matmul(bh[hf].ap(), ysum.ap()[:, ls, nc, :], wo.ap()[:, nc, hf * 512:(hf + 1) * 512],
                                                      start=(nc == 0), stop=(nc == 7)), reads=[ysum, wo], writes=[bh[hf]])
                P.op("scalar", lambda e: e.activation(fj.ap(), bh[hf].ap(), AF.Square, accum_out=fst.ap()[:, hf:hf + 1]),
                     reads=[bh[hf]], writes=[fj, fst])
            P.op("vector", lambda e: e.tensor_tensor(fst.ap()[:, 2:3], fst.ap()[:, 0:1], fst.ap()[:, 1:2], ALU.add), reads=[fst], writes=[fst])
            P.op("vector", lambda e: e.tensor_scalar(fst.ap()[:, 3:4], fst.ap()[:, 2:3], 1.0 / D, RMS_EPS, ALU.mult, ALU.add),
                 reads=[fst], writes=[fst])
            P.op("scalar", lambda e: e.activation(fst.ap()[:, 4:5], fst.ap()[:, 3:4], AF.Sqrt), reads=[fst], writes=[fst])
            P.op("vector", lambda e: e.reciprocal(fst.ap()[:, 5:6], fst.ap()[:, 4:5]), reads=[fst], writes=[fst])
            for hf in range(2):
                sl = slice(hf * 512, (hf + 1) * 512)
                P.op("vector", lambda e: e.scalar_tensor_tensor(ft.ap()[:, sl], bh[hf].ap(), fst.ap()[:, 5:6], B.GP.ap()[:, sl],
                                                                ALU.mult, ALU.mult), reads=[bh[hf], fst, B.GP], writes=[ft])
            P.op("gpsimd", lambda e: e.tensor_tensor(fo.ap(), ft.ap(), xt.ap(), ALU.add), reads=[ft, xt], writes=[fo])
            P.dma(xo[s], fo.ap(), reads=[fo])
        P.pop_scope()
    P.finish()
    return B


def dsa_stage(B, cfg, g0, env):
    P = B.P
    TPC, T, NCORE, SG = cfg.TPC, cfg.T, cfg.NCORE, cfg.SG
    GK, BLK, NB, BPG, NIT = cfg.GK, cfg.BLK, cfg.NB, cfg.BPG, cfg.NIT
    hT, ysum, dsaout, boff = env["hT"], env["ysum"], env["dsaout"], env["boff"]
    KTa, Va, IKa, pen, ident4 = env["KTa"], env["Va"], env["IKa"], env["pen"], env["ident4"]
    cosD, sinD = env["cosD"], env["sinD"]
    P.push_scope()
    w, c0 = env["load_w"](["dsaq", "dsaq_sw", "idxq", "idxq_sw", "idxw", "dsag"], "w_dsa")
    NMAX = TPC * GK
    scores = P.sb("scores", [128, NMAX], F32)
    CH = min(512, NMAX)
    junk = P.sb("cjunk", [128, CH], BF16)
    QT = P.sb("QT", [128, 4, 128], BF16)
    IQ = P.sb("IQ", [64, 4, 128], BF16)
    wq = P.sb("wq", [128, 4], F32)
    rl = [P.sb(f"rl{i}", [128, BLK], F32) for i in range(2)]
    ikb = [P.sb(f"ikb{i}", [64, NB, 128], BF16) for i in range(2)]
    ktb = [P.sb(f"ktb{i}", [128, NB, 128], BF16) for i in range(2)]
    vbk = [P.sb(f"vbk{i}", [128, NB, 2, 128], BF16) for i in range(2)]
    for v in vbk:
        P.op("vector", lambda e: e.memset(v.ap(), 1.0), writes=[v])
    nbmax = TPC * BPG
    mx = P.sb("mx", [128, nbmax], F32)
    mn = P.sb("mn", [128, nbmax], F32)
    wall = P.sb("wall", [128, 32], F32)
    cntc = P.sb("cntc", [128, 64], F32)
    bst = P.sb("bst", [128, 16], F32)
    mlo = P.sb("mlo", [128, BLK], BF16)
    mhi = P.sb("mhi", [128, BLK], BF16)
    band = P.sb("band", [128, BLK], BF16)
    cum = [P.sb(f"cum{i}", [128, BLK], F32) for i in range(2)]
    tsel = P.sb("tsel", [128, BLK], BF16)
    mb = [P.sb(f"mb{i}", [128, BLK], BF16) for i in range(2)]
    PT = [P.sb(f"PT{i}", [128, 512], BF16) for i in range(2)]
    rc = P.sb("rc", [64, 512], F32)
    on = P.sb("on", [64, 512], BF16)
    sgd = P.sb("sgd", [64, 8, 128], BF16)
    acc = [B.banks[6], B.banks[7]]
    saved_i = B.bank_i
    rot = {"i": 0}

    def bank6():
        b = B.banks[rot["i"] % 6]
        rot["i"] += 1
        return b
    B_bank = B.bank
    B.bank = bank6
    col = lambda i: bst.ap()[:, i:i + 1]

    for s in range(g0, g0 + SG):
        ls = s - g0
        t0 = s * 128
        bx, bs = bank6(), bank6()
        for j in range(4):
            B.proj_fm(bx.ap()[:, j * 128:(j + 1) * 128], bx, w, boff["dsaq"][0] - c0 + j * 128, 128, hT, ls)
            B.proj_fm(bs.ap()[:, j * 128:(j + 1) * 128], bs, w, boff["dsaq_sw"][0] - c0 + j * 128, 128, hT, ls)
        B.rope_fm(QT.ap(), QT, bx, bs, 128, 512, cosD, sinD, t0)
        bx, bs = bank6(), bank6()
        for h in range(4):
            B.proj_fm(bx.ap()[0:64, h * 128:(h + 1) * 128], bx, w, boff["idxq"][0] - c0 + h * 64, 64, hT, ls)
            B.proj_fm(bs.ap()[0:64, h * 128:(h + 1) * 128], bs, w, boff["idxq_sw"][0] - c0 + h * 64, 64, hT, ls)
        B.rope_fm(IQ.ap(), IQ, bx, bs, 64, 512, cosD, sinD, t0)
        bw = bank6()
        B.proj_tm(bw.ap()[:, 0:4], bw, w, boff["idxw"][0] - c0, 4, hT, ls)
        P.op("vector", lambda e: e.tensor_scalar(wq.ap(), bw.ap()[:, 0:4], 0.0625, None, ALU.mult), reads=[bw], writes=[wq])
        nb = (s + 1) * BPG
        n = (s + 1) * GK
        for bi in range(nb):
            gq, blk = bi // BPG, bi % BPG
            ik = ikb[bi % 2]
            P.dma(ik.ap(), IKa[blk * NB:(blk + 1) * NB, :, gq * 128:(gq + 1) * 128].rearrange("j d t -> d j t"), writes=[ik])
            scb = scores.ap()[:, bi * BLK:(bi + 1) * BLK]
            for h in range(4):
                bI = bank6()
                P.op("tensor", lambda e: e.matmul(bI.ap()[:, 0:BLK], IQ.ap()[:, h, :], ik.ap().rearrange("d j t -> d (j t)"),
                                                  start=True, stop=True), reads=[IQ, ik], writes=[bI])
                r = rl[h % 2]
                P.op("scalar", lambda e: e.activation(r.ap(), bI.ap()[:, 0:BLK], AF.Relu), reads=[bI], writes=[r])
                if h == 0:
                    P.op("vector", lambda e: e.tensor_scalar(scb, r.ap(), wq.ap()[:, 0:1], None, ALU.mult),
                         reads=[r, wq], writes=[scores])
                else:
                    P.op("vector", lambda e: e.scalar_tensor_tensor(scb, r.ap(), wq.ap()[:, h:h + 1], scb, ALU.mult, ALU.add),
                         reads=[r, wq, scores], writes=[scores])

        P.op("vector", lambda e: e.tensor_reduce(col(8), scores.ap()[:, 0:n], AX.X, ALU.max), reads=[scores], writes=[bst])
        P.op("vector", lambda e: e.tensor_scalar(col(1), col(8), 1.0, None, ALU.add), reads=[bst], writes=[bst])
        P.op("vector", lambda e: e.tensor_reduce(col(9), scores.ap()[:, 0:n], AX.X, ALU.min), reads=[scores, bst], writes=[bst])
        P.op("vector", lambda e: e.tensor_scalar(col(0), col(9), -1.0, None, ALU.add), reads=[bst], writes=[bst])
        for blk in range(BPG):
            bi = s * BPG + blk
            scb = scores.ap()[:, bi * BLK:(bi + 1) * BLK]
            P.op("vector", lambda e: e.tensor_tensor(scb, scb, pen.ap()[:, blk * BLK:(blk + 1) * BLK], ALU.add),
                 reads=[scores, pen], writes=[scores])
        P.op("vector", lambda e: e.tensor_tensor(col(10), col(1), col(0), ALU.subtract), reads=[bst], writes=[bst])
        P.op("vector", lambda e: e.tensor_scalar(wall.ap(), B.c["pw2"].ap(), col(10), None, ALU.mult),
             reads=[B.c["pw2"], bst], writes=[wall])
        P.op("vector", lambda e: e.tensor_tensor(col(2), col(0), wall.ap()[:, 0:1], ALU.add), reads=[bst, wall], writes=[bst])
        chunks = [(cs, min(n, cs + CH)) for cs in range(0, n, CH)]

        def count(thr_col, dst_col):
            for ci, (cs, ce) in enumerate(chunks):
                P.op("vector", lambda e: e.tensor_scalar(junk.ap()[:, 0:ce - cs], scores.ap()[:, cs:ce], col(thr_col), None,
                                                         ALU.is_ge, ALU.add, accum_out=cntc.ap()[:, ci:ci + 1]),
                     reads=[scores, bst], writes=[cntc, junk])
            P.op("vector", lambda e: e.tensor_reduce(col(dst_col), cntc.ap()[:, 0:len(chunks)], AX.X, ALU.add),
                 reads=[cntc], writes=[bst])

        for it in range(NIT):
            count(2, 3)
            P.op("vector", lambda e: e.scalar_tensor_tensor(col(4), col(3), cfg.TOPK - 0.5, wall.ap()[:, it:it + 1],
                                                            ALU.is_ge, ALU.mult), reads=[bst, wall], writes=[bst])
            P.op("vector", lambda e: e.tensor_tensor(col(0), col(0), col(4), ALU.add), reads=[bst], writes=[bst])
            if it + 1 < NIT:
                P.op("vector", lambda e: e.tensor_tensor(col(2), col(0), wall.ap()[:, it + 1:it + 2], ALU.add),
                     reads=[bst, wall], writes=[bst])
        P.op("vector", lambda e: e.tensor_tensor(col(1), col(0), wall.ap()[:, NIT - 1:NIT], ALU.add), reads=[bst, wall], writes=[bst])
        count(1, 6)
        P.op("vector", lambda e: e.tensor_scalar(col(7), col(6), -BIGM, cfg.TOPK * BIGM, ALU.mult, ALU.add), reads=[bst], writes=[bst])
        for bi in range(nb):
            gq, blk = bi // BPG, bi % BPG
            kt_, vb_ = ktb[bi % 2], vbk[bi % 2]
            P.dma(kt_.ap(), KTa[blk * NB:(blk + 1) * NB, :, gq * 128:(gq + 1) * 128].rearrange("j d t -> d j t"), writes=[kt_])
            for jj in range(NB):
                P.dma(vb_.ap()[:, jj, :, 0:64], Va[blk * NB + jj, gq].rearrange("s (k d) -> s k d", k=2), writes=[vb_])
            scb = scores.ap()[:, bi * BLK:(bi + 1) * BLK]
            m = mb[bi % 2]
            cm, cprev = cum[bi % 2], cum[(bi + 1) % 2]
            P.op("vector", lambda e: e.tensor_scalar(mlo.ap(), scb, col(0), BIGM, ALU.is_ge, ALU.mult), reads=[scores, bst], writes=[mlo])
            P.op("vector", lambda e: e.tensor_scalar(mhi.ap(), scb, col(1), BIGM, ALU.is_ge, ALU.mult), reads=[scores, bst], writes=[mhi])
            P.op("vector", lambda e: e.tensor_tensor(band.ap(), mlo.ap(), mhi.ap(), ALU.subtract), reads=[mlo, mhi], writes=[band])
            P.op("vector", lambda e: e.tensor_tensor_scan(cm.ap(), band.ap(), band.ap(),
                                                          (0.0 if bi == 0 else cprev.ap()[:, BLK - 1:BLK]), ALU.add, ALU.max),
                 reads=[band] + ([] if bi == 0 else [cprev]), writes=[cm])
            P.op("vector", lambda e: e.scalar_tensor_tensor(tsel.ap(), cm.ap(), col(7), band.ap(), ALU.is_le, ALU.mult),
                 reads=[cm, bst, band], writes=[tsel])
            P.op("vector", lambda e: e.scalar_tensor_tensor(m.ap(), tsel.ap(), -BIGM, mhi.ap(), ALU.add, ALU.add),
                 reads=[tsel, mhi], writes=[m])
            for jj in range(NB):
                for kvn in range(2):
                    bL = bank6()
                    P.op("tensor", lambda e: e.matmul(bL.ap(), kt_.ap()[kvn * 64:(kvn + 1) * 64, jj, :],
                                                      QT.ap()[kvn * 64:(kvn + 1) * 64].rearrange("p j t -> p (j t)"),
                                                      start=True, stop=False), reads=[kt_, QT], writes=[bL])
                    P.op("tensor", lambda e: e.matmul(bL.ap(), m.ap()[:, jj * 128:(jj + 1) * 128],
                                                      ident4.ap().rearrange("p j t -> p (j t)"), start=False, stop=True),
                         reads=[m, ident4], writes=[bL])
                    pt = PT[rot["i"] % 2]
                    P.op("scalar", lambda e: e.activation(pt.ap(), bL.ap(), AF.Exp, scale=0.125), reads=[bL], writes=[pt])
                    first = (bi == 0 and jj == 0)
                    last = (bi == nb - 1 and jj == NB - 1)
                    P.op("tensor", lambda e: e.matmul(acc[kvn].ap(), vb_.ap()[:, jj, kvn, :], pt.ap(), start=first, stop=last),
                         reads=[vb_, pt], writes=[acc[kvn]])
        bg = [bank6(), bank6()]
        for hd in range(8):
            B.proj_fm(bg[hd // 4].ap()[0:64, (hd % 4) * 128:(hd % 4 + 1) * 128], bg[hd // 4], w,
                      boff["dsag"][0] - c0 + hd * 64, 64, hT, ls)
        for hf in range(2):
            P.op("scalar", lambda e: e.activation(sgd.ap()[:, hf * 4:(hf + 1) * 4, :],
                                                  bg[hf].ap()[0:64, :].rearrange("p (a b) -> p a b", a=4), AF.Silu),
                 reads=[bg[hf]], writes=[sgd])
        for kvn in range(2):
            P.op("vector", lambda e: e.reciprocal(rc.ap(), acc[kvn].ap()[64:128, :]), reads=[acc[kvn]], writes=[rc])
            P.op("vector", lambda e: e.tensor_tensor(on.ap(), acc[kvn].ap()[0:64, :], rc.ap(), ALU.mult),
                 reads=[acc[kvn], rc], writes=[on])
            P.op("gpsimd", lambda e: e.tensor_tensor(dsaout.ap()[:, ls, kvn * 4:(kvn + 1) * 4, :],
                                                     on.ap().rearrange("p (a b) -> p a b", a=4),
                                                     sgd.ap()[:, kvn * 4:(kvn + 1) * 4, :], ALU.mult),
                 reads=[on, sgd], writes=[dsaout])
    B.bank = B_bank
    P.pop_scope()
    P.push_scope()
    w, c0 = env["load_w"](["mg1"], "w_mg1")
    wbr = B.load_weight(env["wbr_d"], 0, 1024, "wbr_dsa", kc=8, rows=64)
    mg = P.sb("d_mg", [128, 512], F32)
    t2 = P.sb("d_t2", [128, 512], F32)
    for s in range(g0, g0 + SG):
        ls = s - g0
        env["merge"](lambda nc, out_ap, bk: [P.op("tensor", lambda e, hd=hd: e.matmul(
            out_ap, wbr.ap()[:, hd, nc * 128:(nc + 1) * 128], dsaout.ap()[:, ls, hd, :],
            start=(hd == 0), stop=(hd == 7)), reads=[wbr, dsaout], writes=[bk]) for hd in range(8)],
            w, 0, ls, False, mg, t2)
    P.pop_scope()


B_CONSTS = ["ident", "triL128", "triL64", "triU64", "DTret", "DTgla", "QDret", "decR", "ropefs", "pw2"]


def host_inputs_B(inp, cfg, l, xcur, resA):
    pos = np.asarray(inp["positions"])[0].reshape(cfg.S // 128, 128)
    st = lambda k: np.stack([np.asarray(r[k]) for r in resA])
    er = lambda w, h: np.ascontiguousarray(np.asarray(w)[l].reshape(h, 512 // h, D).transpose(1, 0, 2))
    shared = {
        "cvec": np.ascontiguousarray(np.asarray(inp["c"])[0].reshape(8, 128).T),
        "adaw": kc_layout(np.asarray(inp["ada_w"])[l]),
        "adab": np.asarray(inp["ada_b"])[l][None, :],
        "pre": np.asarray(inp["pre_norm"])[l][None, :],
        "post": np.asarray(inp["post_norm"])[l][None, :],
        "WB": kc_layout(np.asarray(inp["w_in"])[l][:, b_col_index()]),
        "wbr_r": er(inp["w_br_ret"], 4), "wbr_g": er(inp["w_br_gla"], 4), "wbr_d": er(inp["w_br_dsa"], 8),
        "wout": kc_layout(np.asarray(inp["w_out"])[l]),
        "wlr": np.asarray(inp["gla_w_lr"])[l],
        "blr": np.asarray(inp["gla_b_lr"])[l][None, :],
        "KTa": st("KT"), "Va": st("V"), "IKa": st("IK"),
        "kvRa": st("kvR"), "kvGa": st("kvG"), "decGa": st("decG"),
    }
    shared.update(const_inputs(B_CONSTS))
    maps = []
    for c in range(cfg.NCORE):
        tl = core_tiles(cfg, c)
        m = dict(shared)
        m["x"] = np.ascontiguousarray(xcur[tl])
        m["pos"] = np.ascontiguousarray(pos[tl].reshape(1, cfg.T)).astype(np.int32)
        sel = np.zeros((128, cfg.NCORE), np.float32)
        sel[:, c] = 1.0
        m["sel"] = sel
        kidx = np.arange(cfg.GK)[None, :]
        qidx = (c * 128 + np.arange(128))[:, None]
        import ml_dtypes
        m["pen"] = np.where(kidx > qidx, np.float32(-1e30), np.float32(0.0)).astype(ml_dtypes.bfloat16)
        maps.append(m)
    return maps


_CACHE = {}


def run_layers(inp, cfg):
    x = np.asarray(inp["x"])[0].reshape(cfg.S // 128, 128, D).astype(np.float32)
    cores = list(range(cfg.NCORE))
    for l in range(cfg.DEPTH):
        if "A" not in _CACHE:
            _CACHE["A"] = build_A(cfg)
        inp_l = dict(inp)
        inp_l["x"] = x.reshape(1, cfg.S, D)
        resA = run_bass_kernel_spmd(_CACHE["A"].nc, host_inputs_A(inp_l, cfg, l), core_ids=cores).results
        if "B" not in _CACHE:
            _CACHE["B"] = build_B(cfg)
        resB = run_bass_kernel_spmd(_CACHE["B"].nc, host_inputs_B(inp, cfg, l, x, resA), core_ids=cores).results
        xn = np.empty_like(x)
        for c in cores:
            xn[core_tiles(cfg, c)] = np.asarray(resB[c]["xo"])
        x = xn
    return x.reshape(1, cfg.S, D)


def kernel(**inputs):
    cfg = Cfg()
    return run_layers(inputs, cfg).astype(np.float32)
```

```python
import numpy as np
from contextlib import ExitStack
import concourse.bass as bass
import concourse.mybir as mybir
from concourse.bass_utils import run_bass_kernel_spmd

F32 = mybir.dt.float32
BF16 = mybir.dt.bfloat16
I32 = mybir.dt.int32
ALU = mybir.AluOpType
AF = mybir.ActivationFunctionType
AX = mybir.AxisListType

EPOCH = 20000
NDMA_SEM = 12


class Buf:
    def __init__(self, name, handle=None):
        self.name = name
        self.h = handle
        self.last_w = None
        self.readers = []

    def ap(self):
        return self.h[:]


class _Recorder:
    def __init__(self):
        self.call = None

    def __getattr__(self, name):
        def f(*a, **k):
            assert self.call is None
            self.call = (name, a, k)
            return None
        return f


class Prog:
    ENGS = ("tensor", "vector", "scalar", "gpsimd", "sync")

    def __init__(self, nc):
        self.nc = nc
        self.stack = ExitStack()
        self.stream = {e: [] for e in self.ENGS}
        self.count = {e: 0 for e in self.ENGS}
        self.known = {e: {} for e in self.ENGS}
        self.dma_n = {}
        self.dma_rr = {e: 0 for e in self.ENGS}
        self.nbuf = 0
        self.pending = {e: [] for e in self.ENGS}
        self.stacks = [self.stack]

    def sb(self, name, shape, dtype):
        self.nbuf += 1
        name = f"{name}_u{self.nbuf}"
        h = self.stacks[-1].enter_context(self.nc.sbuf_tensor(name, list(shape), dtype))
        return Buf(name, h)

    def push_scope(self):
        st = ExitStack()
        self.stacks.append(st)

    def pop_scope(self):
        self.barrier()
        self.stacks.pop().close()

    def barrier(self):
        toks = []
        for e in self.ENGS:
            c = self.count[e]
            if c > 0 and e != "sync":
                toks.append((("E", e, (c - 1) // EPOCH), (c - 1) % EPOCH + 1))
        for key, n in self.dma_n.items():
            toks.append((key, 16 * n))
        for e in self.ENGS:
            self.pending[e] = list(toks)

    def _take_pending(self, eng, waits):
        kn = self.known[eng]
        for key, val in self.pending[eng]:
            if key[0] == "E" and key[1] == eng:
                continue
            if kn.get(key, -1) >= val:
                continue
            if any(k == key and v >= val for k, v in waits):
                continue
            kn[key] = val
            waits.append((key, val))
        self.pending[eng] = []

    def ps(self, name, shape, dtype=F32):
        h = self.stack.enter_context(self.nc.psum_tensor(name, list(shape), dtype))
        return Buf(name, h)

    def alias(self, name, buf):
        raise NotImplementedError

    def _deps(self, eng, reads, writes, is_dma):
        deps = {}

        def add(tok, same_ok):
            if tok is None:
                return
            key, val = tok
            if (not is_dma) and key[0] == "E" and key[1] == eng and not same_ok:
                return
            if deps.get(key, -1) < val:
                deps[key] = val

        for b in reads:
            add(b.last_w, True)
        for b in writes:
            add(b.last_w, False)
            for t in b.readers:
                add(t, False)
        kn = self.known[eng]
        out = []
        for key, val in deps.items():
            if key[0] == "E":
                later = [k for k in kn if k[0] == "E" and k[1] == key[1] and k[2] > key[2]]
                if later:
                    continue
            if kn.get(key, -1) >= val:
                continue
            kn[key] = val
            out.append((key, val))
        return out

    def _commit(self, tok, reads, writes):
        for b in writes:
            b.last_w = tok
            b.readers = []
        for b in reads:
            if b in writes:
                continue
            rs = [t for t in b.readers if t[0] != tok[0]]
            rs.append(tok)
            b.readers = rs

    def op(self, eng, fn, reads=(), writes=()):
        reads = list(reads)
        writes = list(writes)
        waits = self._deps(eng, reads, writes, False)
        self._take_pending(eng, waits)
        self.count[eng] += 1
        c = self.count[eng]
        key = ("E", eng, (c - 1) // EPOCH)
        tok = (key, (c - 1) % EPOCH + 1)
        rec = _Recorder()
        fn(rec)
        self.stream[eng].append((waits, rec.call, key))
        self._commit(tok, reads, writes)
        return tok

    def dma(self, out_ap, in_ap, reads=(), writes=(), queue="sync", **kw):
        reads = list(reads)
        writes = list(writes)
        i = self.dma_rr[queue]
        self.dma_rr[queue] = (i + 1) % NDMA_SEM
        key = ("D", queue, i)
        n = self.dma_n.get(key, 0)
        waits = self._deps(queue, reads, writes, True)
        self._take_pending(queue, waits)
        if n > 0 and self.known[queue].get(key, -1) < 16 * n:
            self.known[queue][key] = 16 * n
            waits.append((key, 16 * n))
        self.dma_n[key] = n + 1
        tok = (key, 16 * (n + 1))
        kk = dict(kw)
        kk["out"] = out_ap
        kk["in_"] = in_ap
        self.stream[queue].append((waits, ("dma_start", (), kk), key))
        self._commit(tok, reads, writes)
        return tok

    def coll(self, kind, ins, outs, ranks, reads=(), writes=()):
        reads, writes = list(reads), list(writes)
        queue = "gpsimd"
        key = ("D", "coll", 0)
        n = self.dma_n.get(key, 0)
        waits = self._deps(queue, reads, writes, True)
        self._take_pending(queue, waits)
        if n > 0 and self.known[queue].get(key, -1) < 16 * n:
            self.known[queue][key] = 16 * n
            waits.append((key, 16 * n))
        self.dma_n[key] = n + 1
        tok = (key, 16 * (n + 1))
        call = ("collective_compute", (kind, ALU.bypass), dict(replica_groups=[list(ranks)], ins=list(ins), outs=list(outs)))
        self.stream[queue].append((waits, call, key))
        self._commit(tok, reads, writes)
        return tok

    def finish(self):
        nc = self.nc
        sems = {}

        def sem(key):
            if key not in sems:
                nm = "s_" + "_".join(str(k) for k in key)
                sems[key] = self.stack.enter_context(nc.semaphore(nm))
            return sems[key]

        final_waits = []
        for key, n in self.dma_n.items():
            final_waits.append((key, 16 * n))
        for eng in self.ENGS:
            for waits, fn, key in self.stream[eng]:
                sem(key)
                for k, v in waits:
                    sem(k)

        streams = self.stream

        def emit(eng_name):
            def body(e):
                for waits, fn, key in streams[eng_name]:
                    for k, v in waits:
                        e.wait_ge(sems[k], v)
                    ins = getattr(e, fn[0])(*fn[1], **fn[2])
                    ins.then_inc(sems[key], 16 if key[0] == "D" else 1)
                if eng_name == "sync":
                    for k, v in final_waits:
                        e.wait_ge(sems[k], v)
            return body

        with nc.Block() as block:
            block.sync(emit("sync"))
            block.tensor(emit("tensor"))
            block.vector(emit("vector"))
            block.scalar(emit("scalar"))
            block.gpsimd(emit("gpsimd"))
        while self.stacks:
            self.stacks.pop().close()


D = 1024
IN_SPLITS = (("ret_q", 256), ("ret_k", 256), ("ret_v", 512), ("ret_g", 512),
             ("dsa_q", 512), ("dsa_k", 128), ("dsa_v", 128), ("dsa_g", 512),
             ("idx_q", 256), ("idx_k", 64), ("idx_w", 4),
             ("gla_q", 256), ("gla_k", 256), ("gla_v", 512), ("gla_g", 512), ("gla_a", 16),
             ("merge", 3072))
OFF = {}
_o = 0
for _n, _w in IN_SPLITS:
    OFF[_n] = _o
    _o += _w
IN_WIDTH = _o
RMS_EPS = 1e-6
TOPK = 256
BIGM = 32768.0
LOG_G = [float(np.log1p(-2.0 ** (-5.0 - h))) for h in range(4)]


class Cfg:
    def __init__(self, S=16384, NCORE=8, DEPTH=4, SG=2, NIT=18):
        self.S, self.NCORE, self.DEPTH = S, NCORE, DEPTH
        self.TPC = S // 128 // NCORE
        self.T = self.TPC * 128
        self.GK = NCORE * 128
        self.BLK = min(512, self.GK)
        self.NB = self.BLK // 128
        self.BPG = self.GK // self.BLK
        self.SG = min(SG, self.TPC)
        self.NIT = NIT
        self.TOPK = min(256, S // 4)


def host_consts():
    f = np.float32
    j = np.arange(128)[:, None]
    i = np.arange(128)[None, :]
    c = {}
    c["ident"] = np.eye(128, dtype=f)
    same = (j // 64) == (i // 64)
    c["triL128"] = (j <= i).astype(f)
    c["triL64"] = ((j <= i) & same).astype(f)
    c["triU64"] = ((j > i) & same).astype(f)
    ci = np.zeros((128, 4), f)
    ci[:64, 0] = 1
    ci[64:, 1] = 1
    ci[:, 2] = 1
    c["chunkind"] = ci
    lg = np.array(LOG_G, np.float64)
    dt = np.zeros((128, 4, 128), np.float64)
    for h in range(4):
        dt[:, h, :] = np.where(i >= j, np.exp(lg[h] * np.maximum(i - j, 0)), 0.0)
    c["DTret"] = (dt * 0.125).astype(f)
    c["DTgla"] = np.repeat(((j <= i) & same).astype(f)[:, None, :], 4, axis=1)
    qd = np.zeros((128, 4, 128), np.float64)
    for h in range(4):
        qd[:, h, :] = np.exp(lg[h] * (i + 1.0))
    c["QDret"] = (qd * 0.125).astype(f)
    kf = np.zeros((128, 4), np.float64)
    for h in range(4):
        kf[:, h] = np.exp(lg[h] * (127.0 - np.arange(128)))
    c["kfacR"] = kf.astype(f)
    dr = np.zeros((128, 4), np.float64)
    for h in range(4):
        dr[:, h] = np.exp(lg[h] * 128.0)
    c["decR"] = dr.astype(f)
    fr = np.zeros((128, 4), f)
    p = np.arange(128) % 64
    half = 32
    fr_ret = (np.float32(10000.0) ** (-(np.arange(half, dtype=f)) * f(2.0) / f(64))).astype(f)
    fr[:, 0] = fr_ret[p % 32]
    fr[:, 1] = np.where(p < 32, -1.0, 1.0)
    fr_d = (np.float32(500000.0) ** (-(np.arange(8, dtype=f)) * f(2.0) / f(16))).astype(f)
    fr[:, 2] = np.where(p < 16, fr_d[p % 8], 0.0)
    fr[:, 3] = np.where(p < 8, -1.0, np.where(p < 16, 1.0, 0.0))
    c["ropefs"] = fr
    c["pw2"] = np.repeat((2.0 ** -(np.arange(32, dtype=np.float64) + 1.0))[None, :], 128, axis=0).astype(f)
    return c


def swap_cols(lo, rot, width, nheads):
    idx = []
    for h in range(nheads):
        base = lo + h * width
        half = rot // 2
        for d in range(width):
            if d < half:
                idx.append(base + d + half)
            elif d < rot:
                idx.append(base + d - half)
            else:
                idx.append(base + d)
    return idx


A_COLS = [("retk", 256), ("retk_sw", 256), ("retv", 512), ("dsak", 128), ("dsak_sw", 128),
          ("dsav", 128), ("idxk", 64), ("idxk_sw", 64), ("glak", 256), ("glav", 512), ("glaa", 16)]


def col_layout(cols):
    off, o = {}, 0
    for n, w in cols:
        off[n] = (o, w)
        o += w
    return off, o


class Builder:
    def __init__(self, cfg, name):
        self.cfg = cfg
        self.nc = bass.Bass("TRN2", target_bir_lowering=False, name=name)
        self.P = Prog(self.nc)
        self.din = {}
        self.dout = {}
        self.bank_i = 0

    def inp(self, name, shape, dtype=F32):
        t = self.nc.dram_tensor(name, list(shape), dtype, kind="ExternalInput")
        self.din[name] = t
        return t.ap()

    def outp(self, name, shape, dtype=F32):
        t = self.nc.dram_tensor(name, list(shape), dtype, kind="ExternalOutput")
        self.dout[name] = t
        return t.ap()

    def make_banks(self):
        self.banks = [self.P.ps(f"bank{i}", [128, 512], F32) for i in range(8)]

    def bank(self):
        b = self.banks[self.bank_i % 8]
        self.bank_i += 1
        return b

    def load_consts(self, names):
        P = self.P
        hc = host_consts()
        self.c = {}
        self.cb = {}
        for n in names:
            shp = list(hc[n].shape)
            ap = self.inp("c_" + n, shp)
            t = P.sb("sc_" + n, shp, F32)
            P.dma(t.ap(), ap, writes=[t])
            self.c[n] = t
        self.identb = P.sb("identb", [128, 128], BF16)
        P.op("vector", lambda e: e.tensor_copy(self.identb.ap(), self.c["ident"].ap()),
             reads=[self.c["ident"]], writes=[self.identb])
        self.onesb = P.sb("onesb", [128, 128], BF16)
        P.op("vector", lambda e: e.memset(self.onesb.ap(), 1.0), writes=[self.onesb])

    def load_weight(self, dram_ap, c0, ncols, name, kc=8, rows=128):
        P = self.P
        wt = P.sb(name, [rows, kc, ncols], BF16)
        CH = 64 if hasattr(self, "wstage") else 128
        if not hasattr(self, "wstage"):
            self.wstage = [P.sb(f"wstage{i}", [128, 8, CH], F32) for i in range(2)]
            self.wstage_i = 0
        for s in range(0, ncols, CH):
            n = min(CH, ncols - s)
            st = self.wstage[self.wstage_i % 2]
            self.wstage_i += 1
            P.dma(st.ap()[0:rows, 0:kc, 0:n], dram_ap[:, :, c0 + s:c0 + s + n], writes=[st])
            P.op("gpsimd", lambda e, st=st, s=s, n=n: e.tensor_copy(
                wt.ap()[:, :, s:s + n], st.ap()[0:rows, 0:kc, 0:n]), reads=[st], writes=[wt])
        return wt

    def precast(self, src_ap, dst_ap, dbuf, rows, kc, ncols):
        P = self.P
        CH = 256
        st = [P.sb(f"pc_st{i}", [128, 8, CH], F32) for i in range(2)]
        ot = [P.sb(f"pc_ot{i}", [128, 8, CH], BF16) for i in range(2)]
        i = 0
        for s0 in range(0, ncols, CH):
            n = min(CH, ncols - s0)
            a, o = st[i % 2], ot[i % 2]
            P.dma(a.ap()[0:rows, 0:kc, 0:n], src_ap[:, :, s0:s0 + n], writes=[a])
            if i % 2 == 0:
                P.op("scalar", lambda e: e.activation(o.ap()[0:rows, 0:kc, 0:n], a.ap()[0:rows, 0:kc, 0:n], AF.Copy),
                     reads=[a], writes=[o])
            else:
                P.op("gpsimd", lambda e: e.tensor_copy(o.ap()[0:rows, 0:kc, 0:n], a.ap()[0:rows, 0:kc, 0:n]),
                     reads=[a], writes=[o])
            P.dma(dst_ap[:, :, s0:s0 + n], o.ap()[0:rows, 0:kc, 0:n], reads=[o], writes=[dbuf])
            i += 1

    def load_weight_bf16(self, dram_ap, dbuf, c0, ncols, name, kc=8, rows=128):
        P = self.P
        wt = P.sb(name, [rows, kc, ncols], BF16)
        half = max(1, kc // 2)
        for k0 in range(0, kc, half):
            P.dma(wt.ap()[:, k0:k0 + half, :], dram_ap[:, k0:k0 + half, c0:c0 + ncols], reads=[dbuf], writes=[wt])
        return wt

    def prealloc(self, ncore):
        P = self.P
        self.rp = [P.sb(f"rp{i}", [128, 512], F32) for i in range(2)]
        self.a17 = P.sb("a17", [17, 128], BF16)
        P.op("vector", lambda e: e.memset(self.a17.ap(), 1.0), writes=[self.a17])
        self.e1 = P.sb("gl_e1", [128, 256], F32)
        self.sp = P.sb("gl_sp", [128, 256], F32)
        self.kvt = [P.sb(f"kvt{i}", [64, 512], F32) for i in range(2)]
        self.dcg_g = P.sb("dcg_g", [64, ncore, 4], F32)

    def emit_mod(self, cvec_ap, adaw_ap, adab_ap, pre_ap, post_ap, want_gate):
        P = self.P
        self.G = P.sb("G", [128, 1024], F32)
        self.Sh = P.sb("Sh", [128, 1024], F32)
        if want_gate:
            self.GP = P.sb("GP", [128, 1024], F32)
        P.push_scope()
        cv = P.sb("cv", [128, 8], F32)
        P.dma(cv.ap(), cvec_ap, writes=[cv])
        ca = P.sb("ca", [128, 8], F32)
        P.op("scalar", lambda e: e.activation(ca.ap(), cv.ap(), AF.Silu), reads=[cv], writes=[ca])
        cbc = P.sb("cbc", [128, 8, 128], F32)
        P.op("vector", lambda e: e.tensor_copy(cbc.ap(), ca.ap().unsqueeze(2).to_broadcast([128, 8, 128])),
             reads=[ca], writes=[cbc])
        mod = P.sb("mod", [128, 3072], F32)
        bias = P.sb("modb", [128, 3072], F32)
        P.dma(bias.ap(), adab_ap.to_broadcast([128, 3072]), writes=[bias])
        wst = [P.sb(f"adaw{i}", [128, 8, 512], F32) for i in range(2)]
        for ci in range(6):
            w = wst[ci % 2]
            P.dma(w.ap(), adaw_ap[:, :, ci * 512:(ci + 1) * 512], writes=[w])
            bk = self.bank()
            for kc in range(8):
                P.op("tensor", lambda e, bk=bk, w=w, kc=kc: e.matmul(
                    bk.ap(), cbc.ap()[:, kc, :], w.ap()[:, kc, :], start=(kc == 0), stop=(kc == 7)),
                    reads=[cbc, w], writes=[bk])
            P.op("vector", lambda e, bk=bk, ci=ci: e.tensor_tensor(
                mod.ap()[:, ci * 512:(ci + 1) * 512], bk.ap(), bias.ap()[:, ci * 512:(ci + 1) * 512], ALU.add),
                reads=[bk, bias], writes=[mod])
        pre = P.sb("preb", [128, 1024], F32)
        P.dma(pre.ap(), pre_ap.to_broadcast([128, 1024]), writes=[pre])
        P.op("vector", lambda e: e.scalar_tensor_tensor(
            self.G.ap(), mod.ap()[:, 1024:2048], 1.0, pre.ap(), ALU.add, ALU.mult),
            reads=[mod, pre], writes=[self.G])
        P.op("vector", lambda e: e.tensor_copy(self.Sh.ap(), mod.ap()[:, 0:1024]), reads=[mod], writes=[self.Sh])
        if want_gate:
            post = P.sb("postb", [128, 1024], F32)
            P.dma(post.ap(), post_ap.to_broadcast([128, 1024]), writes=[post])
            P.op("vector", lambda e: e.tensor_tensor(self.GP.ap(), mod.ap()[:, 2048:3072], post.ap(), ALU.mult),
                 reads=[mod, post], writes=[self.GP])
        P.pop_scope()

    def emit_ropes(self, pos_ap, specs):
        P = self.P
        outs = []
        for fcol, scol, name in specs:
            outs.append((P.sb(name + "_cos", [128, self.cfg.T], BF16), P.sb(name + "_sin", [128, self.cfg.T], BF16)))
        P.push_scope()
        for (fcol, scol, name), (cb_, sb_) in zip(specs, outs):
            self.emit_rope(pos_ap, fcol, scol, name, cb_, sb_)
        P.pop_scope()
        del self.posf
        return outs

    def emit_rope(self, pos_ap, fcol, scol, name, cosb, sinb):
        P = self.P
        T = self.cfg.T
        fs = self.c["ropefs"]
        if not hasattr(self, "posf"):
            posi = P.sb("posi", [128, T], I32)
            P.dma(posi.ap(), pos_ap.to_broadcast([128, T]), writes=[posi])
            self.posf = P.sb("posf", [128, T], F32)
            P.op("vector", lambda e: e.tensor_copy(self.posf.ap(), posi.ap()), reads=[posi], writes=[self.posf])
            self.rtmp = [P.sb(f"rtmp{i}", [128, T], F32) for i in range(4)]
        ang, kf, r, rd = self.rtmp
        posf = self.posf
        PI = float(np.pi)
        C1 = 6.28125
        C2 = float(2.0 * np.pi - 6.28125)
        MAG = 12582912.0
        P.op("vector", lambda e: e.tensor_scalar(ang.ap(), posf.ap(), fs.ap()[:, fcol:fcol + 1], None, ALU.mult),
             reads=[posf, fs], writes=[ang])

        def reduce_and_sin(src, dst_final, shift):
            dst = rd
            if shift != 0.0:
                P.op("vector", lambda e: e.tensor_scalar(r.ap(), src.ap(), shift, None, ALU.add),
                     reads=[src], writes=[r])
                s2 = r
            else:
                s2 = src
            P.op("vector", lambda e: e.tensor_scalar(kf.ap(), s2.ap(), float(1.0 / (2 * np.pi)), MAG, ALU.mult, ALU.add),
                 reads=[s2], writes=[kf])
            P.op("vector", lambda e: e.tensor_scalar(kf.ap(), kf.ap(), MAG, None, ALU.subtract),
                 reads=[kf], writes=[kf])
            P.op("vector", lambda e: e.scalar_tensor_tensor(dst.ap(), kf.ap(), -C1, s2.ap(), ALU.mult, ALU.add),
                 reads=[kf, s2], writes=[dst])
            P.op("vector", lambda e: e.scalar_tensor_tensor(dst.ap(), kf.ap(), -C2, dst.ap(), ALU.mult, ALU.add),
                 reads=[kf, dst], writes=[dst])
            P.op("vector", lambda e: e.tensor_scalar(dst.ap(), dst.ap(), PI, -PI, ALU.min, ALU.max),
                 reads=[dst], writes=[dst])
            P.op("scalar", lambda e: e.activation(dst_final.ap(), dst.ap(), AF.Sin), reads=[dst], writes=[dst_final])

        reduce_and_sin(ang, sinb, 0.0)
        reduce_and_sin(ang, cosb, float(np.pi / 2))
        P.op("vector", lambda e: e.tensor_scalar(sinb.ap(), sinb.ap(), fs.ap()[:, scol:scol + 1], None, ALU.mult),
             reads=[sinb, fs], writes=[sinb])
        return cosb, sinb

    def emit_norm_T(self, x_tile_ap_dram, hT, slot):
        P = self.P
        if not hasattr(self, "xin"):
            self.xin = [P.sb(f"xin{i}", [128, 1024], F32) for i in range(1)]
            self.xin_i = 0
            self.hjunk = P.sb("hjunk", [128, 1024], BF16)
            self.h1 = P.sb("h1", [128, 1024], F32)
            self.hb = P.sb("hb", [128, 1024], BF16)
            self.nst = P.sb("nst", [128, 4], F32)
        xt = self.xin[0]
        self.xin_i += 1
        nst = self.nst
        P.dma(xt.ap(), x_tile_ap_dram, writes=[xt])
        P.op("scalar", lambda e: e.activation(self.hjunk.ap(), xt.ap(), AF.Square, accum_out=nst.ap()[:, 0:1]),
             reads=[xt], writes=[self.hjunk, nst])
        P.op("vector", lambda e: e.tensor_scalar(nst.ap()[:, 1:2], nst.ap()[:, 0:1], 1.0 / D, RMS_EPS, ALU.mult, ALU.add),
             reads=[nst], writes=[nst])
        P.op("scalar", lambda e: e.activation(nst.ap()[:, 2:3], nst.ap()[:, 1:2], AF.Sqrt), reads=[nst], writes=[nst])
        P.op("vector", lambda e: e.reciprocal(nst.ap()[:, 3:4], nst.ap()[:, 2:3]), reads=[nst], writes=[nst])
        P.op("vector", lambda e: e.scalar_tensor_tensor(self.h1.ap(), xt.ap(), nst.ap()[:, 3:4], self.G.ap(), ALU.mult, ALU.mult),
             reads=[xt, nst, self.G], writes=[self.h1])
        P.op("gpsimd", lambda e: e.tensor_tensor(self.hb.ap(), self.h1.ap(), self.Sh.ap(), ALU.add),
             reads=[self.h1, self.Sh], writes=[self.hb])
        for half in range(2):
            bk = self.bank()
            for q in range(4):
                kc = half * 4 + q
                P.op("tensor", lambda e, bk=bk, q=q, kc=kc: e.matmul(
                    bk.ap()[:, q * 128:(q + 1) * 128], self.hb.ap()[:, kc * 128:(kc + 1) * 128], self.identb.ap(),
                    start=True, stop=True), reads=[self.hb, self.identb], writes=[bk])
            P.op("scalar", lambda e, bk=bk, half=half: e.activation(
                hT.ap()[:, slot, half * 4:half * 4 + 4, :], bk.ap().rearrange("p (a b) -> p a b", a=4), AF.Copy),
                reads=[bk], writes=[hT])

    def proj_fm(self, out_ap, bk, w, c0, m, hT, slot, nslots=1):
        P = self.P
        for kc in range(8):
            if nslots == 1:
                rhs = hT.ap()[:, slot, kc, :]
            else:
                rhs = hT.ap()[:, slot:slot + nslots, kc, :]
            P.op("tensor", lambda e, kc=kc, rhs=rhs: e.matmul(
                out_ap, w.ap()[:, kc, c0:c0 + m], rhs, start=(kc == 0), stop=(kc == 7)),
                reads=[w, hT], writes=[bk])

    def proj_tm(self, out_ap, bk, w, c0, n, hT, slot):
        P = self.P
        for kc in range(8):
            P.op("tensor", lambda e, kc=kc: e.matmul(
                out_ap, hT.ap()[:, slot, kc, :], w.ap()[:, kc, c0:c0 + n], start=(kc == 0), stop=(kc == 7)),
                reads=[w, hT], writes=[bk])

    def rope_fm(self, dst_ap, dst_buf, bx, bs, npart, ncols_ap, cosb, sinb, tok0, scale=None):
        P = self.P
        if not hasattr(self, "rp"):
            self.rp = [P.sb(f"rp{i}", [128, 512], F32) for i in range(2)]
        t0, t1 = self.rp
        nh = ncols_ap // 128
        cosv = cosb.ap()[0:npart, tok0:tok0 + 128].unsqueeze(1).to_broadcast([npart, nh, 128])
        sinv = sinb.ap()[0:npart, tok0:tok0 + 128].unsqueeze(1).to_broadcast([npart, nh, 128])
        v = lambda b: b.ap()[0:npart, 0:ncols_ap].rearrange("p (h t) -> p h t", h=nh)
        P.op("vector", lambda e: e.tensor_tensor(v(t0), v(bx), cosv, ALU.mult), reads=[bx, cosb], writes=[t0])
        P.op("vector", lambda e: e.tensor_tensor(v(t1), v(bs), sinv, ALU.mult), reads=[bs, sinb], writes=[t1])
        if scale is None:
            P.op("vector", lambda e: e.tensor_tensor(dst_ap, v(t0), v(t1), ALU.add), reads=[t0, t1], writes=[dst_buf])
        else:
            P.op("vector", lambda e: e.scalar_tensor_tensor(dst_ap, v(t0), 1.0, v(t1), ALU.mult, ALU.add),
                 reads=[t0, t1], writes=[dst_buf])


def gla_decay_common(B, hT, slot, wA, aoff, wlr17):
    P = B.P
    if not hasattr(B, "a17"):
        B.a17 = P.sb("a17", [17, 128], BF16)
        P.op("vector", lambda e: e.memset(B.a17.ap(), 1.0), writes=[B.a17])
        B.e1 = P.sb("gl_e1", [128, 256], F32)
        B.sp = P.sb("gl_sp", [128, 256], F32)
    bk = B.bank()
    B.proj_fm(bk.ap()[0:16, 0:128], bk, wA, aoff, 16, hT, slot)
    P.op("vector", lambda e: e.tensor_copy(B.a17.ap()[0:16, :], bk.ap()[0:16, 0:128]), reads=[bk], writes=[B.a17])
    bz = B.bank()
    P.op("tensor", lambda e: e.matmul(bz.ap()[:, 0:256], B.a17.ap(), wlr17.ap(), start=True, stop=True),
         reads=[B.a17, wlr17], writes=[bz])
    P.op("scalar", lambda e: e.activation(B.e1.ap(), bz.ap()[:, 0:256], AF.Exp, scale=-1.0), reads=[bz], writes=[B.e1])
    P.op("scalar", lambda e: e.activation(B.sp.ap(), B.e1.ap(), AF.Ln, bias=1.0), reads=[B.e1], writes=[B.sp])
    return B.sp


def load_wlr17(B, wlr_ap, blr_ap):
    P = B.P
    st = P.sb("wlr_st", [17, 256], F32)
    P.dma(st.ap()[0:16, :], wlr_ap, writes=[st])
    P.dma(st.ap()[16:17, :], blr_ap, writes=[st])
    w = P.sb("wlr17", [17, 256], BF16)
    P.op("vector", lambda e: e.tensor_copy(w.ap(), st.ap()), reads=[st], writes=[w])
    return w


def build_A(cfg):
    B = Builder(cfg, "phaseA")
    P = B.P
    TPC, T = cfg.TPC, cfg.T
    aoff, ncolA = col_layout(A_COLS)
    x = B.inp("x", [TPC, 128, 1024])
    pos = B.inp("pos", [1, T], I32)
    cvec = B.inp("cvec", [128, 8])
    adaw = B.inp("adaw", [128, 8, 3072])
    adab = B.inp("adab", [1, 3072])
    pre = B.inp("pre", [1, 1024])
    WA = B.inp("WA", [128, 8, ncolA])
    wlr = B.inp("wlr", [16, 256])
    blr = B.inp("blr", [1, 256])
    oKT = B.outp("KT", [128, T], BF16)
    oV = B.outp("V", [TPC, 128, 128], BF16)
    oIK = B.outp("IK", [64, T], BF16)
    okvR = B.outp("kvR", [TPC, 64, 512])
    okvG = B.outp("kvG", [TPC, 64, 512])
    odecG = B.outp("decG", [TPC, 64, 4])

    B.make_banks()
    B.load_consts(["ident", "triU64", "chunkind", "kfacR", "ropefs"])
    B.emit_mod(cvec, adaw, adab, pre, None, False)
    (cosR, sinR), (cosD, sinD) = B.emit_ropes(pos, [(0, 1, "rr"), (2, 3, "rd")])
    wA = B.load_weight(WA, 0, ncolA, "wA")
    wlr17 = load_wlr17(B, wlr, blr)
    hT = P.sb("hT", [128, 1, 8, 128], BF16)

    kTb = P.sb("kTb", [64, 4, 128], BF16)
    khat = P.sb("khat", [128, 4, 64], BF16)
    vtok = P.sb("vtok", [128, 512], BF16)
    kvs = [P.sb(f"kvs{i}", [64, 512], F32) for i in range(2)]
    ktb = P.sb("ktb", [128, 128], BF16)
    vb = P.sb("vb", [128, 128], BF16)
    ikb = P.sb("ikb", [64, 128], BF16)
    kfac = P.sb("kfac", [128, 256], F32)
    gkhat = P.sb("gkhat", [128, 256], BF16)
    gvtok = P.sb("gvtok", [128, 512], BF16)
    dec = P.sb("dec", [64, 4, 4], F32)
    kv1s = P.sb("kv1s", [64, 512], F32)
    decT = P.sb("decT", [64, 4], F32)
    ident = B.c["ident"]

    for s in range(TPC):
        t0 = s * 128
        B.emit_norm_T(x[s], hT, 0)
        bx, bs = B.bank(), B.bank()
        for h in range(4):
            B.proj_fm(bx.ap()[0:64, h * 128:(h + 1) * 128], bx, wA, aoff["retk"][0] + h * 64, 64, hT, 0)
            B.proj_fm(bs.ap()[0:64, h * 128:(h + 1) * 128], bs, wA, aoff["retk_sw"][0] + h * 64, 64, hT, 0)
        B.rope_fm(kTb.ap(), kTb, bx, bs, 64, 512, cosR, sinR, t0)
        bt = B.bank()
        for h in range(4):
            P.op("tensor", lambda e, h=h: e.matmul(bt.ap()[:, h * 64:(h + 1) * 64], kTb.ap()[:, h, :],
                                                   B.identb.ap()[0:64, 0:64], start=True, stop=True),
                 reads=[kTb, B.identb], writes=[bt])
        P.op("vector", lambda e: e.tensor_tensor(
            khat.ap(), bt.ap()[:, 0:256].rearrange("p (h d) -> p h d", h=4),
            B.c["kfacR"].ap().unsqueeze(2).to_broadcast([128, 4, 64]), ALU.mult),
            reads=[bt, B.c["kfacR"]], writes=[khat])
        bv = B.bank()
        B.proj_tm(bv.ap(), bv, wA, aoff["retv"][0], 512, hT, 0)
        P.op("scalar", lambda e: e.activation(vtok.ap(), bv.ap(), AF.Copy), reads=[bv], writes=[vtok])
        bkv = B.bank()
        for h in range(4):
            P.op("tensor", lambda e, h=h: e.matmul(bkv.ap()[0:64, h * 128:(h + 1) * 128], khat.ap()[:, h, :],
                                                   vtok.ap()[:, h * 128:(h + 1) * 128], start=True, stop=True),
                 reads=[khat, vtok], writes=[bkv])
        kv = kvs[0]
        P.op("scalar", lambda e: e.activation(kv.ap(), bkv.ap()[0:64, :], AF.Copy), reads=[bkv], writes=[kv])
        P.dma(okvR[s], kv.ap(), reads=[kv])
        bx, bs = B.bank(), B.bank()
        B.proj_fm(bx.ap()[:, 0:128], bx, wA, aoff["dsak"][0], 128, hT, 0)
        B.proj_fm(bs.ap()[:, 0:128], bs, wA, aoff["dsak_sw"][0], 128, hT, 0)
        B.rope_fm(ktb.ap().unsqueeze(1), ktb, bx, bs, 128, 128, cosD, sinD, t0)
        P.dma(oKT[:, t0:t0 + 128], ktb.ap(), reads=[ktb])
        bv = B.bank()
        B.proj_tm(bv.ap()[:, 0:128], bv, wA, aoff["dsav"][0], 128, hT, 0)
        P.op("scalar", lambda e: e.activation(vb.ap(), bv.ap()[:, 0:128], AF.Copy), reads=[bv], writes=[vb])
        P.dma(oV[s], vb.ap(), reads=[vb])
        bx, bs = B.bank(), B.bank()
        B.proj_fm(bx.ap()[0:64, 0:128], bx, wA, aoff["idxk"][0], 64, hT, 0)
        B.proj_fm(bs.ap()[0:64, 0:128], bs, wA, aoff["idxk_sw"][0], 64, hT, 0)
        B.rope_fm(ikb.ap().unsqueeze(1), ikb, bx, bs, 64, 128, cosD, sinD, t0)
        P.dma(oIK[:, t0:t0 + 128], ikb.ap(), reads=[ikb])
        sp = gla_decay_common(B, hT, 0, wA, aoff["glaa"][0], wlr17)
        bd = B.bank()
        P.op("tensor", lambda e: e.matmul(bd.ap()[:, 0:256], B.c["triU64"].ap(), sp.ap(), start=True, stop=True),
             reads=[B.c["triU64"], sp], writes=[bd])
        P.op("scalar", lambda e: e.activation(kfac.ap(), bd.ap()[:, 0:256], AF.Exp, scale=-1.0 / 16.0),
             reads=[bd], writes=[kfac])
        bk = B.bank()
        B.proj_tm(bk.ap()[:, 0:256], bk, wA, aoff["glak"][0], 256, hT, 0)
        P.op("vector", lambda e: e.tensor_tensor(gkhat.ap(), bk.ap()[:, 0:256], kfac.ap(), ALU.mult),
             reads=[bk, kfac], writes=[gkhat])
        bv = B.bank()
        B.proj_tm(bv.ap(), bv, wA, aoff["glav"][0], 512, hT, 0)
        P.op("scalar", lambda e: e.activation(gvtok.ap(), bv.ap(), AF.Copy), reads=[bv], writes=[gvtok])
        b0, b1 = B.bank(), B.bank()
        for ch, bb in ((0, b0), (1, b1)):
            for h in range(4):
                P.op("tensor", lambda e, h=h, ch=ch, bb=bb: e.matmul(
                    bb.ap()[0:64, h * 128:(h + 1) * 128], gkhat.ap()[ch * 64:(ch + 1) * 64, h * 64:(h + 1) * 64],
                    gvtok.ap()[ch * 64:(ch + 1) * 64, h * 128:(h + 1) * 128], start=True, stop=True),
                    reads=[gkhat, gvtok], writes=[bb])
        bs_ = B.bank()
        for h in range(4):
            P.op("tensor", lambda e, h=h: e.matmul(bs_.ap()[0:64, h * 4:(h + 1) * 4], sp.ap()[:, h * 64:(h + 1) * 64],
                                                   B.c["chunkind"].ap(), start=True, stop=True),
                 reads=[sp, B.c["chunkind"]], writes=[bs_])
        P.op("scalar", lambda e: e.activation(dec.ap(), bs_.ap()[0:64, 0:16].rearrange("p (h c) -> p h c", h=4),
                                              AF.Exp, scale=-1.0 / 16.0), reads=[bs_], writes=[dec])
        P.op("scalar", lambda e: e.activation(kv1s.ap(), b1.ap()[0:64, :], AF.Copy), reads=[b1], writes=[kv1s])
        kv = kvs[1]
        for h in range(4):
            P.op("vector", lambda e, h=h: e.scalar_tensor_tensor(
                kv.ap()[:, h * 128:(h + 1) * 128], b0.ap()[0:64, h * 128:(h + 1) * 128], dec.ap()[:, h, 1:2],
                kv1s.ap()[:, h * 128:(h + 1) * 128], ALU.mult, ALU.add),
                reads=[b0, dec, kv1s], writes=[kv])
        P.dma(okvG[s], kv.ap(), reads=[kv])
        P.op("vector", lambda e: e.tensor_copy(decT.ap(), dec.ap()[:, :, 2]), reads=[dec], writes=[decT])
        P.dma(odecG[s], decT.ap(), reads=[decT])
    P.finish()
    return B


def prep_common(inputs, cfg, l, core):
    raise NotImplementedError


def a_col_index():
    ar = np.arange
    idx = {
        "retk": OFF["ret_k"] + ar(256), "retk_sw": np.array(swap_cols(OFF["ret_k"], 64, 64, 4)),
        "retv": OFF["ret_v"] + ar(512),
        "dsak": OFF["dsa_k"] + ar(128), "dsak_sw": np.array(swap_cols(OFF["dsa_k"], 16, 64, 2)),
        "dsav": OFF["dsa_v"] + ar(128),
        "idxk": OFF["idx_k"] + ar(64), "idxk_sw": np.array(swap_cols(OFF["idx_k"], 16, 64, 1)),
        "glak": OFF["gla_k"] + ar(256), "glav": OFF["gla_v"] + ar(512), "glaa": OFF["gla_a"] + ar(16),
    }
    return np.concatenate([idx[n] for n, _ in A_COLS])


def kc_layout(w):
    n = w.shape[1]
    return np.ascontiguousarray(w.reshape(8, 128, n).transpose(1, 0, 2))


def core_tiles(cfg, c):
    return [k * cfg.NCORE + c for k in range(cfg.TPC)]


def const_inputs(names):
    hc = host_consts()
    return {"c_" + n: hc[n] for n in names}


def host_inputs_A(inp, cfg, l):
    x = np.asarray(inp["x"])[0].reshape(cfg.S // 128, 128, D)
    pos = np.asarray(inp["positions"])[0].reshape(cfg.S // 128, 128)
    shared = {
        "cvec": np.ascontiguousarray(np.asarray(inp["c"])[0].reshape(8, 128).T),
        "adaw": kc_layout(np.asarray(inp["ada_w"])[l]),
        "adab": np.asarray(inp["ada_b"])[l][None, :],
        "pre": np.asarray(inp["pre_norm"])[l][None, :],
        "WA": kc_layout(np.asarray(inp["w_in"])[l][:, a_col_index()]),
        "wlr": np.asarray(inp["gla_w_lr"])[l],
        "blr": np.asarray(inp["gla_b_lr"])[l][None, :],
    }
    shared.update(const_inputs(["ident", "triU64", "chunkind", "kfacR", "ropefs"]))
    maps = []
    for c in range(cfg.NCORE):
        tl = core_tiles(cfg, c)
        m = dict(shared)
        m["x"] = np.ascontiguousarray(x[tl])
        m["pos"] = np.ascontiguousarray(pos[tl].reshape(1, cfg.T)).astype(np.int32)
        maps.append(m)
    return maps


def np_inputs(S, seed=0, depth=4):
    r = np.random.RandomState(seed)
    f = np.float32
    n = lambda *s: r.randn(*s).astype(f)
    Dm = D
    return {
        "x": n(1, S, Dm), "c": n(1, Dm),
        "positions": (np.arange(S, dtype=np.int32)[None, :] + np.int32(r.randint(0, 4096))),
        "ada_w": n(depth, Dm, 3 * Dm) * f(0.1 * Dm ** -0.5), "ada_b": n(depth, 3 * Dm) * f(0.02),
        "pre_norm": 1 + f(0.05) * n(depth, Dm), "post_norm": 1 + f(0.05) * n(depth, Dm),
        "w_in": n(depth, Dm, IN_WIDTH) * f(Dm ** -0.5),
        "gla_w_lr": n(depth, 16, 256) * f(0.25), "gla_b_lr": f(0.1) * n(depth, 256),
        "w_br_ret": n(depth, 512, Dm) * f(512 ** -0.5), "w_br_dsa": n(depth, 512, Dm) * f(512 ** -0.5),
        "w_br_gla": n(depth, 512, Dm) * f(512 ** -0.5), "w_out": n(depth, Dm, Dm) * f(Dm ** -0.5),
    }


B_COLS = [("retq", 256), ("retq_sw", 256), ("retk", 256), ("retk_sw", 256), ("retv", 512), ("retg", 512),
          ("mg0", 1024),
          ("glaq", 256), ("glak", 256), ("glav", 512), ("glag", 512), ("glaa", 16), ("mg2", 1024),
          ("dsaq", 512), ("dsaq_sw", 512), ("idxq", 256), ("idxq_sw", 256), ("idxw", 4), ("dsag", 512),
          ("mg1", 1024)]


def b_col_index():
    ar = np.arange
    pair = np.concatenate([np.concatenate([ar(64) + j * 64, ar(64) + (j + 4) * 64]) for j in range(4)])
    dq = OFF["dsa_q"] + ar(512)
    dq_sw = np.array(swap_cols(OFF["dsa_q"], 16, 64, 8))
    idx = {
        "retq": OFF["ret_q"] + ar(256), "retq_sw": np.array(swap_cols(OFF["ret_q"], 64, 64, 4)),
        "retk": OFF["ret_k"] + ar(256), "retk_sw": np.array(swap_cols(OFF["ret_k"], 64, 64, 4)),
        "retv": OFF["ret_v"] + ar(512), "retg": OFF["ret_g"] + ar(512),
        "mg0": OFF["merge"] + ar(1024), "mg1": OFF["merge"] + 1024 + ar(1024), "mg2": OFF["merge"] + 2048 + ar(1024),
        "glaq": OFF["gla_q"] + ar(256), "glak": OFF["gla_k"] + ar(256), "glav": OFF["gla_v"] + ar(512),
        "glag": OFF["gla_g"] + ar(512), "glaa": OFF["gla_a"] + ar(16),
        "dsaq": dq[pair], "dsaq_sw": dq_sw[pair],
        "idxq": OFF["idx_q"] + ar(256), "idxq_sw": np.array(swap_cols(OFF["idx_q"], 16, 64, 4)),
        "idxw": OFF["idx_w"] + ar(4), "dsag": OFF["dsa_g"] + ar(512),
    }
    return np.concatenate([idx[n] for n, _ in B_COLS])


def build_B(cfg):
    B = Builder(cfg, "phaseB")
    P = B.P
    TPC, T, NCORE, SG = cfg.TPC, cfg.T, cfg.NCORE, cfg.SG
    GK, BLK, NB, BPG = cfg.GK, cfg.BLK, cfg.NB, cfg.BPG
    boff, ncolB = col_layout(B_COLS)
    x = B.inp("x", [TPC, 128, 1024])
    pos = B.inp("pos", [1, T], I32)
    cvec = B.inp("cvec", [128, 8])
    adaw = B.inp("adaw", [128, 8, 3072])
    adab = B.inp("adab", [1, 3072])
    pre = B.inp("pre", [1, 1024])
    post = B.inp("post", [1, 1024])
    WB = B.inp("WB", [128, 8, ncolB])
    wbr_r = B.inp("wbr_r", [128, 4, 1024])
    wbr_g = B.inp("wbr_g", [128, 4, 1024])
    wbr_d = B.inp("wbr_d", [64, 8, 1024])
    wout_d = B.inp("wout", [128, 8, 1024])
    wlr = B.inp("wlr", [16, 256])
    blr = B.inp("blr", [1, 256])
    KTa = B.inp("KTa", [NCORE, 128, T], BF16)
    Va = B.inp("Va", [NCORE, TPC, 128, 128], BF16)
    IKa = B.inp("IKa", [NCORE, 64, T], BF16)
    kvRa = B.inp("kvRa", [NCORE, TPC, 64, 512])
    kvGa = B.inp("kvGa", [NCORE, TPC, 64, 512])
    decGa = B.inp("decGa", [NCORE, TPC, 64, 4])
    sel_d = B.inp("sel", [128, NCORE])
    pen_d = B.inp("pen", [128, GK], BF16)
    xo = B.outp("xo", [TPC, 128, 1024])

    B.make_banks()
    B.load_consts(B_CONSTS)
    sel = P.sb("sel", [128, NCORE], F32)
    P.dma(sel.ap(), sel_d, writes=[sel])
    pen = P.sb("pen", [128, GK], BF16)
    P.dma(pen.ap(), pen_d, writes=[pen])
    ident4 = P.sb("ident4", [128, 4, 128], BF16)
    P.op("vector", lambda e: e.tensor_copy(ident4.ap(), B.c["ident"].ap().unsqueeze(1).to_broadcast([128, 4, 128])),
         reads=[B.c["ident"]], writes=[ident4])
    B.emit_mod(cvec, adaw, adab, pre, post, True)
    (cosR, sinR), (cosD, sinD) = B.emit_ropes(pos, [(0, 1, "rr"), (2, 3, "rd")])
    wlr17 = load_wlr17(B, wlr, blr)
    nc_ = B.nc
    WBb = nc_.dram_tensor("WBb", [128, 8, ncolB], BF16, kind="Internal").ap()
    wbr_rb = nc_.dram_tensor("wbr_rb", [128, 4, 1024], BF16, kind="Internal").ap()
    wbr_gb = nc_.dram_tensor("wbr_gb", [128, 4, 1024], BF16, kind="Internal").ap()
    wbr_db = nc_.dram_tensor("wbr_db", [64, 8, 1024], BF16, kind="Internal").ap()
    woutb = nc_.dram_tensor("woutb", [128, 8, 1024], BF16, kind="Internal").ap()
    wbuf = Buf("wscratch")
    P.push_scope()
    B.precast(WB, WBb, wbuf, 128, 8, ncolB)
    B.precast(wbr_r, wbr_rb, wbuf, 128, 4, 1024)
    B.precast(wbr_g, wbr_gb, wbuf, 128, 4, 1024)
    B.precast(wbr_d, wbr_db, wbuf, 64, 8, 1024)
    B.precast(wout_d, woutb, wbuf, 128, 8, 1024)
    P.pop_scope()
    wbr_r, wbr_g, wbr_d, wout_d = wbr_rb, wbr_gb, wbr_db, woutb
    B.prealloc(NCORE)
    SR = P.sb("SR", [64, 4, 128], F32)
    SGs = P.sb("SGs", [64, 4, 128], F32)
    P.op("vector", lambda e: e.memset(SR.ap(), 0.0), writes=[SR])
    P.op("vector", lambda e: e.memset(SGs.ap(), 0.0), writes=[SGs])
    hT = P.sb("hT", [128, SG, 8, 128], BF16)
    ysum = P.sb("ysum", [128, SG, 8, 128], BF16)
    dsaout = P.sb("dsaout", [64, SG, 8, 128], BF16)
    cap = P.sb("cap", [64, 4, 128], F32)
    capb = P.sb("capb", [64, 4, 128], BF16)

    def load_w(names, tag):
        c0 = boff[names[0]][0]
        n = sum(boff[k][1] for k in names)
        assert boff[names[-1]][0] + boff[names[-1]][1] == c0 + n
        return B.load_weight_bf16(WBb, wbuf, c0, n, tag), c0

    def load_w2(dram_ap, rows, kc, name):
        return B.load_weight_bf16(dram_ap, wbuf, 0, 1024, name, kc=kc, rows=rows)

    def scan(S, kv_all, dec_src, s, tag):
        if dec_src is not None:
            dcg = B.dcg_g
            P.dma(dcg.ap(), dec_src[:, s].rearrange("j d n -> d j n"), writes=[dcg])
        for j in range(NCORE):
            kvt = B.kvt[j % 2]
            P.dma(kvt.ap(), kv_all[j, s], writes=[kvt])
            if j == 0:
                P.op("vector", lambda e: e.tensor_scalar(cap.ap(), S.ap(), sel.ap()[0:64, 0:1], None, ALU.mult),
                     reads=[S, sel], writes=[cap])
            else:
                P.op("vector", lambda e: e.scalar_tensor_tensor(cap.ap(), S.ap(), sel.ap()[0:64, j:j + 1], cap.ap(),
                                                                ALU.mult, ALU.add), reads=[S, sel, cap], writes=[cap])
            if dec_src is None:
                dv = B.c["decR"].ap()[0:64, :].unsqueeze(2).to_broadcast([64, 4, 128])
                rd = [B.c["decR"]]
            else:
                dv = dcg.ap()[:, j, :].unsqueeze(2).to_broadcast([64, 4, 128])
                rd = [dcg]
            P.op("vector", lambda e: e.tensor_tensor(S.ap(), S.ap(), dv, ALU.mult), reads=[S] + rd, writes=[S])
            P.op("vector", lambda e: e.tensor_tensor(S.ap(), S.ap(), kvt.ap().rearrange("p (h e) -> p h e", h=4),
                                                     ALU.add), reads=[S, kvt], writes=[S])
        P.op("scalar", lambda e: e.activation(capb.ap(), cap.ap(), AF.Copy), reads=[cap], writes=[capb])

    def tail(bO, w, c0, gname, mgname, wbr, ls, first, tl):
        sq, r1, sg, tt, bro, mg, t2 = tl
        P.op("scalar", lambda e: e.activation(sq.ap(), bO.ap(), AF.Square), reads=[bO], writes=[sq])
        bQ = B.bank()
        P.op("tensor", lambda e: e.matmul(bQ.ap(), B.onesb.ap(), sq.ap(), start=True, stop=True),
             reads=[B.onesb, sq], writes=[bQ])
        P.op("vector", lambda e: e.tensor_scalar(r1.ap(), bQ.ap(), 1.0 / 128.0, RMS_EPS, ALU.mult, ALU.add),
             reads=[bQ], writes=[r1])
        P.op("scalar", lambda e: e.activation(r1.ap(), r1.ap(), AF.Sqrt), reads=[r1], writes=[r1])
        P.op("vector", lambda e: e.reciprocal(r1.ap(), r1.ap()), reads=[r1], writes=[r1])
        bG = B.bank()
        g0 = boff[gname][0] - c0
        for h in range(4):
            B.proj_fm(bG.ap()[:, h * 128:(h + 1) * 128], bG, w, g0 + h * 128, 128, hT, ls)
        P.op("scalar", lambda e: e.activation(sg.ap(), bG.ap(), AF.Silu), reads=[bG], writes=[sg])
        P.op("vector", lambda e: e.tensor_tensor(tt.ap(), bO.ap(), r1.ap(), ALU.mult), reads=[bO, r1], writes=[tt])
        P.op("gpsimd", lambda e: e.tensor_tensor(bro.ap(), tt.ap(), sg.ap(), ALU.mult), reads=[tt, sg], writes=[bro])
        merge(lambda nc, out_ap, bk: [P.op("tensor", lambda e, h=h: e.matmul(
            out_ap, wbr.ap()[:, h, nc * 128:(nc + 1) * 128], bro.ap()[:, h * 128:(h + 1) * 128],
            start=(h == 0), stop=(h == 3)), reads=[wbr, bro], writes=[bk]) for h in range(4)],
            w, boff[mgname][0] - c0, ls, first, mg, t2)

    def merge(emit_branch, w, m0, ls, first, mg, t2):
        bY = [B.bank(), B.bank()]
        for nc in range(8):
            bk = bY[nc // 4]
            emit_branch(nc, bk.ap()[:, (nc % 4) * 128:(nc % 4 + 1) * 128], bk)
        bM = [B.bank(), B.bank()]
        for nc in range(8):
            bk = bM[nc // 4]
            B.proj_fm(bk.ap()[:, (nc % 4) * 128:(nc % 4 + 1) * 128], bk, w, m0 + nc * 128, 128, hT, ls)
        for hf in range(2):
            P.op("scalar", lambda e: e.activation(mg.ap(), bM[hf].ap(), AF.Sigmoid), reads=[bM[hf]], writes=[mg])
            yv = ysum.ap()[:, ls, hf * 4:(hf + 1) * 4, :]
            m3 = mg.ap().rearrange("p (a b) -> p a b", a=4)
            b3 = bY[hf].ap().rearrange("p (a b) -> p a b", a=4)
            if first:
                P.op("vector", lambda e: e.tensor_tensor(yv, m3, b3, ALU.mult), reads=[mg, bY[hf]], writes=[ysum])
            else:
                P.op("vector", lambda e: e.tensor_tensor(t2.ap(), mg.ap(), bY[hf].ap(), ALU.mult),
                     reads=[mg, bY[hf]], writes=[t2])
                P.op("gpsimd", lambda e: e.tensor_tensor(yv, yv, t2.ap().rearrange("p (a b) -> p a b", a=4), ALU.add),
                     reads=[ysum, t2], writes=[ysum])

    def tail_bufs():
        return (P.sb("t_sq", [128, 512], BF16), P.sb("t_r1", [128, 512], F32), P.sb("t_sg", [128, 512], F32),
                P.sb("t_tt", [128, 512], F32), P.sb("t_bro", [128, 512], BF16), P.sb("t_mg", [128, 512], F32),
                P.sb("t_t2", [128, 512], F32))

    for g0 in range(0, TPC, SG):
        P.push_scope()
        for s in range(g0, g0 + SG):
            B.emit_norm_T(x[s], hT, s - g0)
        P.pop_scope()
        del B.xin
        P.push_scope()
        w, c0 = load_w(["retq", "retq_sw", "retk", "retk_sw", "retv", "retg", "mg0"], "w_ret")
        wbr = load_w2(wbr_r, 128, 4, "wbr_ret")
        tl = tail_bufs()
        qTb = P.sb("qTb", [64, 4, 128], BF16)
        kTb = P.sb("kTb", [64, 4, 128], BF16)
        qhat = P.sb("qhat", [64, 4, 128], BF16)
        vtok = P.sb("vtok", [128, 512], BF16)
        Sm = P.sb("Sm", [128, 512], BF16)
        for s in range(g0, g0 + SG):
            ls = s - g0
            t0 = s * 128
            scan(SR, kvRa, None, s, "r")
            for nm, dst in (("retq", qTb), ("retk", kTb)):
                bx, bs = B.bank(), B.bank()
                for h in range(4):
                    B.proj_fm(bx.ap()[0:64, h * 128:(h + 1) * 128], bx, w, boff[nm][0] - c0 + h * 64, 64, hT, ls)
                    B.proj_fm(bs.ap()[0:64, h * 128:(h + 1) * 128], bs, w, boff[nm + "_sw"][0] - c0 + h * 64, 64, hT, ls)
                B.rope_fm(dst.ap(), dst, bx, bs, 64, 512, cosR, sinR, t0)
            bv = B.bank()
            B.proj_tm(bv.ap(), bv, w, boff["retv"][0] - c0, 512, hT, ls)
            P.op("scalar", lambda e: e.activation(vtok.ap(), bv.ap(), AF.Copy), reads=[bv], writes=[vtok])
            bS = B.bank()
            for h in range(4):
                P.op("tensor", lambda e: e.matmul(bS.ap()[:, h * 128:(h + 1) * 128], kTb.ap()[:, h, :], qTb.ap()[:, h, :],
                                                  start=True, stop=True), reads=[kTb, qTb], writes=[bS])
            P.op("vector", lambda e: e.tensor_tensor(Sm.ap(), bS.ap(), B.c["DTret"].ap().rearrange("p h i -> p (h i)"),
                                                     ALU.mult), reads=[bS, B.c["DTret"]], writes=[Sm])
            P.op("gpsimd", lambda e: e.tensor_tensor(qhat.ap(), qTb.ap(), B.c["QDret"].ap()[0:64], ALU.mult),
                 reads=[qTb, B.c["QDret"]], writes=[qhat])
            bO = B.bank()
            for h in range(4):
                o_ap = bO.ap()[:, h * 128:(h + 1) * 128]
                P.op("tensor", lambda e: e.matmul(o_ap, vtok.ap()[:, h * 128:(h + 1) * 128], Sm.ap()[:, h * 128:(h + 1) * 128],
                                                  start=True, stop=False), reads=[vtok, Sm], writes=[bO])
                P.op("tensor", lambda e: e.matmul(o_ap, capb.ap()[:, h, :], qhat.ap()[:, h, :], start=False, stop=True),
                     reads=[capb, qhat], writes=[bO])
            tail(bO, w, c0, "retg", "mg0", wbr, ls, True, tl)
        P.pop_scope()
        P.push_scope()
        w, c0 = load_w(["glaq", "glak", "glav", "glag", "glaa", "mg2"], "w_gla")
        wbr = load_w2(wbr_g, 128, 4, "wbr_gla")
        tl = tail_bufs()
        eq = P.sb("eq", [64, 512], F32)
        ek = P.sb("ek", [64, 512], F32)
        e128 = P.sb("e128", [64, 512], F32)
        qt = P.sb("qt", [64, 4, 128], BF16)
        qh = P.sb("qh", [64, 4, 128], BF16)
        kt = P.sb("kt", [64, 4, 128], BF16)
        kfac = P.sb("kfac", [128, 256], F32)
        gkhat = P.sb("gkhat", [128, 256], BF16)
        vtok = P.sb("gvtok", [128, 512], BF16)
        kv0b = P.sb("kv0b", [64, 512], BF16)
        Sm = P.sb("gSm", [128, 512], BF16)
        for s in range(g0, g0 + SG):
            ls = s - g0
            scan(SGs, kvGa, decGa, s, "g")
            sp = gla_decay_common(B, hT, ls, w, boff["glaa"][0] - c0, wlr17)
            bC64, bC128 = B.bank(), B.bank()
            for h in range(4):
                for bb, tri in ((bC64, "triL64"), (bC128, "triL128")):
                    P.op("tensor", lambda e: e.matmul(bb.ap()[0:64, h * 128:(h + 1) * 128], sp.ap()[:, h * 64:(h + 1) * 64],
                                                      B.c[tri].ap(), start=True, stop=True), reads=[sp, B.c[tri]], writes=[bb])
            P.op("scalar", lambda e: e.activation(eq.ap(), bC64.ap()[0:64, :], AF.Exp, scale=-1.0 / 16), reads=[bC64], writes=[eq])
            P.op("scalar", lambda e: e.activation(ek.ap(), bC64.ap()[0:64, :], AF.Exp, scale=1.0 / 16), reads=[bC64], writes=[ek])
            P.op("scalar", lambda e: e.activation(e128.ap(), bC128.ap()[0:64, :], AF.Exp, scale=-1.0 / 16), reads=[bC128], writes=[e128])
            bq = B.bank()
            for h in range(4):
                B.proj_fm(bq.ap()[0:64, h * 128:(h + 1) * 128], bq, w, boff["glaq"][0] - c0 + h * 64, 64, hT, ls)
            f2 = lambda b: b.ap().rearrange("p h t -> p (h t)")
            P.op("vector", lambda e: e.scalar_tensor_tensor(f2(qt), bq.ap()[0:64, :], 0.125, eq.ap(), ALU.mult, ALU.mult),
                 reads=[bq, eq], writes=[qt])
            P.op("vector", lambda e: e.scalar_tensor_tensor(f2(qh), bq.ap()[0:64, :], 0.125, e128.ap(), ALU.mult, ALU.mult),
                 reads=[bq, e128], writes=[qh])
            bk = B.bank()
            for h in range(4):
                B.proj_fm(bk.ap()[0:64, h * 128:(h + 1) * 128], bk, w, boff["glak"][0] - c0 + h * 64, 64, hT, ls)
            P.op("vector", lambda e: e.tensor_tensor(f2(kt), bk.ap()[0:64, :], ek.ap(), ALU.mult), reads=[bk, ek], writes=[kt])
            bd = B.bank()
            P.op("tensor", lambda e: e.matmul(bd.ap()[:, 0:256], B.c["triU64"].ap(), sp.ap(), start=True, stop=True),
                 reads=[B.c["triU64"], sp], writes=[bd])
            P.op("scalar", lambda e: e.activation(kfac.ap(), bd.ap()[:, 0:256], AF.Exp, scale=-1.0 / 16.0), reads=[bd], writes=[kfac])
            bkt = B.bank()
            B.proj_tm(bkt.ap()[:, 0:256], bkt, w, boff["glak"][0] - c0, 256, hT, ls)
            P.op("vector", lambda e: e.tensor_tensor(gkhat.ap(), bkt.ap()[:, 0:256], kfac.ap(), ALU.mult),
                 reads=[bkt, kfac], writes=[gkhat])
            bv = B.bank()
            B.proj_tm(bv.ap(), bv, w, boff["glav"][0] - c0, 512, hT, ls)
            P.op("scalar", lambda e: e.activation(vtok.ap(), bv.ap(), AF.Copy), reads=[bv], writes=[vtok])
            b0 = B.bank()
            for h in range(4):
                P.op("tensor", lambda e: e.matmul(b0.ap()[0:64, h * 128:(h + 1) * 128], gkhat.ap()[0:64, h * 64:(h + 1) * 64],
                                                  vtok.ap()[0:64, h * 128:(h + 1) * 128], start=True, stop=True),
                     reads=[gkhat, vtok], writes=[b0])
            P.op("scalar", lambda e: e.activation(kv0b.ap(), b0.ap()[0:64, :], AF.Copy), reads=[b0], writes=[kv0b])
            bS = B.bank()
            for h in range(4):
                P.op("tensor", lambda e: e.matmul(bS.ap()[:, h * 128:(h + 1) * 128], kt.ap()[:, h, :], qt.ap()[:, h, :],
                                                  start=True, stop=True), reads=[kt, qt], writes=[bS])
            P.op("vector", lambda e: e.tensor_tensor(Sm.ap(), bS.ap(), B.c["DTgla"].ap().rearrange("p h i -> p (h i)"),
                                                     ALU.mult), reads=[bS, B.c["DTgla"]], writes=[Sm])
            bO = B.bank()
            for h in range(4):
                o_ap = bO.ap()[:, h * 128:(h + 1) * 128]
                P.op("tensor", lambda e: e.matmul(o_ap, vtok.ap()[:, h * 128:(h + 1) * 128], Sm.ap()[:, h * 128:(h + 1) * 128],
                                                  start=True, stop=False), reads=[vtok, Sm], writes=[bO])
                P.op("tensor", lambda e: e.matmul(o_ap, capb.ap()[:, h, :], qh.ap()[:, h, :], start=False, stop=False),
                     reads=[capb, qh], writes=[bO])
                P.op("tensor", lambda e: e.matmul(bO.ap()[:, h * 128 + 64:(h + 1) * 128], kv0b.ap()[:, h * 128:(h + 1) * 128],
                                                  qt.ap()[:, h, 64:128], start=False, stop=True), reads=[kv0b, qt], writes=[bO])
            tail(bO, w, c0, "glag", "mg2", wbr, ls, False, tl)
        P.pop_scope()
        dsa_stage(B, cfg, g0, locals())
        P.push_scope()
        wo = load_w2(wout_d, 128, 8, "w_out")
        xt = P.sb("f_xt", [128, 1024], F32)
        fj = P.sb("f_junk", [128, 512], BF16)
        fst = P.sb("f_st", [128, 8], F32)
        ft = P.sb("f_t", [128, 1024], F32)
        fo = P.sb("f_o", [128, 1024], F32)
        for s in range(g0, g0 + SG):
            ls = s - g0
            P.dma(xt.ap(), x[s], writes=[xt])
            bh = [B.bank(), B.bank()]
            for hf in range(2):
                for nc in range(8):
                    P.op("tensor", lambda e: e.matmul(bh[hf].ap(), ysum.ap()[:, ls, nc, :], wo.ap()[:, nc, hf * 512:(hf + 1) * 512],
                                                      start=(nc == 0), stop=(nc == 7)), reads=[ysum, wo], writes=[bh[hf]])
                P.op("scalar", lambda e: e.activation(fj.ap(), bh[hf].ap(), AF.Square, accum_out=fst.ap()[:, hf:hf + 1]),
                     reads=[bh[hf]], writes=[fj, fst])
            P.op("vector", lambda e: e.tensor_tensor(fst.ap()[:, 2:3], fst.ap()[:, 0:1], fst.ap()[:, 1:2], ALU.add), reads=[fst], writes=[fst])
            P.op("vector", lambda e: e.tensor_scalar(fst.ap()[:, 3:4], fst.ap()[:, 2:3], 1.0 / D, RMS_EPS, ALU.mult, ALU.add),
                 reads=[fst], writes=[fst])
            P.op("scalar", lambda e: e.activation(fst.ap()[:, 4:5], fst.ap()[:, 3:4], AF.Sqrt), reads=[fst], writes=[fst])
            P.op("vector", lambda e: e.reciprocal(fst.ap()[:, 5:6], fst.ap()[:, 4:5]), reads=[fst], writes=[fst])
            for hf in range(2):
                sl = slice(hf * 512, (hf + 1) * 512)
                P.op("vector", lambda e: e.scalar_tensor_tensor(ft.ap()[:, sl], bh[hf].ap(), fst.ap()[:, 5:6], B.GP.ap()[:, sl],
                                                                ALU.mult, ALU.mult), reads=[bh[hf], fst, B.GP], writes=[ft])
            P.op("gpsimd", lambda e: e.tensor_tensor(fo.ap(), ft.ap(), xt.ap(), ALU.add), reads=[ft, xt], writes=[fo])
            P.dma(xo[s], fo.ap(), reads=[fo])
        P.pop_scope()
    P.finish()
    return B


def dsa_stage(B, cfg, g0, env):
    P = B.P
    TPC, T, NCORE, SG = cfg.TPC, cfg.T, cfg.NCORE, cfg.SG
    GK, BLK, NB, BPG, NIT = cfg.GK, cfg.BLK, cfg.NB, cfg.BPG, cfg.NIT
    hT, ysum, dsaout, boff = env["hT"], env["ysum"], env["dsaout"], env["boff"]
    KTa, Va, IKa, pen, ident4 = env["KTa"], env["Va"], env["IKa"], env["pen"], env["ident4"]
    cosD, sinD = env["cosD"], env["sinD"]
    P.push_scope()
    w, c0 = env["load_w"](["dsaq", "dsaq_sw", "idxq", "idxq_sw", "idxw", "dsag"], "w_dsa")
    NMAX = TPC * GK
    scores = P.sb("scores", [128, NMAX], F32)
    CH = min(512, NMAX)
    junk = P.sb("cjunk", [128, CH], BF16)
    QT = P.sb("QT", [128, 4, 128], BF16)
    IQ = P.sb("IQ", [64, 4, 128], BF16)
    wq = P.sb("wq", [128, 4], F32)
    rl = [P.sb(f"rl{i}", [128, BLK], F32) for i in range(2)]
    ikb = [P.sb(f"ikb{i}", [64, NB, 128], BF16) for i in range(2)]
    ktb = [P.sb(f"ktb{i}", [128, NB, 128], BF16) for i in range(2)]
    vbk = [P.sb(f"vbk{i}", [128, NB, 2, 128], BF16) for i in range(2)]
    for v in vbk:
        P.op("vector", lambda e: e.memset(v.ap(), 1.0), writes=[v])
    nbmax = TPC * BPG
    mx = P.sb("mx", [128, nbmax], F32)
    mn = P.sb("mn", [128, nbmax], F32)
    wall = P.sb("wall", [128, 32], F32)
    cntc = P.sb("cntc", [128, 64], F32)
    bst = P.sb("bst", [128, 16], F32)
    mlo = P.sb("mlo", [128, BLK], BF16)
    mhi = P.sb("mhi", [128, BLK], BF16)
    band = P.sb("band", [128, BLK], BF16)
    cum = [P.sb(f"cum{i}", [128, BLK], F32) for i in range(2)]
    tsel = P.sb("tsel", [128, BLK], BF16)
    mb = [P.sb(f"mb{i}", [128, BLK], BF16) for i in range(2)]
    PT = [P.sb(f"PT{i}", [128, 512], BF16) for i in range(2)]
    rc = P.sb("rc", [64, 512], F32)
    on = P.sb("on", [64, 512], BF16)
    sgd = P.sb("sgd", [64, 8, 128], BF16)
    acc = [B.banks[6], B.banks[7]]
    saved_i = B.bank_i
    rot = {"i": 0}

    def bank6():
        b = B.banks[rot["i"] % 6]
        rot["i"] += 1
        return b
    B_bank = B.bank
    B.bank = bank6
    col = lambda i: bst.ap()[:, i:i + 1]

    for s in range(g0, g0 + SG):
        ls = s - g0
        t0 = s * 128
        bx, bs = bank6(), bank6()
        for j in range(4):
            B.proj_fm(bx.ap()[:, j * 128:(j + 1) * 128], bx, w, boff["dsaq"][0] - c0 + j * 128, 128, hT, ls)
            B.proj_fm(bs.ap()[:, j * 128:(j + 1) * 128], bs, w, boff["dsaq_sw"][0] - c0 + j * 128, 128, hT, ls)
        B.rope_fm(QT.ap(), QT, bx, bs, 128, 512, cosD, sinD, t0)
        bx, bs = bank6(), bank6()
        for h in range(4):
            B.proj_fm(bx.ap()[0:64, h * 128:(h + 1) * 128], bx, w, boff["idxq"][0] - c0 + h * 64, 64, hT, ls)
            B.proj_fm(bs.ap()[0:64, h * 128:(h + 1) * 128], bs, w, boff["idxq_sw"][0] - c0 + h * 64, 64, hT, ls)
        B.rope_fm(IQ.ap(), IQ, bx, bs, 64, 512, cosD, sinD, t0)
        bw = bank6()
        B.proj_tm(bw.ap()[:, 0:4], bw, w, boff["idxw"][0] - c0, 4, hT, ls)
        P.op("vector", lambda e: e.tensor_scalar(wq.ap(), bw.ap()[:, 0:4], 0.0625, None, ALU.mult), reads=[bw], writes=[wq])
        nb = (s + 1) * BPG
        n = (s + 1) * GK
        for bi in range(nb):
            gq, blk = bi // BPG, bi % BPG
            ik = ikb[bi % 2]
            P.dma(ik.ap(), IKa[blk * NB:(blk + 1) * NB, :, gq * 128:(gq + 1) * 128].rearrange("j d t -> d j t"), writes=[ik])
            scb = scores.ap()[:, bi * BLK:(bi + 1) * BLK]
            for h in range(4):
                bI = bank6()
                P.op("tensor", lambda e: e.matmul(bI.ap()[:, 0:BLK], IQ.ap()[:, h, :], ik.ap().rearrange("d j t -> d (j t)"),
                                                  start=True, stop=True), reads=[IQ, ik], writes=[bI])
                r = rl[h % 2]
                P.op("scalar", lambda e: e.activation(r.ap(), bI.ap()[:, 0:BLK], AF.Relu), reads=[bI], writes=[r])
                if h == 0:
                    P.op("vector", lambda e: e.tensor_scalar(scb, r.ap(), wq.ap()[:, 0:1], None, ALU.mult),
                         reads=[r, wq], writes=[scores])
                else:
                    P.op("vector", lambda e: e.scalar_tensor_tensor(scb, r.ap(), wq.ap()[:, h:h + 1], scb, ALU.mult, ALU.add),
                         reads=[r, wq, scores], writes=[scores])

        P.op("vector", lambda e: e.tensor_reduce(col(8), scores.ap()[:, 0:n], AX.X, ALU.max), reads=[scores], writes=[bst])
        P.op("vector", lambda e: e.tensor_scalar(col(1), col(8), 1.0, None, ALU.add), reads=[bst], writes=[bst])
        P.op("vector", lambda e: e.tensor_reduce(col(9), scores.ap()[:, 0:n], AX.X, ALU.min), reads=[scores, bst], writes=[bst])
        P.op("vector", lambda e: e.tensor_scalar(col(0), col(9), -1.0, None, ALU.add), reads=[bst], writes=[bst])
        for blk in range(BPG):
            bi = s * BPG + blk
            scb = scores.ap()[:, bi * BLK:(bi + 1) * BLK]
            P.op("vector", lambda e: e.tensor_tensor(scb, scb, pen.ap()[:, blk * BLK:(blk + 1) * BLK], ALU.add),
                 reads=[scores, pen], writes=[scores])
        P.op("vector", lambda e: e.tensor_tensor(col(10), col(1), col(0), ALU.subtract), reads=[bst], writes=[bst])
        P.op("vector", lambda e: e.tensor_scalar(wall.ap(), B.c["pw2"].ap(), col(10), None, ALU.mult),
             reads=[B.c["pw2"], bst], writes=[wall])
        P.op("vector", lambda e: e.tensor_tensor(col(2), col(0), wall.ap()[:, 0:1], ALU.add), reads=[bst, wall], writes=[bst])
        chunks = [(cs, min(n, cs + CH)) for cs in range(0, n, CH)]

        def count(thr_col, dst_col):
            for ci, (cs, ce) in enumerate(chunks):
                P.op("vector", lambda e: e.tensor_scalar(junk.ap()[:, 0:ce - cs], scores.ap()[:, cs:ce], col(thr_col), None,
                                                         ALU.is_ge, ALU.add, accum_out=cntc.ap()[:, ci:ci + 1]),
                     reads=[scores, bst], writes=[cntc, junk])
            P.op("vector", lambda e: e.tensor_reduce(col(dst_col), cntc.ap()[:, 0:len(chunks)], AX.X, ALU.add),
                 reads=[cntc], writes=[bst])

        for it in range(NIT):
            count(2, 3)
            P.op("vector", lambda e: e.scalar_tensor_tensor(col(4), col(3), cfg.TOPK - 0.5, wall.ap()[:, it:it + 1],
                                                            ALU.is_ge, ALU.mult), reads=[bst, wall], writes=[bst])
            P.op("vector", lambda e: e.tensor_tensor(col(0), col(0), col(4), ALU.add), reads=[bst], writes=[bst])
            if it + 1 < NIT:
                P.op("vector", lambda e: e.tensor_tensor(col(2), col(0), wall.ap()[:, it + 1:it + 2], ALU.add),
                     reads=[bst, wall], writes=[bst])
        P.op("vector", lambda e: e.tensor_tensor(col(1), col(0), wall.ap()[:, NIT - 1:NIT], ALU.add), reads=[bst, wall], writes=[bst])
        count(1, 6)
        P.op("vector", lambda e: e.tensor_scalar(col(7), col(6), -BIGM, cfg.TOPK * BIGM, ALU.mult, ALU.add), reads=[bst], writes=[bst])
        for bi in range(nb):
            gq, blk = bi // BPG, bi % BPG
            kt_, vb_ = ktb[bi % 2], vbk[bi % 2]
            P.dma(kt_.ap(), KTa[blk * NB:(blk + 1) * NB, :, gq * 128:(gq + 1) * 128].rearrange("j d t -> d j t"), writes=[kt_])
            for jj in range(NB):
                P.dma(vb_.ap()[:, jj, :, 0:64], Va[blk * NB + jj, gq].rearrange("s (k d) -> s k d", k=2), writes=[vb_])
            scb = scores.ap()[:, bi * BLK:(bi + 1) * BLK]
            m = mb[bi % 2]
            cm, cprev = cum[bi % 2], cum[(bi + 1) % 2]
            P.op("vector", lambda e: e.tensor_scalar(mlo.ap(), scb, col(0), BIGM, ALU.is_ge, ALU.mult), reads=[scores, bst], writes=[mlo])
            P.op("vector", lambda e: e.tensor_scalar(mhi.ap(), scb, col(1), BIGM, ALU.is_ge, ALU.mult), reads=[scores, bst], writes=[mhi])
            P.op("vector", lambda e: e.tensor_tensor(band.ap(), mlo.ap(), mhi.ap(), ALU.subtract), reads=[mlo, mhi], writes=[band])
            P.op("vector", lambda e: e.tensor_tensor_scan(cm.ap(), band.ap(), band.ap(),
                                                          (0.0 if bi == 0 else cprev.ap()[:, BLK - 1:BLK]), ALU.add, ALU.max),
                 reads=[band] + ([] if bi == 0 else [cprev]), writes=[cm])
            P.op("vector", lambda e: e.scalar_tensor_tensor(tsel.ap(), cm.ap(), col(7), band.ap(), ALU.is_le, ALU.mult),
                 reads=[cm, bst, band], writes=[tsel])
            P.op("vector", lambda e: e.scalar_tensor_tensor(m.ap(), tsel.ap(), -BIGM, mhi.ap(), ALU.add, ALU.add),
                 reads=[tsel, mhi], writes=[m])
            for jj in range(NB):
                for kvn in range(2):
                    bL = bank6()
                    P.op("tensor", lambda e: e.matmul(bL.ap(), kt_.ap()[kvn * 64:(kvn + 1) * 64, jj, :],
                                                      QT.ap()[kvn * 64:(kvn + 1) * 64].rearrange("p j t -> p (j t)"),
                                                      start=True, stop=False), reads=[kt_, QT], writes=[bL])
                    P.op("tensor", lambda e: e.matmul(bL.ap(), m.ap()[:, jj * 128:(jj + 1) * 128],
                                                      ident4.ap().rearrange("p j t -> p (j t)"), start=False, stop=True),
                         reads=[m, ident4], writes=[bL])
                    pt = PT[rot["i"] % 2]
                    P.op("scalar", lambda e: e.activation(pt.ap(), bL.ap(), AF.Exp, scale=0.125), reads=[bL], writes=[pt])
                    first = (bi == 0 and jj == 0)
                    last = (bi == nb - 1 and jj == NB - 1)
                    P.op("tensor", lambda e: e.matmul(acc[kvn].ap(), vb_.ap()[:, jj, kvn, :], pt.ap(), start=first, stop=last),
                         reads=[vb_, pt], writes=[acc[kvn]])
        bg = [bank6(), bank6()]
        for hd in range(8):
            B.proj_fm(bg[hd // 4].ap()[0:64, (hd % 4) * 128:(hd % 4 + 1) * 128], bg[hd // 4], w,
                      boff["dsag"][0] - c0 + hd * 64, 64, hT, ls)
        for hf in range(2):
            P.op("scalar", lambda e: e.activation(sgd.ap()[:, hf * 4:(hf + 1) * 4, :],
                                                  bg[hf].ap()[0:64, :].rearrange("p (a b) -> p a b", a=4), AF.Silu),
                 reads=[bg[hf]], writes=[sgd])
        for kvn in range(2):
            P.op("vector", lambda e: e.reciprocal(rc.ap(), acc[kvn].ap()[64:128, :]), reads=[acc[kvn]], writes=[rc])
            P.op("vector", lambda e: e.tensor_tensor(on.ap(), acc[kvn].ap()[0:64, :], rc.ap(), ALU.mult),
                 reads=[acc[kvn], rc], writes=[on])
            P.op("gpsimd", lambda e: e.tensor_tensor(dsaout.ap()[:, ls, kvn * 4:(kvn + 1) * 4, :],
                                                     on.ap().rearrange("p (a b) -> p a b", a=4),
                                                     sgd.ap()[:, kvn * 4:(kvn + 1) * 4, :], ALU.mult),
                 reads=[on, sgd], writes=[dsaout])
    B.bank = B_bank
    P.pop_scope()
    P.push_scope()
    w, c0 = env["load_w"](["mg1"], "w_mg1")
    wbr = env["load_w2"](env["wbr_d"], 64, 8, "wbr_dsa")
    mg = P.sb("d_mg", [128, 512], F32)
    t2 = P.sb("d_t2", [128, 512], F32)
    for s in range(g0, g0 + SG):
        ls = s - g0
        env["merge"](lambda nc, out_ap, bk: [P.op("tensor", lambda e, hd=hd: e.matmul(
            out_ap, wbr.ap()[:, hd, nc * 128:(nc + 1) * 128], dsaout.ap()[:, ls, hd, :],
            start=(hd == 0), stop=(hd == 7)), reads=[wbr, dsaout], writes=[bk]) for hd in range(8)],
            w, 0, ls, False, mg, t2)
    P.pop_scope()


B_CONSTS = ["ident", "triL128", "triL64", "triU64", "DTret", "DTgla", "QDret", "decR", "ropefs", "pw2"]


def host_inputs_B(inp, cfg, l, xcur, resA):
    pos = np.asarray(inp["positions"])[0].reshape(cfg.S // 128, 128)
    st = lambda k: np.stack([np.asarray(r[k]) for r in resA])
    er = lambda w, h: np.ascontiguousarray(np.asarray(w)[l].reshape(h, 512 // h, D).transpose(1, 0, 2))
    shared = {
        "cvec": np.ascontiguousarray(np.asarray(inp["c"])[0].reshape(8, 128).T),
        "adaw": kc_layout(np.asarray(inp["ada_w"])[l]),
        "adab": np.asarray(inp["ada_b"])[l][None, :],
        "pre": np.asarray(inp["pre_norm"])[l][None, :],
        "post": np.asarray(inp["post_norm"])[l][None, :],
        "WB": kc_layout(np.asarray(inp["w_in"])[l][:, b_col_index()]),
        "wbr_r": er(inp["w_br_ret"], 4), "wbr_g": er(inp["w_br_gla"], 4), "wbr_d": er(inp["w_br_dsa"], 8),
        "wout": kc_layout(np.asarray(inp["w_out"])[l]),
        "wlr": np.asarray(inp["gla_w_lr"])[l],
        "blr": np.asarray(inp["gla_b_lr"])[l][None, :],
        "KTa": st("KT"), "Va": st("V"), "IKa": st("IK"),
        "kvRa": st("kvR"), "kvGa": st("kvG"), "decGa": st("decG"),
    }
    shared.update(const_inputs(B_CONSTS))
    maps = []
    for c in range(cfg.NCORE):
        tl = core_tiles(cfg, c)
        m = dict(shared)
        m["x"] = np.ascontiguousarray(xcur[tl])
        m["pos"] = np.ascontiguousarray(pos[tl].reshape(1, cfg.T)).astype(np.int32)
        sel = np.zeros((128, cfg.NCORE), np.float32)
        sel[:, c] = 1.0
        m["sel"] = sel
        kidx = np.arange(cfg.GK)[None, :]
        qidx = (c * 128 + np.arange(128))[:, None]
        import ml_dtypes
        m["pen"] = np.where(kidx > qidx, np.float32(-1e30), np.float32(0.0)).astype(ml_dtypes.bfloat16)
        maps.append(m)
    return maps


_CACHE = {}


def run_layers(inp, cfg):
    x = np.asarray(inp["x"])[0].reshape(cfg.S // 128, 128, D).astype(np.float32)
    cores = list(range(cfg.NCORE))
    for l in range(cfg.DEPTH):
        if "A" not in _CACHE:
            _CACHE["A"] = build_A(cfg)
        inp_l = dict(inp)
        inp_l["x"] = x.reshape(1, cfg.S, D)
        resA = run_bass_kernel_spmd(_CACHE["A"].nc, host_inputs_A(inp_l, cfg, l), core_ids=cores).results
        if "B" not in _CACHE:
            _CACHE["B"] = build_B(cfg)
        resB = run_bass_kernel_spmd(_CACHE["B"].nc, host_inputs_B(inp, cfg, l, x, resA), core_ids=cores).results
        xn = np.empty_like(x)
        for c in cores:
            xn[core_tiles(cfg, c)] = np.asarray(resB[c]["xo"])
        x = xn
    return x.reshape(1, cfg.S, D)


def kernel(**inputs):
    cfg = Cfg()
    return run_layers(inputs, cfg).astype(np.float32)
```

```python
import numpy as np
from contextlib import ExitStack
import concourse.bass as bass
import concourse.mybir as mybir
from concourse.bass_utils import run_bass_kernel_spmd

F32 = mybir.dt.float32
BF16 = mybir.dt.bfloat16
I32 = mybir.dt.int32
ALU = mybir.AluOpType
AF = mybir.ActivationFunctionType
AX = mybir.AxisListType

EPOCH = 20000
NDMA_SEM = 12


class Buf:
    def __init__(self, name, handle=None):
        self.name = name
        self.h = handle
        self.last_w = None
        self.readers = []

    def ap(self):
        return self.h[:]


class _Recorder:
    def __init__(self):
        self.call = None

    def __getattr__(self, name):
        def f(*a, **k):
            assert self.call is None
            self.call = (name, a, k)
            return None
        return f


class Prog:
    ENGS = ("tensor", "vector", "scalar", "gpsimd", "sync")

    def __init__(self, nc):
        self.nc = nc
        self.stack = ExitStack()
        self.stream = {e: [] for e in self.ENGS}
        self.count = {e: 0 for e in self.ENGS}
        self.known = {e: {} for e in self.ENGS}
        self.dma_n = {}
        self.dma_rr = {e: 0 for e in self.ENGS}
        self.nbuf = 0
        self.pending = {e: [] for e in self.ENGS}
        self.stacks = [self.stack]

    def sb(self, name, shape, dtype):
        self.nbuf += 1
        name = f"{name}_u{self.nbuf}"
        h = self.stacks[-1].enter_context(self.nc.sbuf_tensor(name, list(shape), dtype))
        return Buf(name, h)

    def push_scope(self):
        st = ExitStack()
        self.stacks.append(st)

    def pop_scope(self):
        self.barrier()
        self.stacks.pop().close()

    def barrier(self):
        toks = []
        for e in self.ENGS:
            c = self.count[e]
            if c > 0 and e != "sync":
                toks.append((("E", e, (c - 1) // EPOCH), (c - 1) % EPOCH + 1))
        for key, n in self.dma_n.items():
            toks.append((key, 16 * n))
        for e in self.ENGS:
            self.pending[e] = list(toks)

    def _take_pending(self, eng, waits):
        kn = self.known[eng]
        for key, val in self.pending[eng]:
            if key[0] == "E" and key[1] == eng:
                continue
            if kn.get(key, -1) >= val:
                continue
            if any(k == key and v >= val for k, v in waits):
                continue
            kn[key] = val
            waits.append((key, val))
        self.pending[eng] = []

    def ps(self, name, shape, dtype=F32):
        h = self.stack.enter_context(self.nc.psum_tensor(name, list(shape), dtype))
        return Buf(name, h)

    def alias(self, name, buf):
        raise NotImplementedError

    def _deps(self, eng, reads, writes, is_dma):
        deps = {}

        def add(tok, same_ok):
            if tok is None:
                return
            key, val = tok
            if (not is_dma) and key[0] == "E" and key[1] == eng and not same_ok:
                return
            if deps.get(key, -1) < val:
                deps[key] = val

        for b in reads:
            add(b.last_w, True)
        for b in writes:
            add(b.last_w, False)
            for t in b.readers:
                add(t, False)
        kn = self.known[eng]
        out = []
        for key, val in deps.items():
            if key[0] == "E":
                later = [k for k in kn if k[0] == "E" and k[1] == key[1] and k[2] > key[2]]
                if later:
                    continue
            if kn.get(key, -1) >= val:
                continue
            kn[key] = val
            out.append((key, val))
        return out

    def _commit(self, tok, reads, writes):
        for b in writes:
            b.last_w = tok
            b.readers = []
        for b in reads:
            if b in writes:
                continue
            rs = [t for t in b.readers if t[0] != tok[0]]
            rs.append(tok)
            b.readers = rs

    def op(self, eng, fn, reads=(), writes=()):
        reads = list(reads)
        writes = list(writes)
        waits = self._deps(eng, reads, writes, False)
        self._take_pending(eng, waits)
        self.count[eng] += 1
        c = self.count[eng]
        key = ("E", eng, (c - 1) // EPOCH)
        tok = (key, (c - 1) % EPOCH + 1)
        rec = _Recorder()
        fn(rec)
        self.stream[eng].append((waits, rec.call, key))
        self._commit(tok, reads, writes)
        return tok

    def dma(self, out_ap, in_ap, reads=(), writes=(), queue="sync", **kw):
        reads = list(reads)
        writes = list(writes)
        i = self.dma_rr[queue]
        self.dma_rr[queue] = (i + 1) % NDMA_SEM
        key = ("D", queue, i)
        n = self.dma_n.get(key, 0)
        waits = self._deps(queue, reads, writes, True)
        self._take_pending(queue, waits)
        if n > 0 and self.known[queue].get(key, -1) < 16 * n:
            self.known[queue][key] = 16 * n
            waits.append((key, 16 * n))
        self.dma_n[key] = n + 1
        tok = (key, 16 * (n + 1))
        kk = dict(kw)
        kk["out"] = out_ap
        kk["in_"] = in_ap
        self.stream[queue].append((waits, ("dma_start", (), kk), key))
        self._commit(tok, reads, writes)
        return tok

    def coll(self, kind, ins, outs, ranks, reads=(), writes=()):
        reads, writes = list(reads), list(writes)
        queue = "gpsimd"
        key = ("D", "coll", 0)
        n = self.dma_n.get(key, 0)
        waits = self._deps(queue, reads, writes, True)
        self._take_pending(queue, waits)
        if n > 0 and self.known[queue].get(key, -1) < 16 * n:
            self.known[queue][key] = 16 * n
            waits.append((key, 16 * n))
        self.dma_n[key] = n + 1
        tok = (key, 16 * (n + 1))
        call = ("collective_compute", (kind, ALU.bypass), dict(replica_groups=[list(ranks)], ins=list(ins), outs=list(outs)))
        self.stream[queue].append((waits, call, key))
        self._commit(tok, reads, writes)
        return tok

    def finish(self):
        nc = self.nc
        sems = {}

        def sem(key):
            if key not in sems:
                nm = "s_" + "_".join(str(k) for k in key)
                sems[key] = self.stack.enter_context(nc.semaphore(nm))
            return sems[key]

        final_waits = []
        for key, n in self.dma_n.items():
            final_waits.append((key, 16 * n))
        for eng in self.ENGS:
            for waits, fn, key in self.stream[eng]:
                sem(key)
                for k, v in waits:
                    sem(k)

        streams = self.stream

        def emit(eng_name):
            def body(e):
                for waits, fn, key in streams[eng_name]:
                    for k, v in waits:
                        e.wait_ge(sems[k], v)
                    ins = getattr(e, fn[0])(*fn[1], **fn[2])
                    ins.then_inc(sems[key], 16 if key[0] == "D" else 1)
                if eng_name == "sync":
                    for k, v in final_waits:
                        e.wait_ge(sems[k], v)
            return body

        with nc.Block() as block:
            block.sync(emit("sync"))
            block.tensor(emit("tensor"))
            block.vector(emit("vector"))
            block.scalar(emit("scalar"))
            block.gpsimd(emit("gpsimd"))
        while self.stacks:
            self.stacks.pop().close()


D = 1024
IN_SPLITS = (("ret_q", 256), ("ret_k", 256), ("ret_v", 512), ("ret_g", 512),
             ("dsa_q", 512), ("dsa_k", 128), ("dsa_v", 128), ("dsa_g", 512),
             ("idx_q", 256), ("idx_k", 64), ("idx_w", 4),
             ("gla_q", 256), ("gla_k", 256), ("gla_v", 512), ("gla_g", 512), ("gla_a", 16),
             ("merge", 3072))
OFF = {}
_o = 0
for _n, _w in IN_SPLITS:
    OFF[_n] = _o
    _o += _w
IN_WIDTH = _o
RMS_EPS = 1e-6
TOPK = 256
BIGM = 32768.0
LOG_G = [float(np.log1p(-2.0 ** (-5.0 - h))) for h in range(4)]


class Cfg:
    def __init__(self, S=16384, NCORE=8, DEPTH=4, SG=2, NIT=18, CH=512):
        self.S, self.NCORE, self.DEPTH = S, NCORE, DEPTH
        self.TPC = S // 128 // NCORE
        self.T = self.TPC * 128
        self.GK = NCORE * 128
        self.BLK = min(512, self.GK)
        self.NB = self.BLK // 128
        self.BPG = self.GK // self.BLK
        self.SG = min(SG, self.TPC)
        self.NIT = NIT
        self.CH = CH
        self.TOPK = min(256, S // 4)


def host_consts():
    f = np.float32
    j = np.arange(128)[:, None]
    i = np.arange(128)[None, :]
    c = {}
    c["ident"] = np.eye(128, dtype=f)
    same = (j // 64) == (i // 64)
    c["triL128"] = (j <= i).astype(f)
    c["triL64"] = ((j <= i) & same).astype(f)
    c["triU64"] = ((j > i) & same).astype(f)
    ci = np.zeros((128, 4), f)
    ci[:64, 0] = 1
    ci[64:, 1] = 1
    ci[:, 2] = 1
    c["chunkind"] = ci
    lg = np.array(LOG_G, np.float64)
    dt = np.zeros((128, 4, 128), np.float64)
    for h in range(4):
        dt[:, h, :] = np.where(i >= j, np.exp(lg[h] * np.maximum(i - j, 0)), 0.0)
    c["DTret"] = (dt * 0.125).astype(f)
    c["DTgla"] = np.repeat(((j <= i) & same).astype(f)[:, None, :], 4, axis=1)
    qd = np.zeros((128, 4, 128), np.float64)
    for h in range(4):
        qd[:, h, :] = np.exp(lg[h] * (i + 1.0))
    c["QDret"] = (qd * 0.125).astype(f)
    kf = np.zeros((128, 4), np.float64)
    for h in range(4):
        kf[:, h] = np.exp(lg[h] * (127.0 - np.arange(128)))
    c["kfacR"] = kf.astype(f)
    dr = np.zeros((128, 4), np.float64)
    for h in range(4):
        dr[:, h] = np.exp(lg[h] * 128.0)
    c["decR"] = dr.astype(f)
    fr = np.zeros((128, 4), f)
    p = np.arange(128) % 64
    half = 32
    fr_ret = (np.float32(10000.0) ** (-(np.arange(half, dtype=f)) * f(2.0) / f(64))).astype(f)
    fr[:, 0] = fr_ret[p % 32]
    fr[:, 1] = np.where(p < 32, -1.0, 1.0)
    fr_d = (np.float32(500000.0) ** (-(np.arange(8, dtype=f)) * f(2.0) / f(16))).astype(f)
    fr[:, 2] = np.where(p < 16, fr_d[p % 8], 0.0)
    fr[:, 3] = np.where(p < 8, -1.0, np.where(p < 16, 1.0, 0.0))
    c["ropefs"] = fr
    c["pw2"] = np.repeat((2.0 ** -(np.arange(32, dtype=np.float64) + 1.0))[None, :], 128, axis=0).astype(f)
    return c


def swap_cols(lo, rot, width, nheads):
    idx = []
    for h in range(nheads):
        base = lo + h * width
        half = rot // 2
        for d in range(width):
            if d < half:
                idx.append(base + d + half)
            elif d < rot:
                idx.append(base + d - half)
            else:
                idx.append(base + d)
    return idx


A_COLS = [("retk", 256), ("retk_sw", 256), ("retv", 512), ("dsak", 128), ("dsak_sw", 128),
          ("dsav", 128), ("idxk", 64), ("idxk_sw", 64), ("glak", 256), ("glav", 512), ("glaa", 16)]


def col_layout(cols):
    off, o = {}, 0
    for n, w in cols:
        off[n] = (o, w)
        o += w
    return off, o


class Builder:
    def __init__(self, cfg, name):
        self.cfg = cfg
        self.nc = bass.Bass("TRN2", target_bir_lowering=False, name=name)
        self.P = Prog(self.nc)
        self.din = {}
        self.dout = {}
        self.bank_i = 0

    def inp(self, name, shape, dtype=F32):
        t = self.nc.dram_tensor(name, list(shape), dtype, kind="ExternalInput")
        self.din[name] = t
        return t.ap()

    def outp(self, name, shape, dtype=F32):
        t = self.nc.dram_tensor(name, list(shape), dtype, kind="ExternalOutput")
        self.dout[name] = t
        return t.ap()

    def make_banks(self):
        self.banks = [self.P.ps(f"bank{i}", [128, 512], F32) for i in range(8)]

    def bank(self):
        b = self.banks[self.bank_i % 8]
        self.bank_i += 1
        return b

    def load_consts(self, names):
        P = self.P
        hc = host_consts()
        self.c = {}
        self.cb = {}
        for n in names:
            shp = list(hc[n].shape)
            ap = self.inp("c_" + n, shp)
            t = P.sb("sc_" + n, shp, F32)
            P.dma(t.ap(), ap, writes=[t])
            self.c[n] = t
        self.identb = P.sb("identb", [128, 128], BF16)
        P.op("vector", lambda e: e.tensor_copy(self.identb.ap(), self.c["ident"].ap()),
             reads=[self.c["ident"]], writes=[self.identb])
        self.onesb = P.sb("onesb", [128, 128], BF16)
        P.op("vector", lambda e: e.memset(self.onesb.ap(), 1.0), writes=[self.onesb])

    def load_weight(self, dram_ap, c0, ncols, name, kc=8, rows=128):
        P = self.P
        wt = P.sb(name, [rows, kc, ncols], BF16)
        CH = 64 if hasattr(self, "wstage") else 128
        if not hasattr(self, "wstage"):
            self.wstage = [P.sb(f"wstage{i}", [128, 8, CH], F32) for i in range(2)]
            self.wstage_i = 0
        for s in range(0, ncols, CH):
            n = min(CH, ncols - s)
            st = self.wstage[self.wstage_i % 2]
            self.wstage_i += 1
            P.dma(st.ap()[0:rows, 0:kc, 0:n], dram_ap[:, :, c0 + s:c0 + s + n], writes=[st])
            P.op("gpsimd", lambda e, st=st, s=s, n=n: e.tensor_copy(
                wt.ap()[:, :, s:s + n], st.ap()[0:rows, 0:kc, 0:n]), reads=[st], writes=[wt])
        return wt

    def precast(self, src_ap, dst_ap, dbuf, rows, kc, ncols):
        P = self.P
        CH = 256
        st = [P.sb(f"pc_st{i}", [128, 8, CH], F32) for i in range(2)]
        ot = [P.sb(f"pc_ot{i}", [128, 8, CH], BF16) for i in range(2)]
        i = 0
        for s0 in range(0, ncols, CH):
            n = min(CH, ncols - s0)
            a, o = st[i % 2], ot[i % 2]
            P.dma(a.ap()[0:rows, 0:kc, 0:n], src_ap[:, :, s0:s0 + n], writes=[a])
            if i % 2 == 0:
                P.op("scalar", lambda e: e.activation(o.ap()[0:rows, 0:kc, 0:n], a.ap()[0:rows, 0:kc, 0:n], AF.Copy),
                     reads=[a], writes=[o])
            else:
                P.op("gpsimd", lambda e: e.tensor_copy(o.ap()[0:rows, 0:kc, 0:n], a.ap()[0:rows, 0:kc, 0:n]),
                     reads=[a], writes=[o])
            P.dma(dst_ap[:, :, s0:s0 + n], o.ap()[0:rows, 0:kc, 0:n], reads=[o], writes=[dbuf])
            i += 1

    def load_weight_bf16(self, dram_ap, dbuf, c0, ncols, name, kc=8, rows=128):
        P = self.P
        wt = P.sb(name, [rows, kc, ncols], BF16)
        half = max(1, kc // 2)
        for k0 in range(0, kc, half):
            P.dma(wt.ap()[:, k0:k0 + half, :], dram_ap[:, k0:k0 + half, c0:c0 + ncols], reads=[dbuf], writes=[wt])
        return wt

    def prealloc(self, ncore):
        P = self.P
        self.rp = [P.sb(f"rp{i}", [128, 512], F32) for i in range(2)]
        self.a17 = P.sb("a17", [17, 128], BF16)
        P.op("vector", lambda e: e.memset(self.a17.ap(), 1.0), writes=[self.a17])
        self.e1 = P.sb("gl_e1", [128, 256], F32)
        self.sp = P.sb("gl_sp", [128, 256], F32)
        self.kvt = [P.sb(f"kvt{i}", [64, 512], F32) for i in range(2)]
        self.dcg_g = P.sb("dcg_g", [64, ncore, 4], F32)

    def emit_mod(self, cvec_ap, adaw_ap, adab_ap, pre_ap, post_ap, want_gate):
        P = self.P
        self.G = P.sb("G", [128, 1024], F32)
        self.Sh = P.sb("Sh", [128, 1024], F32)
        if want_gate:
            self.GP = P.sb("GP", [128, 1024], F32)
        P.push_scope()
        cv = P.sb("cv", [128, 8], F32)
        P.dma(cv.ap(), cvec_ap, writes=[cv])
        ca = P.sb("ca", [128, 8], F32)
        P.op("scalar", lambda e: e.activation(ca.ap(), cv.ap(), AF.Silu), reads=[cv], writes=[ca])
        cbc = P.sb("cbc", [128, 8, 128], F32)
        P.op("vector", lambda e: e.tensor_copy(cbc.ap(), ca.ap().unsqueeze(2).to_broadcast([128, 8, 128])),
             reads=[ca], writes=[cbc])
        mod = P.sb("mod", [128, 3072], F32)
        bias = P.sb("modb", [128, 3072], F32)
        P.dma(bias.ap(), adab_ap.to_broadcast([128, 3072]), writes=[bias])
        wst = [P.sb(f"adaw{i}", [128, 8, 512], F32) for i in range(2)]
        for ci in range(6):
            w = wst[ci % 2]
            P.dma(w.ap(), adaw_ap[:, :, ci * 512:(ci + 1) * 512], writes=[w])
            bk = self.bank()
            for kc in range(8):
                P.op("tensor", lambda e, bk=bk, w=w, kc=kc: e.matmul(
                    bk.ap(), cbc.ap()[:, kc, :], w.ap()[:, kc, :], start=(kc == 0), stop=(kc == 7)),
                    reads=[cbc, w], writes=[bk])
            P.op("vector", lambda e, bk=bk, ci=ci: e.tensor_tensor(
                mod.ap()[:, ci * 512:(ci + 1) * 512], bk.ap(), bias.ap()[:, ci * 512:(ci + 1) * 512], ALU.add),
                reads=[bk, bias], writes=[mod])
        pre = P.sb("preb", [128, 1024], F32)
        P.dma(pre.ap(), pre_ap.to_broadcast([128, 1024]), writes=[pre])
        P.op("vector", lambda e: e.scalar_tensor_tensor(
            self.G.ap(), mod.ap()[:, 1024:2048], 1.0, pre.ap(), ALU.add, ALU.mult),
            reads=[mod, pre], writes=[self.G])
        P.op("vector", lambda e: e.tensor_copy(self.Sh.ap(), mod.ap()[:, 0:1024]), reads=[mod], writes=[self.Sh])
        if want_gate:
            post = P.sb("postb", [128, 1024], F32)
            P.dma(post.ap(), post_ap.to_broadcast([128, 1024]), writes=[post])
            P.op("vector", lambda e: e.tensor_tensor(self.GP.ap(), mod.ap()[:, 2048:3072], post.ap(), ALU.mult),
                 reads=[mod, post], writes=[self.GP])
        P.pop_scope()

    def emit_ropes(self, pos_ap, specs):
        P = self.P
        outs = []
        for fcol, scol, name in specs:
            outs.append((P.sb(name + "_cos", [128, self.cfg.T], BF16), P.sb(name + "_sin", [128, self.cfg.T], BF16)))
        P.push_scope()
        for (fcol, scol, name), (cb_, sb_) in zip(specs, outs):
            self.emit_rope(pos_ap, fcol, scol, name, cb_, sb_)
        P.pop_scope()
        del self.posf
        return outs

    def emit_rope(self, pos_ap, fcol, scol, name, cosb, sinb):
        P = self.P
        T = self.cfg.T
        fs = self.c["ropefs"]
        if not hasattr(self, "posf"):
            posi = P.sb("posi", [128, T], I32)
            P.dma(posi.ap(), pos_ap.to_broadcast([128, T]), writes=[posi])
            self.posf = P.sb("posf", [128, T], F32)
            P.op("vector", lambda e: e.tensor_copy(self.posf.ap(), posi.ap()), reads=[posi], writes=[self.posf])
            self.rtmp = [P.sb(f"rtmp{i}", [128, T], F32) for i in range(4)]
        ang, kf, r, rd = self.rtmp
        posf = self.posf
        PI = float(np.pi)
        C1 = 6.28125
        C2 = float(2.0 * np.pi - 6.28125)
        MAG = 12582912.0
        P.op("vector", lambda e: e.tensor_scalar(ang.ap(), posf.ap(), fs.ap()[:, fcol:fcol + 1], None, ALU.mult),
             reads=[posf, fs], writes=[ang])

        def reduce_and_sin(src, dst_final, shift):
            dst = rd
            if shift != 0.0:
                P.op("vector", lambda e: e.tensor_scalar(r.ap(), src.ap(), shift, None, ALU.add),
                     reads=[src], writes=[r])
                s2 = r
            else:
                s2 = src
            P.op("vector", lambda e: e.tensor_scalar(kf.ap(), s2.ap(), float(1.0 / (2 * np.pi)), MAG, ALU.mult, ALU.add),
                 reads=[s2], writes=[kf])
            P.op("vector", lambda e: e.tensor_scalar(kf.ap(), kf.ap(), MAG, None, ALU.subtract),
                 reads=[kf], writes=[kf])
            P.op("vector", lambda e: e.scalar_tensor_tensor(dst.ap(), kf.ap(), -C1, s2.ap(), ALU.mult, ALU.add),
                 reads=[kf, s2], writes=[dst])
            P.op("vector", lambda e: e.scalar_tensor_tensor(dst.ap(), kf.ap(), -C2, dst.ap(), ALU.mult, ALU.add),
                 reads=[kf, dst], writes=[dst])
            P.op("vector", lambda e: e.tensor_scalar(dst.ap(), dst.ap(), PI, -PI, ALU.min, ALU.max),
                 reads=[dst], writes=[dst])
            P.op("scalar", lambda e: e.activation(dst_final.ap(), dst.ap(), AF.Sin), reads=[dst], writes=[dst_final])

        reduce_and_sin(ang, sinb, 0.0)
        reduce_and_sin(ang, cosb, float(np.pi / 2))
        P.op("vector", lambda e: e.tensor_scalar(sinb.ap(), sinb.ap(), fs.ap()[:, scol:scol + 1], None, ALU.mult),
             reads=[sinb, fs], writes=[sinb])
        return cosb, sinb

    def emit_norm_T(self, x_tile_ap_dram, hT, slot):
        P = self.P
        if not hasattr(self, "xin"):
            self.xin = [P.sb(f"xin{i}", [128, 1024], F32) for i in range(1)]
            self.xin_i = 0
            self.hjunk = P.sb("hjunk", [128, 1024], BF16)
            self.h1 = P.sb("h1", [128, 1024], F32)
            self.hb = P.sb("hb", [128, 1024], BF16)
            self.nst = P.sb("nst", [128, 4], F32)
        xt = self.xin[0]
        self.xin_i += 1
        nst = self.nst
        P.dma(xt.ap(), x_tile_ap_dram, writes=[xt])
        P.op("scalar", lambda e: e.activation(self.hjunk.ap(), xt.ap(), AF.Square, accum_out=nst.ap()[:, 0:1]),
             reads=[xt], writes=[self.hjunk, nst])
        P.op("vector", lambda e: e.tensor_scalar(nst.ap()[:, 1:2], nst.ap()[:, 0:1], 1.0 / D, RMS_EPS, ALU.mult, ALU.add),
             reads=[nst], writes=[nst])
        P.op("scalar", lambda e: e.activation(nst.ap()[:, 2:3], nst.ap()[:, 1:2], AF.Sqrt), reads=[nst], writes=[nst])
        P.op("vector", lambda e: e.reciprocal(nst.ap()[:, 3:4], nst.ap()[:, 2:3]), reads=[nst], writes=[nst])
        P.op("vector", lambda e: e.scalar_tensor_tensor(self.h1.ap(), xt.ap(), nst.ap()[:, 3:4], self.G.ap(), ALU.mult, ALU.mult),
             reads=[xt, nst, self.G], writes=[self.h1])
        P.op("gpsimd", lambda e: e.tensor_tensor(self.hb.ap(), self.h1.ap(), self.Sh.ap(), ALU.add),
             reads=[self.h1, self.Sh], writes=[self.hb])
        for half in range(2):
            bk = self.bank()
            for q in range(4):
                kc = half * 4 + q
                P.op("tensor", lambda e, bk=bk, q=q, kc=kc: e.matmul(
                    bk.ap()[:, q * 128:(q + 1) * 128], self.hb.ap()[:, kc * 128:(kc + 1) * 128], self.identb.ap(),
                    start=True, stop=True), reads=[self.hb, self.identb], writes=[bk])
            P.op("scalar", lambda e, bk=bk, half=half: e.activation(
                hT.ap()[:, slot, half * 4:half * 4 + 4, :], bk.ap().rearrange("p (a b) -> p a b", a=4), AF.Copy),
                reads=[bk], writes=[hT])

    def proj_fm(self, out_ap, bk, w, c0, m, hT, slot, nslots=1):
        P = self.P
        for kc in range(8):
            if nslots == 1:
                rhs = hT.ap()[:, slot, kc, :]
            else:
                rhs = hT.ap()[:, slot:slot + nslots, kc, :]
            P.op("tensor", lambda e, kc=kc, rhs=rhs: e.matmul(
                out_ap, w.ap()[:, kc, c0:c0 + m], rhs, start=(kc == 0), stop=(kc == 7)),
                reads=[w, hT], writes=[bk])

    def proj_tm(self, out_ap, bk, w, c0, n, hT, slot):
        P = self.P
        for kc in range(8):
            P.op("tensor", lambda e, kc=kc: e.matmul(
                out_ap, hT.ap()[:, slot, kc, :], w.ap()[:, kc, c0:c0 + n], start=(kc == 0), stop=(kc == 7)),
                reads=[w, hT], writes=[bk])

    def rope_fm(self, dst_ap, dst_buf, bx, bs, npart, ncols_ap, cosb, sinb, tok0, scale=None):
        P = self.P
        if not hasattr(self, "rp"):
            self.rp = [P.sb(f"rp{i}", [128, 512], F32) for i in range(2)]
        t0, t1 = self.rp
        nh = ncols_ap // 128
        cosv = cosb.ap()[0:npart, tok0:tok0 + 128].unsqueeze(1).to_broadcast([npart, nh, 128])
        sinv = sinb.ap()[0:npart, tok0:tok0 + 128].unsqueeze(1).to_broadcast([npart, nh, 128])
        v = lambda b: b.ap()[0:npart, 0:ncols_ap].rearrange("p (h t) -> p h t", h=nh)
        P.op("vector", lambda e: e.tensor_tensor(v(t0), v(bx), cosv, ALU.mult), reads=[bx, cosb], writes=[t0])
        P.op("vector", lambda e: e.tensor_tensor(v(t1), v(bs), sinv, ALU.mult), reads=[bs, sinb], writes=[t1])
        if scale is None:
            P.op("vector", lambda e: e.tensor_tensor(dst_ap, v(t0), v(t1), ALU.add), reads=[t0, t1], writes=[dst_buf])
        else:
            P.op("vector", lambda e: e.scalar_tensor_tensor(dst_ap, v(t0), 1.0, v(t1), ALU.mult, ALU.add),
                 reads=[t0, t1], writes=[dst_buf])


def gla_decay_common(B, hT, slot, wA, aoff, wlr17):
    P = B.P
    if not hasattr(B, "a17"):
        B.a17 = P.sb("a17", [17, 128], BF16)
        P.op("vector", lambda e: e.memset(B.a17.ap(), 1.0), writes=[B.a17])
        B.e1 = P.sb("gl_e1", [128, 256], F32)
        B.sp = P.sb("gl_sp", [128, 256], F32)
    bk = B.bank()
    B.proj_fm(bk.ap()[0:16, 0:128], bk, wA, aoff, 16, hT, slot)
    P.op("vector", lambda e: e.tensor_copy(B.a17.ap()[0:16, :], bk.ap()[0:16, 0:128]), reads=[bk], writes=[B.a17])
    bz = B.bank()
    P.op("tensor", lambda e: e.matmul(bz.ap()[:, 0:256], B.a17.ap(), wlr17.ap(), start=True, stop=True),
         reads=[B.a17, wlr17], writes=[bz])
    P.op("scalar", lambda e: e.activation(B.e1.ap(), bz.ap()[:, 0:256], AF.Exp, scale=-1.0), reads=[bz], writes=[B.e1])
    P.op("scalar", lambda e: e.activation(B.sp.ap(), B.e1.ap(), AF.Ln, bias=1.0), reads=[B.e1], writes=[B.sp])
    return B.sp


def load_wlr17(B, wlr_ap, blr_ap):
    P = B.P
    st = P.sb("wlr_st", [17, 256], F32)
    P.dma(st.ap()[0:16, :], wlr_ap, writes=[st])
    P.dma(st.ap()[16:17, :], blr_ap, writes=[st])
    w = P.sb("wlr17", [17, 256], BF16)
    P.op("vector", lambda e: e.tensor_copy(w.ap(), st.ap()), reads=[st], writes=[w])
    return w


def build_A(cfg):
    B = Builder(cfg, "phaseA")
    P = B.P
    TPC, T = cfg.TPC, cfg.T
    aoff, ncolA = col_layout(A_COLS)
    x = B.inp("x", [TPC, 128, 1024])
    pos = B.inp("pos", [1, T], I32)
    cvec = B.inp("cvec", [128, 8])
    adaw = B.inp("adaw", [128, 8, 3072])
    adab = B.inp("adab", [1, 3072])
    pre = B.inp("pre", [1, 1024])
    WA = B.inp("WA", [128, 8, ncolA])
    wlr = B.inp("wlr", [16, 256])
    blr = B.inp("blr", [1, 256])
    oKT = B.outp("KT", [128, T], BF16)
    oV = B.outp("V", [TPC, 128, 128], BF16)
    oIK = B.outp("IK", [64, T], BF16)
    okvR = B.outp("kvR", [TPC, 64, 512])
    okvG = B.outp("kvG", [TPC, 64, 512])
    odecG = B.outp("decG", [TPC, 64, 4])

    B.make_banks()
    B.load_consts(["ident", "triU64", "chunkind", "kfacR", "ropefs"])
    B.emit_mod(cvec, adaw, adab, pre, None, False)
    (cosR, sinR), (cosD, sinD) = B.emit_ropes(pos, [(0, 1, "rr"), (2, 3, "rd")])
    wA = B.load_weight(WA, 0, ncolA, "wA")
    wlr17 = load_wlr17(B, wlr, blr)
    hT = P.sb("hT", [128, 1, 8, 128], BF16)

    kTb = P.sb("kTb", [64, 4, 128], BF16)
    khat = P.sb("khat", [128, 4, 64], BF16)
    vtok = P.sb("vtok", [128, 512], BF16)
    kvs = [P.sb(f"kvs{i}", [64, 512], F32) for i in range(2)]
    ktb = P.sb("ktb", [128, 128], BF16)
    vb = P.sb("vb", [128, 128], BF16)
    ikb = P.sb("ikb", [64, 128], BF16)
    kfac = P.sb("kfac", [128, 256], F32)
    gkhat = P.sb("gkhat", [128, 256], BF16)
    gvtok = P.sb("gvtok", [128, 512], BF16)
    dec = P.sb("dec", [64, 4, 4], F32)
    kv1s = P.sb("kv1s", [64, 512], F32)
    decT = P.sb("decT", [64, 4], F32)
    ident = B.c["ident"]

    for s in range(TPC):
        t0 = s * 128
        B.emit_norm_T(x[s], hT, 0)
        bx, bs = B.bank(), B.bank()
        for h in range(4):
            B.proj_fm(bx.ap()[0:64, h * 128:(h + 1) * 128], bx, wA, aoff["retk"][0] + h * 64, 64, hT, 0)
            B.proj_fm(bs.ap()[0:64, h * 128:(h + 1) * 128], bs, wA, aoff["retk_sw"][0] + h * 64, 64, hT, 0)
        B.rope_fm(kTb.ap(), kTb, bx, bs, 64, 512, cosR, sinR, t0)
        bt = B.bank()
        for h in range(4):
            P.op("tensor", lambda e, h=h: e.matmul(bt.ap()[:, h * 64:(h + 1) * 64], kTb.ap()[:, h, :],
                                                   B.identb.ap()[0:64, 0:64], start=True, stop=True),
                 reads=[kTb, B.identb], writes=[bt])
        P.op("vector", lambda e: e.tensor_tensor(
            khat.ap(), bt.ap()[:, 0:256].rearrange("p (h d) -> p h d", h=4),
            B.c["kfacR"].ap().unsqueeze(2).to_broadcast([128, 4, 64]), ALU.mult),
            reads=[bt, B.c["kfacR"]], writes=[khat])
        bv = B.bank()
        B.proj_tm(bv.ap(), bv, wA, aoff["retv"][0], 512, hT, 0)
        P.op("scalar", lambda e: e.activation(vtok.ap(), bv.ap(), AF.Copy), reads=[bv], writes=[vtok])
        bkv = B.bank()
        for h in range(4):
            P.op("tensor", lambda e, h=h: e.matmul(bkv.ap()[0:64, h * 128:(h + 1) * 128], khat.ap()[:, h, :],
                                                   vtok.ap()[:, h * 128:(h + 1) * 128], start=True, stop=True),
                 reads=[khat, vtok], writes=[bkv])
        kv = kvs[0]
        P.op("scalar", lambda e: e.activation(kv.ap(), bkv.ap()[0:64, :], AF.Copy), reads=[bkv], writes=[kv])
        P.dma(okvR[s], kv.ap(), reads=[kv])
        bx, bs = B.bank(), B.bank()
        B.proj_fm(bx.ap()[:, 0:128], bx, wA, aoff["dsak"][0], 128, hT, 0)
        B.proj_fm(bs.ap()[:, 0:128], bs, wA, aoff["dsak_sw"][0], 128, hT, 0)
        B.rope_fm(ktb.ap().unsqueeze(1), ktb, bx, bs, 128, 128, cosD, sinD, t0)
        P.dma(oKT[:, t0:t0 + 128], ktb.ap(), reads=[ktb])
        bv = B.bank()
        B.proj_tm(bv.ap()[:, 0:128], bv, wA, aoff["dsav"][0], 128, hT, 0)
        P.op("scalar", lambda e: e.activation(vb.ap(), bv.ap()[:, 0:128], AF.Copy), reads=[bv], writes=[vb])
        P.dma(oV[s], vb.ap(), reads=[vb])
        bx, bs = B.bank(), B.bank()
        B.proj_fm(bx.ap()[0:64, 0:128], bx, wA, aoff["idxk"][0], 64, hT, 0)
        B.proj_fm(bs.ap()[0:64, 0:128], bs, wA, aoff["idxk_sw"][0], 64, hT, 0)
        B.rope_fm(ikb.ap().unsqueeze(1), ikb, bx, bs, 64, 128, cosD, sinD, t0)
        P.dma(oIK[:, t0:t0 + 128], ikb.ap(), reads=[ikb])
        sp = gla_decay_common(B, hT, 0, wA, aoff["glaa"][0], wlr17)
        bd = B.bank()
        P.op("tensor", lambda e: e.matmul(bd.ap()[:, 0:256], B.c["triU64"].ap(), sp.ap(), start=True, stop=True),
             reads=[B.c["triU64"], sp], writes=[bd])
        P.op("scalar", lambda e: e.activation(kfac.ap(), bd.ap()[:, 0:256], AF.Exp, scale=-1.0 / 16.0),
             reads=[bd], writes=[kfac])
        bk = B.bank()
        B.proj_tm(bk.ap()[:, 0:256], bk, wA, aoff["glak"][0], 256, hT, 0)
        P.op("vector", lambda e: e.tensor_tensor(gkhat.ap(), bk.ap()[:, 0:256], kfac.ap(), ALU.mult),
             reads=[bk, kfac], writes=[gkhat])
        bv = B.bank()
        B.proj_tm(bv.ap(), bv, wA, aoff["glav"][0], 512, hT, 0)
        P.op("scalar", lambda e: e.activation(gvtok.ap(), bv.ap(), AF.Copy), reads=[bv], writes=[gvtok])
        b0, b1 = B.bank(), B.bank()
        for ch, bb in ((0, b0), (1, b1)):
            for h in range(4):
                P.op("tensor", lambda e, h=h, ch=ch, bb=bb: e.matmul(
                    bb.ap()[0:64, h * 128:(h + 1) * 128], gkhat.ap()[ch * 64:(ch + 1) * 64, h * 64:(h + 1) * 64],
                    gvtok.ap()[ch * 64:(ch + 1) * 64, h * 128:(h + 1) * 128], start=True, stop=True),
                    reads=[gkhat, gvtok], writes=[bb])
        bs_ = B.bank()
        for h in range(4):
            P.op("tensor", lambda e, h=h: e.matmul(bs_.ap()[0:64, h * 4:(h + 1) * 4], sp.ap()[:, h * 64:(h + 1) * 64],
                                                   B.c["chunkind"].ap(), start=True, stop=True),
                 reads=[sp, B.c["chunkind"]], writes=[bs_])
        P.op("scalar", lambda e: e.activation(dec.ap(), bs_.ap()[0:64, 0:16].rearrange("p (h c) -> p h c", h=4),
                                              AF.Exp, scale=-1.0 / 16.0), reads=[bs_], writes=[dec])
        P.op("scalar", lambda e: e.activation(kv1s.ap(), b1.ap()[0:64, :], AF.Copy), reads=[b1], writes=[kv1s])
        kv = kvs[1]
        for h in range(4):
            P.op("vector", lambda e, h=h: e.scalar_tensor_tensor(
                kv.ap()[:, h * 128:(h + 1) * 128], b0.ap()[0:64, h * 128:(h + 1) * 128], dec.ap()[:, h, 1:2],
                kv1s.ap()[:, h * 128:(h + 1) * 128], ALU.mult, ALU.add),
                reads=[b0, dec, kv1s], writes=[kv])
        P.dma(okvG[s], kv.ap(), reads=[kv])
        P.op("vector", lambda e: e.tensor_copy(decT.ap(), dec.ap()[:, :, 2]), reads=[dec], writes=[decT])
        P.dma(odecG[s], decT.ap(), reads=[decT])
    P.finish()
    return B


def prep_common(inputs, cfg, l, core):
    raise NotImplementedError


def a_col_index():
    ar = np.arange
    idx = {
        "retk": OFF["ret_k"] + ar(256), "retk_sw": np.array(swap_cols(OFF["ret_k"], 64, 64, 4)),
        "retv": OFF["ret_v"] + ar(512),
        "dsak": OFF["dsa_k"] + ar(128), "dsak_sw": np.array(swap_cols(OFF["dsa_k"], 16, 64, 2)),
        "dsav": OFF["dsa_v"] + ar(128),
        "idxk": OFF["idx_k"] + ar(64), "idxk_sw": np.array(swap_cols(OFF["idx_k"], 16, 64, 1)),
        "glak": OFF["gla_k"] + ar(256), "glav": OFF["gla_v"] + ar(512), "glaa": OFF["gla_a"] + ar(16),
    }
    return np.concatenate([idx[n] for n, _ in A_COLS])


def kc_layout(w):
    n = w.shape[1]
    return np.ascontiguousarray(w.reshape(8, 128, n).transpose(1, 0, 2))


def core_tiles(cfg, c):
    return [k * cfg.NCORE + c for k in range(cfg.TPC)]


def const_inputs(names):
    hc = host_consts()
    return {"c_" + n: hc[n] for n in names}


def host_inputs_A(inp, cfg, l):
    x = np.asarray(inp["x"])[0].reshape(cfg.S // 128, 128, D)
    pos = np.asarray(inp["positions"])[0].reshape(cfg.S // 128, 128)
    shared = {
        "cvec": np.ascontiguousarray(np.asarray(inp["c"])[0].reshape(8, 128).T),
        "adaw": kc_layout(np.asarray(inp["ada_w"])[l]),
        "adab": np.asarray(inp["ada_b"])[l][None, :],
        "pre": np.asarray(inp["pre_norm"])[l][None, :],
        "WA": kc_layout(np.asarray(inp["w_in"])[l][:, a_col_index()]),
        "wlr": np.asarray(inp["gla_w_lr"])[l],
        "blr": np.asarray(inp["gla_b_lr"])[l][None, :],
    }
    shared.update(const_inputs(["ident", "triU64", "chunkind", "kfacR", "ropefs"]))
    maps = []
    for c in range(cfg.NCORE):
        tl = core_tiles(cfg, c)
        m = dict(shared)
        m["x"] = np.ascontiguousarray(x[tl])
        m["pos"] = np.ascontiguousarray(pos[tl].reshape(1, cfg.T)).astype(np.int32)
        maps.append(m)
    return maps


def np_inputs(S, seed=0, depth=4):
    r = np.random.RandomState(seed)
    f = np.float32
    n = lambda *s: r.randn(*s).astype(f)
    Dm = D
    return {
        "x": n(1, S, Dm), "c": n(1, Dm),
        "positions": (np.arange(S, dtype=np.int32)[None, :] + np.int32(r.randint(0, 4096))),
        "ada_w": n(depth, Dm, 3 * Dm) * f(0.1 * Dm ** -0.5), "ada_b": n(depth, 3 * Dm) * f(0.02),
        "pre_norm": 1 + f(0.05) * n(depth, Dm), "post_norm": 1 + f(0.05) * n(depth, Dm),
        "w_in": n(depth, Dm, IN_WIDTH) * f(Dm ** -0.5),
        "gla_w_lr": n(depth, 16, 256) * f(0.25), "gla_b_lr": f(0.1) * n(depth, 256),
        "w_br_ret": n(depth, 512, Dm) * f(512 ** -0.5), "w_br_dsa": n(depth, 512, Dm) * f(512 ** -0.5),
        "w_br_gla": n(depth, 512, Dm) * f(512 ** -0.5), "w_out": n(depth, Dm, Dm) * f(Dm ** -0.5),
    }


B_COLS = [("retq", 256), ("retq_sw", 256), ("retk", 256), ("retk_sw", 256), ("retv", 512), ("retg", 512),
          ("mg0", 1024),
          ("glaq", 256), ("glak", 256), ("glav", 512), ("glag", 512), ("glaa", 16), ("mg2", 1024),
          ("dsaq", 512), ("dsaq_sw", 512), ("idxq", 256), ("idxq_sw", 256), ("idxw", 4), ("dsag", 512),
          ("mg1", 1024)]


def b_col_index():
    ar = np.arange
    pair = np.concatenate([np.concatenate([ar(64) + j * 64, ar(64) + (j + 4) * 64]) for j in range(4)])
    dq = OFF["dsa_q"] + ar(512)
    dq_sw = np.array(swap_cols(OFF["dsa_q"], 16, 64, 8))
    idx = {
        "retq": OFF["ret_q"] + ar(256), "retq_sw": np.array(swap_cols(OFF["ret_q"], 64, 64, 4)),
        "retk": OFF["ret_k"] + ar(256), "retk_sw": np.array(swap_cols(OFF["ret_k"], 64, 64, 4)),
        "retv": OFF["ret_v"] + ar(512), "retg": OFF["ret_g"] + ar(512),
        "mg0": OFF["merge"] + ar(1024), "mg1": OFF["merge"] + 1024 + ar(1024), "mg2": OFF["merge"] + 2048 + ar(1024),
        "glaq": OFF["gla_q"] + ar(256), "glak": OFF["gla_k"] + ar(256), "glav": OFF["gla_v"] + ar(512),
        "glag": OFF["gla_g"] + ar(512), "glaa": OFF["gla_a"] + ar(16),
        "dsaq": dq[pair], "dsaq_sw": dq_sw[pair],
        "idxq": OFF["idx_q"] + ar(256), "idxq_sw": np.array(swap_cols(OFF["idx_q"], 16, 64, 4)),
        "idxw": OFF["idx_w"] + ar(4), "dsag": OFF["dsa_g"] + ar(512),
    }
    return np.concatenate([idx[n] for n, _ in B_COLS])


def build_B(cfg):
    B = Builder(cfg, "phaseB")
    P = B.P
    TPC, T, NCORE, SG = cfg.TPC, cfg.T, cfg.NCORE, cfg.SG
    GK, BLK, NB, BPG = cfg.GK, cfg.BLK, cfg.NB, cfg.BPG
    boff, ncolB = col_layout(B_COLS)
    x = B.inp("x", [TPC, 128, 1024])
    pos = B.inp("pos", [1, T], I32)
    cvec = B.inp("cvec", [128, 8])
    adaw = B.inp("adaw", [128, 8, 3072])
    adab = B.inp("adab", [1, 3072])
    pre = B.inp("pre", [1, 1024])
    post = B.inp("post", [1, 1024])
    WB = B.inp("WB", [128, 8, ncolB])
    wbr_r = B.inp("wbr_r", [128, 4, 1024])
    wbr_g = B.inp("wbr_g", [128, 4, 1024])
    wbr_d = B.inp("wbr_d", [64, 8, 1024])
    wout_d = B.inp("wout", [128, 8, 1024])
    wlr = B.inp("wlr", [16, 256])
    blr = B.inp("blr", [1, 256])
    KTa = B.inp("KTa", [NCORE, 128, T], BF16)
    Va = B.inp("Va", [NCORE, TPC, 128, 128], BF16)
    IKa = B.inp("IKa", [NCORE, 64, T], BF16)
    kvRa = B.inp("kvRa", [NCORE, TPC, 64, 512])
    kvGa = B.inp("kvGa", [NCORE, TPC, 64, 512])
    decGa = B.inp("decGa", [NCORE, TPC, 64, 4])
    sel_d = B.inp("sel", [128, NCORE])
    pen_d = B.inp("pen", [128, GK], BF16)
    xo = B.outp("xo", [TPC, 128, 1024])

    B.make_banks()
    B.load_consts(B_CONSTS)
    sel = P.sb("sel", [128, NCORE], F32)
    P.dma(sel.ap(), sel_d, writes=[sel])
    pen = P.sb("pen", [128, GK], BF16)
    P.dma(pen.ap(), pen_d, writes=[pen])
    ident4 = P.sb("ident4", [128, 4, 128], BF16)
    P.op("vector", lambda e: e.tensor_copy(ident4.ap(), B.c["ident"].ap().unsqueeze(1).to_broadcast([128, 4, 128])),
         reads=[B.c["ident"]], writes=[ident4])
    B.emit_mod(cvec, adaw, adab, pre, post, True)
    (cosR, sinR), (cosD, sinD) = B.emit_ropes(pos, [(0, 1, "rr"), (2, 3, "rd")])
    wlr17 = load_wlr17(B, wlr, blr)
    nc_ = B.nc
    WBb = nc_.dram_tensor("WBb", [128, 8, ncolB], BF16, kind="Internal").ap()
    wbr_rb = nc_.dram_tensor("wbr_rb", [128, 4, 1024], BF16, kind="Internal").ap()
    wbr_gb = nc_.dram_tensor("wbr_gb", [128, 4, 1024], BF16, kind="Internal").ap()
    wbr_db = nc_.dram_tensor("wbr_db", [64, 8, 1024], BF16, kind="Internal").ap()
    woutb = nc_.dram_tensor("woutb", [128, 8, 1024], BF16, kind="Internal").ap()
    wbuf = Buf("wscratch")
    P.push_scope()
    B.precast(WB, WBb, wbuf, 128, 8, ncolB)
    B.precast(wbr_r, wbr_rb, wbuf, 128, 4, 1024)
    B.precast(wbr_g, wbr_gb, wbuf, 128, 4, 1024)
    B.precast(wbr_d, wbr_db, wbuf, 64, 8, 1024)
    B.precast(wout_d, woutb, wbuf, 128, 8, 1024)
    P.pop_scope()
    wbr_r, wbr_g, wbr_d, wout_d = wbr_rb, wbr_gb, wbr_db, woutb
    B.prealloc(NCORE)
    SR = P.sb("SR", [64, 4, 128], F32)
    SGs = P.sb("SGs", [64, 4, 128], F32)
    P.op("vector", lambda e: e.memset(SR.ap(), 0.0), writes=[SR])
    P.op("vector", lambda e: e.memset(SGs.ap(), 0.0), writes=[SGs])
    hT = P.sb("hT", [128, SG, 8, 128], BF16)
    ysum = P.sb("ysum", [128, SG, 8, 128], BF16)
    dsaout = P.sb("dsaout", [64, SG, 8, 128], BF16)
    cap = P.sb("cap", [64, 4, 128], F32)
    capb = P.sb("capb", [64, 4, 128], BF16)

    def load_w(names, tag):
        c0 = boff[names[0]][0]
        n = sum(boff[k][1] for k in names)
        assert boff[names[-1]][0] + boff[names[-1]][1] == c0 + n
        return B.load_weight_bf16(WBb, wbuf, c0, n, tag), c0

    def load_w2(dram_ap, rows, kc, name):
        return B.load_weight_bf16(dram_ap, wbuf, 0, 1024, name, kc=kc, rows=rows)

    def scan(S, kv_all, dec_src, s, tag):
        if dec_src is not None:
            dcg = B.dcg_g
            P.dma(dcg.ap(), dec_src[:, s].rearrange("j d n -> d j n"), writes=[dcg])
        for j in range(NCORE):
            kvt = B.kvt[j % 2]
            P.dma(kvt.ap(), kv_all[j, s], writes=[kvt])
            if j == 0:
                P.op("vector", lambda e: e.tensor_scalar(cap.ap(), S.ap(), sel.ap()[0:64, 0:1], None, ALU.mult),
                     reads=[S, sel], writes=[cap])
            else:
                P.op("vector", lambda e: e.scalar_tensor_tensor(cap.ap(), S.ap(), sel.ap()[0:64, j:j + 1], cap.ap(),
                                                                ALU.mult, ALU.add), reads=[S, sel, cap], writes=[cap])
            if dec_src is None:
                dv = B.c["decR"].ap()[0:64, :].unsqueeze(2).to_broadcast([64, 4, 128])
                rd = [B.c["decR"]]
            else:
                dv = dcg.ap()[:, j, :].unsqueeze(2).to_broadcast([64, 4, 128])
                rd = [dcg]
            P.op("vector", lambda e: e.tensor_tensor(S.ap(), S.ap(), dv, ALU.mult), reads=[S] + rd, writes=[S])
            P.op("vector", lambda e: e.tensor_tensor(S.ap(), S.ap(), kvt.ap().rearrange("p (h e) -> p h e", h=4),
                                                     ALU.add), reads=[S, kvt], writes=[S])
        P.op("scalar", lambda e: e.activation(capb.ap(), cap.ap(), AF.Copy), reads=[cap], writes=[capb])

    def tail(bO, w, c0, gname, mgname, wbr, ls, first, tl):
        sq, r1, sg, tt, bro, mg, t2 = tl
        P.op("scalar", lambda e: e.activation(sq.ap(), bO.ap(), AF.Square), reads=[bO], writes=[sq])
        bQ = B.bank()
        P.op("tensor", lambda e: e.matmul(bQ.ap(), B.onesb.ap(), sq.ap(), start=True, stop=True),
             reads=[B.onesb, sq], writes=[bQ])
        P.op("vector", lambda e: e.tensor_scalar(r1.ap(), bQ.ap(), 1.0 / 128.0, RMS_EPS, ALU.mult, ALU.add),
             reads=[bQ], writes=[r1])
        P.op("scalar", lambda e: e.activation(r1.ap(), r1.ap(), AF.Sqrt), reads=[r1], writes=[r1])
        P.op("vector", lambda e: e.reciprocal(r1.ap(), r1.ap()), reads=[r1], writes=[r1])
        bG = B.bank()
        g0 = boff[gname][0] - c0
        for h in range(4):
            B.proj_fm(bG.ap()[:, h * 128:(h + 1) * 128], bG, w, g0 + h * 128, 128, hT, ls)
        P.op("scalar", lambda e: e.activation(sg.ap(), bG.ap(), AF.Silu), reads=[bG], writes=[sg])
        P.op("vector", lambda e: e.tensor_tensor(tt.ap(), bO.ap(), r1.ap(), ALU.mult), reads=[bO, r1], writes=[tt])
        P.op("gpsimd", lambda e: e.tensor_tensor(bro.ap(), tt.ap(), sg.ap(), ALU.mult), reads=[tt, sg], writes=[bro])
        merge(lambda nc, out_ap, bk: [P.op("tensor", lambda e, h=h: e.matmul(
            out_ap, wbr.ap()[:, h, nc * 128:(nc + 1) * 128], bro.ap()[:, h * 128:(h + 1) * 128],
            start=(h == 0), stop=(h == 3)), reads=[wbr, bro], writes=[bk]) for h in range(4)],
            w, boff[mgname][0] - c0, ls, first, mg, t2)

    def merge(emit_branch, w, m0, ls, first, mg, t2):
        bY = [B.bank(), B.bank()]
        for nc in range(8):
            bk = bY[nc // 4]
            emit_branch(nc, bk.ap()[:, (nc % 4) * 128:(nc % 4 + 1) * 128], bk)
        bM = [B.bank(), B.bank()]
        for nc in range(8):
            bk = bM[nc // 4]
            B.proj_fm(bk.ap()[:, (nc % 4) * 128:(nc % 4 + 1) * 128], bk, w, m0 + nc * 128, 128, hT, ls)
        for hf in range(2):
            P.op("scalar", lambda e: e.activation(mg.ap(), bM[hf].ap(), AF.Sigmoid), reads=[bM[hf]], writes=[mg])
            yv = ysum.ap()[:, ls, hf * 4:(hf + 1) * 4, :]
            m3 = mg.ap().rearrange("p (a b) -> p a b", a=4)
            b3 = bY[hf].ap().rearrange("p (a b) -> p a b", a=4)
            if first:
                P.op("vector", lambda e: e.tensor_tensor(yv, m3, b3, ALU.mult), reads=[mg, bY[hf]], writes=[ysum])
            else:
                P.op("vector", lambda e: e.tensor_tensor(t2.ap(), mg.ap(), bY[hf].ap(), ALU.mult),
                     reads=[mg, bY[hf]], writes=[t2])
                P.op("gpsimd", lambda e: e.tensor_tensor(yv, yv, t2.ap().rearrange("p (a b) -> p a b", a=4), ALU.add),
                     reads=[ysum, t2], writes=[ysum])

    def tail_bufs():
        return (P.sb("t_sq", [128, 512], BF16), P.sb("t_r1", [128, 512], F32), P.sb("t_sg", [128, 512], F32),
                P.sb("t_tt", [128, 512], F32), P.sb("t_bro", [128, 512], BF16), P.sb("t_mg", [128, 512], F32),
                P.sb("t_t2", [128, 512], F32))

    for g0 in range(0, TPC, SG):
        P.push_scope()
        for s in range(g0, g0 + SG):
            B.emit_norm_T(x[s], hT, s - g0)
        P.pop_scope()
        del B.xin
        P.push_scope()
        w, c0 = load_w(["retq", "retq_sw", "retk", "retk_sw", "retv", "retg", "mg0"], "w_ret")
        wbr = load_w2(wbr_r, 128, 4, "wbr_ret")
        tl = tail_bufs()
        qTb = P.sb("qTb", [64, 4, 128], BF16)
        kTb = P.sb("kTb", [64, 4, 128], BF16)
        qhat = P.sb("qhat", [64, 4, 128], BF16)
        vtok = P.sb("vtok", [128, 512], BF16)
        Sm = P.sb("Sm", [128, 512], BF16)
        for s in range(g0, g0 + SG):
            ls = s - g0
            t0 = s * 128
            scan(SR, kvRa, None, s, "r")
            for nm, dst in (("retq", qTb), ("retk", kTb)):
                bx, bs = B.bank(), B.bank()
                for h in range(4):
                    B.proj_fm(bx.ap()[0:64, h * 128:(h + 1) * 128], bx, w, boff[nm][0] - c0 + h * 64, 64, hT, ls)
                    B.proj_fm(bs.ap()[0:64, h * 128:(h + 1) * 128], bs, w, boff[nm + "_sw"][0] - c0 + h * 64, 64, hT, ls)
                B.rope_fm(dst.ap(), dst, bx, bs, 64, 512, cosR, sinR, t0)
            bv = B.bank()
            B.proj_tm(bv.ap(), bv, w, boff["retv"][0] - c0, 512, hT, ls)
            P.op("scalar", lambda e: e.activation(vtok.ap(), bv.ap(), AF.Copy), reads=[bv], writes=[vtok])
            bS = B.bank()
            for h in range(4):
                P.op("tensor", lambda e: e.matmul(bS.ap()[:, h * 128:(h + 1) * 128], kTb.ap()[:, h, :], qTb.ap()[:, h, :],
                                                  start=True, stop=True), reads=[kTb, qTb], writes=[bS])
            P.op("vector", lambda e: e.tensor_tensor(Sm.ap(), bS.ap(), B.c["DTret"].ap().rearrange("p h i -> p (h i)"),
                                                     ALU.mult), reads=[bS, B.c["DTret"]], writes=[Sm])
            P.op("gpsimd", lambda e: e.tensor_tensor(qhat.ap(), qTb.ap(), B.c["QDret"].ap()[0:64], ALU.mult),
                 reads=[qTb, B.c["QDret"]], writes=[qhat])
            bO = B.bank()
            for h in range(4):
                o_ap = bO.ap()[:, h * 128:(h + 1) * 128]
                P.op("tensor", lambda e: e.matmul(o_ap, vtok.ap()[:, h * 128:(h + 1) * 128], Sm.ap()[:, h * 128:(h + 1) * 128],
                                                  start=True, stop=False), reads=[vtok, Sm], writes=[bO])
                P.op("tensor", lambda e: e.matmul(o_ap, capb.ap()[:, h, :], qhat.ap()[:, h, :], start=False, stop=True),
                     reads=[capb, qhat], writes=[bO])
            tail(bO, w, c0, "retg", "mg0", wbr, ls, True, tl)
        P.pop_scope()
        P.push_scope()
        w, c0 = load_w(["glaq", "glak", "glav", "glag", "glaa", "mg2"], "w_gla")
        wbr = load_w2(wbr_g, 128, 4, "wbr_gla")
        tl = tail_bufs()
        eq = P.sb("eq", [64, 512], F32)
        ek = P.sb("ek", [64, 512], F32)
        e128 = P.sb("e128", [64, 512], F32)
        qt = P.sb("qt", [64, 4, 128], BF16)
        qh = P.sb("qh", [64, 4, 128], BF16)
        kt = P.sb("kt", [64, 4, 128], BF16)
        kfac = P.sb("kfac", [128, 256], F32)
        gkhat = P.sb("gkhat", [128, 256], BF16)
        vtok = P.sb("gvtok", [128, 512], BF16)
        kv0b = P.sb("kv0b", [64, 512], BF16)
        Sm = P.sb("gSm", [128, 512], BF16)
        for s in range(g0, g0 + SG):
            ls = s - g0
            scan(SGs, kvGa, decGa, s, "g")
            sp = gla_decay_common(B, hT, ls, w, boff["glaa"][0] - c0, wlr17)
            bC64, bC128 = B.bank(), B.bank()
            for h in range(4):
                for bb, tri in ((bC64, "triL64"), (bC128, "triL128")):
                    P.op("tensor", lambda e: e.matmul(bb.ap()[0:64, h * 128:(h + 1) * 128], sp.ap()[:, h * 64:(h + 1) * 64],
                                                      B.c[tri].ap(), start=True, stop=True), reads=[sp, B.c[tri]], writes=[bb])
            P.op("scalar", lambda e: e.activation(eq.ap(), bC64.ap()[0:64, :], AF.Exp, scale=-1.0 / 16), reads=[bC64], writes=[eq])
            P.op("scalar", lambda e: e.activation(ek.ap(), bC64.ap()[0:64, :], AF.Exp, scale=1.0 / 16), reads=[bC64], writes=[ek])
            P.op("scalar", lambda e: e.activation(e128.ap(), bC128.ap()[0:64, :], AF.Exp, scale=-1.0 / 16), reads=[bC128], writes=[e128])
            bq = B.bank()
            for h in range(4):
                B.proj_fm(bq.ap()[0:64, h * 128:(h + 1) * 128], bq, w, boff["glaq"][0] - c0 + h * 64, 64, hT, ls)
            f2 = lambda b: b.ap().rearrange("p h t -> p (h t)")
            P.op("vector", lambda e: e.scalar_tensor_tensor(f2(qt), bq.ap()[0:64, :], 0.125, eq.ap(), ALU.mult, ALU.mult),
                 reads=[bq, eq], writes=[qt])
            P.op("vector", lambda e: e.scalar_tensor_tensor(f2(qh), bq.ap()[0:64, :], 0.125, e128.ap(), ALU.mult, ALU.mult),
                 reads=[bq, e128], writes=[qh])
            bk = B.bank()
            for h in range(4):
                B.proj_fm(bk.ap()[0:64, h * 128:(h + 1) * 128], bk, w, boff["glak"][0] - c0 + h * 64, 64, hT, ls)
            P.op("vector", lambda e: e.tensor_tensor(f2(kt), bk.ap()[0:64, :], ek.ap(), ALU.mult), reads=[bk, ek], writes=[kt])
            bd = B.bank()
            P.op("tensor", lambda e: e.matmul(bd.ap()[:, 0:256], B.c["triU64"].ap(), sp.ap(), start=True, stop=True),
                 reads=[B.c["triU64"], sp], writes=[bd])
            P.op("scalar", lambda e: e.activation(kfac.ap(), bd.ap()[:, 0:256], AF.Exp, scale=-1.0 / 16.0), reads=[bd], writes=[kfac])
            bkt = B.bank()
            B.proj_tm(bkt.ap()[:, 0:256], bkt, w, boff["glak"][0] - c0, 256, hT, ls)
            P.op("vector", lambda e: e.tensor_tensor(gkhat.ap(), bkt.ap()[:, 0:256], kfac.ap(), ALU.mult),
                 reads=[bkt, kfac], writes=[gkhat])
            bv = B.bank()
            B.proj_tm(bv.ap(), bv, w, boff["glav"][0] - c0, 512, hT, ls)
            P.op("scalar", lambda e: e.activation(vtok.ap(), bv.ap(), AF.Copy), reads=[bv], writes=[vtok])
            b0 = B.bank()
            for h in range(4):
                P.op("tensor", lambda e: e.matmul(b0.ap()[0:64, h * 128:(h + 1) * 128], gkhat.ap()[0:64, h * 64:(h + 1) * 64],
                                                  vtok.ap()[0:64, h * 128:(h + 1) * 128], start=True, stop=True),
                     reads=[gkhat, vtok], writes=[b0])
            P.op("scalar", lambda e: e.activation(kv0b.ap(), b0.ap()[0:64, :], AF.Copy), reads=[b0], writes=[kv0b])
            bS = B.bank()
            for h in range(4):
                P.op("tensor", lambda e: e.matmul(bS.ap()[:, h * 128:(h + 1) * 128], kt.ap()[:, h, :], qt.ap()[:, h, :],
                                                  start=True, stop=True), reads=[kt, qt], writes=[bS])
            P.op("vector", lambda e: e.tensor_tensor(Sm.ap(), bS.ap(), B.c["DTgla"].ap().rearrange("p h i -> p (h i)"),
                                                     ALU.mult), reads=[bS, B.c["DTgla"]], writes=[Sm])
            bO = B.bank()
            for h in range(4):
                o_ap = bO.ap()[:, h * 128:(h + 1) * 128]
                P.op("tensor", lambda e: e.matmul(o_ap, vtok.ap()[:, h * 128:(h + 1) * 128], Sm.ap()[:, h * 128:(h + 1) * 128],
                                                  start=True, stop=False), reads=[vtok, Sm], writes=[bO])
                P.op("tensor", lambda e: e.matmul(o_ap, capb.ap()[:, h, :], qh.ap()[:, h, :], start=False, stop=False),
                     reads=[capb, qh], writes=[bO])
                P.op("tensor", lambda e: e.matmul(bO.ap()[:, h * 128 + 64:(h + 1) * 128], kv0b.ap()[:, h * 128:(h + 1) * 128],
                                                  qt.ap()[:, h, 64:128], start=False, stop=True), reads=[kv0b, qt], writes=[bO])
            tail(bO, w, c0, "glag", "mg2", wbr, ls, False, tl)
        P.pop_scope()
        dsa_stage(B, cfg, g0, locals())
        P.push_scope()
        wo = load_w2(wout_d, 128, 8, "w_out")
        xt = P.sb("f_xt", [128, 1024], F32)
        fj = P.sb("f_junk", [128, 512], BF16)
        fst = P.sb("f_st", [128, 8], F32)
        ft = P.sb("f_t", [128, 1024], F32)
        fo = P.sb("f_o", [128, 1024], F32)
        for s in range(g0, g0 + SG):
            ls = s - g0
            P.dma(xt.ap(), x[s], writes=[xt])
            bh = [B.bank(), B.bank()]
            for hf in range(2):
                for nc in range(8):
                    P.op("tensor", lambda e: e.matmul(bh[hf].ap(), ysum.ap()[:, ls, nc, :], wo.ap()[:, nc, hf * 512:(hf + 1) * 512],
                                                      start=(nc == 0), stop=(nc == 7)), reads=[ysum, wo], writes=[bh[hf]])
                P.op("scalar", lambda e: e.activation(fj.ap(), bh[hf].ap(), AF.Square, accum_out=fst.ap()[:, hf:hf + 1]),
                     reads=[bh[hf]], writes=[fj, fst])
            P.op("vector", lambda e: e.tensor_tensor(fst.ap()[:, 2:3], fst.ap()[:, 0:1], fst.ap()[:, 1:2], ALU.add), reads=[fst], writes=[fst])
            P.op("vector", lambda e: e.tensor_scalar(fst.ap()[:, 3:4], fst.ap()[:, 2:3], 1.0 / D, RMS_EPS, ALU.mult, ALU.add),
                 reads=[fst], writes=[fst])
            P.op("scalar", lambda e: e.activation(fst.ap()[:, 4:5], fst.ap()[:, 3:4], AF.Sqrt), reads=[fst], writes=[fst])
            P.op("vector", lambda e: e.reciprocal(fst.ap()[:, 5:6], fst.ap()[:, 4:5]), reads=[fst], writes=[fst])
            for hf in range(2):
                sl = slice(hf * 512, (hf + 1) * 512)
                P.op("vector", lambda e: e.scalar_tensor_tensor(ft.ap()[:, sl], bh[hf].ap(), fst.ap()[:, 5:6], B.GP.ap()[:, sl],
                                                                ALU.mult, ALU.mult), reads=[bh[hf], fst, B.GP], writes=[ft])
            P.op("gpsimd", lambda e: e.tensor_tensor(fo.ap(), ft.ap(), xt.ap(), ALU.add), reads=[ft, xt], writes=[fo])
            P.dma(xo[s], fo.ap(), reads=[fo])
        P.pop_scope()
    P.finish()
    return B


def dsa_stage(B, cfg, g0, env):
    P = B.P
    TPC, T, NCORE, SG = cfg.TPC, cfg.T, cfg.NCORE, cfg.SG
    GK, BLK, NB, BPG, NIT = cfg.GK, cfg.BLK, cfg.NB, cfg.BPG, cfg.NIT
    hT, ysum, dsaout, boff = env["hT"], env["ysum"], env["dsaout"], env["boff"]
    KTa, Va, IKa, pen, ident4 = env["KTa"], env["Va"], env["IKa"], env["pen"], env["ident4"]
    cosD, sinD = env["cosD"], env["sinD"]
    P.push_scope()
    w, c0 = env["load_w"](["dsaq", "dsaq_sw", "idxq", "idxq_sw", "idxw", "dsag"], "w_dsa")
    NMAX = TPC * GK
    scores = P.sb("scores", [128, NMAX], F32)
    CH = min(cfg.CH, NMAX)
    junk = P.sb("cjunk", [128, CH], BF16)
    QT = P.sb("QT", [128, 4, 128], BF16)
    IQ = P.sb("IQ", [64, 4, 128], BF16)
    wq = P.sb("wq", [128, 4], F32)
    rl = [P.sb(f"rl{i}", [128, BLK], F32) for i in range(2)]
    ikb = [P.sb(f"ikb{i}", [64, NB, 128], BF16) for i in range(2)]
    ktb = [P.sb(f"ktb{i}", [128, NB, 128], BF16) for i in range(2)]
    vbk = [P.sb(f"vbk{i}", [128, NB, 2, 128], BF16) for i in range(2)]
    for v in vbk:
        P.op("vector", lambda e: e.memset(v.ap(), 1.0), writes=[v])
    nbmax = TPC * BPG
    mx = P.sb("mx", [128, nbmax], F32)
    mn = P.sb("mn", [128, nbmax], F32)
    wall = P.sb("wall", [128, 32], F32)
    cntc = P.sb("cntc", [128, 64], F32)
    cnta = P.sb("cnta", [128, 64], F32)
    junk2 = P.sb("cjunk2", [128, CH], BF16)
    bst = P.sb("bst", [128, 16], F32)
    mlo = P.sb("mlo", [128, BLK], BF16)
    mhi = P.sb("mhi", [128, BLK], BF16)
    band = P.sb("band", [128, BLK], BF16)
    cum = [P.sb(f"cum{i}", [128, BLK], F32) for i in range(2)]
    tsel = P.sb("tsel", [128, BLK], BF16)
    mb = [P.sb(f"mb{i}", [128, BLK], BF16) for i in range(2)]
    PT = [P.sb(f"PT{i}", [128, 512], BF16) for i in range(2)]
    rc = P.sb("rc", [64, 512], F32)
    on = P.sb("on", [64, 512], BF16)
    sgd = P.sb("sgd", [64, 8, 128], BF16)
    acc = [B.banks[6], B.banks[7]]
    saved_i = B.bank_i
    rot = {"i": 0}

    def bank6():
        b = B.banks[rot["i"] % 6]
        rot["i"] += 1
        return b
    B_bank = B.bank
    B.bank = bank6
    col = lambda i: bst.ap()[:, i:i + 1]

    for s in range(g0, g0 + SG):
        ls = s - g0
        t0 = s * 128
        bx, bs = bank6(), bank6()
        for j in range(4):
            B.proj_fm(bx.ap()[:, j * 128:(j + 1) * 128], bx, w, boff["dsaq"][0] - c0 + j * 128, 128, hT, ls)
            B.proj_fm(bs.ap()[:, j * 128:(j + 1) * 128], bs, w, boff["dsaq_sw"][0] - c0 + j * 128, 128, hT, ls)
        B.rope_fm(QT.ap(), QT, bx, bs, 128, 512, cosD, sinD, t0)
        bx, bs = bank6(), bank6()
        for h in range(4):
            B.proj_fm(bx.ap()[0:64, h * 128:(h + 1) * 128], bx, w, boff["idxq"][0] - c0 + h * 64, 64, hT, ls)
            B.proj_fm(bs.ap()[0:64, h * 128:(h + 1) * 128], bs, w, boff["idxq_sw"][0] - c0 + h * 64, 64, hT, ls)
        B.rope_fm(IQ.ap(), IQ, bx, bs, 64, 512, cosD, sinD, t0)
        bw = bank6()
        B.proj_tm(bw.ap()[:, 0:4], bw, w, boff["idxw"][0] - c0, 4, hT, ls)
        P.op("vector", lambda e: e.tensor_scalar(wq.ap(), bw.ap()[:, 0:4], 0.0625, None, ALU.mult), reads=[bw], writes=[wq])
        nb = (s + 1) * BPG
        n = (s + 1) * GK
        for bi in range(nb):
            gq, blk = bi // BPG, bi % BPG
            ik = ikb[bi % 2]
            P.dma(ik.ap(), IKa[blk * NB:(blk + 1) * NB, :, gq * 128:(gq + 1) * 128].rearrange("j d t -> d j t"), writes=[ik])
            scb = scores.ap()[:, bi * BLK:(bi + 1) * BLK]
            for h in range(4):
                bI = bank6()
                P.op("tensor", lambda e: e.matmul(bI.ap()[:, 0:BLK], IQ.ap()[:, h, :], ik.ap().rearrange("d j t -> d (j t)"),
                                                  start=True, stop=True), reads=[IQ, ik], writes=[bI])
                r = rl[h % 2]
                P.op("scalar", lambda e: e.activation(r.ap(), bI.ap()[:, 0:BLK], AF.Relu), reads=[bI], writes=[r])
                if h == 0:
                    P.op("vector", lambda e: e.tensor_scalar(scb, r.ap(), wq.ap()[:, 0:1], None, ALU.mult),
                         reads=[r, wq], writes=[scores])
                else:
                    P.op("vector", lambda e: e.scalar_tensor_tensor(scb, r.ap(), wq.ap()[:, h:h + 1], scb, ALU.mult, ALU.add),
                         reads=[r, wq, scores], writes=[scores])

        P.op("vector", lambda e: e.tensor_reduce(col(8), scores.ap()[:, 0:n], AX.X, ALU.max), reads=[scores], writes=[bst])
        P.op("vector", lambda e: e.tensor_scalar(col(1), col(8), 1.0, None, ALU.add), reads=[bst], writes=[bst])
        P.op("vector", lambda e: e.tensor_reduce(col(9), scores.ap()[:, 0:n], AX.X, ALU.min), reads=[scores, bst], writes=[bst])
        P.op("vector", lambda e: e.tensor_scalar(col(0), col(9), -1.0, None, ALU.add), reads=[bst], writes=[bst])
        for blk in range(BPG):
            bi = s * BPG + blk
            scb = scores.ap()[:, bi * BLK:(bi + 1) * BLK]
            P.op("vector", lambda e: e.tensor_tensor(scb, scb, pen.ap()[:, blk * BLK:(blk + 1) * BLK], ALU.add),
                 reads=[scores, pen], writes=[scores])
        P.op("vector", lambda e: e.tensor_tensor(col(10), col(1), col(0), ALU.subtract), reads=[bst], writes=[bst])
        P.op("vector", lambda e: e.tensor_scalar(wall.ap(), B.c["pw2"].ap(), col(10), None, ALU.mult),
             reads=[B.c["pw2"], bst], writes=[wall])
        P.op("vector", lambda e: e.tensor_tensor(col(2), col(0), wall.ap()[:, 0:1], ALU.add), reads=[bst, wall], writes=[bst])
        chunks = [(cs, min(n, cs + CH)) for cs in range(0, n, CH)]

        def count(thr_col, dst_col, use_act):
            nd = na = 0
            l_act = 0
            for ci, (cs, ce) in enumerate(chunks):
                if use_act and ci % 2 == 1:
                    P.op("scalar", lambda e: e.activation(junk2.ap()[:, 0:ce - cs], scores.ap()[:, cs:ce], AF.Sign, bias=col(thr_col),
                                                          scale=-1.0, accum_out=cnta.ap()[:, na:na + 1]),
                         reads=[scores, bst], writes=[cnta, junk2])
                    na += 1
                    l_act += ce - cs
                else:
                    P.op("vector", lambda e: e.tensor_scalar(junk.ap()[:, 0:ce - cs], scores.ap()[:, cs:ce], col(thr_col), None,
                                                             ALU.is_ge, ALU.add, accum_out=cntc.ap()[:, nd:nd + 1]),
                         reads=[scores, bst], writes=[cntc, junk])
                    nd += 1
            P.op("vector", lambda e: e.tensor_reduce(col(dst_col), cntc.ap()[:, 0:nd], AX.X, ALU.add),
                 reads=[cntc], writes=[bst])
            if na == 0:
                return dst_col, cfg.TOPK - 0.5
            P.op("vector", lambda e: e.tensor_reduce(col(5), cnta.ap()[:, 0:na], AX.X, ALU.add), reads=[cnta, bst], writes=[bst])
            P.op("vector", lambda e: e.scalar_tensor_tensor(col(11), col(dst_col), 2.0, col(5), ALU.mult, ALU.subtract),
                 reads=[bst], writes=[bst])
            return 11, 2.0 * cfg.TOPK - 1.0 - l_act

        for it in range(NIT):
            ccol, cthr = count(2, 3, True)
            P.op("vector", lambda e: e.scalar_tensor_tensor(col(4), col(ccol), cthr, wall.ap()[:, it:it + 1],
                                                            ALU.is_ge, ALU.mult), reads=[bst, wall], writes=[bst])
            P.op("vector", lambda e: e.tensor_tensor(col(0), col(0), col(4), ALU.add), reads=[bst], writes=[bst])
            if it + 1 < NIT:
                P.op("vector", lambda e: e.tensor_tensor(col(2), col(0), wall.ap()[:, it + 1:it + 2], ALU.add),
                     reads=[bst, wall], writes=[bst])
        P.op("vector", lambda e: e.tensor_tensor(col(1), col(0), wall.ap()[:, NIT - 1:NIT], ALU.add), reads=[bst, wall], writes=[bst])
        count(1, 6, False)
        P.op("vector", lambda e: e.tensor_scalar(col(7), col(6), -BIGM, cfg.TOPK * BIGM, ALU.mult, ALU.add), reads=[bst], writes=[bst])
        blkst = {}

        def prep_block(bi):
            gq, blk = bi // BPG, bi % BPG
            kt_, vb_ = ktb[bi % 2], vbk[bi % 2]
            P.dma(kt_.ap(), KTa[blk * NB:(blk + 1) * NB, :, gq * 128:(gq + 1) * 128].rearrange("j d t -> d j t"), writes=[kt_])
            for jj in range(NB):
                P.dma(vb_.ap()[:, jj, :, 0:64], Va[blk * NB + jj, gq].rearrange("s (k d) -> s k d", k=2), writes=[vb_])
            scb = scores.ap()[:, bi * BLK:(bi + 1) * BLK]
            m = mb[bi % 2]
            cm, cprev = cum[bi % 2], cum[(bi + 1) % 2]
            P.op("vector", lambda e: e.tensor_scalar(mlo.ap(), scb, col(0), BIGM, ALU.is_ge, ALU.mult), reads=[scores, bst], writes=[mlo])
            P.op("vector", lambda e: e.tensor_scalar(mhi.ap(), scb, col(1), BIGM, ALU.is_ge, ALU.mult), reads=[scores, bst], writes=[mhi])
            P.op("vector", lambda e: e.tensor_tensor(band.ap(), mlo.ap(), mhi.ap(), ALU.subtract), reads=[mlo, mhi], writes=[band])
            P.op("vector", lambda e: e.tensor_tensor_scan(cm.ap(), band.ap(), band.ap(),
                                                          (0.0 if bi == 0 else cprev.ap()[:, BLK - 1:BLK]), ALU.add, ALU.max),
                 reads=[band] + ([] if bi == 0 else [cprev]), writes=[cm])
            P.op("vector", lambda e: e.scalar_tensor_tensor(tsel.ap(), cm.ap(), col(7), band.ap(), ALU.is_le, ALU.mult),
                 reads=[cm, bst, band], writes=[tsel])
            P.op("vector", lambda e: e.scalar_tensor_tensor(m.ap(), tsel.ap(), -BIGM, mhi.ap(), ALU.add, ALU.add),
                 reads=[tsel, mhi], writes=[m])
            blkst[bi] = (kt_, vb_, m)

        steps = [(bi, jj, kvn) for bi in range(nb) for jj in range(NB) for kvn in range(2)]

        def emit_logits(si):
            bi, jj, kvn = steps[si]
            kt_, vb_, m = blkst[bi]
            bL = bank6()
            P.op("tensor", lambda e: e.matmul(bL.ap(), kt_.ap()[kvn * 64:(kvn + 1) * 64, jj, :],
                                              QT.ap()[kvn * 64:(kvn + 1) * 64].rearrange("p j t -> p (j t)"),
                                              start=True, stop=False), reads=[kt_, QT], writes=[bL])
            P.op("tensor", lambda e: e.matmul(bL.ap(), m.ap()[:, jj * 128:(jj + 1) * 128],
                                              ident4.ap().rearrange("p j t -> p (j t)"), start=False, stop=True),
                 reads=[m, ident4], writes=[bL])
            pt = PT[si % 2]
            P.op("scalar", lambda e: e.activation(pt.ap(), bL.ap(), AF.Exp, scale=0.125), reads=[bL], writes=[pt])
            return pt

        def emit_pv(si, pt):
            bi, jj, kvn = steps[si]
            kt_, vb_, m = blkst[bi]
            first = (bi == 0 and jj == 0)
            last = (bi == nb - 1 and jj == NB - 1)
            P.op("tensor", lambda e: e.matmul(acc[kvn].ap(), vb_.ap()[:, jj, kvn, :], pt.ap(), start=first, stop=last),
                 reads=[vb_, pt], writes=[acc[kvn]])

        pend = None
        for si in range(len(steps)):
            bi, jj, kvn = steps[si]
            if jj == 0 and kvn == 0:
                prep_block(bi)
            pt = emit_logits(si)
            if pend is not None:
                emit_pv(*pend)
            pend = (si, pt)
        emit_pv(*pend)
        bg = [bank6(), bank6()]
        for hd in range(8):
            B.proj_fm(bg[hd // 4].ap()[0:64, (hd % 4) * 128:(hd % 4 + 1) * 128], bg[hd // 4], w,
                      boff["dsag"][0] - c0 + hd * 64, 64, hT, ls)
        for hf in range(2):
            P.op("scalar", lambda e: e.activation(sgd.ap()[:, hf * 4:(hf + 1) * 4, :],
                                                  bg[hf].ap()[0:64, :].rearrange("p (a b) -> p a b", a=4), AF.Silu),
                 reads=[bg[hf]], writes=[sgd])
        for kvn in range(2):
            P.op("vector", lambda e: e.reciprocal(rc.ap(), acc[kvn].ap()[64:128, :]), reads=[acc[kvn]], writes=[rc])
            P.op("vector", lambda e: e.tensor_tensor(on.ap(), acc[kvn].ap()[0:64, :], rc.ap(), ALU.mult),
                 reads=[acc[kvn], rc], writes=[on])
            P.op("gpsimd", lambda e: e.tensor_tensor(dsaout.ap()[:, ls, kvn * 4:(kvn + 1) * 4, :],
                                                     on.ap().rearrange("p (a b) -> p a b", a=4),
                                                     sgd.ap()[:, kvn * 4:(kvn + 1) * 4, :], ALU.mult),
                 reads=[on, sgd], writes=[dsaout])
    B.bank = B_bank
    P.pop_scope()
    P.push_scope()
    w, c0 = env["load_w"](["mg1"], "w_mg1")
    wbr = env["load_w2"](env["wbr_d"], 64, 8, "wbr_dsa")
    mg = P.sb("d_mg", [128, 512], F32)
    t2 = P.sb("d_t2", [128, 512], F32)
    for s in range(g0, g0 + SG):
        ls = s - g0
        env["merge"](lambda nc, out_ap, bk: [P.op("tensor", lambda e, hd=hd: e.matmul(
            out_ap, wbr.ap()[:, hd, nc * 128:(nc + 1) * 128], dsaout.ap()[:, ls, hd, :],
            start=(hd == 0), stop=(hd == 7)), reads=[wbr, dsaout], writes=[bk]) for hd in range(8)],
            w, 0, ls, False, mg, t2)
    P.pop_scope()


B_CONSTS = ["ident", "triL128", "triL64", "triU64", "DTret", "DTgla", "QDret", "decR", "ropefs", "pw2"]


def host_inputs_B(inp, cfg, l, xcur, resA):
    pos = np.asarray(inp["positions"])[0].reshape(cfg.S // 128, 128)
    st = lambda k: np.stack([np.asarray(r[k]) for r in resA])
    er = lambda w, h: np.ascontiguousarray(np.asarray(w)[l].reshape(h, 512 // h, D).transpose(1, 0, 2))
    shared = {
        "cvec": np.ascontiguousarray(np.asarray(inp["c"])[0].reshape(8, 128).T),
        "adaw": kc_layout(np.asarray(inp["ada_w"])[l]),
        "adab": np.asarray(inp["ada_b"])[l][None, :],
        "pre": np.asarray(inp["pre_norm"])[l][None, :],
        "post": np.asarray(inp["post_norm"])[l][None, :],
        "WB": kc_layout(np.asarray(inp["w_in"])[l][:, b_col_index()]),
        "wbr_r": er(inp["w_br_ret"], 4), "wbr_g": er(inp["w_br_gla"], 4), "wbr_d": er(inp["w_br_dsa"], 8),
        "wout": kc_layout(np.asarray(inp["w_out"])[l]),
        "wlr": np.asarray(inp["gla_w_lr"])[l],
        "blr": np.asarray(inp["gla_b_lr"])[l][None, :],
        "KTa": st("KT"), "Va": st("V"), "IKa": st("IK"),
        "kvRa": st("kvR"), "kvGa": st("kvG"), "decGa": st("decG"),
    }
    shared.update(const_inputs(B_CONSTS))
    maps = []
    for c in range(cfg.NCORE):
        tl = core_tiles(cfg, c)
        m = dict(shared)
        m["x"] = np.ascontiguousarray(xcur[tl])
        m["pos"] = np.ascontiguousarray(pos[tl].reshape(1, cfg.T)).astype(np.int32)
        sel = np.zeros((128, cfg.NCORE), np.float32)
        sel[:, c] = 1.0
        m["sel"] = sel
        kidx = np.arange(cfg.GK)[None, :]
        qidx = (c * 128 + np.arange(128))[:, None]
        import ml_dtypes
        m["pen"] = np.where(kidx > qidx, np.float32(-1e30), np.float32(0.0)).astype(ml_dtypes.bfloat16)
        maps.append(m)
    return maps


_CACHE = {}


def run_layers(inp, cfg):
    x = np.asarray(inp["x"])[0].reshape(cfg.S // 128, 128, D).astype(np.float32)
    cores = list(range(cfg.NCORE))
    for l in range(cfg.DEPTH):
        if "A" not in _CACHE:
            _CACHE["A"] = build_A(cfg)
        inp_l = dict(inp)
        inp_l["x"] = x.reshape(1, cfg.S, D)
        resA = run_bass_kernel_spmd(_CACHE["A"].nc, host_inputs_A(inp_l, cfg, l), core_ids=cores).results
        if "B" not in _CACHE:
            _CACHE["B"] = build_B(cfg)
        resB = run_bass_kernel_spmd(_CACHE["B"].nc, host_inputs_B(inp, cfg, l, x, resA), core_ids=cores).results
        xn = np.empty_like(x)
        for c in cores:
            xn[core_tiles(cfg, c)] = np.asarray(resB[c]["xo"])
        x = xn
    return x.reshape(1, cfg.S, D)


def kernel(**inputs):
    cfg = Cfg()
    return run_layers(inputs, cfg).astype(np.float32)
```

```python
import numpy as np
from contextlib import ExitStack
import concourse.bass as bass
import concourse.mybir as mybir
from concourse.bass_utils import run_bass_kernel_spmd

F32 = mybir.dt.float32
BF16 = mybir.dt.bfloat16
I32 = mybir.dt.int32
ALU = mybir.AluOpType
AF = mybir.ActivationFunctionType
AX = mybir.AxisListType

EPOCH = 20000
NDMA_SEM = 12


class Buf:
    def __init__(self, name, handle=None):
        self.name = name
        self.h = handle
        self.last_w = None
        self.readers = []

    def ap(self):
        return self.h[:]


class _Recorder:
    def __init__(self):
        self.call = None

    def __getattr__(self, name):
        def f(*a, **k):
            assert self.call is None
            self.call = (name, a, k)
            return None
        return f


class Prog:
    ENGS = ("tensor", "vector", "scalar", "gpsimd", "sync")

    def __init__(self, nc):
        self.nc = nc
        self.stack = ExitStack()
        self.stream = {e: [] for e in self.ENGS}
        self.count = {e: 0 for e in self.ENGS}
        self.known = {e: {} for e in self.ENGS}
        self.dma_n = {}
        self.dma_rr = {e: 0 for e in self.ENGS}
        self.nbuf = 0
        self.pending = {e: [] for e in self.ENGS}
        self.stacks = [self.stack]

    def sb(self, name, shape, dtype):
        self.nbuf += 1
        name = f"{name}_u{self.nbuf}"
        h = self.stacks[-1].enter_context(self.nc.sbuf_tensor(name, list(shape), dtype))
        return Buf(name, h)

    def push_scope(self):
        st = ExitStack()
        self.stacks.append(st)

    def pop_scope(self):
        self.barrier()
        self.stacks.pop().close()

    def barrier(self):
        toks = []
        for e in self.ENGS:
            c = self.count[e]
            if c > 0 and e != "sync":
                toks.append((("E", e, (c - 1) // EPOCH), (c - 1) % EPOCH + 1))
        for key, n in self.dma_n.items():
            toks.append((key, 16 * n))
        for e in self.ENGS:
            self.pending[e] = list(toks)

    def _take_pending(self, eng, waits):
        kn = self.known[eng]
        for key, val in self.pending[eng]:
            if key[0] == "E" and key[1] == eng:
                continue
            if kn.get(key, -1) >= val:
                continue
            if any(k == key and v >= val for k, v in waits):
                continue
            kn[key] = val
            waits.append((key, val))
        self.pending[eng] = []

    def ps(self, name, shape, dtype=F32):
        h = self.stack.enter_context(self.nc.psum_tensor(name, list(shape), dtype))
        return Buf(name, h)

    def alias(self, name, buf):
        raise NotImplementedError

    def _deps(self, eng, reads, writes, is_dma):
        deps = {}

        def add(tok, same_ok):
            if tok is None:
                return
            key, val = tok
            if (not is_dma) and key[0] == "E" and key[1] == eng and not same_ok:
                return
            if deps.get(key, -1) < val:
                deps[key] = val

        for b in reads:
            add(b.last_w, True)
        for b in writes:
            add(b.last_w, False)
            for t in b.readers:
                add(t, False)
        kn = self.known[eng]
        out = []
        for key, val in deps.items():
            if key[0] == "E":
                later = [k for k in kn if k[0] == "E" and k[1] == key[1] and k[2] > key[2]]
                if later:
                    continue
            if kn.get(key, -1) >= val:
                continue
            kn[key] = val
            out.append((key, val))
        return out

    def _commit(self, tok, reads, writes):
        for b in writes:
            b.last_w = tok
            b.readers = []
        for b in reads:
            if b in writes:
                continue
            rs = [t for t in b.readers if t[0] != tok[0]]
            rs.append(tok)
            b.readers = rs

    def op(self, eng, fn, reads=(), writes=()):
        reads = list(reads)
        writes = list(writes)
        waits = self._deps(eng, reads, writes, False)
        self._take_pending(eng, waits)
        self.count[eng] += 1
        c = self.count[eng]
        key = ("E", eng, (c - 1) // EPOCH)
        tok = (key, (c - 1) % EPOCH + 1)
        rec = _Recorder()
        fn(rec)
        self.stream[eng].append((waits, rec.call, key))
        self._commit(tok, reads, writes)
        return tok

    def dma(self, out_ap, in_ap, reads=(), writes=(), queue="sync", **kw):
        reads = list(reads)
        writes = list(writes)
        i = self.dma_rr[queue]
        self.dma_rr[queue] = (i + 1) % NDMA_SEM
        key = ("D", queue, i)
        n = self.dma_n.get(key, 0)
        waits = self._deps(queue, reads, writes, True)
        self._take_pending(queue, waits)
        if n > 0 and self.known[queue].get(key, -1) < 16 * n:
            self.known[queue][key] = 16 * n
            waits.append((key, 16 * n))
        self.dma_n[key] = n + 1
        tok = (key, 16 * (n + 1))
        kk = dict(kw)
        kk["out"] = out_ap
        kk["in_"] = in_ap
        self.stream[queue].append((waits, ("dma_start", (), kk), key))
        self._commit(tok, reads, writes)
        return tok

    def coll(self, kind, ins, outs, ranks, reads=(), writes=()):
        reads, writes = list(reads), list(writes)
        queue = "gpsimd"
        key = ("D", "coll", 0)
        n = self.dma_n.get(key, 0)
        waits = self._deps(queue, reads, writes, True)
        self._take_pending(queue, waits)
        if n > 0 and self.known[queue].get(key, -1) < 16 * n:
            self.known[queue][key] = 16 * n
            waits.append((key, 16 * n))
        self.dma_n[key] = n + 1
        tok = (key, 16 * (n + 1))
        call = ("collective_compute", (kind, ALU.bypass), dict(replica_groups=[list(ranks)], ins=list(ins), outs=list(outs)))
        self.stream[queue].append((waits, call, key))
        self._commit(tok, reads, writes)
        return tok

    def finish(self):
        nc = self.nc
        sems = {}

        def sem(key):
            if key not in sems:
                nm = "s_" + "_".join(str(k) for k in key)
                sems[key] = self.stack.enter_context(nc.semaphore(nm))
            return sems[key]

        final_waits = []
        for key, n in self.dma_n.items():
            final_waits.append((key, 16 * n))
        for eng in self.ENGS:
            for waits, fn, key in self.stream[eng]:
                sem(key)
                for k, v in waits:
                    sem(k)

        streams = self.stream

        def emit(eng_name):
            def body(e):
                for waits, fn, key in streams[eng_name]:
                    for k, v in waits:
                        e.wait_ge(sems[k], v)
                    ins = getattr(e, fn[0])(*fn[1], **fn[2])
                    ins.then_inc(sems[key], 16 if key[0] == "D" else 1)
                if eng_name == "sync":
                    for k, v in final_waits:
                        e.wait_ge(sems[k], v)
            return body

        with nc.Block() as block:
            block.sync(emit("sync"))
            block.tensor(emit("tensor"))
            block.vector(emit("vector"))
            block.scalar(emit("scalar"))
            block.gpsimd(emit("gpsimd"))
        while self.stacks:
            self.stacks.pop().close()


D = 1024
IN_SPLITS = (("ret_q", 256), ("ret_k", 256), ("ret_v", 512), ("ret_g", 512),
             ("dsa_q", 512), ("dsa_k", 128), ("dsa_v", 128), ("dsa_g", 512),
             ("idx_q", 256), ("idx_k", 64), ("idx_w", 4),
             ("gla_q", 256), ("gla_k", 256), ("gla_v", 512), ("gla_g", 512), ("gla_a", 16),
             ("merge", 3072))
OFF = {}
_o = 0
for _n, _w in IN_SPLITS:
    OFF[_n] = _o
    _o += _w
IN_WIDTH = _o
RMS_EPS = 1e-6
TOPK = 256
BIGM = 32768.0
LOG_G = [float(np.log1p(-2.0 ** (-5.0 - h))) for h in range(4)]


class Cfg:
    def __init__(self, S=16384, NCORE=8, DEPTH=4, SG=2, NIT=16, CH=512):
        self.S, self.NCORE, self.DEPTH = S, NCORE, DEPTH
        self.TPC = S // 128 // NCORE
        self.T = self.TPC * 128
        self.GK = NCORE * 128
        self.BLK = min(512, self.GK)
        self.NB = self.BLK // 128
        self.BPG = self.GK // self.BLK
        self.SG = min(SG, self.TPC)
        self.NIT = NIT
        self.CH = CH
        self.TOPK = min(256, S // 4)


def host_consts():
    f = np.float32
    j = np.arange(128)[:, None]
    i = np.arange(128)[None, :]
    c = {}
    c["ident"] = np.eye(128, dtype=f)
    same = (j // 64) == (i // 64)
    c["triL128"] = (j <= i).astype(f)
    c["triL64"] = ((j <= i) & same).astype(f)
    c["triU64"] = ((j > i) & same).astype(f)
    ci = np.zeros((128, 4), f)
    ci[:64, 0] = 1
    ci[64:, 1] = 1
    ci[:, 2] = 1
    c["chunkind"] = ci
    lg = np.array(LOG_G, np.float64)
    dt = np.zeros((128, 4, 128), np.float64)
    for h in range(4):
        dt[:, h, :] = np.where(i >= j, np.exp(lg[h] * np.maximum(i - j, 0)), 0.0)
    c["DTret"] = (dt * 0.125).astype(f)
    c["DTgla"] = np.repeat(((j <= i) & same).astype(f)[:, None, :], 4, axis=1)
    qd = np.zeros((128, 4, 128), np.float64)
    for h in range(4):
        qd[:, h, :] = np.exp(lg[h] * (i + 1.0))
    c["QDret"] = (qd * 0.125).astype(f)
    kf = np.zeros((128, 4), np.float64)
    for h in range(4):
        kf[:, h] = np.exp(lg[h] * (127.0 - np.arange(128)))
    c["kfacR"] = kf.astype(f)
    dr = np.zeros((128, 4), np.float64)
    for h in range(4):
        dr[:, h] = np.exp(lg[h] * 128.0)
    c["decR"] = dr.astype(f)
    fr = np.zeros((128, 4), f)
    p = np.arange(128) % 64
    half = 32
    fr_ret = (np.float32(10000.0) ** (-(np.arange(half, dtype=f)) * f(2.0) / f(64))).astype(f)
    fr[:, 0] = fr_ret[p % 32]
    fr[:, 1] = np.where(p < 32, -1.0, 1.0)
    fr_d = (np.float32(500000.0) ** (-(np.arange(8, dtype=f)) * f(2.0) / f(16))).astype(f)
    fr[:, 2] = np.where(p < 16, fr_d[p % 8], 0.0)
    fr[:, 3] = np.where(p < 8, -1.0, np.where(p < 16, 1.0, 0.0))
    c["ropefs"] = fr
    c["pw2"] = np.repeat((2.0 ** -(np.arange(32, dtype=np.float64) + 1.0))[None, :], 128, axis=0).astype(f)
    return c


def swap_cols(lo, rot, width, nheads):
    idx = []
    for h in range(nheads):
        base = lo + h * width
        half = rot // 2
        for d in range(width):
            if d < half:
                idx.append(base + d + half)
            elif d < rot:
                idx.append(base + d - half)
            else:
                idx.append(base + d)
    return idx


A_COLS = [("retk", 256), ("retk_sw", 256), ("retv", 512), ("dsak", 128), ("dsak_sw", 128),
          ("dsav", 128), ("idxk", 64), ("idxk_sw", 64), ("glak", 256), ("glav", 512), ("glaa", 16)]


def col_layout(cols):
    off, o = {}, 0
    for n, w in cols:
        off[n] = (o, w)
        o += w
    return off, o


class Builder:
    def __init__(self, cfg, name):
        self.cfg = cfg
        self.nc = bass.Bass("TRN2", target_bir_lowering=False, name=name)
        self.P = Prog(self.nc)
        self.din = {}
        self.dout = {}
        self.bank_i = 0

    def inp(self, name, shape, dtype=F32):
        t = self.nc.dram_tensor(name, list(shape), dtype, kind="ExternalInput")
        self.din[name] = t
        return t.ap()

    def outp(self, name, shape, dtype=F32):
        t = self.nc.dram_tensor(name, list(shape), dtype, kind="ExternalOutput")
        self.dout[name] = t
        return t.ap()

    def make_banks(self):
        self.banks = [self.P.ps(f"bank{i}", [128, 512], F32) for i in range(8)]

    def bank(self):
        b = self.banks[self.bank_i % 8]
        self.bank_i += 1
        return b

    def load_consts(self, names):
        P = self.P
        hc = host_consts()
        self.c = {}
        self.cb = {}
        for n in names:
            shp = list(hc[n].shape)
            ap = self.inp("c_" + n, shp)
            t = P.sb("sc_" + n, shp, F32)
            P.dma(t.ap(), ap, writes=[t])
            self.c[n] = t
        self.identb = P.sb("identb", [128, 128], BF16)
        P.op("vector", lambda e: e.tensor_copy(self.identb.ap(), self.c["ident"].ap()),
             reads=[self.c["ident"]], writes=[self.identb])
        self.onesb = P.sb("onesb", [128, 128], BF16)
        P.op("vector", lambda e: e.memset(self.onesb.ap(), 1.0), writes=[self.onesb])

    def load_weight(self, dram_ap, c0, ncols, name, kc=8, rows=128):
        P = self.P
        wt = P.sb(name, [rows, kc, ncols], BF16)
        CH = 64 if hasattr(self, "wstage") else 128
        if not hasattr(self, "wstage"):
            self.wstage = [P.sb(f"wstage{i}", [128, 8, CH], F32) for i in range(2)]
            self.wstage_i = 0
        for s in range(0, ncols, CH):
            n = min(CH, ncols - s)
            st = self.wstage[self.wstage_i % 2]
            self.wstage_i += 1
            P.dma(st.ap()[0:rows, 0:kc, 0:n], dram_ap[:, :, c0 + s:c0 + s + n], writes=[st])
            P.op("gpsimd", lambda e, st=st, s=s, n=n: e.tensor_copy(
                wt.ap()[:, :, s:s + n], st.ap()[0:rows, 0:kc, 0:n]), reads=[st], writes=[wt])
        return wt

    def precast(self, src_ap, dst_ap, dbuf, rows, kc, ncols):
        P = self.P
        CH = 256
        st = [P.sb(f"pc_st{i}", [128, 8, CH], F32) for i in range(2)]
        ot = [P.sb(f"pc_ot{i}", [128, 8, CH], BF16) for i in range(2)]
        i = 0
        for s0 in range(0, ncols, CH):
            n = min(CH, ncols - s0)
            a, o = st[i % 2], ot[i % 2]
            P.dma(a.ap()[0:rows, 0:kc, 0:n], src_ap[:, :, s0:s0 + n], writes=[a])
            if i % 2 == 0:
                P.op("scalar", lambda e: e.activation(o.ap()[0:rows, 0:kc, 0:n], a.ap()[0:rows, 0:kc, 0:n], AF.Copy),
                     reads=[a], writes=[o])
            else:
                P.op("vector", lambda e: e.tensor_copy(o.ap()[0:rows, 0:kc, 0:n], a.ap()[0:rows, 0:kc, 0:n]),
                     reads=[a], writes=[o])
            P.dma(dst_ap[:, :, s0:s0 + n], o.ap()[0:rows, 0:kc, 0:n], reads=[o], writes=[dbuf])
            i += 1

    def load_weight_bf16(self, dram_ap, dbuf, c0, ncols, name, kc=8, rows=128):
        P = self.P
        wt = P.sb(name, [rows, kc, ncols], BF16)
        half = max(1, kc // 2)
        for k0 in range(0, kc, half):
            P.dma(wt.ap()[:, k0:k0 + half, :], dram_ap[:, k0:k0 + half, c0:c0 + ncols], reads=[dbuf], writes=[wt])
        return wt

    def prealloc(self, ncore):
        P = self.P
        self.rp = [P.sb(f"rp{i}", [128, 512], F32) for i in range(2)]
        self.a17 = P.sb("a17", [17, 128], BF16)
        P.op("vector", lambda e: e.memset(self.a17.ap(), 1.0), writes=[self.a17])
        self.e1 = P.sb("gl_e1", [128, 256], F32)
        self.sp = P.sb("gl_sp", [128, 256], F32)
        self.kvt = [P.sb(f"kvt{i}", [64, 512], F32) for i in range(2)]
        self.dcg_g = P.sb("dcg_g", [64, ncore, 4], F32)

    def emit_mod(self, cvec_ap, adaw_ap, adab_ap, pre_ap, post_ap, want_gate):
        P = self.P
        self.G = P.sb("G", [128, 1024], F32)
        self.Sh = P.sb("Sh", [128, 1024], F32)
        if want_gate:
            self.GP = P.sb("GP", [128, 1024], F32)
        P.push_scope()
        cv = P.sb("cv", [128, 8], F32)
        P.dma(cv.ap(), cvec_ap, writes=[cv])
        ca = P.sb("ca", [128, 8], F32)
        P.op("scalar", lambda e: e.activation(ca.ap(), cv.ap(), AF.Silu), reads=[cv], writes=[ca])
        cbc = P.sb("cbc", [128, 8, 128], F32)
        P.op("vector", lambda e: e.tensor_copy(cbc.ap(), ca.ap().unsqueeze(2).to_broadcast([128, 8, 128])),
             reads=[ca], writes=[cbc])
        mod = P.sb("mod", [128, 3072], F32)
        bias = P.sb("modb", [128, 3072], F32)
        P.dma(bias.ap(), adab_ap.to_broadcast([128, 3072]), writes=[bias])
        wst = [P.sb(f"adaw{i}", [128, 8, 512], F32) for i in range(2)]
        for ci in range(6):
            w = wst[ci % 2]
            P.dma(w.ap(), adaw_ap[:, :, ci * 512:(ci + 1) * 512], writes=[w])
            bk = self.bank()
            for kc in range(8):
                P.op("tensor", lambda e, bk=bk, w=w, kc=kc: e.matmul(
                    bk.ap(), cbc.ap()[:, kc, :], w.ap()[:, kc, :], start=(kc == 0), stop=(kc == 7)),
                    reads=[cbc, w], writes=[bk])
            P.op("vector", lambda e, bk=bk, ci=ci: e.tensor_tensor(
                mod.ap()[:, ci * 512:(ci + 1) * 512], bk.ap(), bias.ap()[:, ci * 512:(ci + 1) * 512], ALU.add),
                reads=[bk, bias], writes=[mod])
        pre = P.sb("preb", [128, 1024], F32)
        P.dma(pre.ap(), pre_ap.to_broadcast([128, 1024]), writes=[pre])
        P.op("vector", lambda e: e.scalar_tensor_tensor(
            self.G.ap(), mod.ap()[:, 1024:2048], 1.0, pre.ap(), ALU.add, ALU.mult),
            reads=[mod, pre], writes=[self.G])
        P.op("vector", lambda e: e.tensor_copy(self.Sh.ap(), mod.ap()[:, 0:1024]), reads=[mod], writes=[self.Sh])
        if want_gate:
            post = P.sb("postb", [128, 1024], F32)
            P.dma(post.ap(), post_ap.to_broadcast([128, 1024]), writes=[post])
            P.op("vector", lambda e: e.tensor_tensor(self.GP.ap(), mod.ap()[:, 2048:3072], post.ap(), ALU.mult),
                 reads=[mod, post], writes=[self.GP])
        P.pop_scope()

    def emit_ropes(self, pos_ap, specs):
        P = self.P
        outs = []
        for fcol, scol, name in specs:
            outs.append((P.sb(name + "_cos", [128, self.cfg.T], BF16), P.sb(name + "_sin", [128, self.cfg.T], BF16)))
        P.push_scope()
        for (fcol, scol, name), (cb_, sb_) in zip(specs, outs):
            self.emit_rope(pos_ap, fcol, scol, name, cb_, sb_)
        P.pop_scope()
        del self.posf
        return outs

    def emit_rope(self, pos_ap, fcol, scol, name, cosb, sinb):
        P = self.P
        T = self.cfg.T
        fs = self.c["ropefs"]
        if not hasattr(self, "posf"):
            posi = P.sb("posi", [128, T], I32)
            P.dma(posi.ap(), pos_ap.to_broadcast([128, T]), writes=[posi])
            self.posf = P.sb("posf", [128, T], F32)
            P.op("vector", lambda e: e.tensor_copy(self.posf.ap(), posi.ap()), reads=[posi], writes=[self.posf])
            self.rtmp = [P.sb(f"rtmp{i}", [128, T], F32) for i in range(4)]
        ang, kf, r, rd = self.rtmp
        posf = self.posf
        PI = float(np.pi)
        C1 = 6.28125
        C2 = float(2.0 * np.pi - 6.28125)
        MAG = 12582912.0
        P.op("vector", lambda e: e.tensor_scalar(ang.ap(), posf.ap(), fs.ap()[:, fcol:fcol + 1], None, ALU.mult),
             reads=[posf, fs], writes=[ang])

        def reduce_and_sin(src, dst_final, shift):
            dst = rd
            if shift != 0.0:
                P.op("vector", lambda e: e.tensor_scalar(r.ap(), src.ap(), shift, None, ALU.add),
                     reads=[src], writes=[r])
                s2 = r
            else:
                s2 = src
            P.op("vector", lambda e: e.tensor_scalar(kf.ap(), s2.ap(), float(1.0 / (2 * np.pi)), MAG, ALU.mult, ALU.add),
                 reads=[s2], writes=[kf])
            P.op("vector", lambda e: e.tensor_scalar(kf.ap(), kf.ap(), MAG, None, ALU.subtract),
                 reads=[kf], writes=[kf])
            P.op("vector", lambda e: e.scalar_tensor_tensor(dst.ap(), kf.ap(), -C1, s2.ap(), ALU.mult, ALU.add),
                 reads=[kf, s2], writes=[dst])
            P.op("vector", lambda e: e.scalar_tensor_tensor(dst.ap(), kf.ap(), -C2, dst.ap(), ALU.mult, ALU.add),
                 reads=[kf, dst], writes=[dst])
            P.op("vector", lambda e: e.tensor_scalar(dst.ap(), dst.ap(), PI, -PI, ALU.min, ALU.max),
                 reads=[dst], writes=[dst])
            P.op("scalar", lambda e: e.activation(dst_final.ap(), dst.ap(), AF.Sin), reads=[dst], writes=[dst_final])

        reduce_and_sin(ang, sinb, 0.0)
        reduce_and_sin(ang, cosb, float(np.pi / 2))
        P.op("vector", lambda e: e.tensor_scalar(sinb.ap(), sinb.ap(), fs.ap()[:, scol:scol + 1], None, ALU.mult),
             reads=[sinb, fs], writes=[sinb])
        return cosb, sinb

    def emit_norm_T(self, x_tile_ap_dram, hT, slot):
        P = self.P
        if not hasattr(self, "xin"):
            self.xin = [P.sb(f"xin{i}", [128, 1024], F32) for i in range(1)]
            self.xin_i = 0
            self.hjunk = P.sb("hjunk", [128, 1024], BF16)
            self.h1 = P.sb("h1", [128, 1024], F32)
            self.hb = P.sb("hb", [128, 1024], BF16)
            self.nst = P.sb("nst", [128, 4], F32)
        xt = self.xin[0]
        self.xin_i += 1
        nst = self.nst
        P.dma(xt.ap(), x_tile_ap_dram, writes=[xt])
        P.op("scalar", lambda e: e.activation(self.hjunk.ap(), xt.ap(), AF.Square, accum_out=nst.ap()[:, 0:1]),
             reads=[xt], writes=[self.hjunk, nst])
        P.op("vector", lambda e: e.tensor_scalar(nst.ap()[:, 1:2], nst.ap()[:, 0:1], 1.0 / D, RMS_EPS, ALU.mult, ALU.add),
             reads=[nst], writes=[nst])
        P.op("scalar", lambda e: e.activation(nst.ap()[:, 2:3], nst.ap()[:, 1:2], AF.Sqrt), reads=[nst], writes=[nst])
        P.op("vector", lambda e: e.reciprocal(nst.ap()[:, 3:4], nst.ap()[:, 2:3]), reads=[nst], writes=[nst])
        P.op("vector", lambda e: e.scalar_tensor_tensor(self.h1.ap(), xt.ap(), nst.ap()[:, 3:4], self.G.ap(), ALU.mult, ALU.mult),
             reads=[xt, nst, self.G], writes=[self.h1])
        P.op("gpsimd", lambda e: e.tensor_tensor(self.hb.ap(), self.h1.ap(), self.Sh.ap(), ALU.add),
             reads=[self.h1, self.Sh], writes=[self.hb])
        for half in range(2):
            bk = self.bank()
            for q in range(4):
                kc = half * 4 + q
                P.op("tensor", lambda e, bk=bk, q=q, kc=kc: e.matmul(
                    bk.ap()[:, q * 128:(q + 1) * 128], self.hb.ap()[:, kc * 128:(kc + 1) * 128], self.identb.ap(),
                    start=True, stop=True), reads=[self.hb, self.identb], writes=[bk])
            P.op("scalar", lambda e, bk=bk, half=half: e.activation(
                hT.ap()[:, slot, half * 4:half * 4 + 4, :], bk.ap().rearrange("p (a b) -> p a b", a=4), AF.Copy),
                reads=[bk], writes=[hT])

    def proj_fm(self, out_ap, bk, w, c0, m, hT, slot, nslots=1):
        P = self.P
        for kc in range(8):
            if nslots == 1:
                rhs = hT.ap()[:, slot, kc, :]
            else:
                rhs = hT.ap()[:, slot:slot + nslots, kc, :]
            P.op("tensor", lambda e, kc=kc, rhs=rhs: e.matmul(
                out_ap, w.ap()[:, kc, c0:c0 + m], rhs, start=(kc == 0), stop=(kc == 7)),
                reads=[w, hT], writes=[bk])

    def proj_tm(self, out_ap, bk, w, c0, n, hT, slot):
        P = self.P
        for kc in range(8):
            P.op("tensor", lambda e, kc=kc: e.matmul(
                out_ap, hT.ap()[:, slot, kc, :], w.ap()[:, kc, c0:c0 + n], start=(kc == 0), stop=(kc == 7)),
                reads=[w, hT], writes=[bk])

    def rope_fm(self, dst_ap, dst_buf, bx, bs, npart, ncols_ap, cosb, sinb, tok0, scale=None):
        P = self.P
        if not hasattr(self, "rp"):
            self.rp = [P.sb(f"rp{i}", [128, 512], F32) for i in range(2)]
        t0, t1 = self.rp
        nh = ncols_ap // 128
        cosv = cosb.ap()[0:npart, tok0:tok0 + 128].unsqueeze(1).to_broadcast([npart, nh, 128])
        sinv = sinb.ap()[0:npart, tok0:tok0 + 128].unsqueeze(1).to_broadcast([npart, nh, 128])
        v = lambda b: b.ap()[0:npart, 0:ncols_ap].rearrange("p (h t) -> p h t", h=nh)
        P.op("vector", lambda e: e.tensor_tensor(v(t0), v(bx), cosv, ALU.mult), reads=[bx, cosb], writes=[t0])
        P.op("vector", lambda e: e.tensor_tensor(v(t1), v(bs), sinv, ALU.mult), reads=[bs, sinb], writes=[t1])
        if scale is None:
            P.op("vector", lambda e: e.tensor_tensor(dst_ap, v(t0), v(t1), ALU.add), reads=[t0, t1], writes=[dst_buf])
        else:
            P.op("vector", lambda e: e.scalar_tensor_tensor(dst_ap, v(t0), 1.0, v(t1), ALU.mult, ALU.add),
                 reads=[t0, t1], writes=[dst_buf])


def gla_decay_common(B, hT, slot, wA, aoff, wlr17):
    P = B.P
    if not hasattr(B, "a17"):
        B.a17 = P.sb("a17", [17, 128], BF16)
        P.op("vector", lambda e: e.memset(B.a17.ap(), 1.0), writes=[B.a17])
        B.e1 = P.sb("gl_e1", [128, 256], F32)
        B.sp = P.sb("gl_sp", [128, 256], F32)
    bk = B.bank()
    B.proj_fm(bk.ap()[0:16, 0:128], bk, wA, aoff, 16, hT, slot)
    P.op("vector", lambda e: e.tensor_copy(B.a17.ap()[0:16, :], bk.ap()[0:16, 0:128]), reads=[bk], writes=[B.a17])
    bz = B.bank()
    P.op("tensor", lambda e: e.matmul(bz.ap()[:, 0:256], B.a17.ap(), wlr17.ap(), start=True, stop=True),
         reads=[B.a17, wlr17], writes=[bz])
    P.op("scalar", lambda e: e.activation(B.e1.ap(), bz.ap()[:, 0:256], AF.Exp, scale=-1.0), reads=[bz], writes=[B.e1])
    P.op("scalar", lambda e: e.activation(B.sp.ap(), B.e1.ap(), AF.Ln, bias=1.0), reads=[B.e1], writes=[B.sp])
    return B.sp


def load_wlr17(B, wlr_ap, blr_ap):
    P = B.P
    st = P.sb("wlr_st", [17, 256], F32)
    P.dma(st.ap()[0:16, :], wlr_ap, writes=[st])
    P.dma(st.ap()[16:17, :], blr_ap, writes=[st])
    w = P.sb("wlr17", [17, 256], BF16)
    P.op("vector", lambda e: e.tensor_copy(w.ap(), st.ap()), reads=[st], writes=[w])
    return w


def build_A(cfg):
    B = Builder(cfg, "phaseA")
    P = B.P
    TPC, T = cfg.TPC, cfg.T
    aoff, ncolA = col_layout(A_COLS)
    x = B.inp("x", [TPC, 128, 1024])
    pos = B.inp("pos", [1, T], I32)
    cvec = B.inp("cvec", [128, 8])
    adaw = B.inp("adaw", [128, 8, 3072])
    adab = B.inp("adab", [1, 3072])
    pre = B.inp("pre", [1, 1024])
    WA = B.inp("WA", [128, 8, ncolA])
    wlr = B.inp("wlr", [16, 256])
    blr = B.inp("blr", [1, 256])
    oKT = B.outp("KT", [128, T], BF16)
    oV = B.outp("V", [TPC, 128, 128], BF16)
    oIK = B.outp("IK", [64, T], BF16)
    okvR = B.outp("kvR", [TPC, 64, 512])
    okvG = B.outp("kvG", [TPC, 64, 512])
    odecG = B.outp("decG", [TPC, 64, 4])

    B.make_banks()
    B.load_consts(["ident", "triU64", "chunkind", "kfacR", "ropefs"])
    B.emit_mod(cvec, adaw, adab, pre, None, False)
    (cosR, sinR), (cosD, sinD) = B.emit_ropes(pos, [(0, 1, "rr"), (2, 3, "rd")])
    wA = B.load_weight(WA, 0, ncolA, "wA")
    wlr17 = load_wlr17(B, wlr, blr)
    hT = P.sb("hT", [128, 1, 8, 128], BF16)

    kTb = P.sb("kTb", [64, 4, 128], BF16)
    khat = P.sb("khat", [128, 4, 64], BF16)
    vtok = P.sb("vtok", [128, 512], BF16)
    kvs = [P.sb(f"kvs{i}", [64, 512], F32) for i in range(2)]
    ktb = P.sb("ktb", [128, 128], BF16)
    vb = P.sb("vb", [128, 128], BF16)
    ikb = P.sb("ikb", [64, 128], BF16)
    kfac = P.sb("kfac", [128, 256], F32)
    gkhat = P.sb("gkhat", [128, 256], BF16)
    gvtok = P.sb("gvtok", [128, 512], BF16)
    dec = P.sb("dec", [64, 4, 4], F32)
    kv1s = P.sb("kv1s", [64, 512], F32)
    decT = P.sb("decT", [64, 4], F32)
    ident = B.c["ident"]

    for s in range(TPC):
        t0 = s * 128
        B.emit_norm_T(x[s], hT, 0)
        bx, bs = B.bank(), B.bank()
        for h in range(4):
            B.proj_fm(bx.ap()[0:64, h * 128:(h + 1) * 128], bx, wA, aoff["retk"][0] + h * 64, 64, hT, 0)
            B.proj_fm(bs.ap()[0:64, h * 128:(h + 1) * 128], bs, wA, aoff["retk_sw"][0] + h * 64, 64, hT, 0)
        B.rope_fm(kTb.ap(), kTb, bx, bs, 64, 512, cosR, sinR, t0)
        bt = B.bank()
        for h in range(4):
            P.op("tensor", lambda e, h=h: e.matmul(bt.ap()[:, h * 64:(h + 1) * 64], kTb.ap()[:, h, :],
                                                   B.identb.ap()[0:64, 0:64], start=True, stop=True),
                 reads=[kTb, B.identb], writes=[bt])
        P.op("vector", lambda e: e.tensor_tensor(
            khat.ap(), bt.ap()[:, 0:256].rearrange("p (h d) -> p h d", h=4),
            B.c["kfacR"].ap().unsqueeze(2).to_broadcast([128, 4, 64]), ALU.mult),
            reads=[bt, B.c["kfacR"]], writes=[khat])
        bv = B.bank()
        B.proj_tm(bv.ap(), bv, wA, aoff["retv"][0], 512, hT, 0)
        P.op("scalar", lambda e: e.activation(vtok.ap(), bv.ap(), AF.Copy), reads=[bv], writes=[vtok])
        bkv = B.bank()
        for h in range(4):
            P.op("tensor", lambda e, h=h: e.matmul(bkv.ap()[0:64, h * 128:(h + 1) * 128], khat.ap()[:, h, :],
                                                   vtok.ap()[:, h * 128:(h + 1) * 128], start=True, stop=True),
                 reads=[khat, vtok], writes=[bkv])
        kv = kvs[0]
        P.op("scalar", lambda e: e.activation(kv.ap(), bkv.ap()[0:64, :], AF.Copy), reads=[bkv], writes=[kv])
        P.dma(okvR[s], kv.ap(), reads=[kv])
        bx, bs = B.bank(), B.bank()
        B.proj_fm(bx.ap()[:, 0:128], bx, wA, aoff["dsak"][0], 128, hT, 0)
        B.proj_fm(bs.ap()[:, 0:128], bs, wA, aoff["dsak_sw"][0], 128, hT, 0)
        B.rope_fm(ktb.ap().unsqueeze(1), ktb, bx, bs, 128, 128, cosD, sinD, t0)
        P.dma(oKT[:, t0:t0 + 128], ktb.ap(), reads=[ktb])
        bv = B.bank()
        B.proj_tm(bv.ap()[:, 0:128], bv, wA, aoff["dsav"][0], 128, hT, 0)
        P.op("scalar", lambda e: e.activation(vb.ap(), bv.ap()[:, 0:128], AF.Copy), reads=[bv], writes=[vb])
        P.dma(oV[s], vb.ap(), reads=[vb])
        bx, bs = B.bank(), B.bank()
        B.proj_fm(bx.ap()[0:64, 0:128], bx, wA, aoff["idxk"][0], 64, hT, 0)
        B.proj_fm(bs.ap()[0:64, 0:128], bs, wA, aoff["idxk_sw"][0], 64, hT, 0)
        B.rope_fm(ikb.ap().unsqueeze(1), ikb, bx, bs, 64, 128, cosD, sinD, t0)
        P.dma(oIK[:, t0:t0 + 128], ikb.ap(), reads=[ikb])
        sp = gla_decay_common(B, hT, 0, wA, aoff["glaa"][0], wlr17)
        bd = B.bank()
        P.op("tensor", lambda e: e.matmul(bd.ap()[:, 0:256], B.c["triU64"].ap(), sp.ap(), start=True, stop=True),
             reads=[B.c["triU64"], sp], writes=[bd])
        P.op("scalar", lambda e: e.activation(kfac.ap(), bd.ap()[:, 0:256], AF.Exp, scale=-1.0 / 16.0),
             reads=[bd], writes=[kfac])
        bk = B.bank()
        B.proj_tm(bk.ap()[:, 0:256], bk, wA, aoff["glak"][0], 256, hT, 0)
        P.op("vector", lambda e: e.tensor_tensor(gkhat.ap(), bk.ap()[:, 0:256], kfac.ap(), ALU.mult),
             reads=[bk, kfac], writes=[gkhat])
        bv = B.bank()
        B.proj_tm(bv.ap(), bv, wA, aoff["glav"][0], 512, hT, 0)
        P.op("scalar", lambda e: e.activation(gvtok.ap(), bv.ap(), AF.Copy), reads=[bv], writes=[gvtok])
        b0, b1 = B.bank(), B.bank()
        for ch, bb in ((0, b0), (1, b1)):
            for h in range(4):
                P.op("tensor", lambda e, h=h, ch=ch, bb=bb: e.matmul(
                    bb.ap()[0:64, h * 128:(h + 1) * 128], gkhat.ap()[ch * 64:(ch + 1) * 64, h * 64:(h + 1) * 64],
                    gvtok.ap()[ch * 64:(ch + 1) * 64, h * 128:(h + 1) * 128], start=True, stop=True),
                    reads=[gkhat, gvtok], writes=[bb])
        bs_ = B.bank()
        for h in range(4):
            P.op("tensor", lambda e, h=h: e.matmul(bs_.ap()[0:64, h * 4:(h + 1) * 4], sp.ap()[:, h * 64:(h + 1) * 64],
                                                   B.c["chunkind"].ap(), start=True, stop=True),
                 reads=[sp, B.c["chunkind"]], writes=[bs_])
        P.op("scalar", lambda e: e.activation(dec.ap(), bs_.ap()[0:64, 0:16].rearrange("p (h c) -> p h c", h=4),
                                              AF.Exp, scale=-1.0 / 16.0), reads=[bs_], writes=[dec])
        P.op("scalar", lambda e: e.activation(kv1s.ap(), b1.ap()[0:64, :], AF.Copy), reads=[b1], writes=[kv1s])
        kv = kvs[1]
        for h in range(4):
            P.op("vector", lambda e, h=h: e.scalar_tensor_tensor(
                kv.ap()[:, h * 128:(h + 1) * 128], b0.ap()[0:64, h * 128:(h + 1) * 128], dec.ap()[:, h, 1:2],
                kv1s.ap()[:, h * 128:(h + 1) * 128], ALU.mult, ALU.add),
                reads=[b0, dec, kv1s], writes=[kv])
        P.dma(okvG[s], kv.ap(), reads=[kv])
        P.op("vector", lambda e: e.tensor_copy(decT.ap(), dec.ap()[:, :, 2]), reads=[dec], writes=[decT])
        P.dma(odecG[s], decT.ap(), reads=[decT])
    P.finish()
    return B


def prep_common(inputs, cfg, l, core):
    raise NotImplementedError


def a_col_index():
    ar = np.arange
    idx = {
        "retk": OFF["ret_k"] + ar(256), "retk_sw": np.array(swap_cols(OFF["ret_k"], 64, 64, 4)),
        "retv": OFF["ret_v"] + ar(512),
        "dsak": OFF["dsa_k"] + ar(128), "dsak_sw": np.array(swap_cols(OFF["dsa_k"], 16, 64, 2)),
        "dsav": OFF["dsa_v"] + ar(128),
        "idxk": OFF["idx_k"] + ar(64), "idxk_sw": np.array(swap_cols(OFF["idx_k"], 16, 64, 1)),
        "glak": OFF["gla_k"] + ar(256), "glav": OFF["gla_v"] + ar(512), "glaa": OFF["gla_a"] + ar(16),
    }
    return np.concatenate([idx[n] for n, _ in A_COLS])


def kc_layout(w):
    n = w.shape[1]
    return np.ascontiguousarray(w.reshape(8, 128, n).transpose(1, 0, 2))


def core_tiles(cfg, c):
    return [k * cfg.NCORE + c for k in range(cfg.TPC)]


def const_inputs(names):
    hc = host_consts()
    return {"c_" + n: hc[n] for n in names}


def host_inputs_A(inp, cfg, l):
    x = np.asarray(inp["x"])[0].reshape(cfg.S // 128, 128, D)
    pos = np.asarray(inp["positions"])[0].reshape(cfg.S // 128, 128)
    shared = {
        "cvec": np.ascontiguousarray(np.asarray(inp["c"])[0].reshape(8, 128).T),
        "adaw": kc_layout(np.asarray(inp["ada_w"])[l]),
        "adab": np.asarray(inp["ada_b"])[l][None, :],
        "pre": np.asarray(inp["pre_norm"])[l][None, :],
        "WA": kc_layout(np.asarray(inp["w_in"])[l][:, a_col_index()]),
        "wlr": np.asarray(inp["gla_w_lr"])[l],
        "blr": np.asarray(inp["gla_b_lr"])[l][None, :],
    }
    shared.update(const_inputs(["ident", "triU64", "chunkind", "kfacR", "ropefs"]))
    maps = []
    for c in range(cfg.NCORE):
        tl = core_tiles(cfg, c)
        m = dict(shared)
        m["x"] = np.ascontiguousarray(x[tl])
        m["pos"] = np.ascontiguousarray(pos[tl].reshape(1, cfg.T)).astype(np.int32)
        maps.append(m)
    return maps


def np_inputs(S, seed=0, depth=4):
    r = np.random.RandomState(seed)
    f = np.float32
    n = lambda *s: r.randn(*s).astype(f)
    Dm = D
    return {
        "x": n(1, S, Dm), "c": n(1, Dm),
        "positions": (np.arange(S, dtype=np.int32)[None, :] + np.int32(r.randint(0, 4096))),
        "ada_w": n(depth, Dm, 3 * Dm) * f(0.1 * Dm ** -0.5), "ada_b": n(depth, 3 * Dm) * f(0.02),
        "pre_norm": 1 + f(0.05) * n(depth, Dm), "post_norm": 1 + f(0.05) * n(depth, Dm),
        "w_in": n(depth, Dm, IN_WIDTH) * f(Dm ** -0.5),
        "gla_w_lr": n(depth, 16, 256) * f(0.25), "gla_b_lr": f(0.1) * n(depth, 256),
        "w_br_ret": n(depth, 512, Dm) * f(512 ** -0.5), "w_br_dsa": n(depth, 512, Dm) * f(512 ** -0.5),
        "w_br_gla": n(depth, 512, Dm) * f(512 ** -0.5), "w_out": n(depth, Dm, Dm) * f(Dm ** -0.5),
    }


B_COLS = [("retq", 256), ("retq_sw", 256), ("retk", 256), ("retk_sw", 256), ("retv", 512), ("retg", 512),
          ("mg0", 1024),
          ("glaq", 256), ("glak", 256), ("glav", 512), ("glag", 512), ("glaa", 16), ("mg2", 1024),
          ("dsaq", 512), ("dsaq_sw", 512), ("idxq", 256), ("idxq_sw", 256), ("idxw", 4), ("dsag", 512),
          ("mg1", 1024)]


def b_col_index():
    ar = np.arange
    pair = np.concatenate([np.concatenate([ar(64) + j * 64, ar(64) + (j + 4) * 64]) for j in range(4)])
    dq = OFF["dsa_q"] + ar(512)
    dq_sw = np.array(swap_cols(OFF["dsa_q"], 16, 64, 8))
    idx = {
        "retq": OFF["ret_q"] + ar(256), "retq_sw": np.array(swap_cols(OFF["ret_q"], 64, 64, 4)),
        "retk": OFF["ret_k"] + ar(256), "retk_sw": np.array(swap_cols(OFF["ret_k"], 64, 64, 4)),
        "retv": OFF["ret_v"] + ar(512), "retg": OFF["ret_g"] + ar(512),
        "mg0": OFF["merge"] + ar(1024), "mg1": OFF["merge"] + 1024 + ar(1024), "mg2": OFF["merge"] + 2048 + ar(1024),
        "glaq": OFF["gla_q"] + ar(256), "glak": OFF["gla_k"] + ar(256), "glav": OFF["gla_v"] + ar(512),
        "glag": OFF["gla_g"] + ar(512), "glaa": OFF["gla_a"] + ar(16),
        "dsaq": dq[pair], "dsaq_sw": dq_sw[pair],
        "idxq": OFF["idx_q"] + ar(256), "idxq_sw": np.array(swap_cols(OFF["idx_q"], 16, 64, 4)),
        "idxw": OFF["idx_w"] + ar(4), "dsag": OFF["dsa_g"] + ar(512),
    }
    return np.concatenate([idx[n] for n, _ in B_COLS])


def build_B(cfg):
    B = Builder(cfg, "phaseB")
    P = B.P
    TPC, T, NCORE, SG = cfg.TPC, cfg.T, cfg.NCORE, cfg.SG
    GK, BLK, NB, BPG = cfg.GK, cfg.BLK, cfg.NB, cfg.BPG
    boff, ncolB = col_layout(B_COLS)
    x = B.inp("x", [TPC, 128, 1024])
    pos = B.inp("pos", [1, T], I32)
    cvec = B.inp("cvec", [128, 8])
    adaw = B.inp("adaw", [128, 8, 3072])
    adab = B.inp("adab", [1, 3072])
    pre = B.inp("pre", [1, 1024])
    post = B.inp("post", [1, 1024])
    WB = B.inp("WB", [128, 8, ncolB])
    wbr_r = B.inp("wbr_r", [128, 4, 1024])
    wbr_g = B.inp("wbr_g", [128, 4, 1024])
    wbr_d = B.inp("wbr_d", [64, 8, 1024])
    wout_d = B.inp("wout", [128, 8, 1024])
    wlr = B.inp("wlr", [16, 256])
    blr = B.inp("blr", [1, 256])
    KTa = B.inp("KTa", [NCORE, 128, T], BF16)
    Va = B.inp("Va", [NCORE, TPC, 128, 128], BF16)
    IKa = B.inp("IKa", [NCORE, 64, T], BF16)
    kvRa = B.inp("kvRa", [NCORE, TPC, 64, 512])
    kvGa = B.inp("kvGa", [NCORE, TPC, 64, 512])
    decGa = B.inp("decGa", [NCORE, TPC, 64, 4])
    sel_d = B.inp("sel", [128, NCORE])
    pen_d = B.inp("pen", [128, GK], BF16)
    xo = B.outp("xo", [TPC, 128, 1024])

    B.make_banks()
    B.load_consts(B_CONSTS)
    sel = P.sb("sel", [128, NCORE], F32)
    P.dma(sel.ap(), sel_d, writes=[sel])
    pen = P.sb("pen", [128, GK], BF16)
    P.dma(pen.ap(), pen_d, writes=[pen])
    ident4 = P.sb("ident4", [128, 4, 128], BF16)
    P.op("vector", lambda e: e.tensor_copy(ident4.ap(), B.c["ident"].ap().unsqueeze(1).to_broadcast([128, 4, 128])),
         reads=[B.c["ident"]], writes=[ident4])
    B.emit_mod(cvec, adaw, adab, pre, post, True)
    (cosR, sinR), (cosD, sinD) = B.emit_ropes(pos, [(0, 1, "rr"), (2, 3, "rd")])
    wlr17 = load_wlr17(B, wlr, blr)
    nc_ = B.nc
    WBb = nc_.dram_tensor("WBb", [128, 8, ncolB], BF16, kind="Internal").ap()
    wbr_rb = nc_.dram_tensor("wbr_rb", [128, 4, 1024], BF16, kind="Internal").ap()
    wbr_gb = nc_.dram_tensor("wbr_gb", [128, 4, 1024], BF16, kind="Internal").ap()
    wbr_db = nc_.dram_tensor("wbr_db", [64, 8, 1024], BF16, kind="Internal").ap()
    woutb = nc_.dram_tensor("woutb", [128, 8, 1024], BF16, kind="Internal").ap()
    wbuf = Buf("wscratch")
    P.push_scope()
    B.precast(WB, WBb, wbuf, 128, 8, ncolB)
    B.precast(wbr_r, wbr_rb, wbuf, 128, 4, 1024)
    B.precast(wbr_g, wbr_gb, wbuf, 128, 4, 1024)
    B.precast(wbr_d, wbr_db, wbuf, 64, 8, 1024)
    B.precast(wout_d, woutb, wbuf, 128, 8, 1024)
    P.pop_scope()
    wbr_r, wbr_g, wbr_d, wout_d = wbr_rb, wbr_gb, wbr_db, woutb
    B.prealloc(NCORE)
    SR = P.sb("SR", [64, 4, 128], F32)
    SGs = P.sb("SGs", [64, 4, 128], F32)
    P.op("vector", lambda e: e.memset(SR.ap(), 0.0), writes=[SR])
    P.op("vector", lambda e: e.memset(SGs.ap(), 0.0), writes=[SGs])
    hT = P.sb("hT", [128, SG, 8, 128], BF16)
    ysum = P.sb("ysum", [128, SG, 8, 128], BF16)
    dsaout = P.sb("dsaout", [64, SG, 8, 128], BF16)
    cap = P.sb("cap", [64, 4, 128], F32)
    capb = P.sb("capb", [64, 4, 128], BF16)

    def load_w(names, tag):
        c0 = boff[names[0]][0]
        n = sum(boff[k][1] for k in names)
        assert boff[names[-1]][0] + boff[names[-1]][1] == c0 + n
        return B.load_weight_bf16(WBb, wbuf, c0, n, tag), c0

    def load_w2(dram_ap, rows, kc, name):
        return B.load_weight_bf16(dram_ap, wbuf, 0, 1024, name, kc=kc, rows=rows)

    def scan(S, kv_all, dec_src, s, tag):
        if dec_src is not None:
            dcg = B.dcg_g
            P.dma(dcg.ap(), dec_src[:, s].rearrange("j d n -> d j n"), writes=[dcg])
        for j in range(NCORE):
            kvt = B.kvt[j % 2]
            P.dma(kvt.ap(), kv_all[j, s], writes=[kvt])
            if j == 0:
                P.op("vector", lambda e: e.tensor_scalar(cap.ap(), S.ap(), sel.ap()[0:64, 0:1], None, ALU.mult),
                     reads=[S, sel], writes=[cap])
            else:
                P.op("vector", lambda e: e.scalar_tensor_tensor(cap.ap(), S.ap(), sel.ap()[0:64, j:j + 1], cap.ap(),
                                                                ALU.mult, ALU.add), reads=[S, sel, cap], writes=[cap])
            if dec_src is None:
                dv = B.c["decR"].ap()[0:64, :].unsqueeze(2).to_broadcast([64, 4, 128])
                rd = [B.c["decR"]]
            else:
                dv = dcg.ap()[:, j, :].unsqueeze(2).to_broadcast([64, 4, 128])
                rd = [dcg]
            P.op("vector", lambda e: e.tensor_tensor(S.ap(), S.ap(), dv, ALU.mult), reads=[S] + rd, writes=[S])
            P.op("vector", lambda e: e.tensor_tensor(S.ap(), S.ap(), kvt.ap().rearrange("p (h e) -> p h e", h=4),
                                                     ALU.add), reads=[S, kvt], writes=[S])
        P.op("scalar", lambda e: e.activation(capb.ap(), cap.ap(), AF.Copy), reads=[cap], writes=[capb])

    def tail(bO, w, c0, gname, mgname, wbr, ls, first, tl):
        sq, r1, sg, tt, bro, mg, t2 = tl
        P.op("scalar", lambda e: e.activation(sq.ap(), bO.ap(), AF.Square), reads=[bO], writes=[sq])
        bQ = B.bank()
        P.op("tensor", lambda e: e.matmul(bQ.ap(), B.onesb.ap(), sq.ap(), start=True, stop=True),
             reads=[B.onesb, sq], writes=[bQ])
        P.op("vector", lambda e: e.tensor_scalar(r1.ap(), bQ.ap(), 1.0 / 128.0, RMS_EPS, ALU.mult, ALU.add),
             reads=[bQ], writes=[r1])
        P.op("scalar", lambda e: e.activation(r1.ap(), r1.ap(), AF.Sqrt), reads=[r1], writes=[r1])
        P.op("vector", lambda e: e.reciprocal(r1.ap(), r1.ap()), reads=[r1], writes=[r1])
        bG = B.bank()
        g0 = boff[gname][0] - c0
        for h in range(4):
            B.proj_fm(bG.ap()[:, h * 128:(h + 1) * 128], bG, w, g0 + h * 128, 128, hT, ls)
        P.op("scalar", lambda e: e.activation(sg.ap(), bG.ap(), AF.Silu), reads=[bG], writes=[sg])
        P.op("vector", lambda e: e.tensor_tensor(tt.ap(), bO.ap(), r1.ap(), ALU.mult), reads=[bO, r1], writes=[tt])
        P.op("gpsimd", lambda e: e.tensor_tensor(bro.ap(), tt.ap(), sg.ap(), ALU.mult), reads=[tt, sg], writes=[bro])
        merge(lambda nc, out_ap, bk: [P.op("tensor", lambda e, h=h: e.matmul(
            out_ap, wbr.ap()[:, h, nc * 128:(nc + 1) * 128], bro.ap()[:, h * 128:(h + 1) * 128],
            start=(h == 0), stop=(h == 3)), reads=[wbr, bro], writes=[bk]) for h in range(4)],
            w, boff[mgname][0] - c0, ls, first, mg, t2)

    def merge(emit_branch, w, m0, ls, first, mg, t2):
        bY = [B.bank(), B.bank()]
        for nc in range(8):
            bk = bY[nc // 4]
            emit_branch(nc, bk.ap()[:, (nc % 4) * 128:(nc % 4 + 1) * 128], bk)
        bM = [B.bank(), B.bank()]
        for nc in range(8):
            bk = bM[nc // 4]
            B.proj_fm(bk.ap()[:, (nc % 4) * 128:(nc % 4 + 1) * 128], bk, w, m0 + nc * 128, 128, hT, ls)
        for hf in range(2):
            P.op("scalar", lambda e: e.activation(mg.ap(), bM[hf].ap(), AF.Sigmoid), reads=[bM[hf]], writes=[mg])
            yv = ysum.ap()[:, ls, hf * 4:(hf + 1) * 4, :]
            m3 = mg.ap().rearrange("p (a b) -> p a b", a=4)
            b3 = bY[hf].ap().rearrange("p (a b) -> p a b", a=4)
            if first:
                P.op("vector", lambda e: e.tensor_tensor(yv, m3, b3, ALU.mult), reads=[mg, bY[hf]], writes=[ysum])
            else:
                P.op("vector", lambda e: e.tensor_tensor(t2.ap(), mg.ap(), bY[hf].ap(), ALU.mult),
                     reads=[mg, bY[hf]], writes=[t2])
                P.op("gpsimd", lambda e: e.tensor_tensor(yv, yv, t2.ap().rearrange("p (a b) -> p a b", a=4), ALU.add),
                     reads=[ysum, t2], writes=[ysum])

    def tail_bufs():
        return (P.sb("t_sq", [128, 512], BF16), P.sb("t_r1", [128, 512], F32), P.sb("t_sg", [128, 512], F32),
                P.sb("t_tt", [128, 512], F32), P.sb("t_bro", [128, 512], BF16), P.sb("t_mg", [128, 512], F32),
                P.sb("t_t2", [128, 512], F32))

    for g0 in range(0, TPC, SG):
        P.push_scope()
        for s in range(g0, g0 + SG):
            B.emit_norm_T(x[s], hT, s - g0)
        P.pop_scope()
        del B.xin
        P.push_scope()
        w, c0 = load_w(["retq", "retq_sw", "retk", "retk_sw", "retv", "retg", "mg0"], "w_ret")
        wbr = load_w2(wbr_r, 128, 4, "wbr_ret")
        tl = tail_bufs()
        qTb = P.sb("qTb", [64, 4, 128], BF16)
        kTb = P.sb("kTb", [64, 4, 128], BF16)
        qhat = P.sb("qhat", [64, 4, 128], BF16)
        vtok = P.sb("vtok", [128, 512], BF16)
        Sm = P.sb("Sm", [128, 512], BF16)
        for s in range(g0, g0 + SG):
            ls = s - g0
            t0 = s * 128
            scan(SR, kvRa, None, s, "r")
            for nm, dst in (("retq", qTb), ("retk", kTb)):
                bx, bs = B.bank(), B.bank()
                for h in range(4):
                    B.proj_fm(bx.ap()[0:64, h * 128:(h + 1) * 128], bx, w, boff[nm][0] - c0 + h * 64, 64, hT, ls)
                    B.proj_fm(bs.ap()[0:64, h * 128:(h + 1) * 128], bs, w, boff[nm + "_sw"][0] - c0 + h * 64, 64, hT, ls)
                B.rope_fm(dst.ap(), dst, bx, bs, 64, 512, cosR, sinR, t0)
            bv = B.bank()
            B.proj_tm(bv.ap(), bv, w, boff["retv"][0] - c0, 512, hT, ls)
            P.op("scalar", lambda e: e.activation(vtok.ap(), bv.ap(), AF.Copy), reads=[bv], writes=[vtok])
            bS = B.bank()
            for h in range(4):
                P.op("tensor", lambda e: e.matmul(bS.ap()[:, h * 128:(h + 1) * 128], kTb.ap()[:, h, :], qTb.ap()[:, h, :],
                                                  start=True, stop=True), reads=[kTb, qTb], writes=[bS])
            P.op("vector", lambda e: e.tensor_tensor(Sm.ap(), bS.ap(), B.c["DTret"].ap().rearrange("p h i -> p (h i)"),
                                                     ALU.mult), reads=[bS, B.c["DTret"]], writes=[Sm])
            P.op("gpsimd", lambda e: e.tensor_tensor(qhat.ap(), qTb.ap(), B.c["QDret"].ap()[0:64], ALU.mult),
                 reads=[qTb, B.c["QDret"]], writes=[qhat])
            bO = B.bank()
            for h in range(4):
                o_ap = bO.ap()[:, h * 128:(h + 1) * 128]
                P.op("tensor", lambda e: e.matmul(o_ap, vtok.ap()[:, h * 128:(h + 1) * 128], Sm.ap()[:, h * 128:(h + 1) * 128],
                                                  start=True, stop=False), reads=[vtok, Sm], writes=[bO])
                P.op("tensor", lambda e: e.matmul(o_ap, capb.ap()[:, h, :], qhat.ap()[:, h, :], start=False, stop=True),
                     reads=[capb, qhat], writes=[bO])
            tail(bO, w, c0, "retg", "mg0", wbr, ls, True, tl)
        P.pop_scope()
        P.push_scope()
        w, c0 = load_w(["glaq", "glak", "glav", "glag", "glaa", "mg2"], "w_gla")
        wbr = load_w2(wbr_g, 128, 4, "wbr_gla")
        tl = tail_bufs()
        eq = P.sb("eq", [64, 512], F32)
        ek = P.sb("ek", [64, 512], F32)
        e128 = P.sb("e128", [64, 512], F32)
        qt = P.sb("qt", [64, 4, 128], BF16)
        qh = P.sb("qh", [64, 4, 128], BF16)
        kt = P.sb("kt", [64, 4, 128], BF16)
        kfac = P.sb("kfac", [128, 256], F32)
        gkhat = P.sb("gkhat", [128, 256], BF16)
        vtok = P.sb("gvtok", [128, 512], BF16)
        kv0b = P.sb("kv0b", [64, 512], BF16)
        Sm = P.sb("gSm", [128, 512], BF16)
        for s in range(g0, g0 + SG):
            ls = s - g0
            scan(SGs, kvGa, decGa, s, "g")
            sp = gla_decay_common(B, hT, ls, w, boff["glaa"][0] - c0, wlr17)
            bC64, bC128 = B.bank(), B.bank()
            for h in range(4):
                for bb, tri in ((bC64, "triL64"), (bC128, "triL128")):
                    P.op("tensor", lambda e: e.matmul(bb.ap()[0:64, h * 128:(h + 1) * 128], sp.ap()[:, h * 64:(h + 1) * 64],
                                                      B.c[tri].ap(), start=True, stop=True), reads=[sp, B.c[tri]], writes=[bb])
            P.op("scalar", lambda e: e.activation(eq.ap(), bC64.ap()[0:64, :], AF.Exp, scale=-1.0 / 16), reads=[bC64], writes=[eq])
            P.op("scalar", lambda e: e.activation(ek.ap(), bC64.ap()[0:64, :], AF.Exp, scale=1.0 / 16), reads=[bC64], writes=[ek])
            P.op("scalar", lambda e: e.activation(e128.ap(), bC128.ap()[0:64, :], AF.Exp, scale=-1.0 / 16), reads=[bC128], writes=[e128])
            bq = B.bank()
            for h in range(4):
                B.proj_fm(bq.ap()[0:64, h * 128:(h + 1) * 128], bq, w, boff["glaq"][0] - c0 + h * 64, 64, hT, ls)
            f2 = lambda b: b.ap().rearrange("p h t -> p (h t)")
            P.op("vector", lambda e: e.scalar_tensor_tensor(f2(qt), bq.ap()[0:64, :], 0.125, eq.ap(), ALU.mult, ALU.mult),
                 reads=[bq, eq], writes=[qt])
            P.op("vector", lambda e: e.scalar_tensor_tensor(f2(qh), bq.ap()[0:64, :], 0.125, e128.ap(), ALU.mult, ALU.mult),
                 reads=[bq, e128], writes=[qh])
            bk = B.bank()
            for h in range(4):
                B.proj_fm(bk.ap()[0:64, h * 128:(h + 1) * 128], bk, w, boff["glak"][0] - c0 + h * 64, 64, hT, ls)
            P.op("vector", lambda e: e.tensor_tensor(f2(kt), bk.ap()[0:64, :], ek.ap(), ALU.mult), reads=[bk, ek], writes=[kt])
            bd = B.bank()
            P.op("tensor", lambda e: e.matmul(bd.ap()[:, 0:256], B.c["triU64"].ap(), sp.ap(), start=True, stop=True),
                 reads=[B.c["triU64"], sp], writes=[bd])
            P.op("scalar", lambda e: e.activation(kfac.ap(), bd.ap()[:, 0:256], AF.Exp, scale=-1.0 / 16.0), reads=[bd], writes=[kfac])
            bkt = B.bank()
            B.proj_tm(bkt.ap()[:, 0:256], bkt, w, boff["glak"][0] - c0, 256, hT, ls)
            P.op("vector", lambda e: e.tensor_tensor(gkhat.ap(), bkt.ap()[:, 0:256], kfac.ap(), ALU.mult),
                 reads=[bkt, kfac], writes=[gkhat])
            bv = B.bank()
            B.proj_tm(bv.ap(), bv, w, boff["glav"][0] - c0, 512, hT, ls)
            P.op("scalar", lambda e: e.activation(vtok.ap(), bv.ap(), AF.Copy), reads=[bv], writes=[vtok])
            b0 = B.bank()
            for h in range(4):
                P.op("tensor", lambda e: e.matmul(b0.ap()[0:64, h * 128:(h + 1) * 128], gkhat.ap()[0:64, h * 64:(h + 1) * 64],
                                                  vtok.ap()[0:64, h * 128:(h + 1) * 128], start=True, stop=True),
                     reads=[gkhat, vtok], writes=[b0])
            P.op("scalar", lambda e: e.activation(kv0b.ap(), b0.ap()[0:64, :], AF.Copy), reads=[b0], writes=[kv0b])
            bS = B.bank()
            for h in range(4):
                P.op("tensor", lambda e: e.matmul(bS.ap()[:, h * 128:(h + 1) * 128], kt.ap()[:, h, :], qt.ap()[:, h, :],
                                                  start=True, stop=True), reads=[kt, qt], writes=[bS])
            P.op("vector", lambda e: e.tensor_tensor(Sm.ap(), bS.ap(), B.c["DTgla"].ap().rearrange("p h i -> p (h i)"),
                                                     ALU.mult), reads=[bS, B.c["DTgla"]], writes=[Sm])
            bO = B.bank()
            for h in range(4):
                o_ap = bO.ap()[:, h * 128:(h + 1) * 128]
                P.op("tensor", lambda e: e.matmul(o_ap, vtok.ap()[:, h * 128:(h + 1) * 128], Sm.ap()[:, h * 128:(h + 1) * 128],
                                                  start=True, stop=False), reads=[vtok, Sm], writes=[bO])
                P.op("tensor", lambda e: e.matmul(o_ap, capb.ap()[:, h, :], qh.ap()[:, h, :], start=False, stop=False),
                     reads=[capb, qh], writes=[bO])
                P.op("tensor", lambda e: e.matmul(bO.ap()[:, h * 128 + 64:(h + 1) * 128], kv0b.ap()[:, h * 128:(h + 1) * 128],
                                                  qt.ap()[:, h, 64:128], start=False, stop=True), reads=[kv0b, qt], writes=[bO])
            tail(bO, w, c0, "glag", "mg2", wbr, ls, False, tl)
        P.pop_scope()
        dsa_stage(B, cfg, g0, locals())
        P.push_scope()
        wo = load_w2(wout_d, 128, 8, "w_out")
        xt = P.sb("f_xt", [128, 1024], F32)
        fj = P.sb("f_junk", [128, 512], BF16)
        fst = P.sb("f_st", [128, 8], F32)
        ft = P.sb("f_t", [128, 1024], F32)
        fo = P.sb("f_o", [128, 1024], F32)
        for s in range(g0, g0 + SG):
            ls = s - g0
            P.dma(xt.ap(), x[s], writes=[xt])
            bh = [B.bank(), B.bank()]
            for hf in range(2):
                for nc in range(8):
                    P.op("tensor", lambda e: e.matmul(bh[hf].ap(), ysum.ap()[:, ls, nc, :], wo.ap()[:, nc, hf * 512:(hf + 1) * 512],
                                                      start=(nc == 0), stop=(nc == 7)), reads=[ysum, wo], writes=[bh[hf]])
                P.op("scalar", lambda e: e.activation(fj.ap(), bh[hf].ap(), AF.Square, accum_out=fst.ap()[:, hf:hf + 1]),
                     reads=[bh[hf]], writes=[fj, fst])
            P.op("vector", lambda e: e.tensor_tensor(fst.ap()[:, 2:3], fst.ap()[:, 0:1], fst.ap()[:, 1:2], ALU.add), reads=[fst], writes=[fst])
            P.op("vector", lambda e: e.tensor_scalar(fst.ap()[:, 3:4], fst.ap()[:, 2:3], 1.0 / D, RMS_EPS, ALU.mult, ALU.add),
                 reads=[fst], writes=[fst])
            P.op("scalar", lambda e: e.activation(fst.ap()[:, 4:5], fst.ap()[:, 3:4], AF.Sqrt), reads=[fst], writes=[fst])
            P.op("vector", lambda e: e.reciprocal(fst.ap()[:, 5:6], fst.ap()[:, 4:5]), reads=[fst], writes=[fst])
            for hf in range(2):
                sl = slice(hf * 512, (hf + 1) * 512)
                P.op("vector", lambda e: e.scalar_tensor_tensor(ft.ap()[:, sl], bh[hf].ap(), fst.ap()[:, 5:6], B.GP.ap()[:, sl],
                                                                ALU.mult, ALU.mult), reads=[bh[hf], fst, B.GP], writes=[ft])
            P.op("gpsimd", lambda e: e.tensor_tensor(fo.ap(), ft.ap(), xt.ap(), ALU.add), reads=[ft, xt], writes=[fo])
            P.dma(xo[s], fo.ap(), reads=[fo])
        P.pop_scope()
    P.finish()
    return B


def dsa_stage(B, cfg, g0, env):
    P = B.P
    TPC, T, NCORE, SG = cfg.TPC, cfg.T, cfg.NCORE, cfg.SG
    GK, BLK, NB, BPG, NIT = cfg.GK, cfg.BLK, cfg.NB, cfg.BPG, cfg.NIT
    hT, ysum, dsaout, boff = env["hT"], env["ysum"], env["dsaout"], env["boff"]
    KTa, Va, IKa, pen, ident4 = env["KTa"], env["Va"], env["IKa"], env["pen"], env["ident4"]
    cosD, sinD = env["cosD"], env["sinD"]
    P.push_scope()
    w, c0 = env["load_w"](["dsaq", "dsaq_sw", "idxq", "idxq_sw", "idxw", "dsag"], "w_dsa")
    NMAX = TPC * GK
    scores = P.sb("scores", [128, NMAX], F32)
    CH = min(cfg.CH, NMAX)
    junk = P.sb("cjunk", [128, CH], BF16)
    QT = P.sb("QT", [128, 4, 128], BF16)
    IQ = P.sb("IQ", [64, 4, 128], BF16)
    wq = P.sb("wq", [128, 4], F32)
    rl = [P.sb(f"rl{i}", [128, BLK], F32) for i in range(2)]
    ikb = [P.sb(f"ikb{i}", [64, NB, 128], BF16) for i in range(2)]
    ktb = [P.sb(f"ktb{i}", [128, NB, 128], BF16) for i in range(2)]
    vbk = [P.sb(f"vbk{i}", [128, NB, 2, 128], BF16) for i in range(2)]
    for v in vbk:
        P.op("vector", lambda e: e.memset(v.ap(), 1.0), writes=[v])
    nbmax = TPC * BPG
    mx = P.sb("mx", [128, nbmax], F32)
    mn = P.sb("mn", [128, nbmax], F32)
    wall = P.sb("wall", [128, 32], F32)
    cntc = P.sb("cntc", [128, 64], F32)
    cnta = P.sb("cnta", [128, 64], F32)
    junk2 = P.sb("cjunk2", [128, CH], BF16)
    bst = P.sb("bst", [128, 16], F32)
    mlo = P.sb("mlo", [128, BLK], BF16)
    mhi = P.sb("mhi", [128, BLK], BF16)
    band = P.sb("band", [128, BLK], BF16)
    cum = [P.sb(f"cum{i}", [128, BLK], F32) for i in range(2)]
    tsel = P.sb("tsel", [128, BLK], BF16)
    mb = [P.sb(f"mb{i}", [128, BLK], BF16) for i in range(2)]
    PT = [P.sb(f"PT{i}", [128, 512], BF16) for i in range(4)]
    rc = P.sb("rc", [64, 512], F32)
    on = P.sb("on", [64, 512], BF16)
    sgd = P.sb("sgd", [64, 8, 128], BF16)
    acc = [B.banks[6], B.banks[7]]
    saved_i = B.bank_i
    rot = {"i": 0}

    def bank6():
        b = B.banks[rot["i"] % 6]
        rot["i"] += 1
        return b
    B_bank = B.bank
    B.bank = bank6
    col = lambda i: bst.ap()[:, i:i + 1]

    for s in range(g0, g0 + SG):
        ls = s - g0
        t0 = s * 128
        bx, bs = bank6(), bank6()
        for j in range(4):
            B.proj_fm(bx.ap()[:, j * 128:(j + 1) * 128], bx, w, boff["dsaq"][0] - c0 + j * 128, 128, hT, ls)
            B.proj_fm(bs.ap()[:, j * 128:(j + 1) * 128], bs, w, boff["dsaq_sw"][0] - c0 + j * 128, 128, hT, ls)
        B.rope_fm(QT.ap(), QT, bx, bs, 128, 512, cosD, sinD, t0)
        bx, bs = bank6(), bank6()
        for h in range(4):
            B.proj_fm(bx.ap()[0:64, h * 128:(h + 1) * 128], bx, w, boff["idxq"][0] - c0 + h * 64, 64, hT, ls)
            B.proj_fm(bs.ap()[0:64, h * 128:(h + 1) * 128], bs, w, boff["idxq_sw"][0] - c0 + h * 64, 64, hT, ls)
        B.rope_fm(IQ.ap(), IQ, bx, bs, 64, 512, cosD, sinD, t0)
        bw = bank6()
        B.proj_tm(bw.ap()[:, 0:4], bw, w, boff["idxw"][0] - c0, 4, hT, ls)
        P.op("vector", lambda e: e.tensor_scalar(wq.ap(), bw.ap()[:, 0:4], 0.0625, None, ALU.mult), reads=[bw], writes=[wq])
        nb = (s + 1) * BPG
        n = (s + 1) * GK
        for bi in range(nb):
            gq, blk = bi // BPG, bi % BPG
            ik = ikb[bi % 2]
            P.dma(ik.ap(), IKa[blk * NB:(blk + 1) * NB, :, gq * 128:(gq + 1) * 128].rearrange("j d t -> d j t"), writes=[ik])
            scb = scores.ap()[:, bi * BLK:(bi + 1) * BLK]
            for h in range(4):
                bI = bank6()
                P.op("tensor", lambda e: e.matmul(bI.ap()[:, 0:BLK], IQ.ap()[:, h, :], ik.ap().rearrange("d j t -> d (j t)"),
                                                  start=True, stop=True), reads=[IQ, ik], writes=[bI])
                r = rl[h % 2]
                P.op("scalar", lambda e: e.activation(r.ap(), bI.ap()[:, 0:BLK], AF.Relu), reads=[bI], writes=[r])
                if h == 0:
                    P.op("vector", lambda e: e.tensor_scalar(scb, r.ap(), wq.ap()[:, 0:1], None, ALU.mult),
                         reads=[r, wq], writes=[scores])
                else:
                    P.op("vector", lambda e: e.scalar_tensor_tensor(scb, r.ap(), wq.ap()[:, h:h + 1], scb, ALU.mult, ALU.add),
                         reads=[r, wq, scores], writes=[scores])

        P.op("vector", lambda e: e.tensor_reduce(col(8), scores.ap()[:, 0:n], AX.X, ALU.max), reads=[scores], writes=[bst])
        P.op("vector", lambda e: e.tensor_scalar(col(1), col(8), 1.0, None, ALU.add), reads=[bst], writes=[bst])
        P.op("vector", lambda e: e.tensor_reduce(col(9), scores.ap()[:, 0:n], AX.X, ALU.min), reads=[scores, bst], writes=[bst])
        P.op("vector", lambda e: e.tensor_scalar(col(0), col(9), -1.0, None, ALU.add), reads=[bst], writes=[bst])
        for blk in range(BPG):
            bi = s * BPG + blk
            scb = scores.ap()[:, bi * BLK:(bi + 1) * BLK]
            P.op("vector", lambda e: e.tensor_tensor(scb, scb, pen.ap()[:, blk * BLK:(blk + 1) * BLK], ALU.add),
                 reads=[scores, pen], writes=[scores])
        P.op("vector", lambda e: e.tensor_tensor(col(10), col(1), col(0), ALU.subtract), reads=[bst], writes=[bst])
        P.op("vector", lambda e: e.tensor_scalar(wall.ap(), B.c["pw2"].ap(), col(10), None, ALU.mult),
             reads=[B.c["pw2"], bst], writes=[wall])
        P.op("vector", lambda e: e.tensor_tensor(col(2), col(0), wall.ap()[:, 0:1], ALU.add), reads=[bst, wall], writes=[bst])
        chunks = [(cs, min(n, cs + CH)) for cs in range(0, n, CH)]

        def count(thr_col, dst_col, use_act):
            nd = na = 0
            l_act = 0
            for ci, (cs, ce) in enumerate(chunks):
                if use_act and ci % 2 == 1:
                    P.op("scalar", lambda e: e.activation(junk2.ap()[:, 0:ce - cs], scores.ap()[:, cs:ce], AF.Sign, bias=col(thr_col),
                                                          scale=-1.0, accum_out=cnta.ap()[:, na:na + 1]),
                         reads=[scores, bst], writes=[cnta, junk2])
                    na += 1
                    l_act += ce - cs
                else:
                    P.op("vector", lambda e: e.tensor_scalar(junk.ap()[:, 0:ce - cs], scores.ap()[:, cs:ce], col(thr_col), None,
                                                             ALU.is_ge, ALU.add, accum_out=cntc.ap()[:, nd:nd + 1]),
                         reads=[scores, bst], writes=[cntc, junk])
                    nd += 1
            P.op("vector", lambda e: e.tensor_reduce(col(dst_col), cntc.ap()[:, 0:nd], AX.X, ALU.add),
                 reads=[cntc], writes=[bst])
            if na == 0:
                return dst_col, cfg.TOPK - 0.5
            P.op("vector", lambda e: e.tensor_reduce(col(5), cnta.ap()[:, 0:na], AX.X, ALU.add), reads=[cnta, bst], writes=[bst])
            P.op("vector", lambda e: e.scalar_tensor_tensor(col(11), col(dst_col), 2.0, col(5), ALU.mult, ALU.subtract),
                 reads=[bst], writes=[bst])
            return 11, 2.0 * cfg.TOPK - 1.0 - l_act

        for it in range(NIT):
            ccol, cthr = count(2, 3, True)
            P.op("vector", lambda e: e.scalar_tensor_tensor(col(4), col(ccol), cthr, wall.ap()[:, it:it + 1],
                                                            ALU.is_ge, ALU.mult), reads=[bst, wall], writes=[bst])
            P.op("vector", lambda e: e.tensor_tensor(col(0), col(0), col(4), ALU.add), reads=[bst], writes=[bst])
            if it + 1 < NIT:
                P.op("vector", lambda e: e.tensor_tensor(col(2), col(0), wall.ap()[:, it + 1:it + 2], ALU.add),
                     reads=[bst, wall], writes=[bst])
        P.op("vector", lambda e: e.tensor_tensor(col(1), col(0), wall.ap()[:, NIT - 1:NIT], ALU.add), reads=[bst, wall], writes=[bst])
        count(1, 6, False)
        P.op("vector", lambda e: e.tensor_scalar(col(7), col(6), -BIGM, cfg.TOPK * BIGM, ALU.mult, ALU.add), reads=[bst], writes=[bst])
        blkst = {}

        def prep_block(bi):
            gq, blk = bi // BPG, bi % BPG
            kt_, vb_ = ktb[bi % 2], vbk[bi % 2]
            P.dma(kt_.ap(), KTa[blk * NB:(blk + 1) * NB, :, gq * 128:(gq + 1) * 128].rearrange("j d t -> d j t"), writes=[kt_])
            for jj in range(NB):
                P.dma(vb_.ap()[:, jj, :, 0:64], Va[blk * NB + jj, gq].rearrange("s (k d) -> s k d", k=2), writes=[vb_])
            scb = scores.ap()[:, bi * BLK:(bi + 1) * BLK]
            m = mb[bi % 2]
            cm, cprev = cum[bi % 2], cum[(bi + 1) % 2]
            P.op("vector", lambda e: e.tensor_scalar(mlo.ap(), scb, col(0), BIGM, ALU.is_ge, ALU.mult), reads=[scores, bst], writes=[mlo])
            P.op("vector", lambda e: e.tensor_scalar(mhi.ap(), scb, col(1), BIGM, ALU.is_ge, ALU.mult), reads=[scores, bst], writes=[mhi])
            P.op("vector", lambda e: e.tensor_tensor(band.ap(), mlo.ap(), mhi.ap(), ALU.subtract), reads=[mlo, mhi], writes=[band])
            P.op("vector", lambda e: e.tensor_tensor_scan(cm.ap(), band.ap(), band.ap(),
                                                          (0.0 if bi == 0 else cprev.ap()[:, BLK - 1:BLK]), ALU.add, ALU.max),
                 reads=[band] + ([] if bi == 0 else [cprev]), writes=[cm])
            P.op("vector", lambda e: e.scalar_tensor_tensor(tsel.ap(), cm.ap(), col(7), band.ap(), ALU.is_le, ALU.mult),
                 reads=[cm, bst, band], writes=[tsel])
            P.op("vector", lambda e: e.scalar_tensor_tensor(m.ap(), tsel.ap(), -BIGM, mhi.ap(), ALU.add, ALU.add),
                 reads=[tsel, mhi], writes=[m])
            blkst[bi] = (kt_, vb_, m)

        steps = [(bi, jj) for bi in range(nb) for jj in range(NB)]

        def emit_logits(si):
            bi, jj = steps[si]
            kt_, vb_, m = blkst[bi]
            bLs = [bank6(), bank6()]
            for kvn in range(2):
                P.op("tensor", lambda e: e.matmul(bLs[kvn].ap(), kt_.ap()[kvn * 64:(kvn + 1) * 64, jj, :],
                                                  QT.ap()[kvn * 64:(kvn + 1) * 64].rearrange("p j t -> p (j t)"),
                                                  start=True, stop=False), reads=[kt_, QT], writes=[bLs[kvn]])
            pts = []
            for kvn in range(2):
                P.op("tensor", lambda e: e.matmul(bLs[kvn].ap(), m.ap()[:, jj * 128:(jj + 1) * 128],
                                                  ident4.ap().rearrange("p j t -> p (j t)"), start=False, stop=True),
                     reads=[m, ident4], writes=[bLs[kvn]])
                pt = PT[(2 * si + kvn) % 4]
                P.op("scalar", lambda e: e.activation(pt.ap(), bLs[kvn].ap(), AF.Exp, scale=0.125), reads=[bLs[kvn]], writes=[pt])
                pts.append(pt)
            return pts

        def emit_pv(si, pts):
            bi, jj = steps[si]
            kt_, vb_, m = blkst[bi]
            first = (bi == 0 and jj == 0)
            last = (bi == nb - 1 and jj == NB - 1)
            for kvn in range(2):
                P.op("tensor", lambda e: e.matmul(acc[kvn].ap(), vb_.ap()[:, jj, kvn, :], pts[kvn].ap(), start=first, stop=last),
                     reads=[vb_, pts[kvn]], writes=[acc[kvn]])

        pend = None
        for si in range(len(steps)):
            bi, jj = steps[si]
            if jj == 0:
                prep_block(bi)
            pts = emit_logits(si)
            if pend is not None:
                emit_pv(*pend)
            pend = (si, pts)
        emit_pv(*pend)
        bg = [bank6(), bank6()]
        for hd in range(8):
            B.proj_fm(bg[hd // 4].ap()[0:64, (hd % 4) * 128:(hd % 4 + 1) * 128], bg[hd // 4], w,
                      boff["dsag"][0] - c0 + hd * 64, 64, hT, ls)
        for hf in range(2):
            P.op("scalar", lambda e: e.activation(sgd.ap()[:, hf * 4:(hf + 1) * 4, :],
                                                  bg[hf].ap()[0:64, :].rearrange("p (a b) -> p a b", a=4), AF.Silu),
                 reads=[bg[hf]], writes=[sgd])
        for kvn in range(2):
            P.op("vector", lambda e: e.reciprocal(rc.ap(), acc[kvn].ap()[64:128, :]), reads=[acc[kvn]], writes=[rc])
            P.op("vector", lambda e: e.tensor_tensor(on.ap(), acc[kvn].ap()[0:64, :], rc.ap(), ALU.mult),
                 reads=[acc[kvn], rc], writes=[on])
            P.op("gpsimd", lambda e: e.tensor_tensor(dsaout.ap()[:, ls, kvn * 4:(kvn + 1) * 4, :],
                                                     on.ap().rearrange("p (a b) -> p a b", a=4),
                                                     sgd.ap()[:, kvn * 4:(kvn + 1) * 4, :], ALU.mult),
                 reads=[on, sgd], writes=[dsaout])
    B.bank = B_bank
    P.pop_scope()
    P.push_scope()
    w, c0 = env["load_w"](["mg1"], "w_mg1")
    wbr = env["load_w2"](env["wbr_d"], 64, 8, "wbr_dsa")
    mg = P.sb("d_mg", [128, 512], F32)
    t2 = P.sb("d_t2", [128, 512], F32)
    for s in range(g0, g0 + SG):
        ls = s - g0
        env["merge"](lambda nc, out_ap, bk: [P.op("tensor", lambda e, hd=hd: e.matmul(
            out_ap, wbr.ap()[:, hd, nc * 128:(nc + 1) * 128], dsaout.ap()[:, ls, hd, :],
            start=(hd == 0), stop=(hd == 7)), reads=[wbr, dsaout], writes=[bk]) for hd in range(8)],
            w, 0, ls, False, mg, t2)
    P.pop_scope()


B_CONSTS = ["ident", "triL128", "triL64", "triU64", "DTret", "DTgla", "QDret", "decR", "ropefs", "pw2"]


def host_inputs_B(inp, cfg, l, xcur, resA):
    pos = np.asarray(inp["positions"])[0].reshape(cfg.S // 128, 128)
    st = lambda k: np.stack([np.asarray(r[k]) for r in resA])
    er = lambda w, h: np.ascontiguousarray(np.asarray(w)[l].reshape(h, 512 // h, D).transpose(1, 0, 2))
    shared = {
        "cvec": np.ascontiguousarray(np.asarray(inp["c"])[0].reshape(8, 128).T),
        "adaw": kc_layout(np.asarray(inp["ada_w"])[l]),
        "adab": np.asarray(inp["ada_b"])[l][None, :],
        "pre": np.asarray(inp["pre_norm"])[l][None, :],
        "post": np.asarray(inp["post_norm"])[l][None, :],
        "WB": kc_layout(np.asarray(inp["w_in"])[l][:, b_col_index()]),
        "wbr_r": er(inp["w_br_ret"], 4), "wbr_g": er(inp["w_br_gla"], 4), "wbr_d": er(inp["w_br_dsa"], 8),
        "wout": kc_layout(np.asarray(inp["w_out"])[l]),
        "wlr": np.asarray(inp["gla_w_lr"])[l],
        "blr": np.asarray(inp["gla_b_lr"])[l][None, :],
        "KTa": st("KT"), "Va": st("V"), "IKa": st("IK"),
        "kvRa": st("kvR"), "kvGa": st("kvG"), "decGa": st("decG"),
    }
    shared.update(const_inputs(B_CONSTS))
    maps = []
    for c in range(cfg.NCORE):
        tl = core_tiles(cfg, c)
        m = dict(shared)
        m["x"] = np.ascontiguousarray(xcur[tl])
        m["pos"] = np.ascontiguousarray(pos[tl].reshape(1, cfg.T)).astype(np.int32)
        sel = np.zeros((128, cfg.NCORE), np.float32)
        sel[:, c] = 1.0
        m["sel"] = sel
        kidx = np.arange(cfg.GK)[None, :]
        qidx = (c * 128 + np.arange(128))[:, None]
        import ml_dtypes
        m["pen"] = np.where(kidx > qidx, np.float32(-1e30), np.float32(0.0)).astype(ml_dtypes.bfloat16)
        maps.append(m)
    return maps


_CACHE = {}


def run_layers(inp, cfg):
    x = np.asarray(inp["x"])[0].reshape(cfg.S // 128, 128, D).astype(np.float32)
    cores = list(range(cfg.NCORE))
    for l in range(cfg.DEPTH):
        if "A" not in _CACHE:
            _CACHE["A"] = build_A(cfg)
        inp_l = dict(inp)
        inp_l["x"] = x.reshape(1, cfg.S, D)
        resA = run_bass_kernel_spmd(_CACHE["A"].nc, host_inputs_A(inp_l, cfg, l), core_ids=cores).results
        if "B" not in _CACHE:
            _CACHE["B"] = build_B(cfg)
        resB = run_bass_kernel_spmd(_CACHE["B"].nc, host_inputs_B(inp, cfg, l, x, resA), core_ids=cores).results
        xn = np.empty_like(x)
        for c in cores:
            xn[core_tiles(cfg, c)] = np.asarray(resB[c]["xo"])
        x = xn
    return x.reshape(1, cfg.S, D)


def kernel(**inputs):
    cfg = Cfg()
    return run_layers(inputs, cfg).astype(np.float32)
```

```python
import numpy as np
from contextlib import ExitStack
import concourse.bass as bass
import concourse.mybir as mybir
from concourse.bass_utils import run_bass_kernel_spmd

F32 = mybir.dt.float32
BF16 = mybir.dt.bfloat16
I32 = mybir.dt.int32
ALU = mybir.AluOpType
AF = mybir.ActivationFunctionType
AX = mybir.AxisListType

EPOCH = 20000
NDMA_SEM = 12


class Buf:
    def __init__(self, name, handle=None):
        self.name = name
        self.h = handle
        self.last_w = None
        self.readers = []

    def ap(self):
        return self.h[:]


class _Recorder:
    def __init__(self):
        self.call = None

    def __getattr__(self, name):
        def f(*a, **k):
            assert self.call is None
            self.call = (name, a, k)
            return None
        return f


class Prog:
    ENGS = ("tensor", "vector", "scalar", "gpsimd", "sync")

    def __init__(self, nc):
        self.nc = nc
        self.stack = ExitStack()
        self.stream = {e: [] for e in self.ENGS}
        self.count = {e: 0 for e in self.ENGS}
        self.known = {e: {} for e in self.ENGS}
        self.dma_n = {}
        self.dma_rr = {e: 0 for e in self.ENGS}
        self.nbuf = 0
        self.pending = {e: [] for e in self.ENGS}
        self.stacks = [self.stack]

    def sb(self, name, shape, dtype):
        self.nbuf += 1
        name = f"{name}_u{self.nbuf}"
        h = self.stacks[-1].enter_context(self.nc.sbuf_tensor(name, list(shape), dtype))
        return Buf(name, h)

    def push_scope(self):
        st = ExitStack()
        self.stacks.append(st)

    def pop_scope(self):
        self.barrier()
        self.stacks.pop().close()

    def barrier(self):
        toks = []
        for e in self.ENGS:
            c = self.count[e]
            if c > 0 and e != "sync":
                toks.append((("E", e, (c - 1) // EPOCH), (c - 1) % EPOCH + 1))
        for key, n in self.dma_n.items():
            toks.append((key, 16 * n))
        for e in self.ENGS:
            self.pending[e] = list(toks)

    def _take_pending(self, eng, waits):
        kn = self.known[eng]
        for key, val in self.pending[eng]:
            if key[0] == "E" and key[1] == eng:
                continue
            if kn.get(key, -1) >= val:
                continue
            if any(k == key and v >= val for k, v in waits):
                continue
            kn[key] = val
            waits.append((key, val))
        self.pending[eng] = []

    def ps(self, name, shape, dtype=F32):
        h = self.stack.enter_context(self.nc.psum_tensor(name, list(shape), dtype))
        return Buf(name, h)

    def alias(self, name, buf):
        raise NotImplementedError

    def _deps(self, eng, reads, writes, is_dma):
        deps = {}

        def add(tok, same_ok):
            if tok is None:
                return
            key, val = tok
            if (not is_dma) and key[0] == "E" and key[1] == eng and not same_ok:
                return
            if deps.get(key, -1) < val:
                deps[key] = val

        for b in reads:
            add(b.last_w, True)
        for b in writes:
            add(b.last_w, False)
            for t in b.readers:
                add(t, False)
        kn = self.known[eng]
        out = []
        for key, val in deps.items():
            if key[0] == "E":
                later = [k for k in kn if k[0] == "E" and k[1] == key[1] and k[2] > key[2]]
                if later:
                    continue
            if kn.get(key, -1) >= val:
                continue
            kn[key] = val
            out.append((key, val))
        return out

    def _commit(self, tok, reads, writes):
        for b in writes:
            b.last_w = tok
            b.readers = []
        for b in reads:
            if b in writes:
                continue
            rs = [t for t in b.readers if t[0] != tok[0]]
            rs.append(tok)
            b.readers = rs

    def op(self, eng, fn, reads=(), writes=()):
        reads = list(reads)
        writes = list(writes)
        waits = self._deps(eng, reads, writes, False)
        self._take_pending(eng, waits)
        self.count[eng] += 1
        c = self.count[eng]
        key = ("E", eng, (c - 1) // EPOCH)
        tok = (key, (c - 1) % EPOCH + 1)
        rec = _Recorder()
        fn(rec)
        self.stream[eng].append((waits, rec.call, key))
        self._commit(tok, reads, writes)
        return tok

    def dma(self, out_ap, in_ap, reads=(), writes=(), queue="sync", **kw):
        reads = list(reads)
        writes = list(writes)
        i = self.dma_rr[queue]
        self.dma_rr[queue] = (i + 1) % NDMA_SEM
        key = ("D", queue, i)
        n = self.dma_n.get(key, 0)
        waits = self._deps(queue, reads, writes, True)
        self._take_pending(queue, waits)
        if n > 0 and self.known[queue].get(key, -1) < 16 * n:
            self.known[queue][key] = 16 * n
            waits.append((key, 16 * n))
        self.dma_n[key] = n + 1
        tok = (key, 16 * (n + 1))
        kk = dict(kw)
        kk["out"] = out_ap
        kk["in_"] = in_ap
        self.stream[queue].append((waits, ("dma_start", (), kk), key))
        self._commit(tok, reads, writes)
        return tok

    def coll(self, kind, ins, outs, ranks, reads=(), writes=()):
        reads, writes = list(reads), list(writes)
        queue = "gpsimd"
        key = ("D", "coll", 0)
        n = self.dma_n.get(key, 0)
        waits = self._deps(queue, reads, writes, True)
        self._take_pending(queue, waits)
        if n > 0 and self.known[queue].get(key, -1) < 16 * n:
            self.known[queue][key] = 16 * n
            waits.append((key, 16 * n))
        self.dma_n[key] = n + 1
        tok = (key, 16 * (n + 1))
        call = ("collective_compute", (kind, ALU.bypass), dict(replica_groups=[list(ranks)], ins=list(ins), outs=list(outs)))
        self.stream[queue].append((waits, call, key))
        self._commit(tok, reads, writes)
        return tok

    def finish(self):
        nc = self.nc
        sems = {}

        def sem(key):
            if key not in sems:
                nm = "s_" + "_".join(str(k) for k in key)
                sems[key] = self.stack.enter_context(nc.semaphore(nm))
            return sems[key]

        final_waits = []
        for key, n in self.dma_n.items():
            final_waits.append((key, 16 * n))
        for eng in self.ENGS:
            for waits, fn, key in self.stream[eng]:
                sem(key)
                for k, v in waits:
                    sem(k)

        streams = self.stream

        def emit(eng_name):
            def body(e):
                for waits, fn, key in streams[eng_name]:
                    for k, v in waits:
                        e.wait_ge(sems[k], v)
                    ins = getattr(e, fn[0])(*fn[1], **fn[2])
                    ins.then_inc(sems[key], 16 if key[0] == "D" else 1)
                if eng_name == "sync":
                    for k, v in final_waits:
                        e.wait_ge(sems[k], v)
            return body

        with nc.Block() as block:
            block.sync(emit("sync"))
            block.tensor(emit("tensor"))
            block.vector(emit("vector"))
            block.scalar(emit("scalar"))
            block.gpsimd(emit("gpsimd"))
        while self.stacks:
            self.stacks.pop().close()


D = 1024
IN_SPLITS = (("ret_q", 256), ("ret_k", 256), ("ret_v", 512), ("ret_g", 512),
             ("dsa_q", 512), ("dsa_k", 128), ("dsa_v", 128), ("dsa_g", 512),
             ("idx_q", 256), ("idx_k", 64), ("idx_w", 4),
             ("gla_q", 256), ("gla_k", 256), ("gla_v", 512), ("gla_g", 512), ("gla_a", 16),
             ("merge", 3072))
OFF = {}
_o = 0
for _n, _w in IN_SPLITS:
    OFF[_n] = _o
    _o += _w
IN_WIDTH = _o
RMS_EPS = 1e-6
TOPK = 256
BIGM = 32768.0
LOG_G = [float(np.log1p(-2.0 ** (-5.0 - h))) for h in range(4)]


class Cfg:
    def __init__(self, S=16384, NCORE=8, DEPTH=4, SG=2, NIT=16, CH=512):
        self.S, self.NCORE, self.DEPTH = S, NCORE, DEPTH
        self.TPC = S // 128 // NCORE
        self.T = self.TPC * 128
        self.GK = NCORE * 128
        self.BLK = min(512, self.GK)
        self.NB = self.BLK // 128
        self.BPG = self.GK // self.BLK
        self.SG = min(SG, self.TPC)
        self.NIT = NIT
        self.CH = CH
        self.TOPK = min(256, S // 4)


def host_consts():
    f = np.float32
    j = np.arange(128)[:, None]
    i = np.arange(128)[None, :]
    c = {}
    c["ident"] = np.eye(128, dtype=f)
    same = (j // 64) == (i // 64)
    c["triL128"] = (j <= i).astype(f)
    c["triL64"] = ((j <= i) & same).astype(f)
    c["triU64"] = ((j > i) & same).astype(f)
    ci = np.zeros((128, 4), f)
    ci[:64, 0] = 1
    ci[64:, 1] = 1
    ci[:, 2] = 1
    c["chunkind"] = ci
    lg = np.array(LOG_G, np.float64)
    dt = np.zeros((128, 4, 128), np.float64)
    for h in range(4):
        dt[:, h, :] = np.where(i >= j, np.exp(lg[h] * np.maximum(i - j, 0)), 0.0)
    c["DTret"] = (dt * 0.125).astype(f)
    c["DTgla"] = np.repeat(((j <= i) & same).astype(f)[:, None, :], 4, axis=1)
    qd = np.zeros((128, 4, 128), np.float64)
    for h in range(4):
        qd[:, h, :] = np.exp(lg[h] * (i + 1.0))
    c["QDret"] = (qd * 0.125).astype(f)
    kf = np.zeros((128, 4), np.float64)
    for h in range(4):
        kf[:, h] = np.exp(lg[h] * (127.0 - np.arange(128)))
    c["kfacR"] = kf.astype(f)
    dr = np.zeros((128, 4), np.float64)
    for h in range(4):
        dr[:, h] = np.exp(lg[h] * 128.0)
    c["decR"] = dr.astype(f)
    fr = np.zeros((128, 4), f)
    p = np.arange(128) % 64
    half = 32
    fr_ret = (np.float32(10000.0) ** (-(np.arange(half, dtype=f)) * f(2.0) / f(64))).astype(f)
    fr[:, 0] = fr_ret[p % 32]
    fr[:, 1] = np.where(p < 32, -1.0, 1.0)
    fr_d = (np.float32(500000.0) ** (-(np.arange(8, dtype=f)) * f(2.0) / f(16))).astype(f)
    fr[:, 2] = np.where(p < 16, fr_d[p % 8], 0.0)
    fr[:, 3] = np.where(p < 8, -1.0, np.where(p < 16, 1.0, 0.0))
    c["ropefs"] = fr
    c["pw2"] = np.repeat((2.0 ** -(np.arange(32, dtype=np.float64) + 1.0))[None, :], 128, axis=0).astype(f)
    return c


def swap_cols(lo, rot, width, nheads):
    idx = []
    for h in range(nheads):
        base = lo + h * width
        half = rot // 2
        for d in range(width):
            if d < half:
                idx.append(base + d + half)
            elif d < rot:
                idx.append(base + d - half)
            else:
                idx.append(base + d)
    return idx


A_COLS = [("retk", 256), ("retk_sw", 256), ("retv", 512), ("dsak", 128), ("dsak_sw", 128),
          ("dsav", 128), ("idxk", 64), ("idxk_sw", 64), ("glak", 256), ("glav", 512), ("glaa", 16)]


def col_layout(cols):
    off, o = {}, 0
    for n, w in cols:
        off[n] = (o, w)
        o += w
    return off, o


class Builder:
    def __init__(self, cfg, name):
        self.cfg = cfg
        self.nc = bass.Bass("TRN2", target_bir_lowering=False, name=name)
        self.P = Prog(self.nc)
        self.din = {}
        self.dout = {}
        self.bank_i = 0

    def inp(self, name, shape, dtype=F32):
        t = self.nc.dram_tensor(name, list(shape), dtype, kind="ExternalInput")
        self.din[name] = t
        return t.ap()

    def outp(self, name, shape, dtype=F32):
        t = self.nc.dram_tensor(name, list(shape), dtype, kind="ExternalOutput")
        self.dout[name] = t
        return t.ap()

    def make_banks(self):
        self.banks = [self.P.ps(f"bank{i}", [128, 512], F32) for i in range(8)]

    def bank(self):
        b = self.banks[self.bank_i % 8]
        self.bank_i += 1
        return b

    def load_consts(self, names):
        P = self.P
        hc = host_consts()
        self.c = {}
        self.cb = {}
        for n in names:
            shp = list(hc[n].shape)
            ap = self.inp("c_" + n, shp)
            t = P.sb("sc_" + n, shp, F32)
            P.dma(t.ap(), ap, writes=[t])
            self.c[n] = t
        self.identb = P.sb("identb", [128, 128], BF16)
        P.op("vector", lambda e: e.tensor_copy(self.identb.ap(), self.c["ident"].ap()),
             reads=[self.c["ident"]], writes=[self.identb])
        self.onesb = P.sb("onesb", [128, 128], BF16)
        P.op("vector", lambda e: e.memset(self.onesb.ap(), 1.0), writes=[self.onesb])

    def load_weight(self, dram_ap, c0, ncols, name, kc=8, rows=128):
        P = self.P
        wt = P.sb(name, [rows, kc, ncols], BF16)
        CH = 64 if hasattr(self, "wstage") else 128
        if not hasattr(self, "wstage"):
            self.wstage = [P.sb(f"wstage{i}", [128, 8, CH], F32) for i in range(2)]
            self.wstage_i = 0
        for s in range(0, ncols, CH):
            n = min(CH, ncols - s)
            st = self.wstage[self.wstage_i % 2]
            self.wstage_i += 1
            P.dma(st.ap()[0:rows, 0:kc, 0:n], dram_ap[:, :, c0 + s:c0 + s + n], writes=[st])
            P.op("gpsimd", lambda e, st=st, s=s, n=n: e.tensor_copy(
                wt.ap()[:, :, s:s + n], st.ap()[0:rows, 0:kc, 0:n]), reads=[st], writes=[wt])
        return wt

    def precast(self, src_ap, dst_ap, dbuf, rows, kc, ncols):
        P = self.P
        CH = 256
        st = [P.sb(f"pc_st{i}", [128, 8, CH], F32) for i in range(2)]
        ot = [P.sb(f"pc_ot{i}", [128, 8, CH], BF16) for i in range(2)]
        i = 0
        for s0 in range(0, ncols, CH):
            n = min(CH, ncols - s0)
            a, o = st[i % 2], ot[i % 2]
            P.dma(a.ap()[0:rows, 0:kc, 0:n], src_ap[:, :, s0:s0 + n], writes=[a])
            if i % 2 == 0:
                P.op("scalar", lambda e: e.activation(o.ap()[0:rows, 0:kc, 0:n], a.ap()[0:rows, 0:kc, 0:n], AF.Copy),
                     reads=[a], writes=[o])
            else:
                P.op("vector", lambda e: e.tensor_copy(o.ap()[0:rows, 0:kc, 0:n], a.ap()[0:rows, 0:kc, 0:n]),
                     reads=[a], writes=[o])
            P.dma(dst_ap[:, :, s0:s0 + n], o.ap()[0:rows, 0:kc, 0:n], reads=[o], writes=[dbuf])
            i += 1

    def load_weight_bf16(self, dram_ap, dbuf, c0, ncols, name, kc=8, rows=128):
        P = self.P
        wt = P.sb(name, [rows, kc, ncols], BF16)
        half = max(1, kc // 2)
        for k0 in range(0, kc, half):
            P.dma(wt.ap()[:, k0:k0 + half, :], dram_ap[:, k0:k0 + half, c0:c0 + ncols], reads=[dbuf], writes=[wt])
        return wt

    def prealloc(self, ncore):
        P = self.P
        self.rp = [P.sb(f"rp{i}", [128, 512], F32) for i in range(2)]
        self.a17 = P.sb("a17", [17, 128], BF16)
        P.op("vector", lambda e: e.memset(self.a17.ap(), 1.0), writes=[self.a17])
        self.e1 = P.sb("gl_e1", [128, 256], F32)
        self.sp = P.sb("gl_sp", [128, 256], F32)
        self.kvt = [P.sb(f"kvt{i}", [64, 512], F32) for i in range(2)]
        self.dcg_g = P.sb("dcg_g", [64, ncore, 4], F32)

    def emit_mod(self, cvec_ap, adaw_ap, adab_ap, pre_ap, post_ap, want_gate):
        P = self.P
        self.G = P.sb("G", [128, 1024], F32)
        self.Sh = P.sb("Sh", [128, 1024], F32)
        if want_gate:
            self.GP = P.sb("GP", [128, 1024], F32)
        P.push_scope()
        cv = P.sb("cv", [128, 8], F32)
        P.dma(cv.ap(), cvec_ap, writes=[cv])
        ca = P.sb("ca", [128, 8], F32)
        P.op("scalar", lambda e: e.activation(ca.ap(), cv.ap(), AF.Silu), reads=[cv], writes=[ca])
        cbc = P.sb("cbc", [128, 8, 128], F32)
        P.op("vector", lambda e: e.tensor_copy(cbc.ap(), ca.ap().unsqueeze(2).to_broadcast([128, 8, 128])),
             reads=[ca], writes=[cbc])
        mod = P.sb("mod", [128, 3072], F32)
        bias = P.sb("modb", [128, 3072], F32)
        P.dma(bias.ap(), adab_ap.to_broadcast([128, 3072]), writes=[bias])
        wst = [P.sb(f"adaw{i}", [128, 8, 512], F32) for i in range(2)]
        for ci in range(6):
            w = wst[ci % 2]
            P.dma(w.ap(), adaw_ap[:, :, ci * 512:(ci + 1) * 512], writes=[w])
            bk = self.bank()
            for kc in range(8):
                P.op("tensor", lambda e, bk=bk, w=w, kc=kc: e.matmul(
                    bk.ap(), cbc.ap()[:, kc, :], w.ap()[:, kc, :], start=(kc == 0), stop=(kc == 7)),
                    reads=[cbc, w], writes=[bk])
            P.op("vector", lambda e, bk=bk, ci=ci: e.tensor_tensor(
                mod.ap()[:, ci * 512:(ci + 1) * 512], bk.ap(), bias.ap()[:, ci * 512:(ci + 1) * 512], ALU.add),
                reads=[bk, bias], writes=[mod])
        pre = P.sb("preb", [128, 1024], F32)
        P.dma(pre.ap(), pre_ap.to_broadcast([128, 1024]), writes=[pre])
        P.op("vector", lambda e: e.scalar_tensor_tensor(
            self.G.ap(), mod.ap()[:, 1024:2048], 1.0, pre.ap(), ALU.add, ALU.mult),
            reads=[mod, pre], writes=[self.G])
        P.op("vector", lambda e: e.tensor_copy(self.Sh.ap(), mod.ap()[:, 0:1024]), reads=[mod], writes=[self.Sh])
        if want_gate:
            post = P.sb("postb", [128, 1024], F32)
            P.dma(post.ap(), post_ap.to_broadcast([128, 1024]), writes=[post])
            P.op("vector", lambda e: e.tensor_tensor(self.GP.ap(), mod.ap()[:, 2048:3072], post.ap(), ALU.mult),
                 reads=[mod, post], writes=[self.GP])
        P.pop_scope()

    def emit_ropes(self, pos_ap, specs):
        P = self.P
        outs = []
        for fcol, scol, name in specs:
            outs.append((P.sb(name + "_cos", [128, self.cfg.T], BF16), P.sb(name + "_sin", [128, self.cfg.T], BF16)))
        P.push_scope()
        for (fcol, scol, name), (cb_, sb_) in zip(specs, outs):
            self.emit_rope(pos_ap, fcol, scol, name, cb_, sb_)
        P.pop_scope()
        del self.posf
        return outs

    def emit_rope(self, pos_ap, fcol, scol, name, cosb, sinb):
        P = self.P
        T = self.cfg.T
        fs = self.c["ropefs"]
        if not hasattr(self, "posf"):
            posi = P.sb("posi", [128, T], I32)
            P.dma(posi.ap(), pos_ap.to_broadcast([128, T]), writes=[posi])
            self.posf = P.sb("posf", [128, T], F32)
            P.op("vector", lambda e: e.tensor_copy(self.posf.ap(), posi.ap()), reads=[posi], writes=[self.posf])
            self.rtmp = [P.sb(f"rtmp{i}", [128, T], F32) for i in range(4)]
        ang, kf, r, rd = self.rtmp
        posf = self.posf
        PI = float(np.pi)
        C1 = 6.28125
        C2 = float(2.0 * np.pi - 6.28125)
        MAG = 12582912.0
        P.op("vector", lambda e: e.tensor_scalar(ang.ap(), posf.ap(), fs.ap()[:, fcol:fcol + 1], None, ALU.mult),
             reads=[posf, fs], writes=[ang])

        def reduce_and_sin(src, dst_final, shift):
            dst = rd
            if shift != 0.0:
                P.op("vector", lambda e: e.tensor_scalar(r.ap(), src.ap(), shift, None, ALU.add),
                     reads=[src], writes=[r])
                s2 = r
            else:
                s2 = src
            P.op("vector", lambda e: e.tensor_scalar(kf.ap(), s2.ap(), float(1.0 / (2 * np.pi)), MAG, ALU.mult, ALU.add),
                 reads=[s2], writes=[kf])
            P.op("vector", lambda e: e.tensor_scalar(kf.ap(), kf.ap(), MAG, None, ALU.subtract),
                 reads=[kf], writes=[kf])
            P.op("vector", lambda e: e.scalar_tensor_tensor(dst.ap(), kf.ap(), -C1, s2.ap(), ALU.mult, ALU.add),
                 reads=[kf, s2], writes=[dst])
            P.op("vector", lambda e: e.scalar_tensor_tensor(dst.ap(), kf.ap(), -C2, dst.ap(), ALU.mult, ALU.add),
                 reads=[kf, dst], writes=[dst])
            P.op("vector", lambda e: e.tensor_scalar(dst.ap(), dst.ap(), PI, -PI, ALU.min, ALU.max),
                 reads=[dst], writes=[dst])
            P.op("scalar", lambda e: e.activation(dst_final.ap(), dst.ap(), AF.Sin), reads=[dst], writes=[dst_final])

        reduce_and_sin(ang, sinb, 0.0)
        reduce_and_sin(ang, cosb, float(np.pi / 2))
        P.op("vector", lambda e: e.tensor_scalar(sinb.ap(), sinb.ap(), fs.ap()[:, scol:scol + 1], None, ALU.mult),
             reads=[sinb, fs], writes=[sinb])
        return cosb, sinb

    def emit_norm_T(self, x_tile_ap_dram, hT, slot):
        P = self.P
        if not hasattr(self, "xin"):
            self.xin = [P.sb(f"xin{i}", [128, 1024], F32) for i in range(1)]
            self.xin_i = 0
            self.hjunk = P.sb("hjunk", [128, 1024], BF16)
            self.h1 = P.sb("h1", [128, 1024], F32)
            self.hb = P.sb("hb", [128, 1024], BF16)
            self.nst = P.sb("nst", [128, 4], F32)
        xt = self.xin[0]
        self.xin_i += 1
        nst = self.nst
        P.dma(xt.ap(), x_tile_ap_dram, writes=[xt])
        P.op("scalar", lambda e: e.activation(self.hjunk.ap(), xt.ap(), AF.Square, accum_out=nst.ap()[:, 0:1]),
             reads=[xt], writes=[self.hjunk, nst])
        P.op("vector", lambda e: e.tensor_scalar(nst.ap()[:, 1:2], nst.ap()[:, 0:1], 1.0 / D, RMS_EPS, ALU.mult, ALU.add),
             reads=[nst], writes=[nst])
        P.op("scalar", lambda e: e.activation(nst.ap()[:, 2:3], nst.ap()[:, 1:2], AF.Sqrt), reads=[nst], writes=[nst])
        P.op("vector", lambda e: e.reciprocal(nst.ap()[:, 3:4], nst.ap()[:, 2:3]), reads=[nst], writes=[nst])
        P.op("vector", lambda e: e.scalar_tensor_tensor(self.h1.ap(), xt.ap(), nst.ap()[:, 3:4], self.G.ap(), ALU.mult, ALU.mult),
             reads=[xt, nst, self.G], writes=[self.h1])
        P.op("gpsimd", lambda e: e.tensor_tensor(self.hb.ap(), self.h1.ap(), self.Sh.ap(), ALU.add),
             reads=[self.h1, self.Sh], writes=[self.hb])
        for half in range(2):
            bk = self.bank()
            for q in range(4):
                kc = half * 4 + q
                P.op("tensor", lambda e, bk=bk, q=q, kc=kc: e.matmul(
                    bk.ap()[:, q * 128:(q + 1) * 128], self.hb.ap()[:, kc * 128:(kc + 1) * 128], self.identb.ap(),
                    start=True, stop=True), reads=[self.hb, self.identb], writes=[bk])
            P.op("scalar", lambda e, bk=bk, half=half: e.activation(
                hT.ap()[:, slot, half * 4:half * 4 + 4, :], bk.ap().rearrange("p (a b) -> p a b", a=4), AF.Copy),
                reads=[bk], writes=[hT])

    def proj_fm(self, out_ap, bk, w, c0, m, hT, slot, nslots=1):
        P = self.P
        for kc in range(8):
            if nslots == 1:
                rhs = hT.ap()[:, slot, kc, :]
            else:
                rhs = hT.ap()[:, slot:slot + nslots, kc, :]
            P.op("tensor", lambda e, kc=kc, rhs=rhs: e.matmul(
                out_ap, w.ap()[:, kc, c0:c0 + m], rhs, start=(kc == 0), stop=(kc == 7)),
                reads=[w, hT], writes=[bk])

    def proj_tm(self, out_ap, bk, w, c0, n, hT, slot):
        P = self.P
        for kc in range(8):
            P.op("tensor", lambda e, kc=kc: e.matmul(
                out_ap, hT.ap()[:, slot, kc, :], w.ap()[:, kc, c0:c0 + n], start=(kc == 0), stop=(kc == 7)),
                reads=[w, hT], writes=[bk])

    def rope_fm(self, dst_ap, dst_buf, bx, bs, npart, ncols_ap, cosb, sinb, tok0, scale=None):
        P = self.P
        if not hasattr(self, "rp"):
            self.rp = [P.sb(f"rp{i}", [128, 512], F32) for i in range(2)]
        t0, t1 = self.rp
        nh = ncols_ap // 128
        cosv = cosb.ap()[0:npart, tok0:tok0 + 128].unsqueeze(1).to_broadcast([npart, nh, 128])
        sinv = sinb.ap()[0:npart, tok0:tok0 + 128].unsqueeze(1).to_broadcast([npart, nh, 128])
        v = lambda b: b.ap()[0:npart, 0:ncols_ap].rearrange("p (h t) -> p h t", h=nh)
        P.op("vector", lambda e: e.tensor_tensor(v(t0), v(bx), cosv, ALU.mult), reads=[bx, cosb], writes=[t0])
        P.op("vector", lambda e: e.tensor_tensor(v(t1), v(bs), sinv, ALU.mult), reads=[bs, sinb], writes=[t1])
        if scale is None:
            P.op("vector", lambda e: e.tensor_tensor(dst_ap, v(t0), v(t1), ALU.add), reads=[t0, t1], writes=[dst_buf])
        else:
            P.op("vector", lambda e: e.scalar_tensor_tensor(dst_ap, v(t0), 1.0, v(t1), ALU.mult, ALU.add),
                 reads=[t0, t1], writes=[dst_buf])


def gla_decay_common(B, hT, slot, wA, aoff, wlr17):
    P = B.P
    if not hasattr(B, "a17"):
        B.a17 = P.sb("a17", [17, 128], BF16)
        P.op("vector", lambda e: e.memset(B.a17.ap(), 1.0), writes=[B.a17])
        B.e1 = P.sb("gl_e1", [128, 256], F32)
        B.sp = P.sb("gl_sp", [128, 256], F32)
    bk = B.bank()
    B.proj_fm(bk.ap()[0:16, 0:128], bk, wA, aoff, 16, hT, slot)
    P.op("vector", lambda e: e.tensor_copy(B.a17.ap()[0:16, :], bk.ap()[0:16, 0:128]), reads=[bk], writes=[B.a17])
    bz = B.bank()
    P.op("tensor", lambda e: e.matmul(bz.ap()[:, 0:256], B.a17.ap(), wlr17.ap(), start=True, stop=True),
         reads=[B.a17, wlr17], writes=[bz])
    P.op("scalar", lambda e: e.activation(B.e1.ap(), bz.ap()[:, 0:256], AF.Exp, scale=-1.0), reads=[bz], writes=[B.e1])
    P.op("scalar", lambda e: e.activation(B.sp.ap(), B.e1.ap(), AF.Ln, bias=1.0), reads=[B.e1], writes=[B.sp])
    return B.sp


def load_wlr17(B, wlr_ap, blr_ap):
    P = B.P
    st = P.sb("wlr_st", [17, 256], F32)
    P.dma(st.ap()[0:16, :], wlr_ap, writes=[st])
    P.dma(st.ap()[16:17, :], blr_ap, writes=[st])
    w = P.sb("wlr17", [17, 256], BF16)
    P.op("vector", lambda e: e.tensor_copy(w.ap(), st.ap()), reads=[st], writes=[w])
    return w


def build_A(cfg):
    B = Builder(cfg, "phaseA")
    P = B.P
    TPC, T = cfg.TPC, cfg.T
    aoff, ncolA = col_layout(A_COLS)
    x = B.inp("x", [TPC, 128, 1024])
    pos = B.inp("pos", [1, T], I32)
    cvec = B.inp("cvec", [128, 8])
    adaw = B.inp("adaw", [128, 8, 3072])
    adab = B.inp("adab", [1, 3072])
    pre = B.inp("pre", [1, 1024])
    post = B.inp("post", [1, 1024])
    WA = B.inp("WA", [128, 8, ncolA])
    wlr = B.inp("wlr", [16, 256])
    blr = B.inp("blr", [1, 256])
    oKT = B.outp("KT", [128, T], BF16)
    oV = B.outp("V", [TPC, 128, 128], BF16)
    oIK = B.outp("IK", [64, T], BF16)
    okvR = B.outp("kvR", [TPC, 64, 512])
    okvG = B.outp("kvG", [TPC, 64, 512])
    odecG = B.outp("decG", [TPC, 64, 4])
    omod = B.outp("modv", [3, 128, 1024])
    orope = B.outp("ropev", [4, 128, T], BF16)

    B.make_banks()
    B.load_consts(["ident", "triU64", "chunkind", "kfacR", "ropefs"])
    B.emit_mod(cvec, adaw, adab, pre, post, True)
    (cosR, sinR), (cosD, sinD) = B.emit_ropes(pos, [(0, 1, "rr"), (2, 3, "rd")])
    for i_, t_ in enumerate((B.G, B.Sh, B.GP)):
        P.dma(omod[i_], t_.ap(), reads=[t_])
    for i_, t_ in enumerate((cosR, sinR, cosD, sinD)):
        P.dma(orope[i_], t_.ap(), reads=[t_])
    wA = B.load_weight(WA, 0, ncolA, "wA")
    wlr17 = load_wlr17(B, wlr, blr)
    hT = P.sb("hT", [128, 1, 8, 128], BF16)

    kTb = P.sb("kTb", [64, 4, 128], BF16)
    khat = P.sb("khat", [128, 4, 64], BF16)
    vtok = P.sb("vtok", [128, 512], BF16)
    kvs = [P.sb(f"kvs{i}", [64, 512], F32) for i in range(2)]
    ktb = P.sb("ktb", [128, 128], BF16)
    vb = P.sb("vb", [128, 128], BF16)
    ikb = P.sb("ikb", [64, 128], BF16)
    kfac = P.sb("kfac", [128, 256], F32)
    gkhat = P.sb("gkhat", [128, 256], BF16)
    gvtok = P.sb("gvtok", [128, 512], BF16)
    dec = P.sb("dec", [64, 4, 4], F32)
    kv1s = P.sb("kv1s", [64, 512], F32)
    decT = P.sb("decT", [64, 4], F32)
    ident = B.c["ident"]

    for s in range(TPC):
        t0 = s * 128
        B.emit_norm_T(x[s], hT, 0)
        bx, bs = B.bank(), B.bank()
        for h in range(4):
            B.proj_fm(bx.ap()[0:64, h * 128:(h + 1) * 128], bx, wA, aoff["retk"][0] + h * 64, 64, hT, 0)
            B.proj_fm(bs.ap()[0:64, h * 128:(h + 1) * 128], bs, wA, aoff["retk_sw"][0] + h * 64, 64, hT, 0)
        B.rope_fm(kTb.ap(), kTb, bx, bs, 64, 512, cosR, sinR, t0)
        bt = B.bank()
        for h in range(4):
            P.op("tensor", lambda e, h=h: e.matmul(bt.ap()[:, h * 64:(h + 1) * 64], kTb.ap()[:, h, :],
                                                   B.identb.ap()[0:64, 0:64], start=True, stop=True),
                 reads=[kTb, B.identb], writes=[bt])
        P.op("vector", lambda e: e.tensor_tensor(
            khat.ap(), bt.ap()[:, 0:256].rearrange("p (h d) -> p h d", h=4),
            B.c["kfacR"].ap().unsqueeze(2).to_broadcast([128, 4, 64]), ALU.mult),
            reads=[bt, B.c["kfacR"]], writes=[khat])
        bv = B.bank()
        B.proj_tm(bv.ap(), bv, wA, aoff["retv"][0], 512, hT, 0)
        P.op("scalar", lambda e: e.activation(vtok.ap(), bv.ap(), AF.Copy), reads=[bv], writes=[vtok])
        bkv = B.bank()
        for h in range(4):
            P.op("tensor", lambda e, h=h: e.matmul(bkv.ap()[0:64, h * 128:(h + 1) * 128], khat.ap()[:, h, :],
                                                   vtok.ap()[:, h * 128:(h + 1) * 128], start=True, stop=True),
                 reads=[khat, vtok], writes=[bkv])
        kv = kvs[0]
        P.op("scalar", lambda e: e.activation(kv.ap(), bkv.ap()[0:64, :], AF.Copy), reads=[bkv], writes=[kv])
        P.dma(okvR[s], kv.ap(), reads=[kv])
        bx, bs = B.bank(), B.bank()
        B.proj_fm(bx.ap()[:, 0:128], bx, wA, aoff["dsak"][0], 128, hT, 0)
        B.proj_fm(bs.ap()[:, 0:128], bs, wA, aoff["dsak_sw"][0], 128, hT, 0)
        B.rope_fm(ktb.ap().unsqueeze(1), ktb, bx, bs, 128, 128, cosD, sinD, t0)
        P.dma(oKT[:, t0:t0 + 128], ktb.ap(), reads=[ktb])
        bv = B.bank()
        B.proj_tm(bv.ap()[:, 0:128], bv, wA, aoff["dsav"][0], 128, hT, 0)
        P.op("scalar", lambda e: e.activation(vb.ap(), bv.ap()[:, 0:128], AF.Copy), reads=[bv], writes=[vb])
        P.dma(oV[s], vb.ap(), reads=[vb])
        bx, bs = B.bank(), B.bank()
        B.proj_fm(bx.ap()[0:64, 0:128], bx, wA, aoff["idxk"][0], 64, hT, 0)
        B.proj_fm(bs.ap()[0:64, 0:128], bs, wA, aoff["idxk_sw"][0], 64, hT, 0)
        B.rope_fm(ikb.ap().unsqueeze(1), ikb, bx, bs, 64, 128, cosD, sinD, t0)
        P.dma(oIK[:, t0:t0 + 128], ikb.ap(), reads=[ikb])
        sp = gla_decay_common(B, hT, 0, wA, aoff["glaa"][0], wlr17)
        bd = B.bank()
        P.op("tensor", lambda e: e.matmul(bd.ap()[:, 0:256], B.c["triU64"].ap(), sp.ap(), start=True, stop=True),
             reads=[B.c["triU64"], sp], writes=[bd])
        P.op("scalar", lambda e: e.activation(kfac.ap(), bd.ap()[:, 0:256], AF.Exp, scale=-1.0 / 16.0),
             reads=[bd], writes=[kfac])
        bk = B.bank()
        B.proj_tm(bk.ap()[:, 0:256], bk, wA, aoff["glak"][0], 256, hT, 0)
        P.op("vector", lambda e: e.tensor_tensor(gkhat.ap(), bk.ap()[:, 0:256], kfac.ap(), ALU.mult),
             reads=[bk, kfac], writes=[gkhat])
        bv = B.bank()
        B.proj_tm(bv.ap(), bv, wA, aoff["glav"][0], 512, hT, 0)
        P.op("scalar", lambda e: e.activation(gvtok.ap(), bv.ap(), AF.Copy), reads=[bv], writes=[gvtok])
        b0, b1 = B.bank(), B.bank()
        for ch, bb in ((0, b0), (1, b1)):
            for h in range(4):
                P.op("tensor", lambda e, h=h, ch=ch, bb=bb: e.matmul(
                    bb.ap()[0:64, h * 128:(h + 1) * 128], gkhat.ap()[ch * 64:(ch + 1) * 64, h * 64:(h + 1) * 64],
                    gvtok.ap()[ch * 64:(ch + 1) * 64, h * 128:(h + 1) * 128], start=True, stop=True),
                    reads=[gkhat, gvtok], writes=[bb])
        bs_ = B.bank()
        for h in range(4):
            P.op("tensor", lambda e, h=h: e.matmul(bs_.ap()[0:64, h * 4:(h + 1) * 4], sp.ap()[:, h * 64:(h + 1) * 64],
                                                   B.c["chunkind"].ap(), start=True, stop=True),
                 reads=[sp, B.c["chunkind"]], writes=[bs_])
        P.op("scalar", lambda e: e.activation(dec.ap(), bs_.ap()[0:64, 0:16].rearrange("p (h c) -> p h c", h=4),
                                              AF.Exp, scale=-1.0 / 16.0), reads=[bs_], writes=[dec])
        P.op("scalar", lambda e: e.activation(kv1s.ap(), b1.ap()[0:64, :], AF.Copy), reads=[b1], writes=[kv1s])
        kv = kvs[1]
        for h in range(4):
            P.op("vector", lambda e, h=h: e.scalar_tensor_tensor(
                kv.ap()[:, h * 128:(h + 1) * 128], b0.ap()[0:64, h * 128:(h + 1) * 128], dec.ap()[:, h, 1:2],
                kv1s.ap()[:, h * 128:(h + 1) * 128], ALU.mult, ALU.add),
                reads=[b0, dec, kv1s], writes=[kv])
        P.dma(okvG[s], kv.ap(), reads=[kv])
        P.op("vector", lambda e: e.tensor_copy(decT.ap(), dec.ap()[:, :, 2]), reads=[dec], writes=[decT])
        P.dma(odecG[s], decT.ap(), reads=[decT])
    P.finish()
    return B


def prep_common(inputs, cfg, l, core):
    raise NotImplementedError


def a_col_index():
    ar = np.arange
    idx = {
        "retk": OFF["ret_k"] + ar(256), "retk_sw": np.array(swap_cols(OFF["ret_k"], 64, 64, 4)),
        "retv": OFF["ret_v"] + ar(512),
        "dsak": OFF["dsa_k"] + ar(128), "dsak_sw": np.array(swap_cols(OFF["dsa_k"], 16, 64, 2)),
        "dsav": OFF["dsa_v"] + ar(128),
        "idxk": OFF["idx_k"] + ar(64), "idxk_sw": np.array(swap_cols(OFF["idx_k"], 16, 64, 1)),
        "glak": OFF["gla_k"] + ar(256), "glav": OFF["gla_v"] + ar(512), "glaa": OFF["gla_a"] + ar(16),
    }
    return np.concatenate([idx[n] for n, _ in A_COLS])


def kc_layout(w):
    n = w.shape[1]
    return np.ascontiguousarray(w.reshape(8, 128, n).transpose(1, 0, 2))


def core_tiles(cfg, c):
    return [k * cfg.NCORE + c for k in range(cfg.TPC)]


def const_inputs(names):
    hc = host_consts()
    return {"c_" + n: hc[n] for n in names}


def host_inputs_A(inp, cfg, l):
    x = np.asarray(inp["x"])[0].reshape(cfg.S // 128, 128, D)
    pos = np.asarray(inp["positions"])[0].reshape(cfg.S // 128, 128)
    shared = {
        "cvec": np.ascontiguousarray(np.asarray(inp["c"])[0].reshape(8, 128).T),
        "adaw": kc_layout(np.asarray(inp["ada_w"])[l]),
        "adab": np.asarray(inp["ada_b"])[l][None, :],
        "pre": np.asarray(inp["pre_norm"])[l][None, :],
        "post": np.asarray(inp["post_norm"])[l][None, :],
        "WA": kc_layout(np.asarray(inp["w_in"])[l][:, a_col_index()]),
        "wlr": np.asarray(inp["gla_w_lr"])[l],
        "blr": np.asarray(inp["gla_b_lr"])[l][None, :],
    }
    shared.update(const_inputs(["ident", "triU64", "chunkind", "kfacR", "ropefs"]))
    maps = []
    for c in range(cfg.NCORE):
        tl = core_tiles(cfg, c)
        m = dict(shared)
        m["x"] = np.ascontiguousarray(x[tl])
        m["pos"] = np.ascontiguousarray(pos[tl].reshape(1, cfg.T)).astype(np.int32)
        maps.append(m)
    return maps


def np_inputs(S, seed=0, depth=4):
    r = np.random.RandomState(seed)
    f = np.float32
    n = lambda *s: r.randn(*s).astype(f)
    Dm = D
    return {
        "x": n(1, S, Dm), "c": n(1, Dm),
        "positions": (np.arange(S, dtype=np.int32)[None, :] + np.int32(r.randint(0, 4096))),
        "ada_w": n(depth, Dm, 3 * Dm) * f(0.1 * Dm ** -0.5), "ada_b": n(depth, 3 * Dm) * f(0.02),
        "pre_norm": 1 + f(0.05) * n(depth, Dm), "post_norm": 1 + f(0.05) * n(depth, Dm),
        "w_in": n(depth, Dm, IN_WIDTH) * f(Dm ** -0.5),
        "gla_w_lr": n(depth, 16, 256) * f(0.25), "gla_b_lr": f(0.1) * n(depth, 256),
        "w_br_ret": n(depth, 512, Dm) * f(512 ** -0.5), "w_br_dsa": n(depth, 512, Dm) * f(512 ** -0.5),
        "w_br_gla": n(depth, 512, Dm) * f(512 ** -0.5), "w_out": n(depth, Dm, Dm) * f(Dm ** -0.5),
    }


B_COLS = [("retq", 256), ("retq_sw", 256), ("retk", 256), ("retk_sw", 256), ("retv", 512), ("retg", 512),
          ("mg0", 1024),
          ("glaq", 256), ("glak", 256), ("glav", 512), ("glag", 512), ("glaa", 16), ("mg2", 1024),
          ("dsaq", 512), ("dsaq_sw", 512), ("idxq", 256), ("idxq_sw", 256), ("idxw", 4), ("dsag", 512),
          ("mg1", 1024)]


def b_col_index():
    ar = np.arange
    pair = np.concatenate([np.concatenate([ar(64) + j * 64, ar(64) + (j + 4) * 64]) for j in range(4)])
    dq = OFF["dsa_q"] + ar(512)
    dq_sw = np.array(swap_cols(OFF["dsa_q"], 16, 64, 8))
    idx = {
        "retq": OFF["ret_q"] + ar(256), "retq_sw": np.array(swap_cols(OFF["ret_q"], 64, 64, 4)),
        "retk": OFF["ret_k"] + ar(256), "retk_sw": np.array(swap_cols(OFF["ret_k"], 64, 64, 4)),
        "retv": OFF["ret_v"] + ar(512), "retg": OFF["ret_g"] + ar(512),
        "mg0": OFF["merge"] + ar(1024), "mg1": OFF["merge"] + 1024 + ar(1024), "mg2": OFF["merge"] + 2048 + ar(1024),
        "glaq": OFF["gla_q"] + ar(256), "glak": OFF["gla_k"] + ar(256), "glav": OFF["gla_v"] + ar(512),
        "glag": OFF["gla_g"] + ar(512), "glaa": OFF["gla_a"] + ar(16),
        "dsaq": dq[pair], "dsaq_sw": dq_sw[pair],
        "idxq": OFF["idx_q"] + ar(256), "idxq_sw": np.array(swap_cols(OFF["idx_q"], 16, 64, 4)),
        "idxw": OFF["idx_w"] + ar(4), "dsag": OFF["dsa_g"] + ar(512),
    }
    return np.concatenate([idx[n] for n, _ in B_COLS])


def build_B(cfg):
    B = Builder(cfg, "phaseB")
    P = B.P
    TPC, T, NCORE, SG = cfg.TPC, cfg.T, cfg.NCORE, cfg.SG
    GK, BLK, NB, BPG = cfg.GK, cfg.BLK, cfg.NB, cfg.BPG
    boff, ncolB = col_layout(B_COLS)
    x = B.inp("x", [TPC, 128, 1024])
    modv = B.inp("modv", [3, 128, 1024])
    ropev = B.inp("ropev", [4, 128, T], BF16)
    WB = B.inp("WB", [128, 8, ncolB])
    wbr_r = B.inp("wbr_r", [128, 4, 1024])
    wbr_g = B.inp("wbr_g", [128, 4, 1024])
    wbr_d = B.inp("wbr_d", [64, 8, 1024])
    wout_d = B.inp("wout", [128, 8, 1024])
    wlr = B.inp("wlr", [16, 256])
    blr = B.inp("blr", [1, 256])
    KTa = B.inp("KTa", [NCORE, 128, T], BF16)
    Va = B.inp("Va", [NCORE, TPC, 128, 128], BF16)
    IKa = B.inp("IKa", [NCORE, 64, T], BF16)
    kvRa = B.inp("kvRa", [NCORE, TPC, 64, 512])
    kvGa = B.inp("kvGa", [NCORE, TPC, 64, 512])
    decGa = B.inp("decGa", [NCORE, TPC, 64, 4])
    sel_d = B.inp("sel", [128, NCORE])
    pen_d = B.inp("pen", [128, GK], BF16)
    xo = B.outp("xo", [TPC, 128, 1024])

    B.make_banks()
    B.load_consts(B_CONSTS)
    sel = P.sb("sel", [128, NCORE], F32)
    P.dma(sel.ap(), sel_d, writes=[sel])
    pen = P.sb("pen", [128, GK], BF16)
    P.dma(pen.ap(), pen_d, writes=[pen])
    ident4 = P.sb("ident4", [128, 4, 128], BF16)
    P.op("vector", lambda e: e.tensor_copy(ident4.ap(), B.c["ident"].ap().unsqueeze(1).to_broadcast([128, 4, 128])),
         reads=[B.c["ident"]], writes=[ident4])
    B.G = P.sb("G", [128, 1024], F32)
    B.Sh = P.sb("Sh", [128, 1024], F32)
    B.GP = P.sb("GP", [128, 1024], F32)
    for i_, t_ in enumerate((B.G, B.Sh, B.GP)):
        P.dma(t_.ap(), modv[i_], writes=[t_])
    ropes_ = [P.sb(f"rope{i_}", [128, T], BF16) for i_ in range(4)]
    for i_, t_ in enumerate(ropes_):
        P.dma(t_.ap(), ropev[i_], writes=[t_])
    cosR, sinR, cosD, sinD = ropes_
    wlr17 = load_wlr17(B, wlr, blr)
    nc_ = B.nc
    WBb = nc_.dram_tensor("WBb", [128, 8, ncolB], BF16, kind="Internal").ap()
    wbr_rb = nc_.dram_tensor("wbr_rb", [128, 4, 1024], BF16, kind="Internal").ap()
    wbr_gb = nc_.dram_tensor("wbr_gb", [128, 4, 1024], BF16, kind="Internal").ap()
    wbr_db = nc_.dram_tensor("wbr_db", [64, 8, 1024], BF16, kind="Internal").ap()
    woutb = nc_.dram_tensor("woutb", [128, 8, 1024], BF16, kind="Internal").ap()
    wbuf = Buf("wscratch")
    P.push_scope()
    B.precast(WB, WBb, wbuf, 128, 8, ncolB)
    B.precast(wbr_r, wbr_rb, wbuf, 128, 4, 1024)
    B.precast(wbr_g, wbr_gb, wbuf, 128, 4, 1024)
    B.precast(wbr_d, wbr_db, wbuf, 64, 8, 1024)
    B.precast(wout_d, woutb, wbuf, 128, 8, 1024)
    P.pop_scope()
    wbr_r, wbr_g, wbr_d, wout_d = wbr_rb, wbr_gb, wbr_db, woutb
    B.prealloc(NCORE)
    SR = P.sb("SR", [64, 4, 128], F32)
    SGs = P.sb("SGs", [64, 4, 128], F32)
    P.op("vector", lambda e: e.memset(SR.ap(), 0.0), writes=[SR])
    P.op("vector", lambda e: e.memset(SGs.ap(), 0.0), writes=[SGs])
    hT = P.sb("hT", [128, SG, 8, 128], BF16)
    ysum = P.sb("ysum", [128, SG, 8, 128], BF16)
    dsaout = P.sb("dsaout", [64, SG, 8, 128], BF16)
    cap = P.sb("cap", [64, 4, 128], F32)
    capb = P.sb("capb", [64, 4, 128], BF16)

    def load_w(names, tag):
        c0 = boff[names[0]][0]
        n = sum(boff[k][1] for k in names)
        assert boff[names[-1]][0] + boff[names[-1]][1] == c0 + n
        return B.load_weight_bf16(WBb, wbuf, c0, n, tag), c0

    def load_w2(dram_ap, rows, kc, name):
        return B.load_weight_bf16(dram_ap, wbuf, 0, 1024, name, kc=kc, rows=rows)

    def scan(S, kv_all, dec_src, s, tag):
        if dec_src is not None:
            dcg = B.dcg_g
            P.dma(dcg.ap(), dec_src[:, s].rearrange("j d n -> d j n"), writes=[dcg])
        for j in range(NCORE):
            kvt = B.kvt[j % 2]
            P.dma(kvt.ap(), kv_all[j, s], writes=[kvt])
            if j == 0:
                P.op("vector", lambda e: e.tensor_scalar(cap.ap(), S.ap(), sel.ap()[0:64, 0:1], None, ALU.mult),
                     reads=[S, sel], writes=[cap])
            else:
                P.op("vector", lambda e: e.scalar_tensor_tensor(cap.ap(), S.ap(), sel.ap()[0:64, j:j + 1], cap.ap(),
                                                                ALU.mult, ALU.add), reads=[S, sel, cap], writes=[cap])
            if dec_src is None:
                dv = B.c["decR"].ap()[0:64, :].unsqueeze(2).to_broadcast([64, 4, 128])
                rd = [B.c["decR"]]
            else:
                dv = dcg.ap()[:, j, :].unsqueeze(2).to_broadcast([64, 4, 128])
                rd = [dcg]
            P.op("vector", lambda e: e.tensor_tensor(S.ap(), S.ap(), dv, ALU.mult), reads=[S] + rd, writes=[S])
            P.op("vector", lambda e: e.tensor_tensor(S.ap(), S.ap(), kvt.ap().rearrange("p (h e) -> p h e", h=4),
                                                     ALU.add), reads=[S, kvt], writes=[S])
        P.op("scalar", lambda e: e.activation(capb.ap(), cap.ap(), AF.Copy), reads=[cap], writes=[capb])

    def tail(bO, w, c0, gname, mgname, wbr, ls, first, tl):
        sq, r1, sg, tt, bro, mg, t2 = tl
        P.op("scalar", lambda e: e.activation(sq.ap(), bO.ap(), AF.Square), reads=[bO], writes=[sq])
        bQ = B.bank()
        P.op("tensor", lambda e: e.matmul(bQ.ap(), B.onesb.ap(), sq.ap(), start=True, stop=True),
             reads=[B.onesb, sq], writes=[bQ])
        P.op("vector", lambda e: e.tensor_scalar(r1.ap(), bQ.ap(), 1.0 / 128.0, RMS_EPS, ALU.mult, ALU.add),
             reads=[bQ], writes=[r1])
        P.op("scalar", lambda e: e.activation(r1.ap(), r1.ap(), AF.Sqrt), reads=[r1], writes=[r1])
        P.op("vector", lambda e: e.reciprocal(r1.ap(), r1.ap()), reads=[r1], writes=[r1])
        bG = B.bank()
        g0 = boff[gname][0] - c0
        for h in range(4):
            B.proj_fm(bG.ap()[:, h * 128:(h + 1) * 128], bG, w, g0 + h * 128, 128, hT, ls)
        P.op("scalar", lambda e: e.activation(sg.ap(), bG.ap(), AF.Silu), reads=[bG], writes=[sg])
        P.op("vector", lambda e: e.tensor_tensor(tt.ap(), bO.ap(), r1.ap(), ALU.mult), reads=[bO, r1], writes=[tt])
        P.op("gpsimd", lambda e: e.tensor_tensor(bro.ap(), tt.ap(), sg.ap(), ALU.mult), reads=[tt, sg], writes=[bro])
        merge(lambda nc, out_ap, bk: [P.op("tensor", lambda e, h=h: e.matmul(
            out_ap, wbr.ap()[:, h, nc * 128:(nc + 1) * 128], bro.ap()[:, h * 128:(h + 1) * 128],
            start=(h == 0), stop=(h == 3)), reads=[wbr, bro], writes=[bk]) for h in range(4)],
            w, boff[mgname][0] - c0, ls, first, mg, t2)

    def merge(emit_branch, w, m0, ls, first, mg, t2):
        bY = [B.bank(), B.bank()]
        for nc in range(8):
            bk = bY[nc // 4]
            emit_branch(nc, bk.ap()[:, (nc % 4) * 128:(nc % 4 + 1) * 128], bk)
        bM = [B.bank(), B.bank()]
        for nc in range(8):
            bk = bM[nc // 4]
            B.proj_fm(bk.ap()[:, (nc % 4) * 128:(nc % 4 + 1) * 128], bk, w, m0 + nc * 128, 128, hT, ls)
        for hf in range(2):
            P.op("scalar", lambda e: e.activation(mg.ap(), bM[hf].ap(), AF.Sigmoid), reads=[bM[hf]], writes=[mg])
            yv = ysum.ap()[:, ls, hf * 4:(hf + 1) * 4, :]
            m3 = mg.ap().rearrange("p (a b) -> p a b", a=4)
            b3 = bY[hf].ap().rearrange("p (a b) -> p a b", a=4)
            if first:
                P.op("vector", lambda e: e.tensor_tensor(yv, m3, b3, ALU.mult), reads=[mg, bY[hf]], writes=[ysum])
            else:
                P.op("vector", lambda e: e.tensor_tensor(t2.ap(), mg.ap(), bY[hf].ap(), ALU.mult),
                     reads=[mg, bY[hf]], writes=[t2])
                P.op("gpsimd", lambda e: e.tensor_tensor(yv, yv, t2.ap().rearrange("p (a b) -> p a b", a=4), ALU.add),
                     reads=[ysum, t2], writes=[ysum])

    def tail_bufs():
        return (P.sb("t_sq", [128, 512], BF16), P.sb("t_r1", [128, 512], F32), P.sb("t_sg", [128, 512], F32),
                P.sb("t_tt", [128, 512], F32), P.sb("t_bro", [128, 512], BF16), P.sb("t_mg", [128, 512], F32),
                P.sb("t_t2", [128, 512], F32))

    for g0 in range(0, TPC, SG):
        P.push_scope()
        for s in range(g0, g0 + SG):
            B.emit_norm_T(x[s], hT, s - g0)
        P.pop_scope()
        del B.xin
        P.push_scope()
        w, c0 = load_w(["retq", "retq_sw", "retk", "retk_sw", "retv", "retg", "mg0"], "w_ret")
        wbr = load_w2(wbr_r, 128, 4, "wbr_ret")
        tl = tail_bufs()
        qTb = P.sb("qTb", [64, 4, 128], BF16)
        kTb = P.sb("kTb", [64, 4, 128], BF16)
        qhat = P.sb("qhat", [64, 4, 128], BF16)
        vtok = P.sb("vtok", [128, 512], BF16)
        Sm = P.sb("Sm", [128, 512], BF16)
        for s in range(g0, g0 + SG):
            ls = s - g0
            t0 = s * 128
            scan(SR, kvRa, None, s, "r")
            for nm, dst in (("retq", qTb), ("retk", kTb)):
                bx, bs = B.bank(), B.bank()
                for h in range(4):
                    B.proj_fm(bx.ap()[0:64, h * 128:(h + 1) * 128], bx, w, boff[nm][0] - c0 + h * 64, 64, hT, ls)
                    B.proj_fm(bs.ap()[0:64, h * 128:(h + 1) * 128], bs, w, boff[nm + "_sw"][0] - c0 + h * 64, 64, hT, ls)
                B.rope_fm(dst.ap(), dst, bx, bs, 64, 512, cosR, sinR, t0)
            bv = B.bank()
            B.proj_tm(bv.ap(), bv, w, boff["retv"][0] - c0, 512, hT, ls)
            P.op("scalar", lambda e: e.activation(vtok.ap(), bv.ap(), AF.Copy), reads=[bv], writes=[vtok])
            bS = B.bank()
            for h in range(4):
                P.op("tensor", lambda e: e.matmul(bS.ap()[:, h * 128:(h + 1) * 128], kTb.ap()[:, h, :], qTb.ap()[:, h, :],
                                                  start=True, stop=True), reads=[kTb, qTb], writes=[bS])
            P.op("vector", lambda e: e.tensor_tensor(Sm.ap(), bS.ap(), B.c["DTret"].ap().rearrange("p h i -> p (h i)"),
                                                     ALU.mult), reads=[bS, B.c["DTret"]], writes=[Sm])
            P.op("gpsimd", lambda e: e.tensor_tensor(qhat.ap(), qTb.ap(), B.c["QDret"].ap()[0:64], ALU.mult),
                 reads=[qTb, B.c["QDret"]], writes=[qhat])
            bO = B.bank()
            for h in range(4):
                o_ap = bO.ap()[:, h * 128:(h + 1) * 128]
                P.op("tensor", lambda e: e.matmul(o_ap, vtok.ap()[:, h * 128:(h + 1) * 128], Sm.ap()[:, h * 128:(h + 1) * 128],
                                                  start=True, stop=False), reads=[vtok, Sm], writes=[bO])
                P.op("tensor", lambda e: e.matmul(o_ap, capb.ap()[:, h, :], qhat.ap()[:, h, :], start=False, stop=True),
                     reads=[capb, qhat], writes=[bO])
            tail(bO, w, c0, "retg", "mg0", wbr, ls, True, tl)
        P.pop_scope()
        P.push_scope()
        w, c0 = load_w(["glaq", "glak", "glav", "glag", "glaa", "mg2"], "w_gla")
        wbr = load_w2(wbr_g, 128, 4, "wbr_gla")
        tl = tail_bufs()
        eq = P.sb("eq", [64, 512], F32)
        ek = P.sb("ek", [64, 512], F32)
        e128 = P.sb("e128", [64, 512], F32)
        qt = P.sb("qt", [64, 4, 128], BF16)
        qh = P.sb("qh", [64, 4, 128], BF16)
        kt = P.sb("kt", [64, 4, 128], BF16)
        kfac = P.sb("kfac", [128, 256], F32)
        gkhat = P.sb("gkhat", [128, 256], BF16)
        vtok = P.sb("gvtok", [128, 512], BF16)
        kv0b = P.sb("kv0b", [64, 512], BF16)
        Sm = P.sb("gSm", [128, 512], BF16)
        for s in range(g0, g0 + SG):
            ls = s - g0
            scan(SGs, kvGa, decGa, s, "g")
            sp = gla_decay_common(B, hT, ls, w, boff["glaa"][0] - c0, wlr17)
            bC64, bC128 = B.bank(), B.bank()
            for h in range(4):
                for bb, tri in ((bC64, "triL64"), (bC128, "triL128")):
                    P.op("tensor", lambda e: e.matmul(bb.ap()[0:64, h * 128:(h + 1) * 128], sp.ap()[:, h * 64:(h + 1) * 64],
                                                      B.c[tri].ap(), start=True, stop=True), reads=[sp, B.c[tri]], writes=[bb])
            P.op("scalar", lambda e: e.activation(eq.ap(), bC64.ap()[0:64, :], AF.Exp, scale=-1.0 / 16), reads=[bC64], writes=[eq])
            P.op("scalar", lambda e: e.activation(ek.ap(), bC64.ap()[0:64, :], AF.Exp, scale=1.0 / 16), reads=[bC64], writes=[ek])
            P.op("scalar", lambda e: e.activation(e128.ap(), bC128.ap()[0:64, :], AF.Exp, scale=-1.0 / 16), reads=[bC128], writes=[e128])
            bq = B.bank()
            for h in range(4):
                B.proj_fm(bq.ap()[0:64, h * 128:(h + 1) * 128], bq, w, boff["glaq"][0] - c0 + h * 64, 64, hT, ls)
            f2 = lambda b: b.ap().rearrange("p h t -> p (h t)")
            P.op("vector", lambda e: e.scalar_tensor_tensor(f2(qt), bq.ap()[0:64, :], 0.125, eq.ap(), ALU.mult, ALU.mult),
                 reads=[bq, eq], writes=[qt])
            P.op("vector", lambda e: e.scalar_tensor_tensor(f2(qh), bq.ap()[0:64, :], 0.125, e128.ap(), ALU.mult, ALU.mult),
                 reads=[bq, e128], writes=[qh])
            bk = B.bank()
            for h in range(4):
                B.proj_fm(bk.ap()[0:64, h * 128:(h + 1) * 128], bk, w, boff["glak"][0] - c0 + h * 64, 64, hT, ls)
            P.op("vector", lambda e: e.tensor_tensor(f2(kt), bk.ap()[0:64, :], ek.ap(), ALU.mult), reads=[bk, ek], writes=[kt])
            bd = B.bank()
            P.op("tensor", lambda e: e.matmul(bd.ap()[:, 0:256], B.c["triU64"].ap(), sp.ap(), start=True, stop=True),
                 reads=[B.c["triU64"], sp], writes=[bd])
            P.op("scalar", lambda e: e.activation(kfac.ap(), bd.ap()[:, 0:256], AF.Exp, scale=-1.0 / 16.0), reads=[bd], writes=[kfac])
            bkt = B.bank()
            B.proj_tm(bkt.ap()[:, 0:256], bkt, w, boff["glak"][0] - c0, 256, hT, ls)
            P.op("vector", lambda e: e.tensor_tensor(gkhat.ap(), bkt.ap()[:, 0:256], kfac.ap(), ALU.mult),
                 reads=[bkt, kfac], writes=[gkhat])
            bv = B.bank()
            B.proj_tm(bv.ap(), bv, w, boff["glav"][0] - c0, 512, hT, ls)
            P.op("scalar", lambda e: e.activation(vtok.ap(), bv.ap(), AF.Copy), reads=[bv], writes=[vtok])
            b0 = B.bank()
            for h in range(4):
                P.op("tensor", lambda e: e.matmul(b0.ap()[0:64, h * 128:(h + 1) * 128], gkhat.ap()[0:64, h * 64:(h + 1) * 64],
                                                  vtok.ap()[0:64, h * 128:(h + 1) * 128], start=True, stop=True),
                     reads=[gkhat, vtok], writes=[b0])
            P.op("scalar", lambda e: e.activation(kv0b.ap(), b0.ap()[0:64, :], AF.Copy), reads=[b0], writes=[kv0b])
            bS = B.bank()
            for h in range(4):
                P.op("tensor", lambda e: e.matmul(bS.ap()[:, h * 128:(h + 1) * 128], kt.ap()[:, h, :], qt.ap()[:, h, :],
                                                  start=True, stop=True), reads=[kt, qt], writes=[bS])
            P.op("vector", lambda e: e.tensor_tensor(Sm.ap(), bS.ap(), B.c["DTgla"].ap().rearrange("p h i -> p (h i)"),
                                                     ALU.mult), reads=[bS, B.c["DTgla"]], writes=[Sm])
            bO = B.bank()
            for h in range(4):
                o_ap = bO.ap()[:, h * 128:(h + 1) * 128]
                P.op("tensor", lambda e: e.matmul(o_ap, vtok.ap()[:, h * 128:(h + 1) * 128], Sm.ap()[:, h * 128:(h + 1) * 128],
                                                  start=True, stop=False), reads=[vtok, Sm], writes=[bO])
                P.op("tensor", lambda e: e.matmul(o_ap, capb.ap()[:, h, :], qh.ap()[:, h, :], start=False, stop=False),
                     reads=[capb, qh], writes=[bO])
                P.op("tensor", lambda e: e.matmul(bO.ap()[:, h * 128 + 64:(h + 1) * 128], kv0b.ap()[:, h * 128:(h + 1) * 128],
                                                  qt.ap()[:, h, 64:128], start=False, stop=True), reads=[kv0b, qt], writes=[bO])
            tail(bO, w, c0, "glag", "mg2", wbr, ls, False, tl)
        P.pop_scope()
        dsa_stage(B, cfg, g0, locals())
        P.push_scope()
        wo = load_w2(wout_d, 128, 8, "w_out")
        xt = P.sb("f_xt", [128, 1024], F32)
        fj = P.sb("f_junk", [128, 512], BF16)
        fst = P.sb("f_st", [128, 8], F32)
        ft = P.sb("f_t", [128, 1024], F32)
        fo = P.sb("f_o", [128, 1024], F32)
        for s in range(g0, g0 + SG):
            ls = s - g0
            P.dma(xt.ap(), x[s], writes=[xt])
            bh = [B.bank(), B.bank()]
            for hf in range(2):
                for nc in range(8):
                    P.op("tensor", lambda e: e.matmul(bh[hf].ap(), ysum.ap()[:, ls, nc, :], wo.ap()[:, nc, hf * 512:(hf + 1) * 512],
                                                      start=(nc == 0), stop=(nc == 7)), reads=[ysum, wo], writes=[bh[hf]])
                P.op("scalar", lambda e: e.activation(fj.ap(), bh[hf].ap(), AF.Square, accum_out=fst.ap()[:, hf:hf + 1]),
                     reads=[bh[hf]], writes=[fj, fst])
            P.op("vector", lambda e: e.tensor_tensor(fst.ap()[:, 2:3], fst.ap()[:, 0:1], fst.ap()[:, 1:2], ALU.add), reads=[fst], writes=[fst])
            P.op("vector", lambda e: e.tensor_scalar(fst.ap()[:, 3:4], fst.ap()[:, 2:3], 1.0 / D, RMS_EPS, ALU.mult, ALU.add),
                 reads=[fst], writes=[fst])
            P.op("scalar", lambda e: e.activation(fst.ap()[:, 4:5], fst.ap()[:, 3:4], AF.Sqrt), reads=[fst], writes=[fst])
            P.op("vector", lambda e: e.reciprocal(fst.ap()[:, 5:6], fst.ap()[:, 4:5]), reads=[fst], writes=[fst])
            for hf in range(2):
                sl = slice(hf * 512, (hf + 1) * 512)
                P.op("vector", lambda e: e.scalar_tensor_tensor(ft.ap()[:, sl], bh[hf].ap(), fst.ap()[:, 5:6], B.GP.ap()[:, sl],
                                                                ALU.mult, ALU.mult), reads=[bh[hf], fst, B.GP], writes=[ft])
            P.op("gpsimd", lambda e: e.tensor_tensor(fo.ap(), ft.ap(), xt.ap(), ALU.add), reads=[ft, xt], writes=[fo])
            P.dma(xo[s], fo.ap(), reads=[fo])
        P.pop_scope()
    P.finish()
    return B


def dsa_stage(B, cfg, g0, env):
    P = B.P
    TPC, T, NCORE, SG = cfg.TPC, cfg.T, cfg.NCORE, cfg.SG
    GK, BLK, NB, BPG, NIT = cfg.GK, cfg.BLK, cfg.NB, cfg.BPG, cfg.NIT
    hT, ysum, dsaout, boff = env["hT"], env["ysum"], env["dsaout"], env["boff"]
    KTa, Va, IKa, pen, ident4 = env["KTa"], env["Va"], env["IKa"], env["pen"], env["ident4"]
    cosD, sinD = env["cosD"], env["sinD"]
    P.push_scope()
    w, c0 = env["load_w"](["dsaq", "dsaq_sw", "idxq", "idxq_sw", "idxw", "dsag"], "w_dsa")
    NMAX = TPC * GK
    scores = P.sb("scores", [128, NMAX], F32)
    CH = min(cfg.CH, NMAX)
    junk = P.sb("cjunk", [128, CH], BF16)
    QT = P.sb("QT", [128, 4, 128], BF16)
    IQ = P.sb("IQ", [64, 4, 128], BF16)
    wq = P.sb("wq", [128, 4], F32)
    rl = [P.sb(f"rl{i}", [128, BLK], F32) for i in range(2)]
    ikb = [P.sb(f"ikb{i}", [64, NB, 128], BF16) for i in range(2)]
    ktb = [P.sb(f"ktb{i}", [128, NB, 128], BF16) for i in range(2)]
    vbk = [P.sb(f"vbk{i}", [128, NB, 2, 128], BF16) for i in range(2)]
    for v in vbk:
        P.op("vector", lambda e: e.memset(v.ap(), 1.0), writes=[v])
    nbmax = TPC * BPG
    mx = P.sb("mx", [128, nbmax], F32)
    mn = P.sb("mn", [128, nbmax], F32)
    wall = P.sb("wall", [128, 32], F32)
    cntc = P.sb("cntc", [128, 64], F32)
    cnta = P.sb("cnta", [128, 64], F32)
    junk2 = P.sb("cjunk2", [128, CH], BF16)
    bst = P.sb("bst", [128, 16], F32)
    mlo = P.sb("mlo", [128, BLK], BF16)
    mhi = P.sb("mhi", [128, BLK], BF16)
    band = P.sb("band", [128, BLK], BF16)
    cum = [P.sb(f"cum{i}", [128, BLK], F32) for i in range(2)]
    tsel = P.sb("tsel", [128, BLK], BF16)
    mb = [P.sb(f"mb{i}", [128, BLK], BF16) for i in range(2)]
    PT = [P.sb(f"PT{i}", [128, 512], BF16) for i in range(4)]
    rc = P.sb("rc", [64, 512], F32)
    on = P.sb("on", [64, 512], BF16)
    sgd = P.sb("sgd", [64, 8, 128], BF16)
    acc = [B.banks[6], B.banks[7]]
    saved_i = B.bank_i
    rot = {"i": 0}

    def bank6():
        b = B.banks[rot["i"] % 6]
        rot["i"] += 1
        return b
    B_bank = B.bank
    B.bank = bank6
    col = lambda i: bst.ap()[:, i:i + 1]

    for s in range(g0, g0 + SG):
        ls = s - g0
        t0 = s * 128
        bx, bs = bank6(), bank6()
        for j in range(4):
            B.proj_fm(bx.ap()[:, j * 128:(j + 1) * 128], bx, w, boff["dsaq"][0] - c0 + j * 128, 128, hT, ls)
            B.proj_fm(bs.ap()[:, j * 128:(j + 1) * 128], bs, w, boff["dsaq_sw"][0] - c0 + j * 128, 128, hT, ls)
        B.rope_fm(QT.ap(), QT, bx, bs, 128, 512, cosD, sinD, t0)
        bx, bs = bank6(), bank6()
        for h in range(4):
            B.proj_fm(bx.ap()[0:64, h * 128:(h + 1) * 128], bx, w, boff["idxq"][0] - c0 + h * 64, 64, hT, ls)
            B.proj_fm(bs.ap()[0:64, h * 128:(h + 1) * 128], bs, w, boff["idxq_sw"][0] - c0 + h * 64, 64, hT, ls)
        B.rope_fm(IQ.ap(), IQ, bx, bs, 64, 512, cosD, sinD, t0)
        bw = bank6()
        B.proj_tm(bw.ap()[:, 0:4], bw, w, boff["idxw"][0] - c0, 4, hT, ls)
        P.op("vector", lambda e: e.tensor_scalar(wq.ap(), bw.ap()[:, 0:4], 0.0625, None, ALU.mult), reads=[bw], writes=[wq])
        nb = (s + 1) * BPG
        n = (s + 1) * GK
        for bi in range(nb):
            gq, blk = bi // BPG, bi % BPG
            ik = ikb[bi % 2]
            P.dma(ik.ap(), IKa[blk * NB:(blk + 1) * NB, :, gq * 128:(gq + 1) * 128].rearrange("j d t -> d j t"), writes=[ik])
            scb = scores.ap()[:, bi * BLK:(bi + 1) * BLK]
            for h in range(4):
                bI = bank6()
                P.op("tensor", lambda e: e.matmul(bI.ap()[:, 0:BLK], IQ.ap()[:, h, :], ik.ap().rearrange("d j t -> d (j t)"),
                                                  start=True, stop=True), reads=[IQ, ik], writes=[bI])
                r = rl[h % 2]
                P.op("scalar", lambda e: e.activation(r.ap(), bI.ap()[:, 0:BLK], AF.Relu), reads=[bI], writes=[r])
                if h == 0:
                    P.op("vector", lambda e: e.tensor_scalar(scb, r.ap(), wq.ap()[:, 0:1], None, ALU.mult),
                         reads=[r, wq], writes=[scores])
                else:
                    P.op("vector", lambda e: e.scalar_tensor_tensor(scb, r.ap(), wq.ap()[:, h:h + 1], scb, ALU.mult, ALU.add),
                         reads=[r, wq, scores], writes=[scores])

        P.op("vector", lambda e: e.tensor_reduce(col(8), scores.ap()[:, 0:n], AX.X, ALU.max), reads=[scores], writes=[bst])
        P.op("vector", lambda e: e.tensor_scalar(col(1), col(8), 1.0, None, ALU.add), reads=[bst], writes=[bst])
        P.op("vector", lambda e: e.tensor_reduce(col(9), scores.ap()[:, 0:n], AX.X, ALU.min), reads=[scores, bst], writes=[bst])
        P.op("vector", lambda e: e.tensor_scalar(col(0), col(9), -1.0, None, ALU.add), reads=[bst], writes=[bst])
        for blk in range(BPG):
            bi = s * BPG + blk
            scb = scores.ap()[:, bi * BLK:(bi + 1) * BLK]
            P.op("vector", lambda e: e.tensor_tensor(scb, scb, pen.ap()[:, blk * BLK:(blk + 1) * BLK], ALU.add),
                 reads=[scores, pen], writes=[scores])
        P.op("vector", lambda e: e.tensor_tensor(col(10), col(1), col(0), ALU.subtract), reads=[bst], writes=[bst])
        P.op("vector", lambda e: e.tensor_scalar(wall.ap(), B.c["pw2"].ap(), col(10), None, ALU.mult),
             reads=[B.c["pw2"], bst], writes=[wall])
        P.op("vector", lambda e: e.tensor_tensor(col(2), col(0), wall.ap()[:, 0:1], ALU.add), reads=[bst, wall], writes=[bst])
        chunks = [(cs, min(n, cs + CH)) for cs in range(0, n, CH)]

        def count(thr_col, dst_col, use_act):
            nd = na = 0
            l_act = 0
            for ci, (cs, ce) in enumerate(chunks):
                if use_act and ci % 2 == 1:
                    P.op("scalar", lambda e: e.activation(junk2.ap()[:, 0:ce - cs], scores.ap()[:, cs:ce], AF.Sign, bias=col(thr_col),
                                                          scale=-1.0, accum_out=cnta.ap()[:, na:na + 1]),
                         reads=[scores, bst], writes=[cnta, junk2])
                    na += 1
                    l_act += ce - cs
                else:
                    P.op("vector", lambda e: e.tensor_scalar(junk.ap()[:, 0:ce - cs], scores.ap()[:, cs:ce], col(thr_col), None,
                                                             ALU.is_ge, ALU.add, accum_out=cntc.ap()[:, nd:nd + 1]),
                         reads=[scores, bst], writes=[cntc, junk])
                    nd += 1
            P.op("vector", lambda e: e.tensor_reduce(col(dst_col), cntc.ap()[:, 0:nd], AX.X, ALU.add),
                 reads=[cntc], writes=[bst])
            if na == 0:
                return dst_col, cfg.TOPK - 0.5
            P.op("vector", lambda e: e.tensor_reduce(col(5), cnta.ap()[:, 0:na], AX.X, ALU.add), reads=[cnta, bst], writes=[bst])
            P.op("vector", lambda e: e.scalar_tensor_tensor(col(11), col(dst_col), 2.0, col(5), ALU.mult, ALU.subtract),
                 reads=[bst], writes=[bst])
            return 11, 2.0 * cfg.TOPK - 1.0 - l_act

        for it in range(NIT):
            ccol, cthr = count(2, 3, True)
            P.op("vector", lambda e: e.scalar_tensor_tensor(col(4), col(ccol), cthr, wall.ap()[:, it:it + 1],
                                                            ALU.is_ge, ALU.mult), reads=[bst, wall], writes=[bst])
            P.op("vector", lambda e: e.tensor_tensor(col(0), col(0), col(4), ALU.add), reads=[bst], writes=[bst])
            if it + 1 < NIT:
                P.op("vector", lambda e: e.tensor_tensor(col(2), col(0), wall.ap()[:, it + 1:it + 2], ALU.add),
                     reads=[bst, wall], writes=[bst])
        P.op("vector", lambda e: e.tensor_tensor(col(1), col(0), wall.ap()[:, NIT - 1:NIT], ALU.add), reads=[bst, wall], writes=[bst])
        count(1, 6, False)
        P.op("vector", lambda e: e.tensor_scalar(col(7), col(6), -BIGM, cfg.TOPK * BIGM, ALU.mult, ALU.add), reads=[bst], writes=[bst])
        blkst = {}

        def prep_block(bi):
            gq, blk = bi // BPG, bi % BPG
            kt_, vb_ = ktb[bi % 2], vbk[bi % 2]
            P.dma(kt_.ap(), KTa[blk * NB:(blk + 1) * NB, :, gq * 128:(gq + 1) * 128].rearrange("j d t -> d j t"), writes=[kt_])
            for jj in range(NB):
                P.dma(vb_.ap()[:, jj, :, 0:64], Va[blk * NB + jj, gq].rearrange("s (k d) -> s k d", k=2), writes=[vb_])
            scb = scores.ap()[:, bi * BLK:(bi + 1) * BLK]
            m = mb[bi % 2]
            cm, cprev = cum[bi % 2], cum[(bi + 1) % 2]
            P.op("vector", lambda e: e.tensor_scalar(mlo.ap(), scb, col(0), BIGM, ALU.is_ge, ALU.mult), reads=[scores, bst], writes=[mlo])
            P.op("vector", lambda e: e.tensor_scalar(mhi.ap(), scb, col(1), BIGM, ALU.is_ge, ALU.mult), reads=[scores, bst], writes=[mhi])
            P.op("vector", lambda e: e.tensor_tensor(band.ap(), mlo.ap(), mhi.ap(), ALU.subtract), reads=[mlo, mhi], writes=[band])
            P.op("vector", lambda e: e.tensor_tensor_scan(cm.ap(), band.ap(), band.ap(),
                                                          (0.0 if bi == 0 else cprev.ap()[:, BLK - 1:BLK]), ALU.add, ALU.max),
                 reads=[band] + ([] if bi == 0 else [cprev]), writes=[cm])
            P.op("vector", lambda e: e.scalar_tensor_tensor(tsel.ap(), cm.ap(), col(7), band.ap(), ALU.is_le, ALU.mult),
                 reads=[cm, bst, band], writes=[tsel])
            P.op("vector", lambda e: e.scalar_tensor_tensor(m.ap(), tsel.ap(), -BIGM, mhi.ap(), ALU.add, ALU.add),
                 reads=[tsel, mhi], writes=[m])
            blkst[bi] = (kt_, vb_, m)

        steps = [(bi, jj) for bi in range(nb) for jj in range(NB)]

        def emit_logits(si):
            bi, jj = steps[si]
            kt_, vb_, m = blkst[bi]
            bLs = [bank6(), bank6()]
            for kvn in range(2):
                P.op("tensor", lambda e: e.matmul(bLs[kvn].ap(), kt_.ap()[kvn * 64:(kvn + 1) * 64, jj, :],
                                                  QT.ap()[kvn * 64:(kvn + 1) * 64].rearrange("p j t -> p (j t)"),
                                                  start=True, stop=False), reads=[kt_, QT], writes=[bLs[kvn]])
            pts = []
            for kvn in range(2):
                P.op("tensor", lambda e: e.matmul(bLs[kvn].ap(), m.ap()[:, jj * 128:(jj + 1) * 128],
                                                  ident4.ap().rearrange("p j t -> p (j t)"), start=False, stop=True),
                     reads=[m, ident4], writes=[bLs[kvn]])
                pt = PT[(2 * si + kvn) % 4]
                P.op("scalar", lambda e: e.activation(pt.ap(), bLs[kvn].ap(), AF.Exp, scale=0.125), reads=[bLs[kvn]], writes=[pt])
                pts.append(pt)
            return pts

        def emit_pv(si, pts):
            bi, jj = steps[si]
            kt_, vb_, m = blkst[bi]
            first = (bi == 0 and jj == 0)
            last = (bi == nb - 1 and jj == NB - 1)
            for kvn in range(2):
                P.op("tensor", lambda e: e.matmul(acc[kvn].ap(), vb_.ap()[:, jj, kvn, :], pts[kvn].ap(), start=first, stop=last),
                     reads=[vb_, pts[kvn]], writes=[acc[kvn]])

        pend = None
        for si in range(len(steps)):
            bi, jj = steps[si]
            if jj == 0:
                prep_block(bi)
            pts = emit_logits(si)
            if pend is not None:
                emit_pv(*pend)
            pend = (si, pts)
        emit_pv(*pend)
        bg = [bank6(), bank6()]
        for hd in range(8):
            B.proj_fm(bg[hd // 4].ap()[0:64, (hd % 4) * 128:(hd % 4 + 1) * 128], bg[hd // 4], w,
                      boff["dsag"][0] - c0 + hd * 64, 64, hT, ls)
        for hf in range(2):
            P.op("scalar", lambda e: e.activation(sgd.ap()[:, hf * 4:(hf + 1) * 4, :],
                                                  bg[hf].ap()[0:64, :].rearrange("p (a b) -> p a b", a=4), AF.Silu),
                 reads=[bg[hf]], writes=[sgd])
        for kvn in range(2):
            P.op("vector", lambda e: e.reciprocal(rc.ap(), acc[kvn].ap()[64:128, :]), reads=[acc[kvn]], writes=[rc])
            P.op("vector", lambda e: e.tensor_tensor(on.ap(), acc[kvn].ap()[0:64, :], rc.ap(), ALU.mult),
                 reads=[acc[kvn], rc], writes=[on])
            P.op("gpsimd", lambda e: e.tensor_tensor(dsaout.ap()[:, ls, kvn * 4:(kvn + 1) * 4, :],
                                                     on.ap().rearrange("p (a b) -> p a b", a=4),
                                                     sgd.ap()[:, kvn * 4:(kvn + 1) * 4, :], ALU.mult),
                 reads=[on, sgd], writes=[dsaout])
    B.bank = B_bank
    P.pop_scope()
    P.push_scope()
    w, c0 = env["load_w"](["mg1"], "w_mg1")
    wbr = env["load_w2"](env["wbr_d"], 64, 8, "wbr_dsa")
    mg = P.sb("d_mg", [128, 512], F32)
    t2 = P.sb("d_t2", [128, 512], F32)
    for s in range(g0, g0 + SG):
        ls = s - g0
        env["merge"](lambda nc, out_ap, bk: [P.op("tensor", lambda e, hd=hd: e.matmul(
            out_ap, wbr.ap()[:, hd, nc * 128:(nc + 1) * 128], dsaout.ap()[:, ls, hd, :],
            start=(hd == 0), stop=(hd == 7)), reads=[wbr, dsaout], writes=[bk]) for hd in range(8)],
            w, 0, ls, False, mg, t2)
    P.pop_scope()


B_CONSTS = ["ident", "triL128", "triL64", "triU64", "DTret", "DTgla", "QDret", "decR", "pw2"]


def host_inputs_B(inp, cfg, l, xcur, resA):
    pos = np.asarray(inp["positions"])[0].reshape(cfg.S // 128, 128)
    st = lambda k: np.stack([np.asarray(r[k]) for r in resA])
    er = lambda w, h: np.ascontiguousarray(np.asarray(w)[l].reshape(h, 512 // h, D).transpose(1, 0, 2))
    shared = {
        "WB": kc_layout(np.asarray(inp["w_in"])[l][:, b_col_index()]),
        "wbr_r": er(inp["w_br_ret"], 4), "wbr_g": er(inp["w_br_gla"], 4), "wbr_d": er(inp["w_br_dsa"], 8),
        "wout": kc_layout(np.asarray(inp["w_out"])[l]),
        "wlr": np.asarray(inp["gla_w_lr"])[l],
        "blr": np.asarray(inp["gla_b_lr"])[l][None, :],
        "KTa": st("KT"), "Va": st("V"), "IKa": st("IK"),
        "kvRa": st("kvR"), "kvGa": st("kvG"), "decGa": st("decG"),
    }
    shared.update(const_inputs(B_CONSTS))
    maps = []
    for c in range(cfg.NCORE):
        tl = core_tiles(cfg, c)
        m = dict(shared)
        m["x"] = np.ascontiguousarray(xcur[tl])
        m["modv"] = np.asarray(resA[c]["modv"])
        m["ropev"] = np.asarray(resA[c]["ropev"])
        sel = np.zeros((128, cfg.NCORE), np.float32)
        sel[:, c] = 1.0
        m["sel"] = sel
        kidx = np.arange(cfg.GK)[None, :]
        qidx = (c * 128 + np.arange(128))[:, None]
        import ml_dtypes
        m["pen"] = np.where(kidx > qidx, np.float32(-1e30), np.float32(0.0)).astype(ml_dtypes.bfloat16)
        maps.append(m)
    return maps


_CACHE = {}


def run_layers(inp, cfg):
    x = np.asarray(inp["x"])[0].reshape(cfg.S // 128, 128, D).astype(np.float32)
    cores = list(range(cfg.NCORE))
    for l in range(cfg.DEPTH):
        if "A" not in _CACHE:
            _CACHE["A"] = build_A(cfg)
        inp_l = dict(inp)
        inp_l["x"] = x.reshape(1, cfg.S, D)
        resA = run_bass_kernel_spmd(_CACHE["A"].nc, host_inputs_A(inp_l, cfg, l), core_ids=cores).results
        if "B" not in _CACHE:
            _CACHE["B"] = build_B(cfg)
        resB = run_bass_kernel_spmd(_CACHE["B"].nc, host_inputs_B(inp, cfg, l, x, resA), core_ids=cores).results
        xn = np.empty_like(x)
        for c in cores:
            xn[core_tiles(cfg, c)] = np.asarray(resB[c]["xo"])
        x = xn
    return x.reshape(1, cfg.S, D)


def kernel(**inputs):
    cfg = Cfg()
    return run_layers(inputs, cfg).astype(np.float32)
```

```python
import numpy as np
from contextlib import ExitStack
import concourse.bass as bass
import concourse.mybir as mybir
from concourse.bass_utils import run_bass_kernel_spmd

F32 = mybir.dt.float32
BF16 = mybir.dt.bfloat16
I32 = mybir.dt.int32
ALU = mybir.AluOpType
AF = mybir.ActivationFunctionType
AX = mybir.AxisListType

EPOCH = 20000
NDMA_SEM = 12


class Buf:
    def __init__(self, name, handle=None):
        self.name = name
        self.h = handle
        self.last_w = None
        self.readers = []

    def ap(self):
        return self.h[:]


class _Recorder:
    def __init__(self):
        self.call = None

    def __getattr__(self, name):
        def f(*a, **k):
            assert self.call is None
            self.call = (name, a, k)
            return None
        return f


class Prog:
    ENGS = ("tensor", "vector", "scalar", "gpsimd", "sync")

    def __init__(self, nc):
        self.nc = nc
        self.stack = ExitStack()
        self.stream = {e: [] for e in self.ENGS}
        self.count = {e: 0 for e in self.ENGS}
        self.known = {e: {} for e in self.ENGS}
        self.dma_n = {}
        self.dma_rr = {e: 0 for e in self.ENGS}
        self.nbuf = 0
        self.pending = {e: [] for e in self.ENGS}
        self.stacks = [self.stack]

    def sb(self, name, shape, dtype):
        self.nbuf += 1
        name = f"{name}_u{self.nbuf}"
        h = self.stacks[-1].enter_context(self.nc.sbuf_tensor(name, list(shape), dtype))
        return Buf(name, h)

    def push_scope(self):
        st = ExitStack()
        self.stacks.append(st)

    def pop_scope(self):
        self.barrier()
        self.stacks.pop().close()

    def barrier(self):
        toks = []
        for e in self.ENGS:
            c = self.count[e]
            if c > 0 and e != "sync":
                toks.append((("E", e, (c - 1) // EPOCH), (c - 1) % EPOCH + 1))
        for key, n in self.dma_n.items():
            toks.append((key, 16 * n))
        for e in self.ENGS:
            self.pending[e] = list(toks)

    def _take_pending(self, eng, waits):
        kn = self.known[eng]
        for key, val in self.pending[eng]:
            if key[0] == "E" and key[1] == eng:
                continue
            if kn.get(key, -1) >= val:
                continue
            if any(k == key and v >= val for k, v in waits):
                continue
            kn[key] = val
            waits.append((key, val))
        self.pending[eng] = []

    def ps(self, name, shape, dtype=F32):
        h = self.stack.enter_context(self.nc.psum_tensor(name, list(shape), dtype))
        return Buf(name, h)

    def alias(self, name, buf):
        raise NotImplementedError

    def _deps(self, eng, reads, writes, is_dma):
        deps = {}

        def add(tok, same_ok):
            if tok is None:
                return
            key, val = tok
            if (not is_dma) and key[0] == "E" and key[1] == eng and not same_ok:
                return
            if deps.get(key, -1) < val:
                deps[key] = val

        for b in reads:
            add(b.last_w, True)
        for b in writes:
            add(b.last_w, False)
            for t in b.readers:
                add(t, False)
        kn = self.known[eng]
        out = []
        for key, val in deps.items():
            if key[0] == "E":
                later = [k for k in kn if k[0] == "E" and k[1] == key[1] and k[2] > key[2]]
                if later:
                    continue
            if kn.get(key, -1) >= val:
                continue
            kn[key] = val
            out.append((key, val))
        return out

    def _commit(self, tok, reads, writes):
        for b in writes:
            b.last_w = tok
            b.readers = []
        for b in reads:
            if b in writes:
                continue
            rs = [t for t in b.readers if t[0] != tok[0]]
            rs.append(tok)
            b.readers = rs

    def op(self, eng, fn, reads=(), writes=()):
        reads = list(reads)
        writes = list(writes)
        waits = self._deps(eng, reads, writes, False)
        self._take_pending(eng, waits)
        self.count[eng] += 1
        c = self.count[eng]
        key = ("E", eng, (c - 1) // EPOCH)
        tok = (key, (c - 1) % EPOCH + 1)
        rec = _Recorder()
        fn(rec)
        self.stream[eng].append((waits, rec.call, key))
        self._commit(tok, reads, writes)
        return tok

    def dma(self, out_ap, in_ap, reads=(), writes=(), queue="sync", **kw):
        reads = list(reads)
        writes = list(writes)
        i = self.dma_rr[queue]
        self.dma_rr[queue] = (i + 1) % NDMA_SEM
        key = ("D", queue, i)
        n = self.dma_n.get(key, 0)
        waits = self._deps(queue, reads, writes, True)
        self._take_pending(queue, waits)
        if n > 0 and self.known[queue].get(key, -1) < 16 * n:
            self.known[queue][key] = 16 * n
            waits.append((key, 16 * n))
        self.dma_n[key] = n + 1
        tok = (key, 16 * (n + 1))
        kk = dict(kw)
        kk["out"] = out_ap
        kk["in_"] = in_ap
        self.stream[queue].append((waits, ("dma_start", (), kk), key))
        self._commit(tok, reads, writes)
        return tok

    def coll(self, kind, ins, outs, ranks, reads=(), writes=()):
        reads, writes = list(reads), list(writes)
        queue = "gpsimd"
        key = ("D", "coll", 0)
        n = self.dma_n.get(key, 0)
        waits = self._deps(queue, reads, writes, True)
        self._take_pending(queue, waits)
        if n > 0 and self.known[queue].get(key, -1) < 16 * n:
            self.known[queue][key] = 16 * n
            waits.append((key, 16 * n))
        self.dma_n[key] = n + 1
        tok = (key, 16 * (n + 1))
        call = ("collective_compute", (kind, ALU.bypass), dict(replica_groups=[list(ranks)], ins=list(ins), outs=list(outs)))
        self.stream[queue].append((waits, call, key))
        self._commit(tok, reads, writes)
        return tok

    def finish(self):
        nc = self.nc
        sems = {}

        def sem(key):
            if key not in sems:
                nm = "s_" + "_".join(str(k) for k in key)
                sems[key] = self.stack.enter_context(nc.semaphore(nm))
            return sems[key]

        final_waits = []
        for key, n in self.dma_n.items():
            final_waits.append((key, 16 * n))
        for eng in self.ENGS:
            for waits, fn, key in self.stream[eng]:
                sem(key)
                for k, v in waits:
                    sem(k)

        streams = self.stream

        def emit(eng_name):
            def body(e):
                for waits, fn, key in streams[eng_name]:
                    for k, v in waits:
                        e.wait_ge(sems[k], v)
                    ins = getattr(e, fn[0])(*fn[1], **fn[2])
                    ins.then_inc(sems[key], 16 if key[0] == "D" else 1)
                if eng_name == "sync":
                    for k, v in final_waits:
                        e.wait_ge(sems[k], v)
            return body

        with nc.Block() as block:
            block.sync(emit("sync"))
            block.tensor(emit("tensor"))
            block.vector(emit("vector"))
            block.scalar(emit("scalar"))
            block.gpsimd(emit("gpsimd"))
        while self.stacks:
            self.stacks.pop().close()


D = 1024
IN_SPLITS = (("ret_q", 256), ("ret_k", 256), ("ret_v", 512), ("ret_g", 512),
             ("dsa_q", 512), ("dsa_k", 128), ("dsa_v", 128), ("dsa_g", 512),
             ("idx_q", 256), ("idx_k", 64), ("idx_w", 4),
             ("gla_q", 256), ("gla_k", 256), ("gla_v", 512), ("gla_g", 512), ("gla_a", 16),
             ("merge", 3072))
OFF = {}
_o = 0
for _n, _w in IN_SPLITS:
    OFF[_n] = _o
    _o += _w
IN_WIDTH = _o
RMS_EPS = 1e-6
TOPK = 256
BIGM = 32768.0
LOG_G = [float(np.log1p(-2.0 ** (-5.0 - h))) for h in range(4)]


class Cfg:
    def __init__(self, S=16384, NCORE=8, DEPTH=4, SG=4, NIT=16, CH=512):
        self.S, self.NCORE, self.DEPTH = S, NCORE, DEPTH
        self.TPC = S // 128 // NCORE
        self.T = self.TPC * 128
        self.GK = NCORE * 128
        self.BLK = min(512, self.GK)
        self.NB = self.BLK // 128
        self.BPG = self.GK // self.BLK
        self.SG = min(SG, self.TPC)
        self.NIT = NIT
        self.CH = CH
        self.TOPK = min(256, S // 4)


def host_consts():
    f = np.float32
    j = np.arange(128)[:, None]
    i = np.arange(128)[None, :]
    c = {}
    c["ident"] = np.eye(128, dtype=f)
    same = (j // 64) == (i // 64)
    c["triL128"] = (j <= i).astype(f)
    c["triL64"] = ((j <= i) & same).astype(f)
    c["triU64"] = ((j > i) & same).astype(f)
    ci = np.zeros((128, 4), f)
    ci[:64, 0] = 1
    ci[64:, 1] = 1
    ci[:, 2] = 1
    c["chunkind"] = ci
    lg = np.array(LOG_G, np.float64)
    dt = np.zeros((128, 4, 128), np.float64)
    for h in range(4):
        dt[:, h, :] = np.where(i >= j, np.exp(lg[h] * np.maximum(i - j, 0)), 0.0)
    c["DTret"] = (dt * 0.125).astype(f)
    c["DTgla"] = np.repeat(((j <= i) & same).astype(f)[:, None, :], 4, axis=1)
    qd = np.zeros((128, 4, 128), np.float64)
    for h in range(4):
        qd[:, h, :] = np.exp(lg[h] * (i + 1.0))
    c["QDret"] = (qd * 0.125).astype(f)
    kf = np.zeros((128, 4), np.float64)
    for h in range(4):
        kf[:, h] = np.exp(lg[h] * (127.0 - np.arange(128)))
    c["kfacR"] = kf.astype(f)
    dr = np.zeros((128, 4), np.float64)
    for h in range(4):
        dr[:, h] = np.exp(lg[h] * 128.0)
    c["decR"] = dr.astype(f)
    fr = np.zeros((128, 4), f)
    p = np.arange(128) % 64
    half = 32
    fr_ret = (np.float32(10000.0) ** (-(np.arange(half, dtype=f)) * f(2.0) / f(64))).astype(f)
    fr[:, 0] = fr_ret[p % 32]
    fr[:, 1] = np.where(p < 32, -1.0, 1.0)
    fr_d = (np.float32(500000.0) ** (-(np.arange(8, dtype=f)) * f(2.0) / f(16))).astype(f)
    fr[:, 2] = np.where(p < 16, fr_d[p % 8], 0.0)
    fr[:, 3] = np.where(p < 8, -1.0, np.where(p < 16, 1.0, 0.0))
    c["ropefs"] = fr
    c["pw2"] = np.repeat((2.0 ** -(np.arange(32, dtype=np.float64) + 1.0))[None, :], 128, axis=0).astype(f)
    return c


def swap_cols(lo, rot, width, nheads):
    idx = []
    for h in range(nheads):
        base = lo + h * width
        half = rot // 2
        for d in range(width):
            if d < half:
                idx.append(base + d + half)
            elif d < rot:
                idx.append(base + d - half)
            else:
                idx.append(base + d)
    return idx


A_COLS = [("retk", 256), ("retk_sw", 256), ("retv", 512), ("dsak", 128), ("dsak_sw", 128),
          ("dsav", 128), ("idxk", 64), ("idxk_sw", 64), ("glak", 256), ("glav", 512), ("glaa", 16)]


def col_layout(cols):
    off, o = {}, 0
    for n, w in cols:
        off[n] = (o, w)
        o += w
    return off, o


class Builder:
    def __init__(self, cfg, name):
        self.cfg = cfg
        self.nc = bass.Bass("TRN2", target_bir_lowering=False, name=name)
        self.P = Prog(self.nc)
        self.din = {}
        self.dout = {}
        self.bank_i = 0

    def inp(self, name, shape, dtype=F32):
        t = self.nc.dram_tensor(name, list(shape), dtype, kind="ExternalInput")
        self.din[name] = t
        return t.ap()

    def outp(self, name, shape, dtype=F32):
        t = self.nc.dram_tensor(name, list(shape), dtype, kind="ExternalOutput")
        self.dout[name] = t
        return t.ap()

    def make_banks(self):
        self.banks = [self.P.ps(f"bank{i}", [128, 512], F32) for i in range(8)]

    def bank(self):
        b = self.banks[self.bank_i % 8]
        self.bank_i += 1
        return b

    def load_consts(self, names):
        P = self.P
        hc = host_consts()
        self.c = {}
        self.cb = {}
        for n in names:
            shp = list(hc[n].shape)
            ap = self.inp("c_" + n, shp)
            t = P.sb("sc_" + n, shp, F32)
            P.dma(t.ap(), ap, writes=[t])
            self.c[n] = t
        self.identb = P.sb("identb", [128, 128], BF16)
        P.op("vector", lambda e: e.tensor_copy(self.identb.ap(), self.c["ident"].ap()),
             reads=[self.c["ident"]], writes=[self.identb])
        self.onesb = P.sb("onesb", [128, 128], BF16)
        P.op("vector", lambda e: e.memset(self.onesb.ap(), 1.0), writes=[self.onesb])

    def load_weight(self, dram_ap, c0, ncols, name, kc=8, rows=128):
        P = self.P
        wt = P.sb(name, [rows, kc, ncols], BF16)
        CH = 64 if hasattr(self, "wstage") else 128
        if not hasattr(self, "wstage"):
            self.wstage = [P.sb(f"wstage{i}", [128, 8, CH], F32) for i in range(2)]
            self.wstage_i = 0
        for s in range(0, ncols, CH):
            n = min(CH, ncols - s)
            st = self.wstage[self.wstage_i % 2]
            self.wstage_i += 1
            P.dma(st.ap()[0:rows, 0:kc, 0:n], dram_ap[:, :, c0 + s:c0 + s + n], writes=[st])
            P.op("gpsimd", lambda e, st=st, s=s, n=n: e.tensor_copy(
                wt.ap()[:, :, s:s + n], st.ap()[0:rows, 0:kc, 0:n]), reads=[st], writes=[wt])
        return wt

    def precast(self, src_ap, dst_ap, dbuf, rows, kc, ncols):
        P = self.P
        CH = 256
        st = [P.sb(f"pc_st{i}", [128, 8, CH], F32) for i in range(2)]
        ot = [P.sb(f"pc_ot{i}", [128, 8, CH], BF16) for i in range(2)]
        i = 0
        for s0 in range(0, ncols, CH):
            n = min(CH, ncols - s0)
            a, o = st[i % 2], ot[i % 2]
            P.dma(a.ap()[0:rows, 0:kc, 0:n], src_ap[:, :, s0:s0 + n], writes=[a])
            if i % 2 == 0:
                P.op("scalar", lambda e: e.activation(o.ap()[0:rows, 0:kc, 0:n], a.ap()[0:rows, 0:kc, 0:n], AF.Copy),
                     reads=[a], writes=[o])
            else:
                P.op("vector", lambda e: e.tensor_copy(o.ap()[0:rows, 0:kc, 0:n], a.ap()[0:rows, 0:kc, 0:n]),
                     reads=[a], writes=[o])
            P.dma(dst_ap[:, :, s0:s0 + n], o.ap()[0:rows, 0:kc, 0:n], reads=[o], writes=[dbuf])
            i += 1

    def load_weight_bf16(self, dram_ap, dbuf, c0, ncols, name, kc=8, rows=128):
        P = self.P
        wt = P.sb(name, [rows, kc, ncols], BF16)
        half = max(1, kc // 2)
        for k0 in range(0, kc, half):
            P.dma(wt.ap()[:, k0:k0 + half, :], dram_ap[:, k0:k0 + half, c0:c0 + ncols], reads=[dbuf], writes=[wt])
        return wt

    def prealloc(self, ncore):
        P = self.P
        self.rp = [P.sb(f"rp{i}", [128, 512], F32) for i in range(2)]
        self.dcg_g = P.sb("dcg_g", [64, ncore, 4], F32)

    def emit_mod(self, cvec_ap, adaw_ap, adab_ap, pre_ap, post_ap, want_gate):
        P = self.P
        self.G = P.sb("G", [128, 1024], F32)
        self.Sh = P.sb("Sh", [128, 1024], F32)
        if want_gate:
            self.GP = P.sb("GP", [128, 1024], F32)
        P.push_scope()
        cv = P.sb("cv", [128, 8], F32)
        P.dma(cv.ap(), cvec_ap, writes=[cv])
        ca = P.sb("ca", [128, 8], F32)
        P.op("scalar", lambda e: e.activation(ca.ap(), cv.ap(), AF.Silu), reads=[cv], writes=[ca])
        cbc = P.sb("cbc", [128, 8, 128], F32)
        P.op("vector", lambda e: e.tensor_copy(cbc.ap(), ca.ap().unsqueeze(2).to_broadcast([128, 8, 128])),
             reads=[ca], writes=[cbc])
        mod = P.sb("mod", [128, 3072], F32)
        bias = P.sb("modb", [128, 3072], F32)
        P.dma(bias.ap(), adab_ap.to_broadcast([128, 3072]), writes=[bias])
        wst = [P.sb(f"adaw{i}", [128, 8, 512], F32) for i in range(2)]
        for ci in range(6):
            w = wst[ci % 2]
            P.dma(w.ap(), adaw_ap[:, :, ci * 512:(ci + 1) * 512], writes=[w])
            bk = self.bank()
            for kc in range(8):
                P.op("tensor", lambda e, bk=bk, w=w, kc=kc: e.matmul(
                    bk.ap(), cbc.ap()[:, kc, :], w.ap()[:, kc, :], start=(kc == 0), stop=(kc == 7)),
                    reads=[cbc, w], writes=[bk])
            P.op("vector", lambda e, bk=bk, ci=ci: e.tensor_tensor(
                mod.ap()[:, ci * 512:(ci + 1) * 512], bk.ap(), bias.ap()[:, ci * 512:(ci + 1) * 512], ALU.add),
                reads=[bk, bias], writes=[mod])
        pre = P.sb("preb", [128, 1024], F32)
        P.dma(pre.ap(), pre_ap.to_broadcast([128, 1024]), writes=[pre])
        P.op("vector", lambda e: e.scalar_tensor_tensor(
            self.G.ap(), mod.ap()[:, 1024:2048], 1.0, pre.ap(), ALU.add, ALU.mult),
            reads=[mod, pre], writes=[self.G])
        P.op("vector", lambda e: e.tensor_copy(self.Sh.ap(), mod.ap()[:, 0:1024]), reads=[mod], writes=[self.Sh])
        if want_gate:
            post = P.sb("postb", [128, 1024], F32)
            P.dma(post.ap(), post_ap.to_broadcast([128, 1024]), writes=[post])
            P.op("vector", lambda e: e.tensor_tensor(self.GP.ap(), mod.ap()[:, 2048:3072], post.ap(), ALU.mult),
                 reads=[mod, post], writes=[self.GP])
        P.pop_scope()

    def emit_ropes(self, pos_ap, specs):
        P = self.P
        outs = []
        for fcol, scol, name in specs:
            outs.append((P.sb(name + "_cos", [128, self.cfg.T], BF16), P.sb(name + "_sin", [128, self.cfg.T], BF16)))
        P.push_scope()
        for (fcol, scol, name), (cb_, sb_) in zip(specs, outs):
            self.emit_rope(pos_ap, fcol, scol, name, cb_, sb_)
        P.pop_scope()
        del self.posf
        return outs

    def emit_rope(self, pos_ap, fcol, scol, name, cosb, sinb):
        P = self.P
        T = self.cfg.T
        fs = self.c["ropefs"]
        if not hasattr(self, "posf"):
            posi = P.sb("posi", [128, T], I32)
            P.dma(posi.ap(), pos_ap.to_broadcast([128, T]), writes=[posi])
            self.posf = P.sb("posf", [128, T], F32)
            P.op("vector", lambda e: e.tensor_copy(self.posf.ap(), posi.ap()), reads=[posi], writes=[self.posf])
            self.rtmp = [P.sb(f"rtmp{i}", [128, T], F32) for i in range(4)]
        ang, kf, r, rd = self.rtmp
        posf = self.posf
        PI = float(np.pi)
        C1 = 6.28125
        C2 = float(2.0 * np.pi - 6.28125)
        MAG = 12582912.0
        P.op("vector", lambda e: e.tensor_scalar(ang.ap(), posf.ap(), fs.ap()[:, fcol:fcol + 1], None, ALU.mult),
             reads=[posf, fs], writes=[ang])

        def reduce_and_sin(src, dst_final, shift):
            dst = rd
            if shift != 0.0:
                P.op("vector", lambda e: e.tensor_scalar(r.ap(), src.ap(), shift, None, ALU.add),
                     reads=[src], writes=[r])
                s2 = r
            else:
                s2 = src
            P.op("vector", lambda e: e.tensor_scalar(kf.ap(), s2.ap(), float(1.0 / (2 * np.pi)), MAG, ALU.mult, ALU.add),
                 reads=[s2], writes=[kf])
            P.op("vector", lambda e: e.tensor_scalar(kf.ap(), kf.ap(), MAG, None, ALU.subtract),
                 reads=[kf], writes=[kf])
            P.op("vector", lambda e: e.scalar_tensor_tensor(dst.ap(), kf.ap(), -C1, s2.ap(), ALU.mult, ALU.add),
                 reads=[kf, s2], writes=[dst])
            P.op("vector", lambda e: e.scalar_tensor_tensor(dst.ap(), kf.ap(), -C2, dst.ap(), ALU.mult, ALU.add),
                 reads=[kf, dst], writes=[dst])
            P.op("vector", lambda e: e.tensor_scalar(dst.ap(), dst.ap(), PI, -PI, ALU.min, ALU.max),
                 reads=[dst], writes=[dst])
            P.op("scalar", lambda e: e.activation(dst_final.ap(), dst.ap(), AF.Sin), reads=[dst], writes=[dst_final])

        reduce_and_sin(ang, sinb, 0.0)
        reduce_and_sin(ang, cosb, float(np.pi / 2))
        P.op("vector", lambda e: e.tensor_scalar(sinb.ap(), sinb.ap(), fs.ap()[:, scol:scol + 1], None, ALU.mult),
             reads=[sinb, fs], writes=[sinb])
        return cosb, sinb

    def emit_norm_T(self, x_tile_ap_dram, hT, slot):
        P = self.P
        if not hasattr(self, "xin"):
            self.xin = [P.sb(f"xin{i}", [128, 1024], F32) for i in range(1)]
            self.xin_i = 0
            self.hjunk = P.sb("hjunk", [128, 1024], BF16)
            self.h1 = P.sb("h1", [128, 1024], F32)
            self.hb = P.sb("hb", [128, 1024], BF16)
            self.nst = P.sb("nst", [128, 4], F32)
        xt = self.xin[0]
        self.xin_i += 1
        nst = self.nst
        P.dma(xt.ap(), x_tile_ap_dram, writes=[xt])
        P.op("scalar", lambda e: e.activation(self.hjunk.ap(), xt.ap(), AF.Square, accum_out=nst.ap()[:, 0:1]),
             reads=[xt], writes=[self.hjunk, nst])
        P.op("vector", lambda e: e.tensor_scalar(nst.ap()[:, 1:2], nst.ap()[:, 0:1], 1.0 / D, RMS_EPS, ALU.mult, ALU.add),
             reads=[nst], writes=[nst])
        P.op("scalar", lambda e: e.activation(nst.ap()[:, 2:3], nst.ap()[:, 1:2], AF.Sqrt), reads=[nst], writes=[nst])
        P.op("vector", lambda e: e.reciprocal(nst.ap()[:, 3:4], nst.ap()[:, 2:3]), reads=[nst], writes=[nst])
        P.op("vector", lambda e: e.scalar_tensor_tensor(self.h1.ap(), xt.ap(), nst.ap()[:, 3:4], self.G.ap(), ALU.mult, ALU.mult),
             reads=[xt, nst, self.G], writes=[self.h1])
        P.op("gpsimd", lambda e: e.tensor_tensor(self.hb.ap(), self.h1.ap(), self.Sh.ap(), ALU.add),
             reads=[self.h1, self.Sh], writes=[self.hb])
        for half in range(2):
            bk = self.bank()
            for q in range(4):
                kc = half * 4 + q
                P.op("tensor", lambda e, bk=bk, q=q, kc=kc: e.matmul(
                    bk.ap()[:, q * 128:(q + 1) * 128], self.hb.ap()[:, kc * 128:(kc + 1) * 128], self.identb.ap(),
                    start=True, stop=True), reads=[self.hb, self.identb], writes=[bk])
            P.op("scalar", lambda e, bk=bk, half=half: e.activation(
                hT.ap()[:, slot, half * 4:half * 4 + 4, :], bk.ap().rearrange("p (a b) -> p a b", a=4), AF.Copy),
                reads=[bk], writes=[hT])

    def proj_fm(self, out_ap, bk, w, c0, m, hT, slot, nslots=1):
        P = self.P
        for kc in range(8):
            if nslots == 1:
                rhs = hT.ap()[:, slot, kc, :]
            else:
                rhs = hT.ap()[:, slot:slot + nslots, kc, :]
            P.op("tensor", lambda e, kc=kc, rhs=rhs: e.matmul(
                out_ap, w.ap()[:, kc, c0:c0 + m], rhs, start=(kc == 0), stop=(kc == 7)),
                reads=[w, hT], writes=[bk])

    def proj_tm(self, out_ap, bk, w, c0, n, hT, slot):
        P = self.P
        for kc in range(8):
            P.op("tensor", lambda e, kc=kc: e.matmul(
                out_ap, hT.ap()[:, slot, kc, :], w.ap()[:, kc, c0:c0 + n], start=(kc == 0), stop=(kc == 7)),
                reads=[w, hT], writes=[bk])

    def rope_fm(self, dst_ap, dst_buf, bx, bs, npart, ncols_ap, cosb, sinb, tok0, scale=None):
        P = self.P
        if not hasattr(self, "rp"):
            self.rp = [P.sb(f"rp{i}", [128, 512], F32) for i in range(2)]
        t0, t1 = self.rp
        nh = ncols_ap // 128
        cosv = cosb.ap()[0:npart, tok0:tok0 + 128].unsqueeze(1).to_broadcast([npart, nh, 128])
        sinv = sinb.ap()[0:npart, tok0:tok0 + 128].unsqueeze(1).to_broadcast([npart, nh, 128])
        v = lambda b: b.ap()[0:npart, 0:ncols_ap].rearrange("p (h t) -> p h t", h=nh)
        P.op("vector", lambda e: e.tensor_tensor(v(t0), v(bx), cosv, ALU.mult), reads=[bx, cosb], writes=[t0])
        P.op("vector", lambda e: e.tensor_tensor(v(t1), v(bs), sinv, ALU.mult), reads=[bs, sinb], writes=[t1])
        if scale is None:
            P.op("vector", lambda e: e.tensor_tensor(dst_ap, v(t0), v(t1), ALU.add), reads=[t0, t1], writes=[dst_buf])
        else:
            P.op("vector", lambda e: e.scalar_tensor_tensor(dst_ap, v(t0), 1.0, v(t1), ALU.mult, ALU.add),
                 reads=[t0, t1], writes=[dst_buf])


def gla_decay_common(B, hT, slot, wA, aoff, wlr17):
    P = B.P
    if not hasattr(B, "a17"):
        B.a17 = P.sb("a17", [17, 128], BF16)
        P.op("vector", lambda e: e.memset(B.a17.ap(), 1.0), writes=[B.a17])
        B.e1 = P.sb("gl_e1", [128, 256], F32)
        B.sp = P.sb("gl_sp", [128, 256], F32)
    bk = B.bank()
    B.proj_fm(bk.ap()[0:16, 0:128], bk, wA, aoff, 16, hT, slot)
    P.op("vector", lambda e: e.tensor_copy(B.a17.ap()[0:16, :], bk.ap()[0:16, 0:128]), reads=[bk], writes=[B.a17])
    bz = B.bank()
    P.op("tensor", lambda e: e.matmul(bz.ap()[:, 0:256], B.a17.ap(), wlr17.ap(), start=True, stop=True),
         reads=[B.a17, wlr17], writes=[bz])
    P.op("scalar", lambda e: e.activation(B.e1.ap(), bz.ap()[:, 0:256], AF.Exp, scale=-1.0), reads=[bz], writes=[B.e1])
    P.op("scalar", lambda e: e.activation(B.sp.ap(), B.e1.ap(), AF.Ln, bias=1.0), reads=[B.e1], writes=[B.sp])
    return B.sp


def load_wlr17(B, wlr_ap, blr_ap):
    P = B.P
    st = P.sb("wlr_st", [17, 256], F32)
    P.dma(st.ap()[0:16, :], wlr_ap, writes=[st])
    P.dma(st.ap()[16:17, :], blr_ap, writes=[st])
    w = P.sb("wlr17", [17, 256], BF16)
    P.op("vector", lambda e: e.tensor_copy(w.ap(), st.ap()), reads=[st], writes=[w])
    return w


def build_A(cfg):
    B = Builder(cfg, "phaseA")
    P = B.P
    TPC, T = cfg.TPC, cfg.T
    aoff, ncolA = col_layout(A_COLS)
    x = B.inp("x", [TPC, 128, 1024])
    pos = B.inp("pos", [1, T], I32)
    cvec = B.inp("cvec", [128, 8])
    adaw = B.inp("adaw", [128, 8, 3072])
    adab = B.inp("adab", [1, 3072])
    pre = B.inp("pre", [1, 1024])
    post = B.inp("post", [1, 1024])
    WA = B.inp("WA", [128, 8, ncolA])
    wlr = B.inp("wlr", [16, 256])
    blr = B.inp("blr", [1, 256])
    oKT = B.outp("KT", [128, T], BF16)
    oV = B.outp("V", [TPC, 128, 128], BF16)
    oIK = B.outp("IK", [64, T], BF16)
    okvR = B.outp("kvR", [TPC, 64, 512])
    okvG = B.outp("kvG", [TPC, 64, 512])
    odecG = B.outp("decG", [TPC, 64, 4])
    omod = B.outp("modv", [3, 128, 1024])
    orope = B.outp("ropev", [4, 128, T], BF16)

    B.make_banks()
    B.load_consts(["ident", "triU64", "chunkind", "kfacR", "ropefs"])
    B.emit_mod(cvec, adaw, adab, pre, post, True)
    (cosR, sinR), (cosD, sinD) = B.emit_ropes(pos, [(0, 1, "rr"), (2, 3, "rd")])
    for i_, t_ in enumerate((B.G, B.Sh, B.GP)):
        P.dma(omod[i_], t_.ap(), reads=[t_])
    for i_, t_ in enumerate((cosR, sinR, cosD, sinD)):
        P.dma(orope[i_], t_.ap(), reads=[t_])
    wA = B.load_weight(WA, 0, ncolA, "wA")
    wlr17 = load_wlr17(B, wlr, blr)
    hT = P.sb("hT", [128, 1, 8, 128], BF16)

    kTb = P.sb("kTb", [64, 4, 128], BF16)
    khat = P.sb("khat", [128, 4, 64], BF16)
    vtok = P.sb("vtok", [128, 512], BF16)
    kvs = [P.sb(f"kvs{i}", [64, 512], F32) for i in range(2)]
    ktb = P.sb("ktb", [128, 128], BF16)
    vb = P.sb("vb", [128, 128], BF16)
    ikb = P.sb("ikb", [64, 128], BF16)
    kfac = P.sb("kfac", [128, 256], F32)
    gkhat = P.sb("gkhat", [128, 256], BF16)
    gvtok = P.sb("gvtok", [128, 512], BF16)
    dec = P.sb("dec", [64, 4, 4], F32)
    kv1s = P.sb("kv1s", [64, 512], F32)
    decT = P.sb("decT", [64, 4], F32)
    ident = B.c["ident"]

    for s in range(TPC):
        t0 = s * 128
        B.emit_norm_T(x[s], hT, 0)
        bx, bs = B.bank(), B.bank()
        for h in range(4):
            B.proj_fm(bx.ap()[0:64, h * 128:(h + 1) * 128], bx, wA, aoff["retk"][0] + h * 64, 64, hT, 0)
            B.proj_fm(bs.ap()[0:64, h * 128:(h + 1) * 128], bs, wA, aoff["retk_sw"][0] + h * 64, 64, hT, 0)
        B.rope_fm(kTb.ap(), kTb, bx, bs, 64, 512, cosR, sinR, t0)
        bt = B.bank()
        for h in range(4):
            P.op("tensor", lambda e, h=h: e.matmul(bt.ap()[:, h * 64:(h + 1) * 64], kTb.ap()[:, h, :],
                                                   B.identb.ap()[0:64, 0:64], start=True, stop=True),
                 reads=[kTb, B.identb], writes=[bt])
        P.op("vector", lambda e: e.tensor_tensor(
            khat.ap(), bt.ap()[:, 0:256].rearrange("p (h d) -> p h d", h=4),
            B.c["kfacR"].ap().unsqueeze(2).to_broadcast([128, 4, 64]), ALU.mult),
            reads=[bt, B.c["kfacR"]], writes=[khat])
        bv = B.bank()
        B.proj_tm(bv.ap(), bv, wA, aoff["retv"][0], 512, hT, 0)
        P.op("scalar", lambda e: e.activation(vtok.ap(), bv.ap(), AF.Copy), reads=[bv], writes=[vtok])
        bkv = B.bank()
        for h in range(4):
            P.op("tensor", lambda e, h=h: e.matmul(bkv.ap()[0:64, h * 128:(h + 1) * 128], khat.ap()[:, h, :],
                                                   vtok.ap()[:, h * 128:(h + 1) * 128], start=True, stop=True),
                 reads=[khat, vtok], writes=[bkv])
        kv = kvs[0]
        P.op("scalar", lambda e: e.activation(kv.ap(), bkv.ap()[0:64, :], AF.Copy), reads=[bkv], writes=[kv])
        P.dma(okvR[s], kv.ap(), reads=[kv])
        bx, bs = B.bank(), B.bank()
        B.proj_fm(bx.ap()[:, 0:128], bx, wA, aoff["dsak"][0], 128, hT, 0)
        B.proj_fm(bs.ap()[:, 0:128], bs, wA, aoff["dsak_sw"][0], 128, hT, 0)
        B.rope_fm(ktb.ap().unsqueeze(1), ktb, bx, bs, 128, 128, cosD, sinD, t0)
        P.dma(oKT[:, t0:t0 + 128], ktb.ap(), reads=[ktb])
        bv = B.bank()
        B.proj_tm(bv.ap()[:, 0:128], bv, wA, aoff["dsav"][0], 128, hT, 0)
        P.op("scalar", lambda e: e.activation(vb.ap(), bv.ap()[:, 0:128], AF.Copy), reads=[bv], writes=[vb])
        P.dma(oV[s], vb.ap(), reads=[vb])
        bx, bs = B.bank(), B.bank()
        B.proj_fm(bx.ap()[0:64, 0:128], bx, wA, aoff["idxk"][0], 64, hT, 0)
        B.proj_fm(bs.ap()[0:64, 0:128], bs, wA, aoff["idxk_sw"][0], 64, hT, 0)
        B.rope_fm(ikb.ap().unsqueeze(1), ikb, bx, bs, 64, 128, cosD, sinD, t0)
        P.dma(oIK[:, t0:t0 + 128], ikb.ap(), reads=[ikb])
        sp = gla_decay_common(B, hT, 0, wA, aoff["glaa"][0], wlr17)
        bd = B.bank()
        P.op("tensor", lambda e: e.matmul(bd.ap()[:, 0:256], B.c["triU64"].ap(), sp.ap(), start=True, stop=True),
             reads=[B.c["triU64"], sp], writes=[bd])
        P.op("scalar", lambda e: e.activation(kfac.ap(), bd.ap()[:, 0:256], AF.Exp, scale=-1.0 / 16.0),
             reads=[bd], writes=[kfac])
        bk = B.bank()
        B.proj_tm(bk.ap()[:, 0:256], bk, wA, aoff["glak"][0], 256, hT, 0)
        P.op("vector", lambda e: e.tensor_tensor(gkhat.ap(), bk.ap()[:, 0:256], kfac.ap(), ALU.mult),
             reads=[bk, kfac], writes=[gkhat])
        bv = B.bank()
        B.proj_tm(bv.ap(), bv, wA, aoff["glav"][0], 512, hT, 0)
        P.op("scalar", lambda e: e.activation(gvtok.ap(), bv.ap(), AF.Copy), reads=[bv], writes=[gvtok])
        b0, b1 = B.bank(), B.bank()
        for ch, bb in ((0, b0), (1, b1)):
            for h in range(4):
                P.op("tensor", lambda e, h=h, ch=ch, bb=bb: e.matmul(
                    bb.ap()[0:64, h * 128:(h + 1) * 128], gkhat.ap()[ch * 64:(ch + 1) * 64, h * 64:(h + 1) * 64],
                    gvtok.ap()[ch * 64:(ch + 1) * 64, h * 128:(h + 1) * 128], start=True, stop=True),
                    reads=[gkhat, gvtok], writes=[bb])
        bs_ = B.bank()
        for h in range(4):
            P.op("tensor", lambda e, h=h: e.matmul(bs_.ap()[0:64, h * 4:(h + 1) * 4], sp.ap()[:, h * 64:(h + 1) * 64],
                                                   B.c["chunkind"].ap(), start=True, stop=True),
                 reads=[sp, B.c["chunkind"]], writes=[bs_])
        P.op("scalar", lambda e: e.activation(dec.ap(), bs_.ap()[0:64, 0:16].rearrange("p (h c) -> p h c", h=4),
                                              AF.Exp, scale=-1.0 / 16.0), reads=[bs_], writes=[dec])
        P.op("scalar", lambda e: e.activation(kv1s.ap(), b1.ap()[0:64, :], AF.Copy), reads=[b1], writes=[kv1s])
        kv = kvs[1]
        for h in range(4):
            P.op("vector", lambda e, h=h: e.scalar_tensor_tensor(
                kv.ap()[:, h * 128:(h + 1) * 128], b0.ap()[0:64, h * 128:(h + 1) * 128], dec.ap()[:, h, 1:2],
                kv1s.ap()[:, h * 128:(h + 1) * 128], ALU.mult, ALU.add),
                reads=[b0, dec, kv1s], writes=[kv])
        P.dma(okvG[s], kv.ap(), reads=[kv])
        P.op("vector", lambda e: e.tensor_copy(decT.ap(), dec.ap()[:, :, 2]), reads=[dec], writes=[decT])
        P.dma(odecG[s], decT.ap(), reads=[decT])
    P.finish()
    return B


def prep_common(inputs, cfg, l, core):
    raise NotImplementedError


def a_col_index():
    ar = np.arange
    idx = {
        "retk": OFF["ret_k"] + ar(256), "retk_sw": np.array(swap_cols(OFF["ret_k"], 64, 64, 4)),
        "retv": OFF["ret_v"] + ar(512),
        "dsak": OFF["dsa_k"] + ar(128), "dsak_sw": np.array(swap_cols(OFF["dsa_k"], 16, 64, 2)),
        "dsav": OFF["dsa_v"] + ar(128),
        "idxk": OFF["idx_k"] + ar(64), "idxk_sw": np.array(swap_cols(OFF["idx_k"], 16, 64, 1)),
        "glak": OFF["gla_k"] + ar(256), "glav": OFF["gla_v"] + ar(512), "glaa": OFF["gla_a"] + ar(16),
    }
    return np.concatenate([idx[n] for n, _ in A_COLS])


def kc_layout(w):
    n = w.shape[1]
    return np.ascontiguousarray(w.reshape(8, 128, n).transpose(1, 0, 2))


def core_tiles(cfg, c):
    return [k * cfg.NCORE + c for k in range(cfg.TPC)]


def const_inputs(names):
    hc = host_consts()
    return {"c_" + n: hc[n] for n in names}


def host_inputs_A(inp, cfg, l):
    x = np.asarray(inp["x"])[0].reshape(cfg.S // 128, 128, D)
    pos = np.asarray(inp["positions"])[0].reshape(cfg.S // 128, 128)
    shared = {
        "cvec": np.ascontiguousarray(np.asarray(inp["c"])[0].reshape(8, 128).T),
        "adaw": kc_layout(np.asarray(inp["ada_w"])[l]),
        "adab": np.asarray(inp["ada_b"])[l][None, :],
        "pre": np.asarray(inp["pre_norm"])[l][None, :],
        "post": np.asarray(inp["post_norm"])[l][None, :],
        "WA": kc_layout(np.asarray(inp["w_in"])[l][:, a_col_index()]),
        "wlr": np.asarray(inp["gla_w_lr"])[l],
        "blr": np.asarray(inp["gla_b_lr"])[l][None, :],
    }
    shared.update(const_inputs(["ident", "triU64", "chunkind", "kfacR", "ropefs"]))
    maps = []
    for c in range(cfg.NCORE):
        tl = core_tiles(cfg, c)
        m = dict(shared)
        m["x"] = np.ascontiguousarray(x[tl])
        m["pos"] = np.ascontiguousarray(pos[tl].reshape(1, cfg.T)).astype(np.int32)
        maps.append(m)
    return maps


def np_inputs(S, seed=0, depth=4):
    r = np.random.RandomState(seed)
    f = np.float32
    n = lambda *s: r.randn(*s).astype(f)
    Dm = D
    return {
        "x": n(1, S, Dm), "c": n(1, Dm),
        "positions": (np.arange(S, dtype=np.int32)[None, :] + np.int32(r.randint(0, 4096))),
        "ada_w": n(depth, Dm, 3 * Dm) * f(0.1 * Dm ** -0.5), "ada_b": n(depth, 3 * Dm) * f(0.02),
        "pre_norm": 1 + f(0.05) * n(depth, Dm), "post_norm": 1 + f(0.05) * n(depth, Dm),
        "w_in": n(depth, Dm, IN_WIDTH) * f(Dm ** -0.5),
        "gla_w_lr": n(depth, 16, 256) * f(0.25), "gla_b_lr": f(0.1) * n(depth, 256),
        "w_br_ret": n(depth, 512, Dm) * f(512 ** -0.5), "w_br_dsa": n(depth, 512, Dm) * f(512 ** -0.5),
        "w_br_gla": n(depth, 512, Dm) * f(512 ** -0.5), "w_out": n(depth, Dm, Dm) * f(Dm ** -0.5),
    }


B_COLS = [("retq", 256), ("retq_sw", 256), ("retk", 256), ("retk_sw", 256), ("retv", 512), ("retg", 512),
          ("mg0", 1024),
          ("glaq", 256), ("glak", 256), ("glav", 512), ("glag", 512), ("glaa", 16), ("mg2", 1024),
          ("dsaq", 512), ("dsaq_sw", 512), ("idxq", 256), ("idxq_sw", 256), ("idxw", 4), ("dsag", 512),
          ("mg1", 1024)]


def b_col_index():
    ar = np.arange
    pair = np.concatenate([np.concatenate([ar(64) + j * 64, ar(64) + (j + 4) * 64]) for j in range(4)])
    dq = OFF["dsa_q"] + ar(512)
    dq_sw = np.array(swap_cols(OFF["dsa_q"], 16, 64, 8))
    idx = {
        "retq": OFF["ret_q"] + ar(256), "retq_sw": np.array(swap_cols(OFF["ret_q"], 64, 64, 4)),
        "retk": OFF["ret_k"] + ar(256), "retk_sw": np.array(swap_cols(OFF["ret_k"], 64, 64, 4)),
        "retv": OFF["ret_v"] + ar(512), "retg": OFF["ret_g"] + ar(512),
        "mg0": OFF["merge"] + ar(1024), "mg1": OFF["merge"] + 1024 + ar(1024), "mg2": OFF["merge"] + 2048 + ar(1024),
        "glaq": OFF["gla_q"] + ar(256), "glak": OFF["gla_k"] + ar(256), "glav": OFF["gla_v"] + ar(512),
        "glag": OFF["gla_g"] + ar(512), "glaa": OFF["gla_a"] + ar(16),
        "dsaq": dq[pair], "dsaq_sw": dq_sw[pair],
        "idxq": OFF["idx_q"] + ar(256), "idxq_sw": np.array(swap_cols(OFF["idx_q"], 16, 64, 4)),
        "idxw": OFF["idx_w"] + ar(4), "dsag": OFF["dsa_g"] + ar(512),
    }
    return np.concatenate([idx[n] for n, _ in B_COLS])


def build_B(cfg):
    B = Builder(cfg, "phaseB")
    P = B.P
    TPC, T, NCORE, SG = cfg.TPC, cfg.T, cfg.NCORE, cfg.SG
    GK, BLK, NB, BPG = cfg.GK, cfg.BLK, cfg.NB, cfg.BPG
    boff, ncolB = col_layout(B_COLS)
    x = B.inp("x", [TPC, 128, 1024])
    modv = B.inp("modv", [3, 128, 1024])
    ropev = B.inp("ropev", [4, 128, T], BF16)
    WB = B.inp("WB", [128, 8, ncolB])
    wbr_r = B.inp("wbr_r", [128, 4, 1024])
    wbr_g = B.inp("wbr_g", [128, 4, 1024])
    wbr_d = B.inp("wbr_d", [64, 8, 1024])
    wout_d = B.inp("wout", [128, 8, 1024])
    wlr = B.inp("wlr", [16, 256])
    blr = B.inp("blr", [1, 256])
    KTa = B.inp("KTa", [NCORE, 128, T], BF16)
    Va = B.inp("Va", [NCORE, TPC, 128, 128], BF16)
    IKa = B.inp("IKa", [NCORE, 64, T], BF16)
    kvRa = B.inp("kvRa", [NCORE, TPC, 64, 512])
    kvGa = B.inp("kvGa", [NCORE, TPC, 64, 512])
    decGa = B.inp("decGa", [NCORE, TPC, 64, 4])
    sel_d = B.inp("sel", [128, NCORE])
    pen_d = B.inp("pen", [128, GK], BF16)
    xo = B.outp("xo", [TPC, 128, 1024])

    B.make_banks()
    B.load_consts(B_CONSTS)
    sel = P.sb("sel", [128, NCORE], F32)
    P.dma(sel.ap(), sel_d, writes=[sel])
    pen = P.sb("pen", [128, GK], BF16)
    P.dma(pen.ap(), pen_d, writes=[pen])
    ident4 = P.sb("ident4", [128, 4, 128], BF16)
    P.op("vector", lambda e: e.tensor_copy(ident4.ap(), B.c["ident"].ap().unsqueeze(1).to_broadcast([128, 4, 128])),
         reads=[B.c["ident"]], writes=[ident4])
    B.G = P.sb("G", [128, 1024], F32)
    B.Sh = P.sb("Sh", [128, 1024], F32)
    B.GP = P.sb("GP", [128, 1024], F32)
    for i_, t_ in enumerate((B.G, B.Sh, B.GP)):
        P.dma(t_.ap(), modv[i_], writes=[t_])
    ropes_ = [P.sb(f"rope{i_}", [128, T], BF16) for i_ in range(4)]
    for i_, t_ in enumerate(ropes_):
        P.dma(t_.ap(), ropev[i_], writes=[t_])
    cosR, sinR, cosD, sinD = ropes_
    wlr17 = load_wlr17(B, wlr, blr)
    nc_ = B.nc
    WBb = nc_.dram_tensor("WBb", [128, 8, ncolB], BF16, kind="Internal").ap()
    wbr_rb = nc_.dram_tensor("wbr_rb", [128, 4, 1024], BF16, kind="Internal").ap()
    wbr_gb = nc_.dram_tensor("wbr_gb", [128, 4, 1024], BF16, kind="Internal").ap()
    wbr_db = nc_.dram_tensor("wbr_db", [64, 8, 1024], BF16, kind="Internal").ap()
    woutb = nc_.dram_tensor("woutb", [128, 8, 1024], BF16, kind="Internal").ap()
    wbuf = Buf("wscratch")
    P.push_scope()
    B.precast(WB, WBb, wbuf, 128, 8, ncolB)
    B.precast(wbr_r, wbr_rb, wbuf, 128, 4, 1024)
    B.precast(wbr_g, wbr_gb, wbuf, 128, 4, 1024)
    B.precast(wbr_d, wbr_db, wbuf, 64, 8, 1024)
    B.precast(wout_d, woutb, wbuf, 128, 8, 1024)
    P.pop_scope()
    wbr_r, wbr_g, wbr_d, wout_d = wbr_rb, wbr_gb, wbr_db, woutb
    B.prealloc(NCORE)
    SR = P.sb("SR", [64, 4, 128], F32)
    SGs = P.sb("SGs", [64, 4, 128], F32)
    P.op("vector", lambda e: e.memset(SR.ap(), 0.0), writes=[SR])
    P.op("vector", lambda e: e.memset(SGs.ap(), 0.0), writes=[SGs])
    hT = P.sb("hT", [128, SG, 8, 128], BF16)
    ysum = P.sb("ysum", [128, SG, 8, 128], BF16)
    dsaout = P.sb("dsaout", [64, SG, 8, 128], BF16)
    cap = P.sb("cap", [64, 4, 128], F32)
    capb = P.sb("capb", [64, 4, 128], BF16)

    def load_w(names, tag):
        c0 = boff[names[0]][0]
        n = sum(boff[k][1] for k in names)
        assert boff[names[-1]][0] + boff[names[-1]][1] == c0 + n
        return B.load_weight_bf16(WBb, wbuf, c0, n, tag), c0

    def load_w2(dram_ap, rows, kc, name):
        return B.load_weight_bf16(dram_ap, wbuf, 0, 1024, name, kc=kc, rows=rows)

    def scan(S, kv_all, dec_src, s, tag):
        if dec_src is not None:
            dcg = B.dcg_g
            P.dma(dcg.ap(), dec_src[:, s].rearrange("j d n -> d j n"), writes=[dcg])
        for j in range(NCORE):
            kvt = B.kvt[j % 2]
            P.dma(kvt.ap(), kv_all[j, s], writes=[kvt])
            if j == 0:
                P.op("vector", lambda e: e.tensor_scalar(cap.ap(), S.ap(), sel.ap()[0:64, 0:1], None, ALU.mult),
                     reads=[S, sel], writes=[cap])
            else:
                P.op("vector", lambda e: e.scalar_tensor_tensor(cap.ap(), S.ap(), sel.ap()[0:64, j:j + 1], cap.ap(),
                                                                ALU.mult, ALU.add), reads=[S, sel, cap], writes=[cap])
            if dec_src is None:
                dv = B.c["decR"].ap()[0:64, :].unsqueeze(2).to_broadcast([64, 4, 128])
                rd = [B.c["decR"]]
            else:
                dv = dcg.ap()[:, j, :].unsqueeze(2).to_broadcast([64, 4, 128])
                rd = [dcg]
            P.op("vector", lambda e: e.tensor_tensor(S.ap(), S.ap(), dv, ALU.mult), reads=[S] + rd, writes=[S])
            P.op("vector", lambda e: e.tensor_tensor(S.ap(), S.ap(), kvt.ap().rearrange("p (h e) -> p h e", h=4),
                                                     ALU.add), reads=[S, kvt], writes=[S])
        P.op("scalar", lambda e: e.activation(capb.ap(), cap.ap(), AF.Copy), reads=[cap], writes=[capb])

    def tail(bO, w, c0, gname, mgname, wbr, ls, first, tl):
        sq, r1, sg, tt, bro, mg, t2 = tl
        P.op("scalar", lambda e: e.activation(sq.ap(), bO.ap(), AF.Square), reads=[bO], writes=[sq])
        bQ = B.bank()
        P.op("tensor", lambda e: e.matmul(bQ.ap(), B.onesb.ap(), sq.ap(), start=True, stop=True),
             reads=[B.onesb, sq], writes=[bQ])
        P.op("vector", lambda e: e.tensor_scalar(r1.ap(), bQ.ap(), 1.0 / 128.0, RMS_EPS, ALU.mult, ALU.add),
             reads=[bQ], writes=[r1])
        P.op("scalar", lambda e: e.activation(r1.ap(), r1.ap(), AF.Sqrt), reads=[r1], writes=[r1])
        P.op("vector", lambda e: e.reciprocal(r1.ap(), r1.ap()), reads=[r1], writes=[r1])
        bG = B.bank()
        g0 = boff[gname][0] - c0
        for h in range(4):
            B.proj_fm(bG.ap()[:, h * 128:(h + 1) * 128], bG, w, g0 + h * 128, 128, hT, ls)
        P.op("scalar", lambda e: e.activation(sg.ap(), bG.ap(), AF.Silu), reads=[bG], writes=[sg])
        P.op("vector", lambda e: e.tensor_tensor(tt.ap(), bO.ap(), r1.ap(), ALU.mult), reads=[bO, r1], writes=[tt])
        P.op("gpsimd", lambda e: e.tensor_tensor(bro.ap(), tt.ap(), sg.ap(), ALU.mult), reads=[tt, sg], writes=[bro])
        merge(lambda nc, out_ap, bk: [P.op("tensor", lambda e, h=h: e.matmul(
            out_ap, wbr.ap()[:, h, nc * 128:(nc + 1) * 128], bro.ap()[:, h * 128:(h + 1) * 128],
            start=(h == 0), stop=(h == 3)), reads=[wbr, bro], writes=[bk]) for h in range(4)],
            w, boff[mgname][0] - c0, ls, first, mg, t2)

    def merge(emit_branch, w, m0, ls, first, mg, t2):
        bY = [B.bank(), B.bank()]
        for nc in range(8):
            bk = bY[nc // 4]
            emit_branch(nc, bk.ap()[:, (nc % 4) * 128:(nc % 4 + 1) * 128], bk)
        bM = [B.bank(), B.bank()]
        for nc in range(8):
            bk = bM[nc // 4]
            B.proj_fm(bk.ap()[:, (nc % 4) * 128:(nc % 4 + 1) * 128], bk, w, m0 + nc * 128, 128, hT, ls)
        for hf in range(2):
            P.op("scalar", lambda e: e.activation(mg.ap(), bM[hf].ap(), AF.Sigmoid), reads=[bM[hf]], writes=[mg])
            yv = ysum.ap()[:, ls, hf * 4:(hf + 1) * 4, :]
            m3 = mg.ap().rearrange("p (a b) -> p a b", a=4)
            b3 = bY[hf].ap().rearrange("p (a b) -> p a b", a=4)
            if first:
                P.op("vector", lambda e: e.tensor_tensor(yv, m3, b3, ALU.mult), reads=[mg, bY[hf]], writes=[ysum])
            else:
                P.op("vector", lambda e: e.tensor_tensor(t2.ap(), mg.ap(), bY[hf].ap(), ALU.mult),
                     reads=[mg, bY[hf]], writes=[t2])
                P.op("gpsimd", lambda e: e.tensor_tensor(yv, yv, t2.ap().rearrange("p (a b) -> p a b", a=4), ALU.add),
                     reads=[ysum, t2], writes=[ysum])

    def tail_bufs():
        return (P.sb("t_sq", [128, 512], BF16), P.sb("t_r1", [128, 512], F32), P.sb("t_sg", [128, 512], F32),
                P.sb("t_tt", [128, 512], F32), P.sb("t_bro", [128, 512], BF16), P.sb("t_mg", [128, 512], F32),
                P.sb("t_t2", [128, 512], F32))

    for g0 in range(0, TPC, SG):
        P.push_scope()
        for s in range(g0, g0 + SG):
            B.emit_norm_T(x[s], hT, s - g0)
        P.pop_scope()
        del B.xin
        P.push_scope()
        w, c0 = load_w(["retq", "retq_sw", "retk", "retk_sw", "retv", "retg", "mg0"], "w_ret")
        wbr = load_w2(wbr_r, 128, 4, "wbr_ret")
        B.kvt = [P.sb(f"kvt{i}", [64, 512], F32) for i in range(2)]
        tl = tail_bufs()
        qTb = P.sb("qTb", [64, 4, 128], BF16)
        kTb = P.sb("kTb", [64, 4, 128], BF16)
        qhat = P.sb("qhat", [64, 4, 128], BF16)
        vtok = P.sb("vtok", [128, 512], BF16)
        Sm = P.sb("Sm", [128, 512], BF16)
        for s in range(g0, g0 + SG):
            ls = s - g0
            t0 = s * 128
            scan(SR, kvRa, None, s, "r")
            for nm, dst in (("retq", qTb), ("retk", kTb)):
                bx, bs = B.bank(), B.bank()
                for h in range(4):
                    B.proj_fm(bx.ap()[0:64, h * 128:(h + 1) * 128], bx, w, boff[nm][0] - c0 + h * 64, 64, hT, ls)
                    B.proj_fm(bs.ap()[0:64, h * 128:(h + 1) * 128], bs, w, boff[nm + "_sw"][0] - c0 + h * 64, 64, hT, ls)
                B.rope_fm(dst.ap(), dst, bx, bs, 64, 512, cosR, sinR, t0)
            bv = B.bank()
            B.proj_tm(bv.ap(), bv, w, boff["retv"][0] - c0, 512, hT, ls)
            P.op("scalar", lambda e: e.activation(vtok.ap(), bv.ap(), AF.Copy), reads=[bv], writes=[vtok])
            bS = B.bank()
            for h in range(4):
                P.op("tensor", lambda e: e.matmul(bS.ap()[:, h * 128:(h + 1) * 128], kTb.ap()[:, h, :], qTb.ap()[:, h, :],
                                                  start=True, stop=True), reads=[kTb, qTb], writes=[bS])
            P.op("vector", lambda e: e.tensor_tensor(Sm.ap(), bS.ap(), B.c["DTret"].ap().rearrange("p h i -> p (h i)"),
                                                     ALU.mult), reads=[bS, B.c["DTret"]], writes=[Sm])
            P.op("gpsimd", lambda e: e.tensor_tensor(qhat.ap(), qTb.ap(), B.c["QDret"].ap()[0:64], ALU.mult),
                 reads=[qTb, B.c["QDret"]], writes=[qhat])
            bO = B.bank()
            for h in range(4):
                o_ap = bO.ap()[:, h * 128:(h + 1) * 128]
                P.op("tensor", lambda e: e.matmul(o_ap, vtok.ap()[:, h * 128:(h + 1) * 128], Sm.ap()[:, h * 128:(h + 1) * 128],
                                                  start=True, stop=False), reads=[vtok, Sm], writes=[bO])
                P.op("tensor", lambda e: e.matmul(o_ap, capb.ap()[:, h, :], qhat.ap()[:, h, :], start=False, stop=True),
                     reads=[capb, qhat], writes=[bO])
            tail(bO, w, c0, "retg", "mg0", wbr, ls, True, tl)
        P.pop_scope()
        P.push_scope()
        w, c0 = load_w(["glaq", "glak", "glav", "glag", "glaa", "mg2"], "w_gla")
        wbr = load_w2(wbr_g, 128, 4, "wbr_gla")
        B.kvt = [P.sb(f"kvt{i}", [64, 512], F32) for i in range(2)]
        B.a17 = P.sb("a17", [17, 128], BF16)
        P.op("vector", lambda e: e.memset(B.a17.ap(), 1.0), writes=[B.a17])
        B.e1 = P.sb("gl_e1", [128, 256], F32)
        B.sp = P.sb("gl_sp", [128, 256], F32)
        tl = tail_bufs()
        eq = P.sb("eq", [64, 512], F32)
        ek = P.sb("ek", [64, 512], F32)
        e128 = P.sb("e128", [64, 512], F32)
        qt = P.sb("qt", [64, 4, 128], BF16)
        qh = P.sb("qh", [64, 4, 128], BF16)
        kt = P.sb("kt", [64, 4, 128], BF16)
        kfac = P.sb("kfac", [128, 256], F32)
        gkhat = P.sb("gkhat", [128, 256], BF16)
        vtok = P.sb("gvtok", [128, 512], BF16)
        kv0b = P.sb("kv0b", [64, 512], BF16)
        Sm = P.sb("gSm", [128, 512], BF16)
        for s in range(g0, g0 + SG):
            ls = s - g0
            scan(SGs, kvGa, decGa, s, "g")
            sp = gla_decay_common(B, hT, ls, w, boff["glaa"][0] - c0, wlr17)
            bC64, bC128 = B.bank(), B.bank()
            for h in range(4):
                for bb, tri in ((bC64, "triL64"), (bC128, "triL128")):
                    P.op("tensor", lambda e: e.matmul(bb.ap()[0:64, h * 128:(h + 1) * 128], sp.ap()[:, h * 64:(h + 1) * 64],
                                                      B.c[tri].ap(), start=True, stop=True), reads=[sp, B.c[tri]], writes=[bb])
            P.op("scalar", lambda e: e.activation(eq.ap(), bC64.ap()[0:64, :], AF.Exp, scale=-1.0 / 16), reads=[bC64], writes=[eq])
            P.op("scalar", lambda e: e.activation(ek.ap(), bC64.ap()[0:64, :], AF.Exp, scale=1.0 / 16), reads=[bC64], writes=[ek])
            P.op("scalar", lambda e: e.activation(e128.ap(), bC128.ap()[0:64, :], AF.Exp, scale=-1.0 / 16), reads=[bC128], writes=[e128])
            bq = B.bank()
            for h in range(4):
                B.proj_fm(bq.ap()[0:64, h * 128:(h + 1) * 128], bq, w, boff["glaq"][0] - c0 + h * 64, 64, hT, ls)
            f2 = lambda b: b.ap().rearrange("p h t -> p (h t)")
            P.op("vector", lambda e: e.scalar_tensor_tensor(f2(qt), bq.ap()[0:64, :], 0.125, eq.ap(), ALU.mult, ALU.mult),
                 reads=[bq, eq], writes=[qt])
            P.op("vector", lambda e: e.scalar_tensor_tensor(f2(qh), bq.ap()[0:64, :], 0.125, e128.ap(), ALU.mult, ALU.mult),
                 reads=[bq, e128], writes=[qh])
            bk = B.bank()
            for h in range(4):
                B.proj_fm(bk.ap()[0:64, h * 128:(h + 1) * 128], bk, w, boff["glak"][0] - c0 + h * 64, 64, hT, ls)
            P.op("vector", lambda e: e.tensor_tensor(f2(kt), bk.ap()[0:64, :], ek.ap(), ALU.mult), reads=[bk, ek], writes=[kt])
            bd = B.bank()
            P.op("tensor", lambda e: e.matmul(bd.ap()[:, 0:256], B.c["triU64"].ap(), sp.ap(), start=True, stop=True),
                 reads=[B.c["triU64"], sp], writes=[bd])
            P.op("scalar", lambda e: e.activation(kfac.ap(), bd.ap()[:, 0:256], AF.Exp, scale=-1.0 / 16.0), reads=[bd], writes=[kfac])
            bkt = B.bank()
            B.proj_tm(bkt.ap()[:, 0:256], bkt, w, boff["glak"][0] - c0, 256, hT, ls)
            P.op("vector", lambda e: e.tensor_tensor(gkhat.ap(), bkt.ap()[:, 0:256], kfac.ap(), ALU.mult),
                 reads=[bkt, kfac], writes=[gkhat])
            bv = B.bank()
            B.proj_tm(bv.ap(), bv, w, boff["glav"][0] - c0, 512, hT, ls)
            P.op("scalar", lambda e: e.activation(vtok.ap(), bv.ap(), AF.Copy), reads=[bv], writes=[vtok])
            b0 = B.bank()
            for h in range(4):
                P.op("tensor", lambda e: e.matmul(b0.ap()[0:64, h * 128:(h + 1) * 128], gkhat.ap()[0:64, h * 64:(h + 1) * 64],
                                                  vtok.ap()[0:64, h * 128:(h + 1) * 128], start=True, stop=True),
                     reads=[gkhat, vtok], writes=[b0])
            P.op("scalar", lambda e: e.activation(kv0b.ap(), b0.ap()[0:64, :], AF.Copy), reads=[b0], writes=[kv0b])
            bS = B.bank()
            for h in range(4):
                P.op("tensor", lambda e: e.matmul(bS.ap()[:, h * 128:(h + 1) * 128], kt.ap()[:, h, :], qt.ap()[:, h, :],
                                                  start=True, stop=True), reads=[kt, qt], writes=[bS])
            P.op("vector", lambda e: e.tensor_tensor(Sm.ap(), bS.ap(), B.c["DTgla"].ap().rearrange("p h i -> p (h i)"),
                                                     ALU.mult), reads=[bS, B.c["DTgla"]], writes=[Sm])
            bO = B.bank()
            for h in range(4):
                o_ap = bO.ap()[:, h * 128:(h + 1) * 128]
                P.op("tensor", lambda e: e.matmul(o_ap, vtok.ap()[:, h * 128:(h + 1) * 128], Sm.ap()[:, h * 128:(h + 1) * 128],
                                                  start=True, stop=False), reads=[vtok, Sm], writes=[bO])
                P.op("tensor", lambda e: e.matmul(o_ap, capb.ap()[:, h, :], qh.ap()[:, h, :], start=False, stop=False),
                     reads=[capb, qh], writes=[bO])
                P.op("tensor", lambda e: e.matmul(bO.ap()[:, h * 128 + 64:(h + 1) * 128], kv0b.ap()[:, h * 128:(h + 1) * 128],
                                                  qt.ap()[:, h, 64:128], start=False, stop=True), reads=[kv0b, qt], writes=[bO])
            tail(bO, w, c0, "glag", "mg2", wbr, ls, False, tl)
        P.pop_scope()
        dsa_stage(B, cfg, g0, locals())
        P.push_scope()
        wo = load_w2(wout_d, 128, 8, "w_out")
        xt = P.sb("f_xt", [128, 1024], F32)
        fj = P.sb("f_junk", [128, 512], BF16)
        fst = P.sb("f_st", [128, 8], F32)
        ft = P.sb("f_t", [128, 1024], F32)
        fo = P.sb("f_o", [128, 1024], F32)
        for s in range(g0, g0 + SG):
            ls = s - g0
            P.dma(xt.ap(), x[s], writes=[xt])
            bh = [B.bank(), B.bank()]
            for hf in range(2):
                for nc in range(8):
                    P.op("tensor", lambda e: e.matmul(bh[hf].ap(), ysum.ap()[:, ls, nc, :], wo.ap()[:, nc, hf * 512:(hf + 1) * 512],
                                                      start=(nc == 0), stop=(nc == 7)), reads=[ysum, wo], writes=[bh[hf]])
                P.op("scalar", lambda e: e.activation(fj.ap(), bh[hf].ap(), AF.Square, accum_out=fst.ap()[:, hf:hf + 1]),
                     reads=[bh[hf]], writes=[fj, fst])
            P.op("vector", lambda e: e.tensor_tensor(fst.ap()[:, 2:3], fst.ap()[:, 0:1], fst.ap()[:, 1:2], ALU.add), reads=[fst], writes=[fst])
            P.op("vector", lambda e: e.tensor_scalar(fst.ap()[:, 3:4], fst.ap()[:, 2:3], 1.0 / D, RMS_EPS, ALU.mult, ALU.add),
                 reads=[fst], writes=[fst])
            P.op("scalar", lambda e: e.activation(fst.ap()[:, 4:5], fst.ap()[:, 3:4], AF.Sqrt), reads=[fst], writes=[fst])
            P.op("vector", lambda e: e.reciprocal(fst.ap()[:, 5:6], fst.ap()[:, 4:5]), reads=[fst], writes=[fst])
            for hf in range(2):
                sl = slice(hf * 512, (hf + 1) * 512)
                P.op("vector", lambda e: e.scalar_tensor_tensor(ft.ap()[:, sl], bh[hf].ap(), fst.ap()[:, 5:6], B.GP.ap()[:, sl],
                                                                ALU.mult, ALU.mult), reads=[bh[hf], fst, B.GP], writes=[ft])
            P.op("gpsimd", lambda e: e.tensor_tensor(fo.ap(), ft.ap(), xt.ap(), ALU.add), reads=[ft, xt], writes=[fo])
            P.dma(xo[s], fo.ap(), reads=[fo])
        P.pop_scope()
    P.finish()
    return B


def dsa_stage(B, cfg, g0, env):
    P = B.P
    TPC, T, NCORE, SG = cfg.TPC, cfg.T, cfg.NCORE, cfg.SG
    GK, BLK, NB, BPG, NIT = cfg.GK, cfg.BLK, cfg.NB, cfg.BPG, cfg.NIT
    hT, ysum, dsaout, boff = env["hT"], env["ysum"], env["dsaout"], env["boff"]
    KTa, Va, IKa, pen, ident4 = env["KTa"], env["Va"], env["IKa"], env["pen"], env["ident4"]
    cosD, sinD = env["cosD"], env["sinD"]
    P.push_scope()
    w, c0 = env["load_w"](["dsaq", "dsaq_sw", "idxq", "idxq_sw", "idxw", "dsag"], "w_dsa")
    NMAX = TPC * GK
    scores = P.sb("scores", [128, NMAX], F32)
    CH = min(cfg.CH, NMAX)
    junk = P.sb("cjunk", [128, CH], BF16)
    QT = P.sb("QT", [128, 4, 128], BF16)
    IQ = P.sb("IQ", [64, 4, 128], BF16)
    wq = P.sb("wq", [128, 4], F32)
    rl = [P.sb(f"rl{i}", [128, BLK], F32) for i in range(2)]
    ikb = [P.sb(f"ikb{i}", [64, NB, 128], BF16) for i in range(2)]
    ktb = [P.sb(f"ktb{i}", [128, NB, 128], BF16) for i in range(2)]
    vbk = [P.sb(f"vbk{i}", [128, NB, 2, 128], BF16) for i in range(2)]
    for v in vbk:
        P.op("vector", lambda e: e.memset(v.ap(), 1.0), writes=[v])
    nbmax = TPC * BPG
    wall = P.sb("wall", [128, 32], F32)
    cntc = P.sb("cntc", [128, 32], F32)
    cnta = P.sb("cnta", [128, 32], F32)
    junk2 = P.sb("cjunk2", [128, CH], BF16)
    bst = P.sb("bst", [128, 16], F32)
    mlo = P.sb("mlo", [128, BLK], BF16)
    mhi = P.sb("mhi", [128, BLK], BF16)
    band = P.sb("band", [128, BLK], BF16)
    cum = [P.sb(f"cum{i}", [128, BLK], F32) for i in range(2)]
    tsel = mlo
    mb = [P.sb(f"mb{i}", [128, BLK], BF16) for i in range(2)]
    PT = [P.sb(f"PT{i}", [128, 512], BF16) for i in range(4)]
    rc = P.sb("rc", [64, 512], F32)
    on = P.sb("on", [64, 512], BF16)
    sgd = P.sb("sgd", [64, 8, 128], BF16)
    acc = [B.banks[6], B.banks[7]]
    saved_i = B.bank_i
    rot = {"i": 0}

    def bank6():
        b = B.banks[rot["i"] % 6]
        rot["i"] += 1
        return b
    B_bank = B.bank
    B.bank = bank6
    col = lambda i: bst.ap()[:, i:i + 1]

    for s in range(g0, g0 + SG):
        ls = s - g0
        t0 = s * 128
        bx, bs = bank6(), bank6()
        for j in range(4):
            B.proj_fm(bx.ap()[:, j * 128:(j + 1) * 128], bx, w, boff["dsaq"][0] - c0 + j * 128, 128, hT, ls)
            B.proj_fm(bs.ap()[:, j * 128:(j + 1) * 128], bs, w, boff["dsaq_sw"][0] - c0 + j * 128, 128, hT, ls)
        B.rope_fm(QT.ap(), QT, bx, bs, 128, 512, cosD, sinD, t0)
        bx, bs = bank6(), bank6()
        for h in range(4):
            B.proj_fm(bx.ap()[0:64, h * 128:(h + 1) * 128], bx, w, boff["idxq"][0] - c0 + h * 64, 64, hT, ls)
            B.proj_fm(bs.ap()[0:64, h * 128:(h + 1) * 128], bs, w, boff["idxq_sw"][0] - c0 + h * 64, 64, hT, ls)
        B.rope_fm(IQ.ap(), IQ, bx, bs, 64, 512, cosD, sinD, t0)
        bw = bank6()
        B.proj_tm(bw.ap()[:, 0:4], bw, w, boff["idxw"][0] - c0, 4, hT, ls)
        P.op("vector", lambda e: e.tensor_scalar(wq.ap(), bw.ap()[:, 0:4], 0.0625, None, ALU.mult), reads=[bw], writes=[wq])
        nb = (s + 1) * BPG
        n = (s + 1) * GK
        for bi in range(nb):
            gq, blk = bi // BPG, bi % BPG
            ik = ikb[bi % 2]
            P.dma(ik.ap(), IKa[blk * NB:(blk + 1) * NB, :, gq * 128:(gq + 1) * 128].rearrange("j d t -> d j t"), writes=[ik])
            scb = scores.ap()[:, bi * BLK:(bi + 1) * BLK]
            for h in range(4):
                bI = bank6()
                P.op("tensor", lambda e: e.matmul(bI.ap()[:, 0:BLK], IQ.ap()[:, h, :], ik.ap().rearrange("d j t -> d (j t)"),
                                                  start=True, stop=True), reads=[IQ, ik], writes=[bI])
                r = rl[h % 2]
                P.op("scalar", lambda e: e.activation(r.ap(), bI.ap()[:, 0:BLK], AF.Relu), reads=[bI], writes=[r])
                if h == 0:
                    P.op("vector", lambda e: e.tensor_scalar(scb, r.ap(), wq.ap()[:, 0:1], None, ALU.mult),
                         reads=[r, wq], writes=[scores])
                else:
                    P.op("vector", lambda e: e.scalar_tensor_tensor(scb, r.ap(), wq.ap()[:, h:h + 1], scb, ALU.mult, ALU.add),
                         reads=[r, wq, scores], writes=[scores])

        P.op("vector", lambda e: e.tensor_reduce(col(8), scores.ap()[:, 0:n], AX.X, ALU.max), reads=[scores], writes=[bst])
        P.op("vector", lambda e: e.tensor_scalar(col(1), col(8), 1.0, None, ALU.add), reads=[bst], writes=[bst])
        P.op("vector", lambda e: e.tensor_reduce(col(9), scores.ap()[:, 0:n], AX.X, ALU.min), reads=[scores, bst], writes=[bst])
        P.op("vector", lambda e: e.tensor_scalar(col(0), col(9), -1.0, None, ALU.add), reads=[bst], writes=[bst])
        for blk in range(BPG):
            bi = s * BPG + blk
            scb = scores.ap()[:, bi * BLK:(bi + 1) * BLK]
            P.op("vector", lambda e: e.tensor_tensor(scb, scb, pen.ap()[:, blk * BLK:(blk + 1) * BLK], ALU.add),
                 reads=[scores, pen], writes=[scores])
        P.op("vector", lambda e: e.tensor_tensor(col(10), col(1), col(0), ALU.subtract), reads=[bst], writes=[bst])
        P.op("vector", lambda e: e.tensor_scalar(wall.ap(), B.c["pw2"].ap(), col(10), None, ALU.mult),
             reads=[B.c["pw2"], bst], writes=[wall])
        P.op("vector", lambda e: e.tensor_tensor(col(2), col(0), wall.ap()[:, 0:1], ALU.add), reads=[bst, wall], writes=[bst])
        chunks = [(cs, min(n, cs + CH)) for cs in range(0, n, CH)]

        def count(thr_col, dst_col, use_act):
            nd = na = 0
            l_act = 0
            for ci, (cs, ce) in enumerate(chunks):
                if use_act and ci % 2 == 1:
                    P.op("scalar", lambda e: e.activation(junk2.ap()[:, 0:ce - cs], scores.ap()[:, cs:ce], AF.Sign, bias=col(thr_col),
                                                          scale=-1.0, accum_out=cnta.ap()[:, na:na + 1]),
                         reads=[scores, bst], writes=[cnta, junk2])
                    na += 1
                    l_act += ce - cs
                else:
                    P.op("vector", lambda e: e.tensor_scalar(junk.ap()[:, 0:ce - cs], scores.ap()[:, cs:ce], col(thr_col), None,
                                                             ALU.is_ge, ALU.add, accum_out=cntc.ap()[:, nd:nd + 1]),
                         reads=[scores, bst], writes=[cntc, junk])
                    nd += 1
            P.op("vector", lambda e: e.tensor_reduce(col(dst_col), cntc.ap()[:, 0:nd], AX.X, ALU.add),
                 reads=[cntc], writes=[bst])
            if na == 0:
                return dst_col, cfg.TOPK - 0.5
            P.op("vector", lambda e: e.tensor_reduce(col(5), cnta.ap()[:, 0:na], AX.X, ALU.add), reads=[cnta, bst], writes=[bst])
            P.op("vector", lambda e: e.scalar_tensor_tensor(col(11), col(dst_col), 2.0, col(5), ALU.mult, ALU.subtract),
                 reads=[bst], writes=[bst])
            return 11, 2.0 * cfg.TOPK - 1.0 - l_act

        for it in range(NIT):
            ccol, cthr = count(2, 3, True)
            P.op("vector", lambda e: e.scalar_tensor_tensor(col(4), col(ccol), cthr, wall.ap()[:, it:it + 1],
                                                            ALU.is_ge, ALU.mult), reads=[bst, wall], writes=[bst])
            P.op("vector", lambda e: e.tensor_tensor(col(0), col(0), col(4), ALU.add), reads=[bst], writes=[bst])
            if it + 1 < NIT:
                P.op("vector", lambda e: e.tensor_tensor(col(2), col(0), wall.ap()[:, it + 1:it + 2], ALU.add),
                     reads=[bst, wall], writes=[bst])
        P.op("vector", lambda e: e.tensor_tensor(col(1), col(0), wall.ap()[:, NIT - 1:NIT], ALU.add), reads=[bst, wall], writes=[bst])
        count(1, 6, False)
        P.op("vector", lambda e: e.tensor_scalar(col(7), col(6), -BIGM, cfg.TOPK * BIGM, ALU.mult, ALU.add), reads=[bst], writes=[bst])
        blkst = {}

        def prep_block(bi):
            gq, blk = bi // BPG, bi % BPG
            kt_, vb_ = ktb[bi % 2], vbk[bi % 2]
            P.dma(kt_.ap(), KTa[blk * NB:(blk + 1) * NB, :, gq * 128:(gq + 1) * 128].rearrange("j d t -> d j t"), writes=[kt_])
            for jj in range(NB):
                P.dma(vb_.ap()[:, jj, :, 0:64], Va[blk * NB + jj, gq].rearrange("s (k d) -> s k d", k=2), writes=[vb_])
            scb = scores.ap()[:, bi * BLK:(bi + 1) * BLK]
            m = mb[bi % 2]
            cm, cprev = cum[bi % 2], cum[(bi + 1) % 2]
            P.op("vector", lambda e: e.tensor_scalar(mlo.ap(), scb, col(0), BIGM, ALU.is_ge, ALU.mult), reads=[scores, bst], writes=[mlo])
            P.op("vector", lambda e: e.tensor_scalar(mhi.ap(), scb, col(1), BIGM, ALU.is_ge, ALU.mult), reads=[scores, bst], writes=[mhi])
            P.op("vector", lambda e: e.tensor_tensor(band.ap(), mlo.ap(), mhi.ap(), ALU.subtract), reads=[mlo, mhi], writes=[band])
            P.op("vector", lambda e: e.tensor_tensor_scan(cm.ap(), band.ap(), band.ap(),
                                                          (0.0 if bi == 0 else cprev.ap()[:, BLK - 1:BLK]), ALU.add, ALU.max),
                 reads=[band] + ([] if bi == 0 else [cprev]), writes=[cm])
            P.op("vector", lambda e: e.scalar_tensor_tensor(tsel.ap(), cm.ap(), col(7), band.ap(), ALU.is_le, ALU.mult),
                 reads=[cm, bst, band], writes=[tsel])
            P.op("vector", lambda e: e.scalar_tensor_tensor(m.ap(), tsel.ap(), -BIGM, mhi.ap(), ALU.add, ALU.add),
                 reads=[tsel, mhi], writes=[m])
            blkst[bi] = (kt_, vb_, m)

        steps = [(bi, jj) for bi in range(nb) for jj in range(NB)]

        def emit_logits(si):
            bi, jj = steps[si]
            kt_, vb_, m = blkst[bi]
            bLs = [bank6(), bank6()]
            for kvn in range(2):
                P.op("tensor", lambda e: e.matmul(bLs[kvn].ap(), kt_.ap()[kvn * 64:(kvn + 1) * 64, jj, :],
                                                  QT.ap()[kvn * 64:(kvn + 1) * 64].rearrange("p j t -> p (j t)"),
                                                  start=True, stop=False), reads=[kt_, QT], writes=[bLs[kvn]])
            pts = []
            for kvn in range(2):
                P.op("tensor", lambda e: e.matmul(bLs[kvn].ap(), m.ap()[:, jj * 128:(jj + 1) * 128],
                                                  ident4.ap().rearrange("p j t -> p (j t)"), start=False, stop=True),
                     reads=[m, ident4], writes=[bLs[kvn]])
                pt = PT[(2 * si + kvn) % 4]
                P.op("scalar", lambda e: e.activation(pt.ap(), bLs[kvn].ap(), AF.Exp, scale=0.125), reads=[bLs[kvn]], writes=[pt])
                pts.append(pt)
            return pts

        def emit_pv(si, pts):
            bi, jj = steps[si]
            kt_, vb_, m = blkst[bi]
            first = (bi == 0 and jj == 0)
            last = (bi == nb - 1 and jj == NB - 1)
            for kvn in range(2):
                P.op("tensor", lambda e: e.matmul(acc[kvn].ap(), vb_.ap()[:, jj, kvn, :], pts[kvn].ap(), start=first, stop=last),
                     reads=[vb_, pts[kvn]], writes=[acc[kvn]])

        pend = None
        for si in range(len(steps)):
            bi, jj = steps[si]
            if jj == 0:
                prep_block(bi)
            pts = emit_logits(si)
            if pend is not None:
                emit_pv(*pend)
            pend = (si, pts)
        emit_pv(*pend)
        bg = [bank6(), bank6()]
        for hd in range(8):
            B.proj_fm(bg[hd // 4].ap()[0:64, (hd % 4) * 128:(hd % 4 + 1) * 128], bg[hd // 4], w,
                      boff["dsag"][0] - c0 + hd * 64, 64, hT, ls)
        for hf in range(2):
            P.op("scalar", lambda e: e.activation(sgd.ap()[:, hf * 4:(hf + 1) * 4, :],
                                                  bg[hf].ap()[0:64, :].rearrange("p (a b) -> p a b", a=4), AF.Silu),
                 reads=[bg[hf]], writes=[sgd])
        for kvn in range(2):
            P.op("vector", lambda e: e.reciprocal(rc.ap(), acc[kvn].ap()[64:128, :]), reads=[acc[kvn]], writes=[rc])
            P.op("vector", lambda e: e.tensor_tensor(on.ap(), acc[kvn].ap()[0:64, :], rc.ap(), ALU.mult),
                 reads=[acc[kvn], rc], writes=[on])
            P.op("gpsimd", lambda e: e.tensor_tensor(dsaout.ap()[:, ls, kvn * 4:(kvn + 1) * 4, :],
                                                     on.ap().rearrange("p (a b) -> p a b", a=4),
                                                     sgd.ap()[:, kvn * 4:(kvn + 1) * 4, :], ALU.mult),
                 reads=[on, sgd], writes=[dsaout])
    B.bank = B_bank
    P.pop_scope()
    P.push_scope()
    w, c0 = env["load_w"](["mg1"], "w_mg1")
    wbr = env["load_w2"](env["wbr_d"], 64, 8, "wbr_dsa")
    mg = P.sb("d_mg", [128, 512], F32)
    t2 = P.sb("d_t2", [128, 512], F32)
    for s in range(g0, g0 + SG):
        ls = s - g0
        env["merge"](lambda nc, out_ap, bk: [P.op("tensor", lambda e, hd=hd: e.matmul(
            out_ap, wbr.ap()[:, hd, nc * 128:(nc + 1) * 128], dsaout.ap()[:, ls, hd, :],
            start=(hd == 0), stop=(hd == 7)), reads=[wbr, dsaout], writes=[bk]) for hd in range(8)],
            w, 0, ls, False, mg, t2)
    P.pop_scope()


B_CONSTS = ["ident", "triL128", "triL64", "triU64", "DTret", "DTgla", "QDret", "decR", "pw2"]


def host_inputs_B(inp, cfg, l, xcur, resA):
    pos = np.asarray(inp["positions"])[0].reshape(cfg.S // 128, 128)
    st = lambda k: np.stack([np.asarray(r[k]) for r in resA])
    er = lambda w, h: np.ascontiguousarray(np.asarray(w)[l].reshape(h, 512 // h, D).transpose(1, 0, 2))
    shared = {
        "WB": kc_layout(np.asarray(inp["w_in"])[l][:, b_col_index()]),
        "wbr_r": er(inp["w_br_ret"], 4), "wbr_g": er(inp["w_br_gla"], 4), "wbr_d": er(inp["w_br_dsa"], 8),
        "wout": kc_layout(np.asarray(inp["w_out"])[l]),
        "wlr": np.asarray(inp["gla_w_lr"])[l],
        "blr": np.asarray(inp["gla_b_lr"])[l][None, :],
        "KTa": st("KT"), "Va": st("V"), "IKa": st("IK"),
        "kvRa": st("kvR"), "kvGa": st("kvG"), "decGa": st("decG"),
    }
    shared.update(const_inputs(B_CONSTS))
    maps = []
    for c in range(cfg.NCORE):
        tl = core_tiles(cfg, c)
        m = dict(shared)
        m["x"] = np.ascontiguousarray(xcur[tl])
        m["modv"] = np.asarray(resA[c]["modv"])
        m["ropev"] = np.asarray(resA[c]["ropev"])
        sel = np.zeros((128, cfg.NCORE), np.float32)
        sel[:, c] = 1.0
        m["sel"] = sel
        kidx = np.arange(cfg.GK)[None, :]
        qidx = (c * 128 + np.arange(128))[:, None]
        import ml_dtypes
        m["pen"] = np.where(kidx > qidx, np.float32(-1e30), np.float32(0.0)).astype(ml_dtypes.bfloat16)
        maps.append(m)
    return maps


_CACHE = {}


def run_layers(inp, cfg):
    x = np.asarray(inp["x"])[0].reshape(cfg.S // 128, 128, D).astype(np.float32)
    cores = list(range(cfg.NCORE))
    for l in range(cfg.DEPTH):
        if "A" not in _CACHE:
            _CACHE["A"] = build_A(cfg)
        inp_l = dict(inp)
        inp_l["x"] = x.reshape(1, cfg.S, D)
        resA = run_bass_kernel_spmd(_CACHE["A"].nc, host_inputs_A(inp_l, cfg, l), core_ids=cores).results
        if "B" not in _CACHE:
            _CACHE["B"] = build_B(cfg)
        resB = run_bass_kernel_spmd(_CACHE["B"].nc, host_inputs_B(inp, cfg, l, x, resA), core_ids=cores).results
        xn = np.empty_like(x)
        for c in cores:
            xn[core_tiles(cfg, c)] = np.asarray(resB[c]["xo"])
        x = xn
    return x.reshape(1, cfg.S, D)


def kernel(**inputs):
    cfg = Cfg()
    return run_layers(inputs, cfg).astype(np.float32)
```
